# Optimizing a Trainium2 kernel written in Bass

```python
import math
import jax, jax.numpy as jnp
from jax import lax
import numpy as np

D_MODEL = 1024
BATCH = 4
SEQ = 8192
DEPTH = 1
DEC_BATCH = 32
DEC_SEQ = 4
PAST_LEN = 16384
PAGE_SIZE = 128

DN_HEADS = 4
DN_DK = 128
DN_DV = 128
CONV_W = 4
DN_CHUNK = 64
DN_QK = DN_HEADS * DN_DK
DN_V = DN_HEADS * DN_DV
CONV_DIM = 2 * DN_QK + DN_V
SW_GROUPS = ((128, 1), (512, 4), (2048, 16))
SW_HEADS = 4
SW_DH = 64
SW_W = SW_HEADS * SW_DH
SW_BLOCK = 128
IN_SIZES = (DN_QK, DN_QK, DN_V, DN_HEADS, DN_HEADS, DN_V) + (SW_W,) * (3 * len(SW_GROUPS))
IN_DIM = sum(IN_SIZES)
MIX_OUT = DN_V + SW_W
MEM_LEN = 256
MEM_HEADS = 4
MEM_DH = D_MODEL // MEM_HEADS
PEER_HEADS = 8
PEER_NKEYS = 128
PEER_N = PEER_NKEYS * PEER_NKEYS
PEER_DQ = 256
PEER_DHALF = PEER_DQ // 2
PEER_TOPK = 16
PEER_BLOCK = 128
NORM_EPS = 1e-6

kernel_name = 'hymba_deltanet_dilated_peer_step'

F32 = jnp.float32


def rmsnorm(x, g):
    xf = x.astype(F32)
    y = xf * lax.rsqrt(jnp.mean(xf * xf, axis=-1, keepdims=True) + NORM_EPS)
    return (y * g.astype(F32)).astype(x.dtype)


def l2norm(x):
    xf = x.astype(F32)
    return xf * lax.rsqrt(jnp.sum(xf * xf, axis=-1, keepdims=True) + NORM_EPS)


def short_conv(x, buf, w):
    t = x.shape[1]
    xp = jnp.concatenate([buf.astype(x.dtype), x], axis=1)
    y = xp[:, 0:t] * w[0]
    for j in range(1, CONV_W):
        y = y + xp[:, j:j + t] * w[j]
    return jax.nn.silu(y), xp[:, t:]


def gated_delta_rule(q, k, v, g, beta, s0):
    bn, t, nh, dk = q.shape
    dv = v.shape[-1]
    c = DN_CHUNK
    pad = (-t) % c
    nc = (t + pad) // c

    def prep(a):
        a = jnp.pad(a, [(0, 0), (0, pad)] + [(0, 0)] * (a.ndim - 2))
        a = a.reshape((bn, nc, c) + a.shape[2:])
        return jnp.moveaxis(a, 3, 2)

    q, k, v, g, beta = prep(q), prep(k), prep(v), prep(g), prep(beta)
    gc = jnp.cumsum(g, axis=-1)
    idx = jnp.arange(c)
    causal = idx[:, None] >= idx[None, :]
    strict = idx[:, None] > idx[None, :]
    gam = jnp.exp(jnp.where(causal, gc[..., :, None] - gc[..., None, :], -jnp.inf))
    kb = k * beta[..., None]
    lmat = jnp.where(strict, jnp.einsum('bnhik,bnhjk->bnhij', kb, k) * gam, 0.0)
    eye = jnp.broadcast_to(jnp.eye(c, dtype=F32), lmat.shape)
    rhs = jnp.concatenate([v * beta[..., None], kb * jnp.exp(gc)[..., None]], axis=-1)
    sol = lax.linalg.triangular_solve(eye + lmat, rhs, left_side=True, lower=True, unit_diagonal=True)
    u0, wk = sol[..., :dv], sol[..., dv:]
    aqk = jnp.where(causal, jnp.einsum('bnhik,bnhjk->bnhij', q, k) * gam, 0.0)
    qg = q * jnp.exp(gc)[..., None]
    kd = k * jnp.exp(gc[..., -1:] - gc)[..., None]
    dlast = jnp.exp(gc[..., -1])

    def step(s, xs):
        u0_c, w_c, aqk_c, qg_c, kd_c, dl_c = xs
        u = u0_c - jnp.einsum('bhck,bhkv->bhcv', w_c, s)
        o = jnp.einsum('bhck,bhkv->bhcv', qg_c, s) + jnp.einsum('bhij,bhjv->bhiv', aqk_c, u)
        s = s * dl_c[..., None, None] + jnp.einsum('bhck,bhcv->bhkv', kd_c, u)
        return s, o

    xs = (jnp.moveaxis(u0, 1, 0), jnp.moveaxis(wk, 1, 0), jnp.moveaxis(aqk, 1, 0),
          jnp.moveaxis(qg, 1, 0), jnp.moveaxis(kd, 1, 0), jnp.moveaxis(dlast, 1, 0))
    s_fin, o = lax.scan(step, s0, xs)
    o = jnp.moveaxis(o, 0, 1)
    o = jnp.moveaxis(o, 2, 3).reshape(bn, nc * c, nh, dv)[:, :t]
    return o, s_fin


def deltanet_group(qa, ka, va, ag, bg, z, conv_buf, s0, conv_w, a_log, dt_bias, g_onorm):
    bn, t, _ = qa.shape
    act, conv_new = short_conv(jnp.concatenate([qa, ka, va], axis=-1), conv_buf, conv_w)
    q, k, v = jnp.split(act, [DN_QK, 2 * DN_QK], axis=-1)
    q = l2norm(q.reshape(bn, t, DN_HEADS, DN_DK)) * (DN_DK ** -0.5)
    k = l2norm(k.reshape(bn, t, DN_HEADS, DN_DK))
    v = v.reshape(bn, t, DN_HEADS, DN_DV).astype(F32)
    beta = jax.nn.sigmoid(bg.astype(F32))
    g = -jnp.exp(a_log.astype(F32)) * jax.nn.softplus(ag.astype(F32) + dt_bias.astype(F32))
    o, s_new = gated_delta_rule(q, k, v, g, beta, s0.astype(F32))
    o = rmsnorm(o, g_onorm) * jax.nn.silu(z.reshape(bn, t, DN_HEADS, DN_DV).astype(F32))
    return o.reshape(bn, t, DN_V).astype(qa.dtype), conv_new, s_new.astype(qa.dtype)


def dilated_prompt(q, k, v, window, dil):
    bn, t, nh, dh = q.shape
    ln = t // dil
    nb = -(-ln // SW_BLOCK)
    lp = nb * SW_BLOCK
    span = window // dil

    def sub(a):
        a = a.reshape(bn, ln, dil, nh, dh).transpose(0, 2, 1, 3, 4)
        a = jnp.pad(a, ((0, 0), (0, 0), (0, lp - ln), (0, 0), (0, 0)))
        return a.reshape(bn, dil, nb, SW_BLOCK, nh, dh)

    def band(a):
        prev = jnp.pad(a, ((0, 0), (0, 0), (1, 0), (0, 0), (0, 0), (0, 0)))[:, :, :-1]
        return jnp.concatenate([prev, a], axis=3)

    qs = sub(q)
    kk, vv = band(sub(k)), band(sub(v))
    s = jnp.einsum('brnqhc,brnkhc->brnhqk', qs, kk, preferred_element_type=F32) * (dh ** -0.5)
    qi = jnp.arange(SW_BLOCK)[:, None]
    ki = jnp.arange(2 * SW_BLOCK)[None, :]
    dist = SW_BLOCK + qi - ki
    kpos = (jnp.arange(nb)[:, None, None] - 1) * SW_BLOCK + ki[None]
    mask = (dist >= 0) & (dist <= span) & (kpos >= 0)
    s = jnp.where(mask[None, None, :, None], s, -jnp.inf)
    lse = jax.nn.logsumexp(s, axis=-1)
    p = jnp.exp(s - lse[..., None])
    o = jnp.einsum('brnhqk,brnkhc->brnqhc', p, vv.astype(F32))
    o = o.reshape(bn, dil, lp, nh, dh)[:, :, :ln].transpose(0, 2, 1, 3, 4).reshape(bn, t, nh, dh)
    lse = lse.transpose(0, 1, 2, 4, 3).reshape(bn, dil, lp, nh)[:, :, :ln]
    lse = lse.transpose(0, 2, 1, 3).reshape(bn, t, nh)
    return o, lse


def dilated_sample(q, k, v, buf, window, dil):
    bn, tn, nh, dh = q.shape
    wb = buf.shape[1]
    kc = jnp.concatenate([buf[:, :, 0].astype(k.dtype), k], axis=1)
    vc = jnp.concatenate([buf[:, :, 1].astype(v.dtype), v], axis=1)
    m = jnp.arange(window // dil + 1)
    j = wb + jnp.arange(tn)[:, None] - m[None, :] * dil
    valid = j >= 0
    jc = jnp.clip(j, 0)
    kg = kc[:, jc]
    vg = vc[:, jc]
    s = jnp.einsum('bthc,btmhc->bhtm', q, kg, preferred_element_type=F32) * (dh ** -0.5)
    s = jnp.where(valid[None, None], s, -jnp.inf)
    lse = jax.nn.logsumexp(s, axis=-1)
    p = jnp.exp(s - lse[..., None])
    o = jnp.einsum('bhtm,btmhc->bthc', p, vg.astype(F32))
    new_buf = jnp.stack([kc, vc], axis=2)[:, -min(window, wb + tn):]
    return o, lse.transpose(0, 2, 1), new_buf


def token_mixer(a, conv_buf, s0, wins, w_in, conv_w, a_log, dt_bias, g_onorm, w_out):
    bn, t, _ = a.shape
    proj = a @ w_in
    parts = jnp.split(proj, [int(o) for o in np.cumsum(IN_SIZES)[:-1]], axis=-1)
    qa, ka, va, ag, bg, z = parts[:6]
    sw = parts[6:]
    if conv_buf is None:
        conv_buf = jnp.zeros((bn, CONV_W - 1, CONV_DIM), a.dtype)
    if s0 is None:
        s0 = jnp.zeros((bn, DN_HEADS, DN_DK, DN_DV), F32)
    o_dn, conv_new, s_new = deltanet_group(qa, ka, va, ag, bg, z, conv_buf, s0,
                                           conv_w, a_log, dt_bias, g_onorm)
    outs, lses, wins_new = [], [], []
    for gi, (win, dil) in enumerate(SW_GROUPS):
        q, k, v = [p.reshape(bn, t, SW_HEADS, SW_DH) for p in sw[3 * gi:3 * gi + 3]]
        if wins is None:
            o, l = dilated_prompt(q, k, v, win, dil)
            nbuf = jnp.stack([k, v], axis=2)[:, -min(win, t):]
        else:
            o, l, nbuf = dilated_sample(q, k, v, wins[gi], win, dil)
        outs.append(o)
        lses.append(l)
        wins_new.append(nbuf)
    wgt = jax.nn.softmax(jnp.stack(lses, axis=0), axis=0)
    o_sw = jnp.sum(wgt[..., None] * jnp.stack(outs, axis=0), axis=0).reshape(bn, t, SW_W)
    y = jnp.concatenate([o_dn, o_sw.astype(a.dtype)], axis=-1) @ w_out
    return y, conv_new, s_new, wins_new


def memory_kv(mem, g_memkv, w_mkv):
    bn = mem.shape[0]
    return (rmsnorm(mem, g_memkv) @ w_mkv).reshape(bn, MEM_LEN, 2, MEM_HEADS, MEM_DH)


def mem_attend(c, kv, w_mq, w_mo):
    bn, t, _ = c.shape
    q = (c @ w_mq).reshape(bn, t, MEM_HEADS, MEM_DH)
    s = jnp.einsum('bthc,bmhc->bhtm', q, kv[:, :, 0], preferred_element_type=F32) * (MEM_DH ** -0.5)
    p = jax.nn.softmax(s, axis=-1)
    o = jnp.einsum('bhtm,bmhc->bthc', p, kv[:, :, 1].astype(F32))
    return o.reshape(bn, t, MEM_HEADS * MEM_DH).astype(c.dtype) @ w_mo


def peer(f, w_pq, sub_keys, expert_u, expert_v):
    bn, t, d = f.shape
    n = bn * t
    pad = (-n) % PEER_BLOCK
    flat = jnp.pad(f.reshape(n, d), ((0, pad), (0, 0))).reshape(-1, PEER_BLOCK, d)

    def block(xb):
        q = (xb @ w_pq).reshape(PEER_BLOCK, PEER_HEADS, 2, PEER_DHALF)
        s = jnp.einsum('thpc,hpnc->thpn', q, sub_keys, preferred_element_type=F32)
        sv, si = lax.top_k(s, PEER_TOPK)
        cand = sv[:, :, 0, :, None] + sv[:, :, 1, None, :]
        cv, ci = lax.top_k(cand.reshape(PEER_BLOCK, PEER_HEADS, PEER_TOPK * PEER_TOPK), PEER_TOPK)
        i1 = jnp.take_along_axis(si[:, :, 0], ci // PEER_TOPK, axis=-1)
        i2 = jnp.take_along_axis(si[:, :, 1], ci % PEER_TOPK, axis=-1)
        eid = i1 * PEER_NKEYS + i2
        gate = jax.nn.softmax(cv, axis=-1)
        hid = jax.nn.gelu(jnp.einsum('thkd,td->thk', expert_u[eid], xb, preferred_element_type=F32))
        out = jnp.einsum('thk,thkd->td', gate * hid, expert_v[eid].astype(F32))
        return out.astype(f.dtype)

    out = lax.map(block, flat)
    return out.reshape(-1, d)[:n].reshape(bn, t, d)


def layer(h, conv_buf, s0, wins, kv, g_mix, w_in, conv_w, a_log, dt_bias, g_onorm, w_out,
          g_memq, w_mq, w_mo, g_ffn, w_pq, sub_keys, expert_u, expert_v):
    mix, conv_new, s_new, wins_new = token_mixer(rmsnorm(h, g_mix), conv_buf, s0, wins, w_in,
                                                 conv_w, a_log, dt_bias, g_onorm, w_out)
    h = h + mix
    h = h + mem_attend(rmsnorm(h, g_memq), kv, w_mq, w_mo)
    h = h + peer(rmsnorm(h, g_ffn), w_pq, sub_keys, expert_u, expert_v)
    return h, conv_new, s_new, wins_new


def setup_inputs(seed: int = 0) -> dict:
    key = jax.random.key(seed)
    ks = iter(jax.random.split(key, 40))

    def nrm(shape, scale):
        return jax.random.normal(next(ks), shape, F32) * scale

    def gain(shape):
        return 1.0 + nrm(shape, 0.02)

    wb = [min(w, PAST_LEN) for w, _ in SW_GROUPS]
    a_log = jnp.log(jax.random.uniform(next(ks), (DEPTH, DN_HEADS), F32, 1.0, 16.0))
    dt = jnp.exp(jax.random.uniform(next(ks), (DEPTH, DN_HEADS), F32, math.log(1e-3), math.log(1e-1)))
    dt_bias = dt + jnp.log(-jnp.expm1(-dt))
    return {
        'x_prompt': nrm((BATCH, SEQ, D_MODEL), 1.0),
        'x_sample': nrm((DEC_BATCH, DEC_SEQ, D_MODEL), 1.0),
        'state_delta': nrm((DEPTH, DEC_BATCH, DN_HEADS, DN_DK, DN_DV), 0.1),
        'state_conv': nrm((DEPTH, DEC_BATCH, CONV_W - 1, CONV_DIM), 1.0),
        'cache_win1': nrm((DEPTH, DEC_BATCH, wb[0], 2, SW_HEADS, SW_DH), 1.0),
        'cache_win2': nrm((DEPTH, DEC_BATCH, wb[1], 2, SW_HEADS, SW_DH), 1.0),
        'cache_win3': nrm((DEPTH, DEC_BATCH, wb[2], 2, SW_HEADS, SW_DH), 1.0),
        'cache_mem_kv': nrm((DEPTH, DEC_BATCH, MEM_LEN, 2, MEM_HEADS, MEM_DH), 1.0),
        'mem_prompt': nrm((BATCH, MEM_LEN, D_MODEL), 1.0),
        'g_mix': gain((DEPTH, D_MODEL)),
        'w_in': nrm((DEPTH, D_MODEL, IN_DIM), D_MODEL ** -0.5),
        'conv_w': nrm((DEPTH, CONV_W, CONV_DIM), CONV_W ** -0.5),
        'a_log': a_log,
        'dt_bias': dt_bias,
        'g_onorm': gain((DEPTH, DN_DV)),
        'w_out': nrm((DEPTH, MIX_OUT, D_MODEL), MIX_OUT ** -0.5),
        'g_memq': gain((DEPTH, D_MODEL)),
        'g_memkv': gain((DEPTH, D_MODEL)),
        'w_mq': nrm((DEPTH, D_MODEL, MEM_HEADS * MEM_DH), D_MODEL ** -0.5),
        'w_mkv': nrm((DEPTH, D_MODEL, 2 * MEM_HEADS * MEM_DH), D_MODEL ** -0.5),
        'w_mo': nrm((DEPTH, MEM_HEADS * MEM_DH, D_MODEL), (MEM_HEADS * MEM_DH) ** -0.5),
        'g_ffn': gain((DEPTH, D_MODEL)),
        'w_pq': nrm((DEPTH, D_MODEL, PEER_HEADS * PEER_DQ), D_MODEL ** -0.5),
        'sub_keys': nrm((DEPTH, PEER_HEADS, 2, PEER_NKEYS, PEER_DHALF), PEER_DHALF ** -0.5),
        'expert_u': nrm((DEPTH, PEER_N, D_MODEL), D_MODEL ** -0.5),
        'expert_v': nrm((DEPTH, PEER_N, D_MODEL), 0.5 * PEER_HEADS ** -0.5),
        'g_final': gain((D_MODEL,)),
    }


def reference(x_prompt, x_sample, state_delta, state_conv, cache_win1, cache_win2, cache_win3,
              cache_mem_kv, mem_prompt, g_mix, w_in, conv_w, a_log, dt_bias, g_onorm, w_out,
              g_memq, g_memkv, w_mq, w_mkv, w_mo, g_ffn, w_pq, sub_keys, expert_u, expert_v,
              g_final):
    per_layer = (g_mix, w_in, conv_w, a_log, dt_bias, g_onorm, w_out, g_memq, w_mq, w_mo,
                 g_ffn, w_pq, sub_keys, expert_u, expert_v)
    h = x_prompt
    p_conv, p_delta, p_win, p_mem = [], [], [], []
    for l in range(DEPTH):
        kv = memory_kv(mem_prompt, g_memkv[l], w_mkv[l])
        h, c_new, s_new, w_new = layer(h, None, None, None, kv, *[p[l] for p in per_layer])
        p_conv.append(c_new)
        p_delta.append(s_new)
        p_win.append(w_new)
        p_mem.append(kv)
    y_prompt = rmsnorm(h, g_final)
    h = x_sample
    s_conv, s_delta, s_win = [], [], []
    for l in range(DEPTH):
        wins = (cache_win1[l], cache_win2[l], cache_win3[l])
        h, c_new, s_new, w_new = layer(h, state_conv[l], state_delta[l], wins, cache_mem_kv[l],
                                       *[p[l] for p in per_layer])
        s_conv.append(c_new)
        s_delta.append(s_new)
        s_win.append(w_new)
    y_sample = rmsnorm(h, g_final)
    return (y_prompt, y_sample,
            jnp.stack(p_delta), jnp.stack(p_conv),
            jnp.stack([w[0] for w in p_win]), jnp.stack([w[1] for w in p_win]),
            jnp.stack([w[2] for w in p_win]), jnp.stack(p_mem),
            jnp.stack(s_delta), jnp.stack(s_conv),
            jnp.stack([w[0] for w in s_win]), jnp.stack([w[1] for w in s_win]),
            jnp.stack([w[2] for w in s_win]))
```

```python
import numpy as np
from contextlib import ExitStack
import concourse.bass as bass
import concourse.mybir as mybir
from concourse.bass_utils import run_bass_kernel_spmd

F32 = mybir.dt.float32
BF16 = mybir.dt.bfloat16
I32 = mybir.dt.int32
U32 = mybir.dt.uint32
AF = mybir.ActivationFunctionType
ALU = mybir.AluOpType
AX = mybir.AxisListType

NCORES = 8
D = 1024
PRE = 4096
MAIN = 4096
EXT = PRE + MAIN
HALO = 2048
KR = HALO + MAIN
NS = 4
TS = 4
IN_DIM = 4360
CQ, CK, CV, CAG, CBG, CZ, CSW = 0, 512, 1024, 1536, 1540, 1544, 2056
DILS = (1, 4, 16)
EPS = 1e-6
NEG = -30000.0
SEM_EPOCH = 20000
import os as _os
EXPROWS = int(_os.environ.get("EXPROWS", "16384"))


class TL:
    def __init__(self, t, name):
        self.t = t
        self.name = name
        self.lw = None
        self.rd = []
        self.ds = None

    def __getitem__(self, k):
        return self.t[k]


class TLsub(TL):
    def __init__(self, bank, off, width, name):
        self.bank = bank
        self.t = bank.t
        self.name = name
        self.off = off
        self.width = width
        self.ds = None

    lw = property(lambda self: self.bank.lw, lambda self, v: setattr(self.bank, "lw", v))
    rd = property(lambda self: self.bank.rd, lambda self, v: setattr(self.bank, "rd", v))

    def __getitem__(self, key):
        r, c = key
        if isinstance(c, slice):
            a = self.off + (c.start or 0)
            b_ = self.off + (self.width if c.stop is None else c.stop)
            return self.t[r, a:b_]
        return self.t[r, self.off + c]


class PsumBlocks:
    def __init__(self, nc, es, prefix):
        self.nc, self.es, self.prefix = nc, es, prefix
        self.banks = {}

    def get(self, name, width, dt, bank):
        per = 2048 // (4 if dt == F32 else 2)
        if bank not in self.banks:
            t = self.es.enter_context(self.nc.psum_tensor(f"{self.prefix}{bank}", [128, per], dt))
            self.banks[bank] = [TL(t, f"{self.prefix}{bank}"), 0]
        b = self.banks[bank]
        assert b[1] + width <= per
        off = b[1]
        b[1] += width
        return TLsub(b[0], off, width, name)


class _Rec:
    def __init__(self):
        self.call = None

    def __getattr__(self, name):
        def f(*a, **kw):
            self.call = (name, a, kw)
            return self
        return f


def _capture(fn):
    r = _Rec()
    fn(r)
    name, a, kw = r.call
    return lambda e: getattr(e, name)(*a, **kw)


class Sched:
    ENG = ("pe", "act", "dve", "pool", "sp")

    def __init__(self, nc):
        self.nc = nc
        self.eng = {"pe": nc.tensor, "act": nc.scalar, "dve": nc.vector, "pool": nc.gpsimd, "sp": nc.sync}
        self.rec = []
        self.dsems = []
        self.dsem_pool = []
        self.ninst = 0
        self.last = {e: None for e in self.ENG}

    def dsem(self, name):
        s = [self.nc.alloc_semaphore("d_" + name), 0, False]
        self.dsems.append(s)
        return s

    def _collect(self, e, r, w, is_dma):
        deps = []
        for t in r:
            if t.lw is not None:
                deps.append(t.lw)
        for t in w:
            if t.lw is not None:
                p = self.rec[t.lw]
                if is_dma or not (p["kind"] == "op" and p["e"] == e and e == "pe"):
                    deps.append(t.lw)
            for d in t.rd:
                p = self.rec[d]
                if is_dma or p["kind"] == "dma" or p["e"] != e or e != "pe":
                    deps.append(d)
        return deps

    def op(self, e, fn, r=(), w=()):
        deps = self._collect(e, r, w, False)
        idx = len(self.rec)
        self.rec.append({"kind": "op", "e": e, "fn": _capture(fn), "deps": deps})
        self.last[e] = idx
        for t in w:
            t.lw = idx
            t.rd = []
        for t in r:
            t.rd.append(idx)
        self.ninst += 1

    def dma(self, q, t, fn, r=(), w=()):
        if q == "pool" and (t.ds is None or not t.ds[2]):
            t.ds = self.dsem(t.name + "_sw")
            t.ds[2] = True
        if t.ds is None:
            t.ds = self.dsem_pool.pop() if self.dsem_pool else self.dsem(t.name)
        ds = t.ds
        deps = self._collect(q, r, w, True)
        if ds[1] + 16 >= 32000:
            sw_ = ds[2]
            t.ds = self.dsem(t.name + f"_r{len(self.dsems)}")
            t.ds[2] = sw_
            ds = t.ds
        ds[1] += 16
        idx = len(self.rec)
        self.rec.append({"kind": "dma", "e": q, "fn": _capture(fn), "deps": deps, "sem": ds[0], "val": ds[1]})
        for x in w:
            x.lw = idx
            x.rd = []
        for x in r:
            x.rd.append(idx)
        self.ninst += 1

    def release(self, tiles):
        for t in tiles:
            if t.ds is not None:
                self.dsem_pool.append(t.ds)
                t.ds = None

    def barrier(self, engines=None):
        deps = [v for v in self.last.values() if v is not None]
        dm = [(s[0], s[1]) for s in self.dsems if s[1]]
        self.rec.append({"kind": "bar", "deps": deps, "dm": dm, "engines": engines or self.ENG})

    def final_wait(self):
        self.barrier(engines=("sp",))

    def emit(self):
        rec = self.rec
        seq = {}
        cnt = {e: 0 for e in self.ENG}
        for i, r in enumerate(rec):
            if r["kind"] == "op":
                cnt[r["e"]] += 1
                seq[i] = cnt[r["e"]]
        awaited = set()

        def sweep(do_emit, ordv=None, sems=None):
            wseq = {e: {p: 0 for p in self.ENG} for e in self.ENG}
            wdma = {e: {} for e in self.ENG}
            for i, r in enumerate(rec):
                targets = r["engines"] if r["kind"] == "bar" else (r["e"],)
                for e in targets:
                    need = {}
                    for d in r["deps"]:
                        p = rec[d]
                        if p["kind"] == "op":
                            if seq[d] > wseq[e][p["e"]] and seq[d] > need.get(p["e"], (0, None))[0]:
                                need[p["e"]] = (seq[d], d)
                        else:
                            key = id(p["sem"])
                            if wdma[e].get(key, 0) < p["val"]:
                                wdma[e][key] = p["val"]
                                if do_emit:
                                    self.eng[e].wait_ge(p["sem"], p["val"])
                    for (sem, val) in r.get("dm", ()):
                        key = id(sem)
                        if wdma[e].get(key, 0) < val:
                            wdma[e][key] = val
                            if do_emit:
                                self.eng[e].wait_ge(sem, val)
                    for pe_, (sq, d) in need.items():
                        wseq[e][pe_] = sq
                        if do_emit:
                            o = ordv[d]
                            self.eng[e].wait_ge(sems[pe_][(o - 1) // SEM_EPOCH], (o - 1) % SEM_EPOCH + 1)
                        else:
                            awaited.add(d)
                if do_emit and r["kind"] != "bar":
                    ins = r["fn"](self.eng[r["e"]])
                    if r["kind"] == "dma":
                        ins.then_inc(r["sem"], 16)
                    elif i in awaited:
                        o = ordv[i]
                        ins.then_inc(sems[r["e"]][(o - 1) // SEM_EPOCH], 1)

        sweep(False)
        ordv = {}
        oc = {e: 0 for e in self.ENG}
        for i, r in enumerate(rec):
            if r["kind"] == "op" and i in awaited:
                oc[r["e"]] += 1
                ordv[i] = oc[r["e"]]
        sems = {e: [self.nc.alloc_semaphore(f"s_{e}_{j}") for j in range((oc[e] + SEM_EPOCH - 1) // SEM_EPOCH)] for e in self.ENG}
        self.nsig = dict(oc)
        sweep(True, ordv, sems)


class K:
    def __init__(self):
        self.tiles = []

    def track(self, t):
        self.tiles.append(t)
        return t

    def end_phase(self):
        self.S.barrier()
        self.S.release(self.tiles)
        self.tiles = []


def make_consts():
    c = {}
    idx = np.arange(128)
    same = (idx[:, None] // 64) == (idx[None, :] // 64)
    c["ident"] = np.eye(128, dtype=np.float32)
    c["trit"] = ((idx[:, None] <= idx[None, :]) & same).astype(np.float32)
    c["blk"] = same.astype(np.float32)
    c["mTneg"] = np.where((idx[None, :] >= idx[:, None]) & same, 0.0, NEG).astype(np.float32)
    c["mSpos"] = np.where((idx[:, None] > idx[None, :]) & same, 0.0, -NEG).astype(np.float32)
    c["swprev"] = np.where(idx[:, None] >= idx[None, :], 0.0, NEG).astype(np.float32)
    c["swcur"] = np.where(idx[:, None] <= idx[None, :], 0.0, NEG).astype(np.float32)
    c["ones"] = np.ones((128, 128), np.float32)
    c["iota"] = np.tile(np.arange(128, dtype=np.float32), (128, 1))
    return c


CONST_NAMES = ("ident", "trit", "blk", "mTneg", "mSpos", "swprev", "swcur", "ones", "iota")


def declare_io(k, debug):
    nc = k.nc
    I = lambda n, s, dt=F32: nc.dram_tensor(n, list(s), dt, kind="ExternalInput")
    O = lambda n, s, dt=F32: nc.dram_tensor(n, list(s), dt, kind="ExternalOutput")
    SCR = (lambda n, s, dt: nc.dram_tensor(n, list(s), dt, kind="ExternalOutput")) if debug else \
          (lambda n, s, dt: nc.dram_tensor(n, list(s), dt, kind="Internal"))
    k.xe = I("xe", [EXT, D])
    k.xs = I("xs", [NS * TS, D])
    k.st_delta = I("st_delta", [NS, 4, 128, 128])
    k.st_conv = I("st_conv", [NS, 3, 1536])
    k.cwin = [I(f"cwin{g}", [NS, 128 * DILS[g], 512]) for g in range(3)]
    k.cmem = I("cmem", [NS, 256, 2048])
    k.mem = I("mem", [256, D])
    k.halo = I("halo", [128, 128])
    k.consts = I("consts", [len(CONST_NAMES), 128, 128])
    for n, s in (("g_mix", [D]), ("w_in", [D, IN_DIM]), ("conv_w", [4, 1536]), ("a_log", [4]), ("dt_bias", [4]),
                 ("g_onorm", [128]), ("w_out", [768, D]), ("g_memq", [D]), ("g_memkv", [D]), ("w_mq", [D, D]),
                 ("w_mkv", [D, 2048]), ("w_mo", [D, D]), ("g_ffn", [D]), ("w_pq", [D, 2048]),
                 ("sub_keys", [16, 128, 128]), ("expert_u", [EXPROWS, D]), ("expert_v", [EXPROWS, D]), ("g_final", [D])):
        setattr(k, n, I(n, s))
    k.y_main = O("y_main", [MAIN, D])
    k.y_smp = O("y_smp", [NS * TS, D])
    k.p_delta = O("p_delta", [4, 128, 128])
    k.p_conv = O("p_conv", [3, 1536])
    k.p_win = [O(f"p_win{g}", [128 * DILS[g], 512]) for g in range(3)]
    k.p_mem = O("p_mem", [256, 2048])
    k.s_delta = O("s_delta", [NS, 4, 128, 128])
    k.s_conv = O("s_conv", [NS, 3, 1536])
    k.s_win = [O(f"s_win{g}", [NS, 128 * DILS[g], 512]) for g in range(3)]
    k.dnq = SCR("dnq", [4, 128, EXT], BF16)
    k.dnk = SCR("dnk", [4, 128, EXT], BF16)
    k.dnv = SCR("dnv", [4, 128, EXT], BF16)
    k.gbs = SCR("gbs", [EXT, 8], F32)
    k.zs = SCR("zs", [MAIN, 512], BF16)
    k.qts = [SCR(f"qts{g}", [2, 128, MAIN], BF16) for g in range(3)]
    k.kts = [SCR(f"kts{g}", [2, 128, KR], BF16) for g in range(3)]
    k.vss = [SCR(f"vss{g}", [KR, 256], BF16) for g in range(3)]
    SWO = (lambda n, s, dt: nc.dram_tensor(n, list(s), dt, kind="ExternalOutput")) if _os.environ.get("SWO_OUT") else SCR
    k.swo = [SWO(f"swo{g}", [MAIN, 260], F32) for g in range(3)]
    k.cat = SCR("cat", [MAIN, 512], BF16)


def evac(S, i, out_t, out_ap, in_t, in_ap, rx=(), **kw):
    if i % 2 == 0 and len(out_ap.shape) == 2 and len(in_ap.shape) == 2:
        S.op("act", lambda e: e.activation(out=out_ap, in_=in_ap, func=AF.Copy, **kw), r=[in_t, *rx], w=[out_t])
    else:
        if "scale" in kw:
            S.op("dve", lambda e: e.tensor_scalar(out=out_ap, in0=in_ap, scalar1=kw["scale"], scalar2=None, op0=ALU.mult),
                 r=[in_t, *rx], w=[out_t])
        else:
            S.op("dve", lambda e: e.tensor_copy(out=out_ap, in_=in_ap), r=[in_t, *rx], w=[out_t])


def rsqrt(S, out_t, out_ap, in_t, in_ap, mul, add):
    S.op("act", lambda e: e.activation(out=out_ap, in_=in_ap, func=AF.Sqrt, scale=mul, bias=add), r=[in_t], w=[out_t])
    S.op("dve", lambda e: e.reciprocal(out=out_ap, in_=out_ap), r=[out_t], w=[out_t])


def load_weight_bf16(k, es, name, wdram, rows, cols, gdram=None, col_chunk=None):
    nc, S = k.nc, k.S
    nch = rows // 128
    wbf = TL(es.enter_context(nc.sbuf_tensor(name + "_bf", [128, nch, cols], BF16)), name)
    gcol = None
    if gdram is not None:
        gcol = k.track(TL(es.enter_context(nc.sbuf_tensor(name + "_g", [128, nch], F32)), name + "_g"))
        S.dma("sp", gcol, lambda e: e.dma_start(out=gcol[:, :], in_=gdram.ap().rearrange("(c p) -> p c", p=128),
                                                 allow_slow_non_contiguous=True), w=[gcol])
    cc = col_chunk or cols
    with ExitStack() as es2:
        st = [k.track(TL(es2.enter_context(nc.sbuf_tensor(f"{name}_st{i}", [128, cc], F32)), f"{name}_st{i}")) for i in range(2)]
        n = 0
        for c in range(nch):
            for c0 in range(0, cols, cc):
                w_ = min(cc, cols - c0)
                s_ = st[n % 2]
                S.dma("sp", s_, lambda e: e.dma_start(out=s_[:, 0:w_], in_=wdram[c * 128:(c + 1) * 128, c0:c0 + w_]), w=[s_])
                if gcol is not None:
                    evac(S, n, wbf, wbf[:, c, c0:c0 + w_], s_, s_[:, 0:w_], rx=[gcol], scale=gcol[:, c:c + 1])
                else:
                    evac(S, n, wbf, wbf[:, c, c0:c0 + w_], s_, s_[:, 0:w_])
                n += 1
        S.barrier()
    return wbf


def phase1(k):
    nc, S = k.nc, k.S
    with ExitStack() as es:
        A = lambda name, shape, dt: k.track(TL(es.enter_context(nc.sbuf_tensor(name, shape, dt)), name))
        P = lambda name, shape, dt: TL(es.enter_context(nc.psum_tensor(name, shape, dt)), name)
        wbf = load_weight_bf16(k, es, "w_in", k.w_in, D, IN_DIM, gdram=k.g_mix, col_chunk=2180)
        cw = A("cw", [128, 12, 4], F32)
        for j in range(4):
            S.dma("sp", cw, lambda e: e.dma_start(out=cw[:, :, j], in_=k.conv_w[j, :].rearrange("(c p) -> p c", p=128),
                                                  allow_slow_non_contiguous=True), w=[cw])
        dtb = A("dtb", [128, 4], F32)
        S.dma("sp", dtb, lambda e: e.dma_start(out=dtb[:, :], in_=k.dt_bias.ap().partition_broadcast(128)), w=[dtb])
        nega = A("nega", [128, 4], F32)
        S.dma("sp", nega, lambda e: e.dma_start(out=nega[:, :], in_=k.a_log.ap().partition_broadcast(128)), w=[nega])
        S.op("act", lambda e: e.activation(out=nega[:, :], in_=nega[:, :], func=AF.Exp), r=[nega], w=[nega])
        S.op("dve", lambda e: e.tensor_scalar(out=nega[:, :], in0=nega[:, :], scalar1=-1.0, scalar2=None, op0=ALU.mult), r=[nega], w=[nega])
        xt = [A(f"xt{i}", [128, D], F32) for i in range(2)]
        xsm = A("xsm", [128, D], F32)
        sqj = A("sqj", [128, D], BF16)
        ssq = [A(f"ssq{i}", [128, 1], F32) for i in range(2)]
        rstd = [A(f"rstd{i}", [128, 1], F32) for i in range(2)]
        ab = [A(f"ab{i}", [128, D], BF16) for i in range(2)]
        aT = [A(f"aT{i}", [128, 8, 512], BF16) for i in range(2)]
        xp = [A(f"xp{i}", [128, 515], F32) for i in range(2)]
        carry = A("carry", [128, 12, 3], F32)
        acc = [A(f"acc{i}", [128, 512], F32) for i in range(2)]
        act_ = [A(f"actt{i}", [128, 512], F32) for i in range(2)]
        sq2 = [A(f"sq2{i}", [128, 512], BF16) for i in range(2)]
        rn = [A(f"rn{i}", [128, 512], F32) for i in range(2)]
        outb = [A(f"outb{i}", [128, 512], BF16) for i in range(3)]
        qkp = [A(f"qkp{i}", [128, 512], BF16) for i in range(3)]
        gb = [A(f"gb{i}", [128, 8], F32) for i in range(2)]
        gtmp = [A(f"gtmp{i}", [128, 4], F32) for i in range(4)]
        zb = [A(f"zb{i}", [128, 512], BF16) for i in range(2)]
        vsb = [A(f"vsb{i}", [128, 256], BF16) for i in range(3)]
        kvf = [A(f"kvf{i}", [128, 512], F32) for i in range(3)]
        xps = A("xps", [128, 12, NS, 7], F32)
        ones_bf = k.ones_bf
        pT = [P(f"pT{i}", [128, 8, 128], BF16) for i in range(2)]
        pacc = [P(f"pacc{i}", [128, 512], F32) for i in range(2)]
        pl2 = P("pl2", [128, 512], F32)
        ptm = [P(f"ptm{i}", [128, 512], F32) for i in range(2)]
        pg = P("pg", [128, 8], F32)
        S.op("dve", lambda e: e.memset(carry[:, :, :], 0.0), w=[carry])
        S.op("dve", lambda e: e.memset(xsm[:, :], 0.0), w=[xsm])
        ctr = {"ev": 0, "acc": 0, "tm": 0, "ob": 0}

        def norm_transpose(xtile, i, aTt, col0):
            S.op("act", lambda e: e.activation(out=sqj[:, :], in_=xtile[:, :], func=AF.Square, accum_out=ssq[i][:, 0:1]),
                 r=[xtile], w=[sqj, ssq[i]])
            rsqrt(S, rstd[i], rstd[i][:, :], ssq[i], ssq[i][:, :], 1.0 / D, EPS)
            S.op("act", lambda e: e.activation(out=ab[i][:, :], in_=xtile[:, :], func=AF.Copy, scale=rstd[i][:, 0:1]),
                 r=[xtile, rstd[i]], w=[ab[i]])
            for c in range(8):
                S.op("pe", lambda e: e.transpose(out=pT[i][:, c, :], in_=ab[i][:, c * 128:(c + 1) * 128], identity=k.ident_bf[:, :]),
                     r=[ab[i], k.ident_bf], w=[pT[i]])
            ctr["ev"] += 1
            evac(S, ctr["ev"], aTt, aTt[:, :, col0:col0 + 128], pT[i], pT[i][:, :, :])

        def fm_chunk(aTt, nt, col):
            ps = pacc[ctr["acc"] % 2]
            ctr["acc"] += 1
            for c in range(8):
                S.op("pe", lambda e: e.matmul(ps[:, 0:nt], lhsT=wbf[:, c, col:col + 128], rhs=aTt[:, c, 0:nt],
                                              start=(c == 0), stop=(c == 7)), r=[wbf, aTt], w=[ps])
            return ps

        def tm_chunk(aTt, t, col, ncol, ps=None):
            if ps is None:
                ps = ptm[ctr["tm"] % 2]
                ctr["tm"] += 1
            for c in range(8):
                S.op("pe", lambda e: e.matmul(ps[:, 0:ncol], lhsT=aTt[:, c, t * 128:(t + 1) * 128], rhs=wbf[:, c, col:col + ncol],
                                              start=(c == 0), stop=(c == 7)), r=[wbf, aTt], w=[ps])
            return ps

        def dn_post(ps, nt, ci, src_t, src_ap, dst_fn):
            j = ctr["ob"] % 2
            S.op("act", lambda e: e.activation(out=act_[j][:, 0:nt], in_=src_ap, func=AF.Silu), r=[src_t], w=[act_[j]])
            ob = outb[ctr["ob"] % 3]
            ctr["ob"] += 1
            if ci < 8:
                S.op("act", lambda e: e.activation(out=sq2[j][:, 0:nt], in_=act_[j][:, 0:nt], func=AF.Square), r=[act_[j]], w=[sq2[j]])
                S.op("pe", lambda e: e.matmul(pl2[:, 0:nt], lhsT=ones_bf[:, :], rhs=sq2[j][:, 0:nt], start=True, stop=True),
                     r=[ones_bf, sq2[j]], w=[pl2])
                rsqrt(S, rn[j], rn[j][:, 0:nt], pl2, pl2[:, 0:nt], 1.0, EPS)
                if ci < 4:
                    S.op("dve", lambda e: e.scalar_tensor_tensor(out=ob[:, 0:nt], in0=act_[j][:, 0:nt], scalar=128 ** -0.5,
                                                                 in1=rn[j][:, 0:nt], op0=ALU.mult, op1=ALU.mult),
                         r=[act_[j], rn[j]], w=[ob])
                else:
                    S.op("dve", lambda e: e.tensor_tensor(out=ob[:, 0:nt], in0=act_[j][:, 0:nt], in1=rn[j][:, 0:nt], op=ALU.mult),
                         r=[act_[j], rn[j]], w=[ob])
            else:
                S.op("dve", lambda e: e.tensor_copy(out=ob[:, 0:nt], in_=act_[j][:, 0:nt]), r=[act_[j]], w=[ob])
            dst_fn(ob)

        def gates(ps_g, rows, dst_t, dst_ap):
            g0, g1, g2, g3 = gtmp
            S.op("act", lambda e: e.activation(out=dst_ap[:, 4:8], in_=ps_g[0:rows, 4:8], func=AF.Sigmoid), r=[ps_g], w=[dst_t])
            S.op("dve", lambda e: e.tensor_tensor(out=g0[0:rows, :], in0=ps_g[0:rows, 0:4], in1=dtb[0:rows, :], op=ALU.add),
                 r=[ps_g, dtb], w=[g0])
            S.op("act", lambda e: e.activation(out=g1[0:rows, :], in_=g0[0:rows, :], func=AF.Abs), r=[g0], w=[g1])
            S.op("act", lambda e: e.activation(out=g2[0:rows, :], in_=g1[0:rows, :], func=AF.Exp, scale=-1.0), r=[g1], w=[g2])
            S.op("act", lambda e: e.activation(out=g3[0:rows, :], in_=g2[0:rows, :], func=AF.Ln, bias=1.0), r=[g2], w=[g3])
            S.op("dve", lambda e: e.scalar_tensor_tensor(out=g1[0:rows, :], in0=g0[0:rows, :], scalar=0.0, in1=g3[0:rows, :],
                                                         op0=ALU.max, op1=ALU.add), r=[g0, g3], w=[g1])
            S.op("dve", lambda e: e.tensor_tensor(out=dst_ap[:, 0:4], in0=g1[0:rows, :], in1=nega[0:rows, :], op=ALU.mult),
                 r=[g1, nega], w=[dst_t])

        nsup = EXT // 512
        import os
        sups = range(nsup) if "P1SUP" not in os.environ else [int(x) for x in os.environ["P1SUP"].split(",") if x]
        for s in sups:
            aTt = aT[s % 2]
            in_main = s * 512 >= PRE
            in_kr = s * 512 >= PRE - HALO
            for t in range(4):
                x_ = xt[t % 2]
                r0 = s * 512 + t * 128
                S.dma("sp", x_, lambda e: e.dma_start(out=x_[:, :], in_=k.xe[r0:r0 + 128, :]), w=[x_])
                norm_transpose(x_, t % 2, aTt, t * 128)
            for ci in range(12):
                ps = fm_chunk(aTt, 512, ci * 128)
                xp_ = xp[ci % 2]
                S.op("act", lambda e: e.activation(out=xp_[:, 3:515], in_=ps[:, :], func=AF.Copy), r=[ps], w=[xp_])
                S.op("pool", lambda e: e.tensor_copy(out=xp_[:, 0:3], in_=carry[:, ci, :]), r=[carry], w=[xp_])
                S.op("pool", lambda e: e.tensor_copy(out=carry[:, ci, :], in_=xp_[:, 512:515]), r=[xp_], w=[carry])
                if s == nsup - 1 and not os.environ.get("SKIP_PCONV"):
                    S.dma("sp", xp_, lambda e: e.dma_start(out=k.p_conv.ap()[:, ci * 128:(ci + 1) * 128].rearrange("j p -> p j"),
                                                             in_=xp_[:, 512:515], allow_slow_non_contiguous=True), r=[xp_])
                ac = acc[ci % 2]
                S.op("dve", lambda e: e.tensor_scalar(out=ac[:, :], in0=xp_[:, 0:512], scalar1=cw[:, ci, 0:1], scalar2=None, op0=ALU.mult),
                     r=[xp_, cw], w=[ac])
                for j in range(1, 4):
                    S.op("dve", lambda e: e.scalar_tensor_tensor(out=ac[:, :], in0=xp_[:, j:j + 512], scalar=cw[:, ci, j:j + 1], in1=ac[:, :],
                                                                 op0=ALU.mult, op1=ALU.add), r=[xp_, cw, ac], w=[ac])
                dst = (k.dnq, k.dnk, k.dnv)[ci // 4]
                h = ci % 4
                dn_post(None, 512, ci, ac, ac[:, :],
                        lambda ob: S.dma("sp", ob, lambda e: e.dma_start(out=dst[h, :, s * 512:(s + 1) * 512], in_=ob[:, :]), r=[ob]))
            for t in range(4):
                r0 = s * 512 + t * 128
                for c in range(8):
                    S.op("pe", lambda e: e.matmul(pg[:, :], lhsT=aTt[:, c, t * 128:(t + 1) * 128], rhs=wbf[:, c, CAG:CAG + 8],
                                                  start=(c == 0), stop=(c == 7)), r=[wbf, aTt], w=[pg])
                g_ = gb[t % 2]
                gates(pg, 128, g_, g_[:, :])
                S.dma("sp", g_, lambda e: e.dma_start(out=k.gbs[r0:r0 + 128, :], in_=g_[:, :]), r=[g_])
                if in_main:
                    ps = tm_chunk(aTt, t, CZ, 512)
                    z_ = zb[t % 2]
                    ctr["ev"] += 1
                    evac(S, ctr["ev"], z_, z_[:, :], ps, ps[:, :])
                    S.dma("sp", z_, lambda e: e.dma_start(out=k.zs[r0 - PRE:r0 - PRE + 128, :], in_=z_[:, :]), r=[z_])
            if not in_kr:
                continue
            sk = s - (PRE - HALO) // 512
            sm = s - PRE // 512
            for g in range(3):
                d = DILS[g]
                if False:
                    continue
                for which in range(2):
                    if which == 0 and not in_main:
                        continue
                    for pair in range(2):
                        ps = fm_chunk(aTt, 512, CSW + 768 * g + 256 * which + 128 * pair)
                        q_ = qkp[ctr["ob"] % 3]
                        ctr["ob"] += 1
                        ctr["ev"] += 1
                        evac(S, ctr["ev"], q_, q_[:, :], ps, ps[:, :])
                        if which == 0:
                            dst = k.qts[g][pair, :, sm * 512:(sm + 1) * 512]
                        else:
                            dst = k.kts[g][pair, :, sk * 512:(sk + 1) * 512]
                        S.dma("sp", q_, lambda e: e.dma_start(out=dst, in_=q_[:, :]), r=[q_])
            for t in range(4):
                r0 = s * 512 + t * 128
                if os.environ.get("SKIP_KV") == "1":
                    continue
                for g in range(3):
                    ps = tm_chunk(aTt, t, CSW + 768 * g + 256, 512)
                    v_ = vsb[g]
                    S.op("dve", lambda e: e.tensor_copy(out=v_[:, :], in_=ps[:, 256:512]), r=[ps], w=[v_])
                    S.dma("sp", v_, lambda e: e.dma_start(out=k.vss[g][r0 - (PRE - HALO):r0 - (PRE - HALO) + 128, :], in_=v_[:, :]), r=[v_])
                    wlen = 128 * DILS[g]
                    if r0 >= EXT - wlen:
                        f_ = kvf[g]
                        S.op("dve", lambda e: e.tensor_copy(out=f_[:, :], in_=ps[:, :]), r=[ps], w=[f_])
                        o0 = r0 - (EXT - wlen)
                        S.dma("sp", f_, lambda e: e.dma_start(out=k.p_win[g][o0:o0 + 128, :], in_=f_[:, :]), r=[f_])

        NT = NS * TS
        if os.environ.get("P1NOSMP"):
            k.end_phase()
            return
        S.dma("sp", xsm, lambda e: e.dma_start(out=xsm[0:NT, :], in_=k.xs[:, :]), w=[xsm])
        aTt = aT[0]
        norm_transpose(xsm, 0, aTt, 0)
        for s_ in range(NS):
            for j in range(3):
                S.dma("sp", xps, lambda e: e.dma_start(out=xps[:, :, s_, j], in_=k.st_conv[s_, j, :].rearrange("(c p) -> p c", p=128),
                                                       allow_slow_non_contiguous=True), w=[xps])
        for ci in range(12):
            ps = fm_chunk(aTt, NT, ci * 128)
            S.op("dve", lambda e: e.tensor_copy(out=xps[:, ci, :, 3:7], in_=ps[:, 0:NT].rearrange("p (s t) -> p s t", s=NS)),
                 r=[ps], w=[xps])
        for ci in range(12):
            for s_ in range(NS):
                S.dma("sp", xps, lambda e: e.dma_start(out=k.s_conv[s_, :, ci * 128:(ci + 1) * 128].rearrange("j p -> p j"),
                                                       in_=xps[:, ci, s_, 4:7], allow_slow_non_contiguous=True), r=[xps])
        for ci in range(12):
            ac = acc[ci % 2]
            av = ac[:, 0:NT].rearrange("p (s t) -> p s t", s=NS)
            S.op("dve", lambda e: e.tensor_scalar(out=av, in0=xps[:, ci, :, 0:4], scalar1=cw[:, ci, 0:1], scalar2=None, op0=ALU.mult),
                 r=[xps, cw], w=[ac])
            for j in range(1, 4):
                S.op("dve", lambda e: e.scalar_tensor_tensor(out=av, in0=xps[:, ci, :, j:j + 4], scalar=cw[:, ci, j:j + 1], in1=av,
                                                             op0=ALU.mult, op1=ALU.add), r=[xps, cw, ac], w=[ac])
            dn_post(None, NT, ci, ac, ac[:, 0:NT],
                    lambda ob: S.op("pool", lambda e: e.tensor_copy(out=k.smp_dn[:, ci, :], in_=ob[:, 0:NT]), r=[ob], w=[k.smp_dn]))
        for c in range(8):
            S.op("pe", lambda e: e.matmul(pg[:, :], lhsT=aTt[:, c, 0:128], rhs=wbf[:, c, CAG:CAG + 8],
                                          start=(c == 0), stop=(c == 7)), r=[wbf, aTt], w=[pg])
        gates(pg, NT, k.smp_gb, k.smp_gb[:, :])
        ps = tm_chunk(aTt, 0, CZ, 512)
        S.op("act", lambda e: e.activation(out=k.smp_z[:, :], in_=ps[0:NT, :], func=AF.Copy), r=[ps], w=[k.smp_z])
        for g in range(3):
            for which in range(2):
                for pair in range(2):
                    ps = fm_chunk(aTt, NT, CSW + 768 * g + 256 * which + 128 * pair)
                    S.op("act", lambda e: e.activation(out=k.smp_qk[:, g, which, pair, :], in_=ps[:, 0:NT], func=AF.Copy),
                         r=[ps], w=[k.smp_qk])
            ps = tm_chunk(aTt, 0, CSW + 768 * g + 256, 512)
            S.op("act", lambda e: e.activation(out=k.smp_kv[:, g, :], in_=ps[0:NT, :], func=AF.Copy), r=[ps], w=[k.smp_kv])
        k.end_phase()


def build_nc(debug=False, phases=(1, 5, 2, 3, 4)):
    nc = bass.Bass("TRN2", target_bir_lowering=False)
    k = K()
    k.nc = nc
    k.S = Sched(nc)
    k.debug = debug
    declare_io(k, debug)
    S = k.S
    with ExitStack() as es:
        A = lambda name, shape, dt: TL(es.enter_context(nc.sbuf_tensor(name, shape, dt)), name)
        k.cst = {}
        for i, n in enumerate(CONST_NAMES):
            t = A("c_" + n, [128, 128], F32)
            S.dma("sp", t, lambda e: e.dma_start(out=t[:, :], in_=k.consts[i, :, :]), w=[t])
            k.cst[n] = t
        k.ident_bf = A("ident_bf", [128, 128], BF16)
        S.op("dve", lambda e: e.tensor_copy(out=k.ident_bf[:, :], in_=k.cst["ident"][:, :]), r=[k.cst["ident"]], w=[k.ident_bf])
        k.ones_bf = A("ones_bf", [128, 128], BF16)
        S.op("dve", lambda e: e.tensor_copy(out=k.ones_bf[:, :], in_=k.cst["ones"][:, :]), r=[k.cst["ones"]], w=[k.ones_bf])
        NT = NS * TS
        k.smp_dn = A("smp_dn", [128, 12, NT], BF16)
        k.smp_gb = A("smp_gb", [NT, 8], F32)
        k.smp_z = A("smp_z", [NT, 512], BF16)
        k.smp_qk = A("smp_qk", [128, 3, 2, 2, NT], BF16)
        k.smp_kv = A("smp_kv", [NT, 3, 512], F32)
        k.smp_cat = A("smp_cat", [NT, 512], BF16)
        k.smp_swo = A("smp_swo", [NT, 3, 260], F32)
        if _os.environ.get("P2DBG"):
            k.dbg = nc.dram_tensor("dbg", [128, 4096], F32, kind="ExternalOutput")
        if 1 in phases:
            phase1(k)
        if 5 in phases:
            phase_mem_and_windows(k)
        if 2 in phases:
            phase2a(k)
        if 3 in phases:
            phase2b(k)
        if 4 in phases:
            phase3(k)
        S.barrier()
        S.final_wait()
        S.emit()
    k.ninst = S.ninst
    return nc, k


def shard_inputs(inp):
    consts = np.stack([make_consts()[n] for n in CONST_NAMES]).astype(np.float32)
    maps = []
    c0 = make_consts()
    for c in range(NCORES):
        b, j = c // 2, c % 2
        xe = np.zeros((EXT, D), np.float32)
        if j == 0:
            xe[PRE:] = inp["x_prompt"][b, :MAIN]
            halo = np.full((128, 128), NEG, np.float32)
        else:
            xe[:] = inp["x_prompt"][b]
            halo = c0["swprev"]
        sl = slice(c * NS, (c + 1) * NS)
        m = {
            "xe": xe,
            "xs": np.ascontiguousarray(inp["x_sample"][sl].reshape(NS * TS, D)),
            "st_delta": np.ascontiguousarray(inp["state_delta"][0, sl]),
            "st_conv": np.ascontiguousarray(inp["state_conv"][0, sl]),
            "cwin0": np.ascontiguousarray(inp["cache_win1"][0, sl].reshape(NS, 128, 512)),
            "cwin1": np.ascontiguousarray(inp["cache_win2"][0, sl].reshape(NS, 512, 512)),
            "cwin2": np.ascontiguousarray(inp["cache_win3"][0, sl].reshape(NS, 2048, 512)),
            "cmem": np.ascontiguousarray(inp["cache_mem_kv"][0, sl].reshape(NS, 256, 2048)),
            "mem": np.ascontiguousarray(inp["mem_prompt"][b]),
            "halo": halo,
            "consts": consts,
            "g_mix": inp["g_mix"][0], "w_in": inp["w_in"][0], "conv_w": inp["conv_w"][0], "a_log": inp["a_log"][0],
            "dt_bias": inp["dt_bias"][0], "g_onorm": inp["g_onorm"][0], "w_out": inp["w_out"][0], "g_memq": inp["g_memq"][0],
            "g_memkv": inp["g_memkv"][0], "w_mq": inp["w_mq"][0], "w_mkv": inp["w_mkv"][0], "w_mo": inp["w_mo"][0],
            "g_ffn": inp["g_ffn"][0], "w_pq": inp["w_pq"][0], "sub_keys": inp["sub_keys"][0].reshape(16, 128, 128),
            "expert_u": inp["expert_u"][0][:EXPROWS], "expert_v": inp["expert_v"][0][:EXPROWS], "g_final": inp["g_final"],
        }
        maps.append({kk: np.ascontiguousarray(np.asarray(v, dtype=np.float32)) for kk, v in m.items()})
    return maps


def assemble(res):
    B, SEQ, DB = 4, 8192, 32
    y_prompt = np.zeros((B, SEQ, D), np.float32)
    y_sample = np.zeros((DB, TS, D), np.float32)
    p_delta = np.zeros((1, B, 4, 128, 128), np.float32)
    p_conv = np.zeros((1, B, 3, 1536), np.float32)
    p_win = [np.zeros((1, B, 128 * d, 2, 4, 64), np.float32) for d in DILS]
    p_mem = np.zeros((1, B, 256, 2, 4, 256), np.float32)
    s_delta = np.zeros((1, DB, 4, 128, 128), np.float32)
    s_conv = np.zeros((1, DB, 3, 1536), np.float32)
    s_win = [np.zeros((1, DB, 128 * d, 2, 4, 64), np.float32) for d in DILS]
    for c in range(NCORES):
        r = res[c]
        b, j = c // 2, c % 2
        y_prompt[b, j * MAIN:(j + 1) * MAIN] = r["y_main"]
        sl = slice(c * NS, (c + 1) * NS)
        y_sample[sl] = r["y_smp"].reshape(NS, TS, D)
        s_delta[0, sl] = r["s_delta"]
        s_conv[0, sl] = r["s_conv"]
        for g in range(3):
            s_win[g][0, sl] = r[f"s_win{g}"].reshape(NS, 128 * DILS[g], 2, 4, 64)
        if j == 1:
            p_delta[0, b] = r["p_delta"]
            p_conv[0, b] = r["p_conv"]
            for g in range(3):
                p_win[g][0, b] = r[f"p_win{g}"].reshape(128 * DILS[g], 2, 4, 64)
            p_mem[0, b] = r["p_mem"].reshape(256, 2, 4, 256)
    return (y_prompt, y_sample, p_delta, p_conv, p_win[0], p_win[1], p_win[2], p_mem,
            s_delta, s_conv, s_win[0], s_win[1], s_win[2])


def kernel(**inputs):
    inp = {kk: np.asarray(v) for kk, v in inputs.items()}
    nc, k = build_nc()
    maps = shard_inputs(inp)
    res = run_bass_kernel_spmd(nc, maps, core_ids=list(range(NCORES)))
    return assemble(res.results)


def phase_mem_and_windows(k):
    nc, S = k.nc, k.S
    d2d = k.track(TL(None, "d2d"))
    for g in range(3):
        wb = 128 * DILS[g]
        for s_ in range(NS):
            S.dma("sp", d2d, lambda e: e.dma_start(out=k.s_win[g][s_, 0:wb - TS, :], in_=k.cwin[g][s_, TS:wb, :]))
            S.dma("sp", k.smp_kv, lambda e: e.dma_start(out=k.s_win[g][s_, wb - TS:wb, :], in_=k.smp_kv[s_ * TS:(s_ + 1) * TS, g, :]),
                  r=[k.smp_kv])
    with ExitStack() as es:
        A = lambda name, shape, dt: k.track(TL(es.enter_context(nc.sbuf_tensor(name, shape, dt)), name))
        P = lambda name, shape, dt: TL(es.enter_context(nc.psum_tensor(name, shape, dt)), name)
        wbf = load_weight_bf16(k, es, "w_mkv", k.w_mkv, D, 2048, gdram=k.g_memkv, col_chunk=2048)
        xt = [A(f"mxt{i}", [128, D], F32) for i in range(2)]
        sqj = A("msqj", [128, D], BF16)
        ssq = A("mssq", [128, 1], F32)
        rstd = A("mrstd", [128, 1], F32)
        ab = A("mab", [128, D], BF16)
        aT = A("maT", [128, 8, 256], BF16)
        ob = [A(f"mob{i}", [128, 512], F32) for i in range(2)]
        pT = P("mpT", [128, 8, 128], BF16)
        ps_ = [P(f"mps{i}", [128, 512], F32) for i in range(2)]
        for t in range(2):
            x_ = xt[t]
            S.dma("sp", x_, lambda e: e.dma_start(out=x_[:, :], in_=k.mem[t * 128:(t + 1) * 128, :]), w=[x_])
            S.op("act", lambda e: e.activation(out=sqj[:, :], in_=x_[:, :], func=AF.Square, accum_out=ssq[:, 0:1]), r=[x_], w=[sqj, ssq])
            rsqrt(S, rstd, rstd[:, :], ssq, ssq[:, :], 1.0 / D, EPS)
            S.op("act", lambda e: e.activation(out=ab[:, :], in_=x_[:, :], func=AF.Copy, scale=rstd[:, 0:1]), r=[x_, rstd], w=[ab])
            for c in range(8):
                S.op("pe", lambda e: e.transpose(out=pT[:, c, :], in_=ab[:, c * 128:(c + 1) * 128], identity=k.ident_bf[:, :]),
                     r=[ab, k.ident_bf], w=[pT])
            S.op("dve", lambda e: e.tensor_copy(out=aT[:, :, t * 128:(t + 1) * 128], in_=pT[:, :, :]), r=[pT], w=[aT])
        n = 0
        for t in range(2):
            for nb in range(4):
                ps = ps_[n % 2]
                o_ = ob[n % 2]
                for c in range(8):
                    S.op("pe", lambda e: e.matmul(ps[:, :], lhsT=aT[:, c, t * 128:(t + 1) * 128], rhs=wbf[:, c, nb * 512:(nb + 1) * 512],
                                                  start=(c == 0), stop=(c == 7)), r=[aT, wbf], w=[ps])
                evac(S, n, o_, o_[:, :], ps, ps[:, :])
                S.dma("sp", o_, lambda e: e.dma_start(out=k.p_mem[t * 128:(t + 1) * 128, nb * 512:(nb + 1) * 512], in_=o_[:, :]), r=[o_])
                n += 1
        k.end_phase()


def phase2a(k):
    import os
    nc, S = k.nc, k.S
    with ExitStack() as es:
        A = lambda name, shape, dt: k.track(TL(es.enter_context(nc.sbuf_tensor(name, shape, dt)), name))
        P = lambda name, shape, dt: TL(es.enter_context(nc.psum_tensor(name, shape, dt)), name)
        cst = k.cst
        ident, trit, blk, mTneg, mSpos, ones = (cst[n] for n in ("ident", "trit", "blk", "mTneg", "mSpos", "ones"))
        identb = k.ident_bf
        cbf = {}
        for n_ in ("trit", "blk", "mTneg", "mSpos"):
            cbf[n_] = A("cbf_" + n_, [128, 128], BF16)
            S.op("dve", lambda e: e.tensor_copy(out=cbf[n_][:, :], in_=cst[n_][:, :]), r=[cst[n_]], w=[cbf[n_]])
        tritb, blkb, mTnegb, mSposb = (cbf[n_] for n_ in ("trit", "blk", "mTneg", "mSpos"))
        onesb = k.ones_bf
        hones = [A(f"hones{c_}", [128, 128], BF16) for c_ in range(2)]
        for c_ in range(2):
            S.op("dve", lambda e: e.tensor_copy(out=hones[c_][:, :], in_=cst["blk"][:, 127 * c_:127 * c_ + 1].to_broadcast([128, 128])),
                 r=[cst["blk"]], w=[hones[c_]])
        ghl = A("ghl", [128, 8], BF16)
        Gall_h = A("Gall_h", [128, 4, 128], BF16)
        Gall_l = A("Gall_l", [128, 4, 128], BF16)
        gon = A("gon", [128, 128], F32)
        S.dma("sp", gon, lambda e: e.dma_start(out=gon[:, :], in_=k.g_onorm.ap().partition_broadcast(128)), w=[gon])
        qT = [A(f"qT{i}", [128, 4, 128], BF16) for i in range(2)]
        kT = [A(f"kT{i}", [128, 4, 128], BF16) for i in range(2)]
        vT = [A(f"vT{i}", [128, 4, 128], BF16) for i in range(2)]
        gbt = [A(f"gbt{i}", [128, 8], F32) for i in range(2)]
        zt = [A(f"zt{i}", [128, 512], BF16) for i in range(2)]
        gc = A("gc", [128, 4], F32); ngc = A("ngc", [128, 4], F32); egc = A("egc", [128, 4], F32)
        negegc = A("negegc", [128, 4], F32); ekd = A("ekd", [128, 4], F32); dl = A("dl", [128, 8], F32)
        dif = A("dif", [128, 4], F32)
        dlraw = A("dlraw", [128, 8], F32)
        Gall = A("Gall", [128, 4, 128], F32)
        egcb = [A(f"egcb{i}", [128, 128], F32) for i in range(2)]
        gamT = [A(f"gamT{i}", [128, 128], F32) for i in range(2)]
        gamS = [A(f"gamS{i}", [128, 128], F32) for i in range(2)]
        Lx = [A(f"Lx{i}", [128, 128], BF16) for i in range(3)]
        Ly = [A(f"Ly{i}", [128, 128], BF16) for i in range(3)]
        Rr = [A(f"Rr{i}", [128, 128], BF16) for i in range(2)]
        AqkT = [[A(f"AqkT{i}_{h}", [128, 128], BF16) for h in range(4)] for i in range(2)]
        qgT = [[A(f"qgT{i}_{h}", [128, 128], BF16) for h in range(4)] for i in range(2)]
        TbT = [[A(f"TbT{i}_{h}", [128, 128], BF16) for h in range(4)] for i in range(2)]
        kd = [[A(f"kd{i}_{h}", [128, 128], BF16) for h in range(4)] for i in range(2)]
        vtok = [[A(f"vtok{i}_{h}", [128, 128], F32) for h in range(4)] for i in range(2)]
        sc_neg = [A(f"scneg{i}", [128, 4], F32) for i in range(2)]
        sc_dl = [A(f"scdl{i}", [128, 8], F32) for i in range(2)]
        rt = [A(f"rt{h}", [128, 128], BF16) for h in range(4)]
        ut = [A(f"ut{h}", [128, 128], BF16) for h in range(4)]
        St = [A(f"St{h}", [128, 128], F32) for h in range(4)]
        Sb = [A(f"Sb{h}", [128, 128], BF16) for h in range(4)]
        ot = [A(f"ot{i}", [128, 512], F32) for i in range(2)]
        ssq = A("ossq", [128, 4], F32); orstd = A("orstd", [128, 4], F32)
        sqj = A("osqj", [128, 128], F32)
        szt = A("szt", [128, 512], F32)
        t1 = A("t1", [128, 512], F32)
        og = [A(f"og{i}", [128, 512], BF16) for i in range(2)]
        pb = PsumBlocks(nc, es, "p2a_")
        P = lambda name, dt, bank: pb.get(name, 128, dt, bank)
        pKS = P("pKS", F32, "scan"); pU = P("pU", F32, "scan"); pO = P("pO", F32, "scan"); pdS = P("pdS", F32, "scan")
        pA = [P(f"pA{i}", F32, f"g{i}") for i in range(2)]
        pB = [P(f"pB{i}", F32, f"g{i}") for i in range(2)]
        pC = [P(f"pC{i}", F32, f"g{i}") for i in range(2)]
        pKK = [P(f"pKK{i}", F32, f"k{i}") for i in range(2)]
        pQK = [P(f"pQK{i}", F32, f"k{i}") for i in range(2)]
        pX = P("pX", F32, "nX"); pY = P("pY", F32, "nY"); pP = P("pP", F32, "nP")
        pgt = pb.get("pgt", 16, F32, "k0")
        pLT = P("pLT", F32, "nY"); pkt = P("pkt", F32, "nP"); pvt = P("pvt", F32, "nY")
        def mm(ps, out_ap, lt, lap, rt_, rap, start=True, stop=True):
            S.op("pe", lambda e: e.matmul(out_ap, lhsT=lap, rhs=rap, start=start, stop=stop), r=[lt, rt_], w=[ps])

        def prep(i, q_, k_, v_, g_):
            S.op("dve", lambda e: e.tensor_copy(out=ghl[:, 0:4], in_=g_[:, 0:4]), r=[g_], w=[ghl])
            S.op("dve", lambda e: e.tensor_tensor(out=ghl[:, 4:8], in0=g_[:, 0:4], in1=ghl[:, 0:4], op=ALU.subtract), r=[g_, ghl], w=[ghl])
            for (c0, lt_, lap) in ((0, tritb, tritb[:, :]), (4, blkb, blkb[:, :])):
                mm(pgt, pgt[:, c0:c0 + 4], lt_, lap, ghl, ghl[:, 0:4], True, False)
                mm(pgt, pgt[:, c0:c0 + 4], lt_, lap, ghl, ghl[:, 4:8], False, True)
            for c_ in range(2):
                mm(pgt, pgt[:, 8 + 4 * c_:12 + 4 * c_], hones[c_], hones[c_][:, :], ghl, ghl[:, 0:4], True, False)
                mm(pgt, pgt[:, 8 + 4 * c_:12 + 4 * c_], hones[c_], hones[c_][:, :], ghl, ghl[:, 4:8], False, True)
            S.op("dve", lambda e: e.tensor_copy(out=gc[:, :], in_=pgt[:, 0:4]), r=[pgt], w=[gc])
            S.op("dve", lambda e: e.tensor_scalar(out=ngc[:, :], in0=gc[:, :], scalar1=-1.0, scalar2=None, op0=ALU.mult), r=[gc], w=[ngc])
            S.op("act", lambda e: e.activation(out=egc[:, :], in_=gc[:, :], func=AF.Exp), r=[gc], w=[egc])
            S.op("dve", lambda e: e.tensor_scalar(out=sc_neg[i][:, :], in0=egc[:, :], scalar1=-1.0, scalar2=None, op0=ALU.mult),
                 r=[egc], w=[sc_neg[i]])
            S.op("dve", lambda e: e.tensor_tensor(out=dif[:, :], in0=pgt[:, 4:8], in1=gc[:, :], op=ALU.subtract), r=[pgt, gc], w=[dif])
            S.op("act", lambda e: e.activation(out=ekd[:, :], in_=dif[:, :], func=AF.Exp), r=[dif], w=[ekd])
            S.op("dve", lambda e: e.tensor_copy(out=dlraw[:, :], in_=pgt[:, 8:16]), r=[pgt], w=[dlraw])
            S.op("act", lambda e: e.activation(out=sc_dl[i][:, :], in_=dlraw[:, :], func=AF.Exp), r=[dlraw], w=[sc_dl[i]])
            stg = int(os.environ.get("P2PREP", "9"))
            if stg < 1:
                return
            for h in range(4):
                S.op("dve", lambda e: e.tensor_copy(out=Gall_h[:, h, :], in_=ghl[:, h:h + 1].to_broadcast([128, 128])), r=[ghl], w=[Gall_h])
                S.op("dve", lambda e: e.tensor_copy(out=Gall_l[:, h, :], in_=ghl[:, 4 + h:5 + h].to_broadcast([128, 128])), r=[ghl], w=[Gall_l])
            for h in range(4):
                j = h % 2
                if stg < 2:
                    continue
                mm(pA[j], pA[j][:, :], Gall_h, Gall_h[:, h, :], tritb, tritb[:, :], True, False)
                mm(pA[j], pA[j][:, :], Gall_l, Gall_l[:, h, :], tritb, tritb[:, :], False, True)
                mm(pB[j], pB[j][:, :], Gall_h, Gall_h[:, h, :], tritb, tritb[:, :], True, False)
                mm(pB[j], pB[j][:, :], Gall_l, Gall_l[:, h, :], tritb, tritb[:, :], False, False)
                mm(pB[j], pB[j][:, :], identb, identb[:, :], mTnegb, mTnegb[:, :], False, True)
                mm(pC[j], pC[j][:, :], Gall_h, Gall_h[:, h, :], tritb, tritb[:, :], True, False)
                mm(pC[j], pC[j][:, :], Gall_l, Gall_l[:, h, :], tritb, tritb[:, :], False, False)
                mm(pC[j], pC[j][:, :], identb, identb[:, :], mSposb, mSposb[:, :], False, True)
                S.op("act", lambda e: e.activation(out=egcb[j][:, :], in_=pA[j][:, :], func=AF.Exp), r=[pA[j]], w=[egcb[j]])
                S.op("act", lambda e: e.activation(out=gamT[j][:, :], in_=pB[j][:, :], func=AF.Exp, bias=ngc[:, h:h + 1]),
                     r=[pB[j], ngc], w=[gamT[j]])
                S.op("act", lambda e: e.activation(out=gamS[j][:, :], in_=pC[j][:, :], func=AF.Exp, bias=gc[:, h:h + 1], scale=-1.0),
                     r=[pC[j], gc], w=[gamS[j]])
                if stg < 3:
                    continue
                mm(pKK[j], pKK[j][:, :], k_, k_[:, h, :], k_, k_[:, h, :])
                mm(pQK[j], pQK[j][:, :], k_, k_[:, h, :], q_, q_[:, h, :])
                X, Y = Lx[0], Ly[0]
                S.op("dve", lambda e: e.scalar_tensor_tensor(out=X[:, :], in0=pKK[j][:, :], scalar=g_[:, 4 + h:5 + h], in1=gamS[j][:, :],
                                                             op0=ALU.mult, op1=ALU.mult), r=[pKK[j], g_, gamS[j]], w=[X])
                S.op("dve", lambda e: e.tensor_tensor(out=AqkT[i][h][:, :], in0=pQK[j][:, :], in1=gamT[j][:, :], op=ALU.mult),
                     r=[pQK[j], gamT[j]], w=[AqkT[i][h]])
                S.op("pool", lambda e: e.tensor_tensor(out=qgT[i][h][:, :], in0=q_[:, h, :], in1=egcb[j][:, :], op=ALU.mult),
                     r=[q_, egcb[j]], w=[qgT[i][h]])
                if stg < 4:
                    continue
                exp_ = os.environ.get("P2EXP", "")
                if exp_ == "A":
                    mm(pLT, pLT[:, :], identb, identb[:, :], identb, identb[:, :])
                else:
                    mm(pLT, pLT[:, :], X, X[:, :], identb, identb[:, :])
                if exp_ == "D":
                    S.op("act", lambda e: e.activation(out=Y[:, :], in_=pLT[:, :], func=AF.Copy), r=[pLT], w=[Y])
                elif exp_ != "B":
                    S.op("dve", lambda e: e.tensor_copy(out=Y[:, :], in_=pLT[:, :]), r=[pLT], w=[Y])
                sub = os.environ.get("P2SUB", "z")
                if sub == "a":
                    continue
                R = Rr[0]
                S.op("pool", lambda e: e.tensor_tensor(out=R[:, :], in0=identb[:, :], in1=Y[:, :], op=ALU.subtract), r=[identb, Y], w=[R])
                if sub == "b":
                    continue
                for it in range(1, 6):
                    Xn, Yn = Lx[it % 3], Ly[it % 3]
                    mm(pX, pX[:, :], Y, Y[:, :], X, X[:, :])
                    if sub == "c":
                        break
                    if it < 5:
                        mm(pY, pY[:, :], X, X[:, :], Y, Y[:, :])
                    S.op("act", lambda e: e.activation(out=Xn[:, :], in_=pX[:, :], func=AF.Copy), r=[pX], w=[Xn])
                    if it < 5:
                        S.op("dve", lambda e: e.tensor_copy(out=Yn[:, :], in_=pY[:, :]), r=[pY], w=[Yn])
                    mm(pP, pP[:, :], Xn, Xn[:, :], R, R[:, :])
                    Rn = Rr[it % 2]
                    S.op("dve", lambda e: e.tensor_tensor(out=Rn[:, :], in0=pP[:, :], in1=R[:, :], op=ALU.add), r=[pP, R], w=[Rn])
                    X, Y, R = Xn, Yn, Rn
                if stg < 5:
                    continue
                S.op("pool", lambda e: e.tensor_scalar(out=TbT[i][h][:, :], in0=R[:, :], scalar1=g_[:, 4 + h:5 + h], scalar2=None, op0=ALU.mult),
                     r=[R, g_], w=[TbT[i][h]])
                mm(pkt, pkt[:, :], k_, k_[:, h, :], identb, identb[:, :])
                S.op("dve", lambda e: e.tensor_scalar(out=kd[i][h][:, :], in0=pkt[:, :], scalar1=ekd[:, h:h + 1], scalar2=None, op0=ALU.mult),
                     r=[pkt, ekd], w=[kd[i][h]])
                mm(pvt, pvt[:, :], v_, v_[:, h, :], identb, identb[:, :])
                S.op("dve", lambda e: e.tensor_copy(out=vtok[i][h][:, :], in_=pvt[:, :]), r=[pvt], w=[vtok[i][h]])

        def scan(i, k_, o_, pre=None, post=None):
            for c in range(2):
                ps_ = slice(64 * c, 64 * c + 64)
                if pre:
                    pre(c)
                for h in range(4):
                    mm(pKS, pKS[:, :], k_, k_[:, h, :], Sb[h], Sb[h][:, :])
                    S.op("dve", lambda e: e.scalar_tensor_tensor(out=rt[h][ps_, :], in0=pKS[ps_, :], scalar=sc_neg[i][ps_, h:h + 1],
                                                                 in1=vtok[i][h][ps_, :], op0=ALU.mult, op1=ALU.add),
                         r=[pKS, sc_neg[i], vtok[i][h]], w=[rt[h]])
                    mm(pU, pU[:, :], TbT[i][h], TbT[i][h][ps_, :], rt[h], rt[h][ps_, :])
                    S.op("act", lambda e: e.activation(out=ut[h][ps_, :], in_=pU[ps_, :], func=AF.Copy), r=[pU], w=[ut[h]])
                    mm(pO, pO[:, :], qgT[i][h], qgT[i][h][:, :], Sb[h], Sb[h][:, :], True, False)
                    mm(pO, pO[:, :], AqkT[i][h], AqkT[i][h][ps_, :], ut[h], ut[h][ps_, :], False, True)
                    S.op("act", lambda e: e.activation(out=o_[ps_, h * 128:(h + 1) * 128], in_=pO[ps_, :], func=AF.Copy), r=[pO], w=[o_])
                    mm(pdS, pdS[:, :], kd[i][h], kd[i][h][ps_, :], ut[h], ut[h][ps_, :])
                    S.op("dve", lambda e: e.scalar_tensor_tensor(out=St[h][:, :], in0=St[h][:, :], scalar=sc_dl[i][:, 4 * c + h:4 * c + h + 1],
                                                                 in1=pdS[:, :], op0=ALU.mult, op1=ALU.add),
                         r=[St[h], sc_dl[i], pdS], w=[St[h]])
                    S.op("act", lambda e: e.activation(out=Sb[h][:, :], in_=St[h][:, :], func=AF.Copy), r=[St[h]], w=[Sb[h]])
                if post:
                    post(c)

        def post_out(o_, z_, j, dst_fn):
            for h in range(4):
                S.op("act", lambda e: e.activation(out=sqj[:, :], in_=o_[:, h * 128:(h + 1) * 128], func=AF.Square, accum_out=ssq[:, h:h + 1]),
                     r=[o_], w=[sqj, ssq])
            rsqrt(S, orstd, orstd[:, :], ssq, ssq[:, :], 1.0 / 128, EPS)
            S.op("act", lambda e: e.activation(out=szt[:, :], in_=z_[:, :], func=AF.Silu), r=[z_], w=[szt])
            for h in range(4):
                S.op("dve", lambda e: e.scalar_tensor_tensor(out=t1[:, h * 128:(h + 1) * 128], in0=o_[:, h * 128:(h + 1) * 128],
                                                             scalar=orstd[:, h:h + 1], in1=gon[:, :], op0=ALU.mult, op1=ALU.mult),
                     r=[o_, orstd, gon], w=[t1])
            S.op("dve", lambda e: e.tensor_tensor(out=og[j][:, :], in0=t1[:, :], in1=szt[:, :], op=ALU.mult), r=[t1, szt], w=[og[j]])
            dst_fn(og[j])

        for h in range(4):
            S.op("dve", lambda e: e.memset(St[h][:, :], 0.0), w=[St[h]])
            S.op("dve", lambda e: e.memset(Sb[h][:, :], 0.0), w=[Sb[h]])
        ntile = EXT // 128
        tiles = range(int(os.environ.get("P2START", "0")), int(os.environ.get("P2TILES", str(ntile))))
        for tau in tiles:
            i = tau % 2
            t0 = tau * 128
            for h in range(4):
                S.dma("sp", qT[i], lambda e: e.dma_start(out=qT[i][:, h, :], in_=k.dnq[h, :, t0:t0 + 128]), w=[qT[i]])
                S.dma("sp", kT[i], lambda e: e.dma_start(out=kT[i][:, h, :], in_=k.dnk[h, :, t0:t0 + 128]), w=[kT[i]])
                S.dma("sp", vT[i], lambda e: e.dma_start(out=vT[i][:, h, :], in_=k.dnv[h, :, t0:t0 + 128]), w=[vT[i]])
            S.dma("sp", gbt[i], lambda e: e.dma_start(out=gbt[i][:, :], in_=k.gbs[t0:t0 + 128, :]), w=[gbt[i]])
            main = t0 >= PRE
            if main:
                S.dma("sp", zt[i], lambda e: e.dma_start(out=zt[i][:, :], in_=k.zs[t0 - PRE:t0 - PRE + 128, :]), w=[zt[i]])
            mode = os.environ.get("P2MODE", "all")
            if mode == "load":
                continue
            prep(i, qT[i], kT[i], vT[i], gbt[i])
            if mode == "prep":
                continue
            scan(i, kT[i], ot[i])
            if main:
                post_out(ot[i], zt[i], i,
                         lambda o_: S.dma("sp", o_, lambda e: e.dma_start(out=k.cat[t0 - PRE:t0 - PRE + 128, :], in_=o_[:, :]), r=[o_]))
        for h in range(4):
            S.dma("sp", St[h], lambda e: e.dma_start(out=k.p_delta[h, :, :], in_=St[h][:, :]), r=[St[h]])

        for tb in range(2 if os.environ.get("P2MODE", "all") == "all" else 0):
            i = tb
            for tt in (qT[i], kT[i], vT[i]):
                S.op("pool", lambda e: e.memset(tt[:, :, :], 0.0), w=[tt])
            S.op("pool", lambda e: e.memset(gbt[i][:, :], 0.0), w=[gbt[i]])
            S.op("pool", lambda e: e.memset(zt[i][:, :], 0.0), w=[zt[i]])
            for c in range(2):
                sq = tb * 2 + c
                for h in range(4):
                    for which, tt in enumerate((qT[i], kT[i], vT[i])):
                        S.op("pool", lambda e: e.tensor_copy(out=tt[:, h, 64 * c:64 * c + TS], in_=k.smp_dn[:, which * 4 + h, sq * TS:(sq + 1) * TS]),
                             r=[k.smp_dn], w=[tt])
                S.dma("sp", gbt[i], lambda e: e.dma_start(out=gbt[i][64 * c:64 * c + TS, :], in_=k.smp_gb[sq * TS:(sq + 1) * TS, :]),
                      r=[k.smp_gb], w=[gbt[i]])
                S.dma("sp", zt[i], lambda e: e.dma_start(out=zt[i][64 * c:64 * c + TS, :], in_=k.smp_z[sq * TS:(sq + 1) * TS, :]),
                      r=[k.smp_z], w=[zt[i]])
            prep(i, qT[i], kT[i], vT[i], gbt[i])

            def pre(c, tb=tb):
                sq = tb * 2 + c
                for h in range(4):
                    S.dma("sp", St[h], lambda e: e.dma_start(out=St[h][:, :], in_=k.st_delta[sq, h, :, :]), w=[St[h]])
                    S.op("act", lambda e: e.activation(out=Sb[h][:, :], in_=St[h][:, :], func=AF.Copy), r=[St[h]], w=[Sb[h]])

            def post(c, tb=tb):
                sq = tb * 2 + c
                for h in range(4):
                    S.dma("sp", St[h], lambda e: e.dma_start(out=k.s_delta[sq, h, :, :], in_=St[h][:, :]), r=[St[h]])

            scan(i, kT[i], ot[i], pre, post)
            if tb == 0 and os.environ.get("P2DBG"):
                col = 0
                dbgt = TL(None, "dbgsem")
                for tt, wdt in ((TbT[0][0], 128), (AqkT[0][0], 128), (kd[0][0], 128), (qgT[0][0], 128), (vtok[0][0], 128),
                                (sc_neg[0], 4), (sc_dl[0], 8), (ot[0], 512), (rt[0], 128), (ut[0], 128), (St[0], 128), (gbt[0], 8)):
                    S.dma("pool", TL(None, f"dbg{col}"), lambda e: e.dma_start(out=k.dbg[:, col:col + wdt], in_=tt[:, 0:wdt]), r=[tt])
                    col += wdt

            def dst(o_, tb=tb):
                for c in range(2):
                    sq = tb * 2 + c
                    S.dma("sp", o_, lambda e: e.dma_start(out=k.smp_cat[sq * TS:(sq + 1) * TS, :], in_=o_[64 * c:64 * c + TS, :]),
                          r=[o_], w=[k.smp_cat])
            post_out(ot[i], zt[i], i, dst)
        k.end_phase()


def phase2b(k):
    import os
    nc, S = k.nc, k.S
    with ExitStack() as es:
        A = lambda name, shape, dt: k.track(TL(es.enter_context(nc.sbuf_tensor(name, shape, dt)), name))
        pb = PsumBlocks(nc, es, "p2b_")
        identb, onesb = k.ident_bf, k.ones_bf
        mprev = A("mprev", [128, 128], BF16); mcur = A("mcur", [128, 128], BF16); mhalo = A("mhalo", [128, 128], BF16)
        halo_f = A("halo_f", [128, 128], F32)
        S.dma("sp", halo_f, lambda e: e.dma_start(out=halo_f[:, :], in_=k.halo[:, :]), w=[halo_f])
        S.op("dve", lambda e: e.tensor_copy(out=mprev[:, :], in_=k.cst["swprev"][:, :]), r=[k.cst["swprev"]], w=[mprev])
        S.op("dve", lambda e: e.tensor_copy(out=mcur[:, :], in_=k.cst["swcur"][:, :]), r=[k.cst["swcur"]], w=[mcur])
        S.op("dve", lambda e: e.tensor_copy(out=mhalo[:, :], in_=halo_f[:, :]), r=[halo_f], w=[mhalo])
        QT = [A(f"QT{i}", [128, 2, 2048], BF16) for i in range(2)]
        KT = [A(f"KT{i}", [128, 2, 4096], BF16) for i in range(2)]
        Vb = [A(f"Vb{i}", [128, 2, 256], BF16) for i in range(2)]
        PT = [A(f"PT{i}", [128, 256], BF16) for i in range(2)]
        ot = [A(f"swot{i}", [128, 260], F32) for i in range(2)]
        psS = [pb.get(f"psS{i}", 256, F32, f"s{i}") for i in range(2)]
        psO = [pb.get(f"psO{i}", 128, F32, f"o{i}") for i in range(2)]
        def core(qf, kf, vf, msk0, o_, deps):
            qt_, kt_, vt_ = deps
            for head in range(4):
                pair, hh = head // 2, head % 2
                ph = slice(64 * hh, 64 * hh + 64)
                sS, sO, p_ = psS[head % 2], psO[head % 2], PT[head % 2]
                for kc in range(2):
                    msk = msk0 if kc == 0 else mcur
                    S.op("pe", lambda e: e.matmul(sS[:, kc * 128:(kc + 1) * 128], lhsT=identb[:, :], rhs=msk[:, :], start=True, stop=False),
                         r=[identb, msk], w=[sS])
                    S.op("pe", lambda e: e.matmul(sS[:, kc * 128:(kc + 1) * 128], lhsT=kf(kc, pair, ph), rhs=qf(pair, ph), start=False, stop=True),
                         r=[kt_, qt_], w=[sS])
                S.op("act", lambda e: e.activation(out=p_[:, :], in_=sS[:, :], func=AF.Exp, scale=0.125), r=[sS], w=[p_])
                for kc in range(2):
                    S.op("pe", lambda e: e.matmul(sO[:, 0:64], lhsT=p_[:, kc * 128:(kc + 1) * 128], rhs=vf(kc, head),
                                                  start=(kc == 0), stop=(kc == 1)), r=[p_, vt_], w=[sO])
                for kc in range(2):
                    S.op("pe", lambda e: e.matmul(sO[:, 64:65], lhsT=p_[:, kc * 128:(kc + 1) * 128], rhs=onesb[:, 0:1],
                                                  start=(kc == 0), stop=(kc == 1)), r=[p_, onesb], w=[sO])
                S.op("dve", lambda e: e.tensor_copy(out=o_[:, head * 65:(head + 1) * 65], in_=sO[:, 0:65]), r=[sO], w=[o_])

        groups = [int(x) for x in os.environ.get("P2BG", "0,1,2").split(",")]
        nblk_lim = int(os.environ.get("P2BN", "999"))
        u = 0
        for g in groups:
            d = DILS[g]
            span = 128 * d
            for n in range(min(MAIN // span, nblk_lim)):
                bi = (g * 64 + n) % 2
                q_, k_ = QT[bi], KT[bi]
                for pair in range(2):
                    S.dma("sp", q_, lambda e: e.dma_start(out=q_[:, pair, 0:span], in_=k.qts[g][pair, :, n * span:(n + 1) * span]), w=[q_])
                    k0 = HALO + (n - 1) * span
                    S.dma("sp", k_, lambda e: e.dma_start(out=k_[:, pair, 0:2 * span], in_=k.kts[g][pair, :, k0:k0 + 2 * span]), w=[k_])
                for r in range(d):
                    v_ = Vb[u % 2]
                    o_ = ot[u % 2]
                    v0 = HALO + (n - 1) * span + r
                    for kc in range(2):
                        S.dma("sp", v_, lambda e: e.dma_start(out=v_[:, kc, :],
                                                             in_=k.vss[g][v0 + kc * span:v0 + kc * span + 127 * d + 1:d, :]), w=[v_])
                    core(lambda pair, ph: q_[ph, pair, r:span:d],
                         lambda kc, pair, ph: k_[ph, pair, kc * span + r:(kc + 1) * span:d],
                         lambda kc, head: v_[:, kc, head * 64:(head + 1) * 64],
                         mhalo if n == 0 else mprev, o_, (q_, k_, v_))
                    t0 = n * span + r
                    S.dma("sp", o_, lambda e: e.dma_start(out=k.swo[g][t0:t0 + 127 * d + 1:d, :], in_=o_[:, :]), r=[o_])
                    u += 1
        if not os.environ.get("P2BNOSMP"):
            sq = A("sq", [128, 2, 128], BF16); sk = A("sk", [128, 2, 256], BF16); sv = A("sv", [128, 2, 256], BF16)
            ck = A("ck", [128, 512], F32); ckb = A("ckb", [128, 512], BF16); vst = A("vst", [128, 256], F32)
            pt = pb.get("ptr", 128, F32, "ptr")
            for s_ in range(NS):
                for g in range(3):
                    d = DILS[g]
                    for r in range(1 if d == 1 else TS):
                        nq = TS if d == 1 else 1
                        tok0 = s_ * TS + (0 if d == 1 else r)
                        o_ = ot[u % 2]
                        for tt in (sq, sk, sv):
                            S.op("pool", lambda e: e.memset(tt[:, :, :], 0.0), w=[tt])
                        S.op("pool", lambda e: e.memset(vst[:, :], 0.0), w=[vst])
                        S.dma("sp", ck, lambda e: e.dma_start(out=ck[:, :], in_=k.cwin[g][s_, r:r + 127 * d + 1:d, :]), w=[ck])
                        S.op("dve", lambda e: e.tensor_copy(out=ckb[:, :], in_=ck[:, :]), r=[ck], w=[ckb])
                        for pair in range(2):
                            S.op("pool", lambda e: e.tensor_copy(out=sq[:, pair, 0:nq], in_=k.smp_qk[:, g, 0, pair, tok0:tok0 + nq]), r=[k.smp_qk], w=[sq])
                            S.op("pool", lambda e: e.tensor_copy(out=sk[:, pair, 128:128 + nq], in_=k.smp_qk[:, g, 1, pair, tok0:tok0 + nq]),
                                 r=[k.smp_qk], w=[sk])
                            S.op("pe", lambda e: e.matmul(pt[:, :], lhsT=ckb[:, pair * 128:(pair + 1) * 128], rhs=identb[:, :], start=True, stop=True),
                                 r=[ckb, identb], w=[pt])
                            S.op("dve", lambda e: e.tensor_copy(out=sk[:, pair, 0:128], in_=pt[:, :]), r=[pt], w=[sk])
                        S.op("pool", lambda e: e.tensor_copy(out=sv[:, 0, :], in_=ckb[:, 256:512]), r=[ckb], w=[sv])
                        S.dma("sp", vst, lambda e: e.dma_start(out=vst[0:nq, :], in_=k.smp_kv[tok0:tok0 + nq, g, 256:512]), r=[k.smp_kv], w=[vst])
                        S.op("dve", lambda e: e.tensor_copy(out=sv[:, 1, :], in_=vst[:, :]), r=[vst], w=[sv])
                        core(lambda pair, ph: sq[ph, pair, :], lambda kc, pair, ph: sk[ph, pair, kc * 128:(kc + 1) * 128],
                             lambda kc, head: sv[:, kc, head * 64:(head + 1) * 64], mprev, o_, (sq, sk, sv))
                        S.dma("sp", o_, lambda e: e.dma_start(out=k.smp_swo[tok0:tok0 + nq, g, :], in_=o_[0:nq, :]), r=[o_], w=[k.smp_swo])
                        u += 1
        k.end_phase()


def phase3(k):
    import os
    nc, S = k.nc, k.S
    NT = NS * TS
    with ExitStack() as es:
        A = lambda name, shape, dt: k.track(TL(es.enter_context(nc.sbuf_tensor(name, shape, dt)), name))
        pb = PsumBlocks(nc, es, "p3_")
        identb, onesb = k.ident_bf, k.ones_bf
        iota = k.cst["iota"]
        KmT = A("KmT", [128, 8, 256], BF16); Vm = A("Vm", [128, 2, 1024], BF16)
        KsT = A("KsT", [128, NS, 8, 256], BF16); Vs = A("Vs", [128, NS, 2, 1024], BF16)
        pbig = [pb.get(f"pbig{i}", 512, F32, f"big{i}") for i in range(4)]
        with ExitStack() as es2:
            A2 = lambda name, shape, dt: k.track(TL(es2.enter_context(nc.sbuf_tensor(name, shape, dt)), name))
            wkv = load_weight_bf16(k, es2, "w_mkv3", k.w_mkv, D, 2048, gdram=k.g_memkv, col_chunk=2048)
            mx = A2("m3x", [128, D], F32); msq = A2("m3sq", [128, D], BF16); mss = A2("m3ss", [128, 1], F32)
            mrs = A2("m3rs", [128, 1], F32); mab = A2("m3ab", [128, D], BF16); maT = A2("m3aT", [128, 8, 256], BF16)
            cs = A2("m3cs", [128, 2048], F32); csb = A2("m3csb", [128, 2048], BF16)
            for t in range(2):
                S.dma("sp", mx, lambda e: e.dma_start(out=mx[:, :], in_=k.mem[t * 128:(t + 1) * 128, :]), w=[mx])
                S.op("act", lambda e: e.activation(out=msq[:, :], in_=mx[:, :], func=AF.Square, accum_out=mss[:, 0:1]), r=[mx], w=[msq, mss])
                rsqrt(S, mrs, mrs[:, :], mss, mss[:, :], 1.0 / D, EPS)
                S.op("act", lambda e: e.activation(out=mab[:, :], in_=mx[:, :], func=AF.Copy, scale=mrs[:, 0:1]), r=[mx, mrs], w=[mab])
                for half in range(2):
                    for c in range(4):
                        cc = half * 4 + c
                        S.op("pe", lambda e: e.matmul(pbig[0][:, c * 128:(c + 1) * 128], lhsT=mab[:, cc * 128:(cc + 1) * 128], rhs=identb[:, :],
                                                      start=True, stop=True), r=[mab, identb], w=[pbig[0]])
                    for c in range(4):
                        cc = half * 4 + c
                        S.op("dve", lambda e: e.tensor_copy(out=maT[:, cc, t * 128:(t + 1) * 128], in_=pbig[0][:, c * 128:(c + 1) * 128]),
                             r=[pbig[0]], w=[maT])
            for hc in range(8):
                for c in range(8):
                    S.op("pe", lambda e: e.matmul(pbig[1][:, 0:256], lhsT=wkv[:, c, hc * 128:(hc + 1) * 128], rhs=maT[:, c, :],
                                                  start=(c == 0), stop=(c == 7)), r=[wkv, maT], w=[pbig[1]])
                S.op("dve", lambda e: e.tensor_copy(out=KmT[:, hc, :], in_=pbig[1][:, 0:256]), r=[pbig[1]], w=[KmT])
            for kc in range(2):
                for nb in range(2):
                    for c in range(8):
                        S.op("pe", lambda e: e.matmul(pbig[2][:, :], lhsT=maT[:, c, kc * 128:(kc + 1) * 128], rhs=wkv[:, c, 1024 + nb * 512:1024 + (nb + 1) * 512],
                                                      start=(c == 0), stop=(c == 7)), r=[wkv, maT], w=[pbig[2]])
                    S.op("dve", lambda e: e.tensor_copy(out=Vm[:, kc, nb * 512:(nb + 1) * 512], in_=pbig[2][:, :]), r=[pbig[2]], w=[Vm])
            for s_ in range(NS):
                for kc in range(2):
                    S.dma("sp", cs, lambda e: e.dma_start(out=cs[:, :], in_=k.cmem[s_, kc * 128:(kc + 1) * 128, :]), w=[cs])
                    S.op("dve", lambda e: e.tensor_copy(out=csb[:, :], in_=cs[:, :]), r=[cs], w=[csb])
                    S.op("pool", lambda e: e.tensor_copy(out=Vs[:, s_, kc, :], in_=csb[:, 1024:2048]), r=[csb], w=[Vs])
                    for half in range(2):
                        for c in range(4):
                            hc = half * 4 + c
                            S.op("pe", lambda e: e.matmul(pbig[3][:, c * 128:(c + 1) * 128], lhsT=csb[:, hc * 128:(hc + 1) * 128], rhs=identb[:, :],
                                                          start=True, stop=True), r=[csb, identb], w=[pbig[3]])
                        for c in range(4):
                            hc = half * 4 + c
                            S.op("dve", lambda e: e.tensor_copy(out=KsT[:, s_, hc, kc * 128:(kc + 1) * 128], in_=pbig[3][:, c * 128:(c + 1) * 128]),
                                 r=[pbig[3]], w=[KsT])
            k.S.barrier()
        wout = load_weight_bf16(k, es, "w_out3", k.w_out, 768, D, col_chunk=1024)
        wmq = load_weight_bf16(k, es, "w_mq3", k.w_mq, D, D, gdram=k.g_memq, col_chunk=1024)
        wmo = load_weight_bf16(k, es, "w_mo3", k.w_mo, D, D, col_chunk=1024)
        wpq = load_weight_bf16(k, es, "w_pq3", k.w_pq, D, 2048, col_chunk=2048)
        skT = A("skT", [128, 16, 128], BF16)
        with ExitStack() as es2:
            A2 = lambda name, shape, dt: k.track(TL(es2.enter_context(nc.sbuf_tensor(name, shape, dt)), name))
            skf = A2("skf", [128, 128], F32); skb = A2("skb", [128, 128], BF16)
            for hp in range(16):
                S.dma("sp", skf, lambda e: e.dma_start(out=skf[:, :], in_=k.sub_keys[hp, :, :]), w=[skf])
                S.op("dve", lambda e: e.tensor_copy(out=skb[:, :], in_=skf[:, :]), r=[skf], w=[skb])
                S.op("pe", lambda e: e.matmul(pbig[0][:, 0:128], lhsT=skb[:, :], rhs=identb[:, :], start=True, stop=True), r=[skb, identb], w=[pbig[0]])
                S.op("dve", lambda e: e.tensor_copy(out=skT[:, hp, :], in_=pbig[0][:, 0:128]), r=[pbig[0]], w=[skT])
            k.S.barrier()
        gffn = A("gffn", [128, D], F32); gfin = A("gfin", [128, D], F32)
        S.dma("sp", gffn, lambda e: e.dma_start(out=gffn[:, :], in_=k.g_ffn.ap().partition_broadcast(128)), w=[gffn])
        S.dma("sp", gfin, lambda e: e.dma_start(out=gfin[:, :], in_=k.g_final.ap().partition_broadcast(128)), w=[gfin])
        LOHI_INIT = True
        xt = A("x3", [128, D], F32); cat = A("cat3", [128, 768], BF16); sw = [A(f"sw3_{g}", [128, 260], F32) for g in range(3)]
        rden = A("rden3", [128, 4], F32); catT = A("catT3", [128, 6, 128], BF16)
        h = A("h3", [128, D], F32); sqj = A("sqj3", [128, D], BF16); ssq = A("ssq3", [128, 1], F32); rstd = A("rstd3", [128, 1], F32)
        cb = A("cb3", [128, D], BF16); cT = A("cT3", [128, 8, 128], BF16)
        qmT = A("qmT3", [128, 8, 128], BF16); PTm = A("PTm3", [128, 256], BF16); rdm = A("rdm3", [128, 128], F32)
        attT = A("attT3", [128, 8, 128], BF16)
        fb = A("fb3", [128, D], BF16); fT = A("fT3", [128, 8, 128], BF16)
        qpT = A("qpT3", [128, 16, 128], BF16); sc = A("sc3", [128, 16, 128], F32); sc2 = A("sc23", [128, 128], F32)
        mv = A("mv3", [128, 16, 16], F32); mi = A("mi3", [128, 16, 16], U32); mif = A("mif3", [128, 16, 16], F32)
        cand = A("cand3", [128, 256], F32); cand2 = A("cand23", [128, 256], F32)
        cv = A("cv3", [128, 8, 16], F32); ci = A("ci3", [128, 8, 16], U32); cif = A("cif3", [128, 8, 16], F32)
        ia = A("ia3", [128, 8, 16], F32); ib = A("ib3", [128, 8, 16], F32)
        oh = A("oh3", [128, 16, 16], F32); lo16 = A("lo163", [128, 16], F32); hi16 = A("hi163", [128, 16], F32); i1 = A("i13", [128, 8, 16], F32); i2 = A("i23", [128, 8, 16], F32)
        eidf = A("eidf3", [128, 128], F32); eid = A("eid3", [128, 128], I32)
        gate = A("gate3", [128, 8, 16], F32); gsum = A("gsum3", [128, 8], F32)
        hid = A("hid3", [128, 128], F32); hx = A("hx3", [128, 128], F32); wgt = A("wgt3", [128, 128], F32)
        NB = 2
        Gu = [A(f"Gu{i}", [128, D], BF16) for i in range(NB)]; Gv = [A(f"Gv{i}", [128, D], BF16) for i in range(NB)]
        junk = sqj; dg = [A(f"dg{i}", [128, 128], BF16) for i in range(2)]
        yo = xt
        pout = [pb.get(f"pout{i}", 512, F32, f"out{i}") for i in range(2)]
        psm = pb.get("psm", 256, F32, "sm"); pden = pb.get("pden", 128, F32, "den")

        S.op("dve", lambda e: e.tensor_scalar(out=lo16[:, :], in0=iota[:, 0:16], scalar1=16.0, scalar2=None, op0=ALU.mult), r=[iota], w=[lo16])
        S.op("dve", lambda e: e.tensor_scalar(out=hi16[:, :], in0=iota[:, 0:16], scalar1=16.0, scalar2=16.0, op0=ALU.mult, op1=ALU.add), r=[iota], w=[hi16])

        def transposes(src_t, nchunk, dstT):
            for c0 in range(0, nchunk, 4):
                n_ = min(4, nchunk - c0)
                for c in range(n_):
                    S.op("pe", lambda e: e.matmul(pbig[0][:, c * 128:(c + 1) * 128], lhsT=src_t[:, (c0 + c) * 128:(c0 + c + 1) * 128], rhs=identb[:, :],
                                                  start=True, stop=True), r=[src_t, identb], w=[pbig[0]])
                for c in range(n_):
                    S.op("dve", lambda e: e.tensor_copy(out=dstT[:, c0 + c, :], in_=pbig[0][:, c * 128:(c + 1) * 128]), r=[pbig[0]], w=[dstT])

        def rmsn(src, out_bf, gvec=None):
            S.op("act", lambda e: e.activation(out=sqj[:, :], in_=src[:, :], func=AF.Square, accum_out=ssq[:, 0:1]), r=[src], w=[sqj, ssq])
            rsqrt(S, rstd, rstd[:, :], ssq, ssq[:, :], 1.0 / D, EPS)
            if gvec is None:
                S.op("act", lambda e: e.activation(out=out_bf[:, :], in_=src[:, :], func=AF.Copy, scale=rstd[:, 0:1]), r=[src, rstd], w=[out_bf])
            else:
                S.op("dve", lambda e: e.scalar_tensor_tensor(out=out_bf[:, :], in0=src[:, :], scalar=rstd[:, 0:1], in1=gvec[:, :],
                                                             op0=ALU.mult, op1=ALU.mult), r=[src, rstd, gvec], w=[out_bf])

        def top16(vals_t, vals_ap, scratch_t, scratch_ap, mv_ap, mi_ap, mv_t, mi_t):
            S.op("dve", lambda e: e.max(out=mv_ap[:, 0:8], in_=vals_ap), r=[vals_t], w=[mv_t])
            S.op("dve", lambda e: e.max_index(out=mi_ap[:, 0:8], in_max=mv_ap[:, 0:8], in_values=vals_ap), r=[vals_t, mv_t], w=[mi_t])
            S.op("dve", lambda e: e.match_replace(out=scratch_ap, in_to_replace=mv_ap[:, 0:8], in_values=vals_ap, imm_value=-1e30),
                 r=[vals_t, mv_t], w=[scratch_t])
            S.op("dve", lambda e: e.max(out=mv_ap[:, 8:16], in_=scratch_ap), r=[scratch_t], w=[mv_t])
            S.op("dve", lambda e: e.max_index(out=mi_ap[:, 8:16], in_max=mv_ap[:, 8:16], in_values=scratch_ap), r=[scratch_t, mv_t], w=[mi_t])

        tiles = list(range(int(os.environ.get("P3TILES", str(MAIN // 128))))) + ([] if os.environ.get("P3NOSMP") else ["smp"])
        for tau in tiles:
            smp = tau == "smp"
            if smp:
                S.op("dve", lambda e: e.memset(xt[:, :], 0.0), w=[xt])
                S.op("pool", lambda e: e.memset(cat[:, :], 0.0), w=[cat])
                for g in range(3):
                    S.op("pool", lambda e: e.memset(sw[g][:, :], 1.0), w=[sw[g]])
                    S.dma("sp", sw[g], lambda e: e.dma_start(out=sw[g][0:NT, :], in_=k.smp_swo[:, g, :]), r=[k.smp_swo], w=[sw[g]])
                S.dma("sp", xt, lambda e: e.dma_start(out=xt[0:NT, :], in_=k.xs[:, :]), w=[xt])
                S.dma("sp", cat, lambda e: e.dma_start(out=cat[0:NT, 0:512], in_=k.smp_cat[:, :]), r=[k.smp_cat], w=[cat])
            else:
                t0 = tau * 128
                S.dma("sp", xt, lambda e: e.dma_start(out=xt[:, :], in_=k.xe[PRE + t0:PRE + t0 + 128, :]), w=[xt])
                S.dma("sp", cat, lambda e: e.dma_start(out=cat[:, 0:512], in_=k.cat[t0:t0 + 128, :]), w=[cat])
                for g in range(3):
                    S.dma("sp", sw[g], lambda e: e.dma_start(out=sw[g][:, :], in_=k.swo[g][t0:t0 + 128, :]), w=[sw[g]])
            S.op("dve", lambda e: e.tensor_tensor(out=sw[0][:, :], in0=sw[0][:, :], in1=sw[1][:, :], op=ALU.add), r=[sw[0], sw[1]], w=[sw[0]])
            S.op("dve", lambda e: e.tensor_tensor(out=sw[0][:, :], in0=sw[0][:, :], in1=sw[2][:, :], op=ALU.add), r=[sw[0], sw[2]], w=[sw[0]])
            for hd in range(4):
                S.op("dve", lambda e: e.reciprocal(out=rden[:, hd:hd + 1], in_=sw[0][:, hd * 65 + 64:hd * 65 + 65]), r=[sw[0]], w=[rden])
                S.op("dve", lambda e: e.tensor_scalar(out=cat[:, 512 + hd * 64:512 + (hd + 1) * 64], in0=sw[0][:, hd * 65:hd * 65 + 64],
                                                      scalar1=rden[:, hd:hd + 1], scalar2=None, op0=ALU.mult), r=[sw[0], rden], w=[cat])
            transposes(cat, 6, catT)
            for nb in range(2):
                for c in range(6):
                    S.op("pe", lambda e: e.matmul(pout[nb][:, :], lhsT=catT[:, c, :], rhs=wout[:, c, nb * 512:(nb + 1) * 512],
                                                  start=(c == 0), stop=(c == 5)), r=[catT, wout], w=[pout[nb]])
                S.op("dve", lambda e: e.tensor_tensor(out=h[:, nb * 512:(nb + 1) * 512], in0=pout[nb][:, :], in1=xt[:, nb * 512:(nb + 1) * 512], op=ALU.add),
                     r=[pout[nb], xt], w=[h])
            rmsn(h, cb)
            transposes(cb, 8, cT)
            for half in range(2):
                for c4 in range(4):
                    hc = half * 4 + c4
                    for c in range(8):
                        S.op("pe", lambda e: e.matmul(pbig[1][:, c4 * 128:(c4 + 1) * 128], lhsT=wmq[:, c, hc * 128:(hc + 1) * 128], rhs=cT[:, c, :],
                                                      start=(c == 0), stop=(c == 7)), r=[wmq, cT], w=[pbig[1]])
                for c4 in range(4):
                    hc = half * 4 + c4
                    S.op("dve", lambda e: e.tensor_copy(out=qmT[:, hc, :], in_=pbig[1][:, c4 * 128:(c4 + 1) * 128]), r=[pbig[1]], w=[qmT])
            segs = [(s_ * TS, TS, s_) for s_ in range(NS)] if smp else [(0, 128, None)]
            if smp:
                S.op("pool", lambda e: e.memset(attT[:, :, :], 0.0), w=[attT])
            for hd in range(4):
                for (q0, qn, s_) in segs:
                    kT_ap = (lambda hc, kc: KmT[:, hc, kc * 128:(kc + 1) * 128]) if s_ is None else (lambda hc, kc: KsT[:, s_, hc, kc * 128:(kc + 1) * 128])
                    v_ap = (lambda kc, col: Vm[:, kc, col:col + 128]) if s_ is None else (lambda kc, col: Vs[:, s_, kc, col:col + 128])
                    kt_t, v_t = (KmT, Vm) if s_ is None else (KsT, Vs)
                    for kc in range(2):
                        for cc in range(2):
                            S.op("pe", lambda e: e.matmul(psm[:, kc * 128:kc * 128 + qn], lhsT=kT_ap(hd * 2 + cc, kc), rhs=qmT[:, hd * 2 + cc, q0:q0 + qn],
                                                          start=(cc == 0), stop=(cc == 1)), r=[kt_t, qmT], w=[psm])
                    for kc in range(2):
                        S.op("act", lambda e: e.activation(out=PTm[:, kc * 128:kc * 128 + qn], in_=psm[:, kc * 128:kc * 128 + qn], func=AF.Exp, scale=1.0 / 16),
                             r=[psm], w=[PTm])
                    for kc in range(2):
                        S.op("pe", lambda e: e.matmul(pden[:, 0:qn], lhsT=onesb[:, :], rhs=PTm[:, kc * 128:kc * 128 + qn], start=(kc == 0), stop=(kc == 1)),
                             r=[onesb, PTm], w=[pden])
                    S.op("dve", lambda e: e.reciprocal(out=rdm[:, 0:qn], in_=pden[:, 0:qn]), r=[pden], w=[rdm])
                    for cc in range(2):
                        for kc in range(2):
                            S.op("pe", lambda e: e.matmul(pbig[2][:, cc * 128:cc * 128 + qn], lhsT=v_ap(kc, hd * 256 + cc * 128), rhs=PTm[:, kc * 128:kc * 128 + qn],
                                                          start=(kc == 0), stop=(kc == 1)), r=[v_t, PTm], w=[pbig[2]])
                    for cc in range(2):
                        S.op("dve", lambda e: e.tensor_tensor(out=attT[:, hd * 2 + cc, q0:q0 + qn], in0=pbig[2][:, cc * 128:cc * 128 + qn], in1=rdm[:, 0:qn], op=ALU.mult),
                             r=[pbig[2], rdm], w=[attT])
            for nb in range(2):
                for c in range(8):
                    S.op("pe", lambda e: e.matmul(pout[nb][:, :], lhsT=attT[:, c, :], rhs=wmo[:, c, nb * 512:(nb + 1) * 512],
                                                  start=(c == 0), stop=(c == 7)), r=[attT, wmo], w=[pout[nb]])
                S.op("dve", lambda e: e.tensor_tensor(out=h[:, nb * 512:(nb + 1) * 512], in0=pout[nb][:, :], in1=h[:, nb * 512:(nb + 1) * 512], op=ALU.add),
                     r=[pout[nb], h], w=[h])
            rmsn(h, fb, gffn)
            transposes(fb, 8, fT)
            for q4 in range(4):
                for c4 in range(4):
                    hp = q4 * 4 + c4
                    for c in range(8):
                        S.op("pe", lambda e: e.matmul(pbig[1][:, c4 * 128:(c4 + 1) * 128], lhsT=wpq[:, c, hp * 128:(hp + 1) * 128], rhs=fT[:, c, :],
                                                      start=(c == 0), stop=(c == 7)), r=[wpq, fT], w=[pbig[1]])
                for c4 in range(4):
                    hp = q4 * 4 + c4
                    S.op("dve", lambda e: e.tensor_copy(out=qpT[:, hp, :], in_=pbig[1][:, c4 * 128:(c4 + 1) * 128]), r=[pbig[1]], w=[qpT])
            for q4 in range(4):
                for c4 in range(4):
                    hp = q4 * 4 + c4
                    S.op("pe", lambda e: e.matmul(pbig[3][:, c4 * 128:(c4 + 1) * 128], lhsT=qpT[:, hp, :], rhs=skT[:, hp, :], start=True, stop=True),
                         r=[qpT, skT], w=[pbig[3]])
                for c4 in range(4):
                    hp = q4 * 4 + c4
                    S.op("dve", lambda e: e.tensor_copy(out=sc[:, hp, :], in_=pbig[3][:, c4 * 128:(c4 + 1) * 128]), r=[pbig[3]], w=[sc])
            for hp in range(16):
                top16(sc, sc[:, hp, :], sc2, sc2[:, :], mv[:, hp, :], mi[:, hp, :], mv, mi)
            S.op("dve", lambda e: e.tensor_copy(out=mif[:, :, :], in_=mi[:, :, :]), r=[mi], w=[mif])
            for hd in range(8):
                S.op("dve", lambda e: e.tensor_tensor(out=cand[:, :].rearrange("p (a b) -> p a b", a=16),
                                                      in0=mv[:, 2 * hd, :].unsqueeze(2).to_broadcast([128, 16, 16]),
                                                      in1=mv[:, 2 * hd + 1, :].unsqueeze(1).to_broadcast([128, 16, 16]), op=ALU.add), r=[mv], w=[cand])
                top16(cand, cand[:, :], cand2, cand2[:, :], cv[:, hd, :], ci[:, hd, :], cv, ci)
            S.op("dve", lambda e: e.tensor_copy(out=cif[:, :, :], in_=ci[:, :, :]), r=[ci], w=[cif])
            for hd in range(8):
                cb_ = cif[:, hd, :].unsqueeze(2).to_broadcast([128, 16, 16])
                S.op("dve", lambda e: e.tensor_tensor(out=oh[:, :, :], in0=cb_, in1=lo16[:, :].unsqueeze(1).to_broadcast([128, 16, 16]), op=ALU.is_ge),
                     r=[cif, lo16], w=[oh])
                S.op("dve", lambda e: e.tensor_tensor(out=cand2[:, :].rearrange("p (a b) -> p a b", a=16), in0=cb_, in1=hi16[:, :].unsqueeze(1).to_broadcast([128, 16, 16]), op=ALU.is_lt),
                     r=[cif, hi16], w=[cand2])
                S.op("dve", lambda e: e.tensor_tensor(out=oh[:, :, :], in0=oh[:, :, :], in1=cand2[:, :].rearrange("p (a b) -> p a b", a=16), op=ALU.mult), r=[oh, cand2], w=[oh])
                S.op("dve", lambda e: e.tensor_tensor(out=cand2[:, :].rearrange("p (a b) -> p a b", a=16), in0=oh[:, :, :], in1=mif[:, 2 * hd, :].unsqueeze(1).to_broadcast([128, 16, 16]), op=ALU.mult),
                     r=[oh, mif], w=[cand2])
                S.op("dve", lambda e: e.tensor_reduce(out=i1[:, hd, :], in_=cand2[:, :].rearrange("p (a b) -> p a b", a=16), axis=AX.X, op=ALU.add), r=[cand2], w=[i1])
                S.op("dve", lambda e: e.tensor_tensor(out=cand2[:, :].rearrange("p (a b) -> p a b", a=16), in0=oh[:, :, :], in1=lo16[:, :].unsqueeze(1).to_broadcast([128, 16, 16]), op=ALU.mult),
                     r=[oh, lo16], w=[cand2])
                S.op("dve", lambda e: e.tensor_reduce(out=ia[:, hd, :], in_=cand2[:, :].rearrange("p (a b) -> p a b", a=16), axis=AX.X, op=ALU.add), r=[cand2], w=[ia])
            S.op("dve", lambda e: e.tensor_tensor(out=ib[:, :, :], in0=cif[:, :, :], in1=ia[:, :, :], op=ALU.subtract), r=[cif, ia], w=[ib])
            for hd in range(8):
                S.op("dve", lambda e: e.tensor_tensor(out=oh[:, :, :], in0=ib[:, hd, :].unsqueeze(2).to_broadcast([128, 16, 16]),
                                                      in1=iota[:, 0:16].unsqueeze(1).to_broadcast([128, 16, 16]), op=ALU.is_equal), r=[ib, iota], w=[oh])
                S.op("dve", lambda e: e.tensor_tensor(out=oh[:, :, :], in0=oh[:, :, :], in1=mif[:, 2 * hd + 1, :].unsqueeze(1).to_broadcast([128, 16, 16]), op=ALU.mult),
                     r=[oh, mif], w=[oh])
                S.op("dve", lambda e: e.tensor_reduce(out=i2[:, hd, :], in_=oh[:, :, :], axis=AX.X, op=ALU.add), r=[oh], w=[i2])
            S.op("dve", lambda e: e.scalar_tensor_tensor(out=eidf[:, :], in0=i1[:, :, :].rearrange("p a b -> p (a b)"), scalar=128.0,
                                                         in1=i2[:, :, :].rearrange("p a b -> p (a b)"), op0=ALU.mult, op1=ALU.add), r=[i1, i2], w=[eidf])
            S.op("dve", lambda e: e.tensor_copy(out=eid[:, :], in_=eidf[:, :]), r=[eidf], w=[eid])
            for hd in range(8):
                S.op("dve", lambda e: e.tensor_scalar(out=gate[:, hd, :], in0=cv[:, hd, :], scalar1=cv[:, hd, 0:1], scalar2=None, op0=ALU.subtract),
                     r=[cv], w=[gate])
            S.op("act", lambda e: e.activation(out=gate[:, :, :].rearrange("p a b -> p (a b)"), in_=gate[:, :, :].rearrange("p a b -> p (a b)"), func=AF.Exp),
                 r=[gate], w=[gate])
            S.op("dve", lambda e: e.tensor_reduce(out=gsum[:, :], in_=gate[:, :, :], axis=AX.X, op=ALU.add), r=[gate], w=[gsum])
            S.op("dve", lambda e: e.reciprocal(out=gsum[:, :], in_=gsum[:, :]), r=[gsum], w=[gsum])
            for hd in range(8):
                S.op("dve", lambda e: e.tensor_scalar(out=gate[:, hd, :], in0=gate[:, hd, :], scalar1=gsum[:, hd:hd + 1], scalar2=None, op0=ALU.mult),
                     r=[gate, gsum], w=[gate])
            for sl in range(128):
                g_ = Gu[sl % NB]
                S.dma("pool", g_, lambda e: e.indirect_dma_start(out=g_[:, :], out_offset=None, in_=k.expert_u.ap(),
                                                                 in_offset=bass.IndirectOffsetOnAxis(ap=eid[:, sl:sl + 1], axis=0)), r=[eid], w=[g_])
                S.op("dve", lambda e: e.tensor_tensor(out=xt[:, :], in0=g_[:, :], in1=fb[:, :], op=ALU.mult), r=[g_, fb], w=[xt])
                S.op("dve", lambda e: e.tensor_reduce(out=hid[:, sl:sl + 1], in_=xt[:, :], axis=AX.X, op=ALU.add), r=[xt], w=[hid])
            S.op("dve", lambda e: e.tensor_tensor(out=hx[:, :], in0=hid[:, :], in1=hid[:, :], op=ALU.mult), r=[hid], w=[hx])
            S.op("dve", lambda e: e.tensor_scalar(out=hx[:, :], in0=hx[:, :], scalar1=0.044715, scalar2=1.0, op0=ALU.mult, op1=ALU.add), r=[hx], w=[hx])
            S.op("dve", lambda e: e.tensor_tensor(out=hx[:, :], in0=hx[:, :], in1=hid[:, :], op=ALU.mult), r=[hx, hid], w=[hx])
            S.op("act", lambda e: e.activation(out=hx[:, :], in_=hx[:, :], func=AF.Tanh, scale=0.7978845608028654), r=[hx], w=[hx])
            S.op("dve", lambda e: e.tensor_scalar(out=hx[:, :], in0=hx[:, :], scalar1=1.0, scalar2=0.5, op0=ALU.add, op1=ALU.mult), r=[hx], w=[hx])
            S.op("dve", lambda e: e.tensor_tensor(out=hx[:, :], in0=hx[:, :], in1=hid[:, :], op=ALU.mult), r=[hx, hid], w=[hx])
            S.op("dve", lambda e: e.tensor_tensor(out=wgt[:, :], in0=hx[:, :], in1=gate[:, :, :].rearrange("p a b -> p (a b)"), op=ALU.mult), r=[hx, gate], w=[wgt])
            for sl in range(128):
                g_ = Gv[sl % NB]
                d_ = dg[sl % 2]
                S.dma("pool", g_, lambda e: e.indirect_dma_start(out=g_[:, :], out_offset=None, in_=k.expert_v.ap(),
                                                                 in_offset=bass.IndirectOffsetOnAxis(ap=eid[:, sl:sl + 1], axis=0)), r=[eid], w=[g_])
                S.op("dve", lambda e: e.tensor_scalar(out=d_[:, :], in0=identb[:, :], scalar1=wgt[:, sl:sl + 1], scalar2=None, op0=ALU.mult),
                     r=[identb, wgt], w=[d_])
                for nb in range(2):
                    S.op("pe", lambda e: e.matmul(pout[nb][:, :], lhsT=d_[:, :], rhs=g_[:, nb * 512:(nb + 1) * 512], start=(sl == 0), stop=(sl == 127)),
                         r=[d_, g_], w=[pout[nb]])
            for nb in range(2):
                S.op("dve", lambda e: e.tensor_tensor(out=h[:, nb * 512:(nb + 1) * 512], in0=pout[nb][:, :], in1=h[:, nb * 512:(nb + 1) * 512], op=ALU.add),
                     r=[pout[nb], h], w=[h])
            S.op("act", lambda e: e.activation(out=sqj[:, :], in_=h[:, :], func=AF.Square, accum_out=ssq[:, 0:1]), r=[h], w=[sqj, ssq])
            rsqrt(S, rstd, rstd[:, :], ssq, ssq[:, :], 1.0 / D, EPS)
            S.op("dve", lambda e: e.scalar_tensor_tensor(out=yo[:, :], in0=h[:, :], scalar=rstd[:, 0:1], in1=gfin[:, :], op0=ALU.mult, op1=ALU.mult),
                 r=[h, rstd, gfin], w=[yo])
            if smp:
                S.dma("sp", yo, lambda e: e.dma_start(out=k.y_smp[:, :], in_=yo[0:NT, :]), r=[yo])
            else:
                S.dma("sp", yo, lambda e: e.dma_start(out=k.y_main[tau * 128:(tau + 1) * 128, :], in_=yo[:, :]), r=[yo])
        k.end_phase()
```

```python
import numpy as np
from contextlib import ExitStack
import concourse.bass as bass
import concourse.mybir as mybir
from concourse.bass_utils import run_bass_kernel_spmd

F32 = mybir.dt.float32
BF16 = mybir.dt.bfloat16
I32 = mybir.dt.int32
U32 = mybir.dt.uint32
AF = mybir.ActivationFunctionType
ALU = mybir.AluOpType
AX = mybir.AxisListType

NCORES = 8
D = 1024
PRE = 4096
MAIN = 4096
EXT = PRE + MAIN
HALO = 2048
KR = HALO + MAIN
NS = 4
TS = 4
IN_DIM = 4360
CQ, CK, CV, CAG, CBG, CZ, CSW = 0, 512, 1024, 1536, 1540, 1544, 2056
DILS = (1, 4, 16)
EPS = 1e-6
NEG = -30000.0
SEM_EPOCH = 20000
import os as _os
EXPROWS = int(_os.environ.get("EXPROWS", "16384"))


class TL:
    def __init__(self, t, name):
        self.t = t
        self.name = name
        self.lw = None
        self.rd = []
        self.ds = None

    def __getitem__(self, k):
        return self.t[k]


class TLsub(TL):
    def __init__(self, bank, off, width, name):
        self.bank = bank
        self.t = bank.t
        self.name = name
        self.off = off
        self.width = width
        self.ds = None

    lw = property(lambda self: self.bank.lw, lambda self, v: setattr(self.bank, "lw", v))
    rd = property(lambda self: self.bank.rd, lambda self, v: setattr(self.bank, "rd", v))

    def __getitem__(self, key):
        r, c = key
        if isinstance(c, slice):
            a = self.off + (c.start or 0)
            b_ = self.off + (self.width if c.stop is None else c.stop)
            return self.t[r, a:b_]
        return self.t[r, self.off + c]


class TLview(TL):
    def __init__(self, parent, ap_fn, name):
        super().__init__(None, name)
        self.p = parent
        self.ap_fn = ap_fn

    def __getitem__(self, key):
        return self.ap_fn()[key]


class PsumBlocks:
    def __init__(self, nc, es, prefix):
        self.nc, self.es, self.prefix = nc, es, prefix
        self.banks = {}

    def get(self, name, width, dt, bank):
        per = 2048 // (4 if dt == F32 else 2)
        if bank not in self.banks:
            t = self.es.enter_context(self.nc.psum_tensor(f"{self.prefix}{bank}", [128, per], dt))
            self.banks[bank] = [TL(t, f"{self.prefix}{bank}"), 0]
        b = self.banks[bank]
        assert b[1] + width <= per
        off = b[1]
        b[1] += width
        return TLsub(b[0], off, width, name)


class _Rec:
    def __init__(self):
        self.call = None

    def __getattr__(self, name):
        def f(*a, **kw):
            self.call = (name, a, kw)
            return self
        return f


def _capture(fn):
    r = _Rec()
    fn(r)
    name, a, kw = r.call
    return lambda e: getattr(e, name)(*a, **kw)


class Sched:
    ENG = ("pe", "act", "dve", "pool", "sp")

    def __init__(self, nc):
        self.nc = nc
        self.eng = {"pe": nc.tensor, "act": nc.scalar, "dve": nc.vector, "pool": nc.gpsimd, "sp": nc.sync}
        self.rec = []
        self.dsems = []
        self.dsem_pool = []
        self.ninst = 0
        self.last = {e: None for e in self.ENG}

    def dsem(self, name):
        s = [self.nc.alloc_semaphore("d_" + name), 0, False]
        self.dsems.append(s)
        return s

    def _collect(self, e, r, w, is_dma):
        deps = []
        for t in r:
            if t.lw is not None:
                deps.append(t.lw)
        for t in w:
            if t.lw is not None:
                p = self.rec[t.lw]
                if is_dma or not (p["kind"] == "op" and p["e"] == e and e == "pe"):
                    deps.append(t.lw)
            for d in t.rd:
                p = self.rec[d]
                if is_dma or p["kind"] == "dma" or p["e"] != e or e != "pe":
                    deps.append(d)
        return deps

    def op(self, e, fn, r=(), w=()):
        deps = self._collect(e, r, w, False)
        idx = len(self.rec)
        self.rec.append({"kind": "op", "e": e, "fn": _capture(fn), "deps": deps})
        self.last[e] = idx
        for t in w:
            t.lw = idx
            t.rd = []
        for t in r:
            t.rd.append(idx)
        self.ninst += 1

    def dma(self, q, t, fn, r=(), w=()):
        if q == "pool" and (t.ds is None or not t.ds[2]):
            t.ds = self.dsem(t.name + "_sw")
            t.ds[2] = True
        if t.ds is None:
            t.ds = self.dsem_pool.pop() if self.dsem_pool else self.dsem(t.name)
        ds = t.ds
        deps = self._collect(q, r, w, True)
        if ds[1] + 16 >= 32000:
            sw_ = ds[2]
            t.ds = self.dsem(t.name + f"_r{len(self.dsems)}")
            t.ds[2] = sw_
            ds = t.ds
        ds[1] += 16
        idx = len(self.rec)
        self.rec.append({"kind": "dma", "e": q, "fn": _capture(fn), "deps": deps, "sem": ds[0], "val": ds[1]})
        for x in w:
            x.lw = idx
            x.rd = []
        for x in r:
            x.rd.append(idx)
        self.ninst += 1

    def release(self, tiles):
        for t in tiles:
            if t.ds is not None:
                self.dsem_pool.append(t.ds)
                t.ds = None

    def barrier(self, engines=None):
        deps = [v for v in self.last.values() if v is not None]
        dm = [(s[0], s[1]) for s in self.dsems if s[1]]
        self.rec.append({"kind": "bar", "deps": deps, "dm": dm, "engines": engines or self.ENG})

    def final_wait(self):
        self.barrier(engines=("sp",))

    def emit(self):
        rec = self.rec
        seq = {}
        cnt = {e: 0 for e in self.ENG}
        for i, r in enumerate(rec):
            if r["kind"] == "op":
                cnt[r["e"]] += 1
                seq[i] = cnt[r["e"]]
        awaited = set()

        def sweep(do_emit, ordv=None, sems=None):
            wseq = {e: {p: 0 for p in self.ENG} for e in self.ENG}
            wdma = {e: {} for e in self.ENG}
            for i, r in enumerate(rec):
                targets = r["engines"] if r["kind"] == "bar" else (r["e"],)
                for e in targets:
                    need = {}
                    for d in r["deps"]:
                        p = rec[d]
                        if p["kind"] == "op":
                            if seq[d] > wseq[e][p["e"]] and seq[d] > need.get(p["e"], (0, None))[0]:
                                need[p["e"]] = (seq[d], d)
                        else:
                            key = id(p["sem"])
                            if wdma[e].get(key, 0) < p["val"]:
                                wdma[e][key] = p["val"]
                                if do_emit:
                                    self.eng[e].wait_ge(p["sem"], p["val"])
                    for (sem, val) in r.get("dm", ()):
                        key = id(sem)
                        if wdma[e].get(key, 0) < val:
                            wdma[e][key] = val
                            if do_emit:
                                self.eng[e].wait_ge(sem, val)
                    for pe_, (sq, d) in need.items():
                        wseq[e][pe_] = sq
                        if do_emit:
                            o = ordv[d]
                            self.eng[e].wait_ge(sems[pe_][(o - 1) // SEM_EPOCH], (o - 1) % SEM_EPOCH + 1)
                        else:
                            awaited.add(d)
                if do_emit and r["kind"] != "bar":
                    ins = r["fn"](self.eng[r["e"]])
                    if r["kind"] == "dma":
                        ins.then_inc(r["sem"], 16)
                    elif i in awaited:
                        o = ordv[i]
                        ins.then_inc(sems[r["e"]][(o - 1) // SEM_EPOCH], 1)

        sweep(False)
        ordv = {}
        oc = {e: 0 for e in self.ENG}
        for i, r in enumerate(rec):
            if r["kind"] == "op" and i in awaited:
                oc[r["e"]] += 1
                ordv[i] = oc[r["e"]]
        sems = {e: [self.nc.alloc_semaphore(f"s_{e}_{j}") for j in range((oc[e] + SEM_EPOCH - 1) // SEM_EPOCH)] for e in self.ENG}
        self.nsig = dict(oc)
        sweep(True, ordv, sems)


class K:
    def __init__(self):
        self.tiles = []

    def track(self, t):
        self.tiles.append(t)
        return t

    def end_phase(self):
        self.S.barrier()
        self.S.release(self.tiles)
        self.tiles = []


def make_consts():
    c = {}
    idx = np.arange(128)
    same = (idx[:, None] // 64) == (idx[None, :] // 64)
    c["ident"] = np.eye(128, dtype=np.float32)
    c["trit"] = ((idx[:, None] <= idx[None, :]) & same).astype(np.float32)
    c["blk"] = same.astype(np.float32)
    c["mTneg"] = np.where((idx[None, :] >= idx[:, None]) & same, 0.0, NEG).astype(np.float32)
    c["mSpos"] = np.where((idx[:, None] > idx[None, :]) & same, 0.0, -NEG).astype(np.float32)
    c["swprev"] = np.where(idx[:, None] >= idx[None, :], 0.0, NEG).astype(np.float32)
    c["swcur"] = np.where(idx[:, None] <= idx[None, :], 0.0, NEG).astype(np.float32)
    c["ones"] = np.ones((128, 128), np.float32)
    c["iota"] = np.tile(np.arange(128, dtype=np.float32), (128, 1))
    return c


CONST_NAMES = ("ident", "trit", "blk", "mTneg", "mSpos", "swprev", "swcur", "ones", "iota")


def declare_io(k, debug):
    nc = k.nc
    I = lambda n, s, dt=F32: nc.dram_tensor(n, list(s), dt, kind="ExternalInput")
    O = lambda n, s, dt=F32: nc.dram_tensor(n, list(s), dt, kind="ExternalOutput")
    SCR = (lambda n, s, dt: nc.dram_tensor(n, list(s), dt, kind="ExternalOutput")) if debug else \
          (lambda n, s, dt: nc.dram_tensor(n, list(s), dt, kind="Internal"))
    k.xe = I("xe", [EXT, D])
    k.xs = I("xs", [NS * TS, D])
    k.st_delta = I("st_delta", [NS, 4, 128, 128])
    k.st_conv = I("st_conv", [NS, 3, 1536])
    k.cwin = [I(f"cwin{g}", [NS, 128 * DILS[g], 512]) for g in range(3)]
    k.cmem = I("cmem", [NS, 256, 2048])
    k.mem = I("mem", [256, D])
    k.halo = I("halo", [128, 128])
    k.consts = I("consts", [len(CONST_NAMES), 128, 128])
    for n, s in (("g_mix", [D]), ("w_in", [D, IN_DIM]), ("conv_w", [4, 1536]), ("a_log", [4]), ("dt_bias", [4]),
                 ("g_onorm", [128]), ("w_out", [768, D]), ("g_memq", [D]), ("g_memkv", [D]), ("w_mq", [D, D]),
                 ("w_mkv", [D, 2048]), ("w_mo", [D, D]), ("g_ffn", [D]), ("w_pq", [D, 2048]),
                 ("sub_keys", [16, 128, 128]), ("expert_u", [EXPROWS, D]), ("expert_v", [EXPROWS, D]), ("g_final", [D])):
        setattr(k, n, I(n, s))
    k.y_main = O("y_main", [MAIN, D])
    k.y_smp = O("y_smp", [NS * TS, D])
    k.p_delta = O("p_delta", [4, 128, 128])
    k.p_conv = O("p_conv", [3, 1536])
    k.p_win = [O(f"p_win{g}", [128 * DILS[g], 512]) for g in range(3)]
    k.p_mem = O("p_mem", [256, 2048])
    k.s_delta = O("s_delta", [NS, 4, 128, 128])
    k.s_conv = O("s_conv", [NS, 3, 1536])
    k.s_win = [O(f"s_win{g}", [NS, 128 * DILS[g], 512]) for g in range(3)]
    k.dnq = SCR("dnq", [4, 128, EXT], BF16)
    k.dnk = SCR("dnk", [4, 128, EXT], BF16)
    k.dnv = SCR("dnv", [4, 128, EXT], BF16)
    k.gbs = SCR("gbs", [EXT, 8], F32)
    k.zs = SCR("zs", [MAIN, 512], BF16)
    k.qts = [SCR(f"qts{g}", [2, 128, MAIN], BF16) for g in range(3)]
    k.kts = [SCR(f"kts{g}", [2, 128, KR], BF16) for g in range(3)]
    k.vss = [SCR(f"vss{g}", [KR, 256], BF16) for g in range(3)]
    SWO = (lambda n, s, dt: nc.dram_tensor(n, list(s), dt, kind="ExternalOutput")) if _os.environ.get("SWO_OUT") else SCR
    k.swo = [SWO(f"swo{g}", [MAIN, 260], F32) for g in range(3)]
    k.cat = SCR("cat", [MAIN, 512], BF16)


def evac(S, i, out_t, out_ap, in_t, in_ap, rx=(), **kw):
    if i % 2 == 0 and len(out_ap.shape) == 2 and len(in_ap.shape) == 2:
        S.op("act", lambda e: e.activation(out=out_ap, in_=in_ap, func=AF.Copy, **kw), r=[in_t, *rx], w=[out_t])
    else:
        if "scale" in kw:
            S.op("dve", lambda e: e.tensor_scalar(out=out_ap, in0=in_ap, scalar1=kw["scale"], scalar2=None, op0=ALU.mult),
                 r=[in_t, *rx], w=[out_t])
        else:
            S.op("dve", lambda e: e.tensor_copy(out=out_ap, in_=in_ap), r=[in_t, *rx], w=[out_t])


def rsqrt(S, out_t, out_ap, in_t, in_ap, mul, add):
    S.op("act", lambda e: e.activation(out=out_ap, in_=in_ap, func=AF.Sqrt, scale=mul, bias=add), r=[in_t], w=[out_t])
    S.op("dve", lambda e: e.reciprocal(out=out_ap, in_=out_ap), r=[out_t], w=[out_t])


def load_weight_bf16(k, es, name, wdram, rows, cols, gdram=None, col_chunk=None):
    nc, S = k.nc, k.S
    nch = rows // 128
    wbf = TL(es.enter_context(nc.sbuf_tensor(name + "_bf", [128, nch, cols], BF16)), name)
    gcol = None
    if gdram is not None:
        gcol = k.track(TL(es.enter_context(nc.sbuf_tensor(name + "_g", [128, nch], F32)), name + "_g"))
        S.dma("sp", gcol, lambda e: e.dma_start(out=gcol[:, :], in_=gdram.ap().rearrange("(c p) -> p c", p=128),
                                                 allow_slow_non_contiguous=True), w=[gcol])
    cc = col_chunk or cols
    with ExitStack() as es2:
        st = [k.track(TL(es2.enter_context(nc.sbuf_tensor(f"{name}_st{i}", [128, cc], F32)), f"{name}_st{i}")) for i in range(2)]
        n = 0
        for c in range(nch):
            for c0 in range(0, cols, cc):
                w_ = min(cc, cols - c0)
                s_ = st[n % 2]
                S.dma("sp", s_, lambda e: e.dma_start(out=s_[:, 0:w_], in_=wdram[c * 128:(c + 1) * 128, c0:c0 + w_]), w=[s_])
                if gcol is not None:
                    evac(S, n, wbf, wbf[:, c, c0:c0 + w_], s_, s_[:, 0:w_], rx=[gcol], scale=gcol[:, c:c + 1])
                else:
                    evac(S, n, wbf, wbf[:, c, c0:c0 + w_], s_, s_[:, 0:w_])
                n += 1
        S.barrier()
    return wbf


def phase1(k):
    nc, S = k.nc, k.S
    with ExitStack() as es:
        A = lambda name, shape, dt: k.track(TL(es.enter_context(nc.sbuf_tensor(name, shape, dt)), name))
        P = lambda name, shape, dt: TL(es.enter_context(nc.psum_tensor(name, shape, dt)), name)
        wbf = load_weight_bf16(k, es, "w_in", k.w_in, D, IN_DIM, gdram=k.g_mix, col_chunk=2180)
        cw = A("cw", [128, 12, 4], F32)
        for j in range(4):
            S.dma("sp", cw, lambda e: e.dma_start(out=cw[:, :, j], in_=k.conv_w[j, :].rearrange("(c p) -> p c", p=128),
                                                  allow_slow_non_contiguous=True), w=[cw])
        dtb = A("dtb", [128, 4], F32)
        S.dma("sp", dtb, lambda e: e.dma_start(out=dtb[:, :], in_=k.dt_bias.ap().partition_broadcast(128)), w=[dtb])
        nega = A("nega", [128, 4], F32)
        S.dma("sp", nega, lambda e: e.dma_start(out=nega[:, :], in_=k.a_log.ap().partition_broadcast(128)), w=[nega])
        S.op("act", lambda e: e.activation(out=nega[:, :], in_=nega[:, :], func=AF.Exp), r=[nega], w=[nega])
        S.op("dve", lambda e: e.tensor_scalar(out=nega[:, :], in0=nega[:, :], scalar1=-1.0, scalar2=None, op0=ALU.mult), r=[nega], w=[nega])
        xt = [A(f"xt{i}", [128, D], F32) for i in range(2)]
        xsm = A("xsm", [128, D], F32)
        sqj = A("sqj", [128, D], BF16)
        ssq = [A(f"ssq{i}", [128, 1], F32) for i in range(2)]
        rstd = [A(f"rstd{i}", [128, 1], F32) for i in range(2)]
        ab = [A(f"ab{i}", [128, D], BF16) for i in range(2)]
        aT = [A(f"aT{i}", [128, 8, 512], BF16) for i in range(2)]
        xp = [A(f"xp{i}", [128, 515], F32) for i in range(2)]
        carry = A("carry", [128, 12, 3], F32)
        acc = [A(f"acc{i}", [128, 512], F32) for i in range(2)]
        act_ = [A(f"actt{i}", [128, 512], F32) for i in range(2)]
        sq2 = [A(f"sq2{i}", [128, 512], BF16) for i in range(2)]
        rn = [A(f"rn{i}", [128, 512], F32) for i in range(2)]
        outb = [A(f"outb{i}", [128, 512], BF16) for i in range(3)]
        qkp = [A(f"qkp{i}", [128, 512], BF16) for i in range(3)]
        gb = [A(f"gb{i}", [128, 8], F32) for i in range(2)]
        gtmp = [A(f"gtmp{i}", [128, 4], F32) for i in range(4)]
        zb = [A(f"zb{i}", [128, 512], BF16) for i in range(2)]
        vsb = [A(f"vsb{i}", [128, 256], BF16) for i in range(3)]
        kvf = [A(f"kvf{i}", [128, 512], F32) for i in range(3)]
        xps = A("xps", [128, 12, NS, 7], F32)
        ones_bf = k.ones_bf
        pT = [P(f"pT{i}", [128, 8, 128], BF16) for i in range(2)]
        pacc = [P(f"pacc{i}", [128, 512], F32) for i in range(2)]
        pl2 = P("pl2", [128, 512], F32)
        ptm = [P(f"ptm{i}", [128, 512], F32) for i in range(2)]
        pg = P("pg", [128, 8], F32)
        S.op("dve", lambda e: e.memset(carry[:, :, :], 0.0), w=[carry])
        S.op("dve", lambda e: e.memset(xsm[:, :], 0.0), w=[xsm])
        ctr = {"ev": 0, "acc": 0, "tm": 0, "ob": 0}

        def norm_transpose(xtile, i, aTt, col0):
            S.op("act", lambda e: e.activation(out=sqj[:, :], in_=xtile[:, :], func=AF.Square, accum_out=ssq[i][:, 0:1]),
                 r=[xtile], w=[sqj, ssq[i]])
            rsqrt(S, rstd[i], rstd[i][:, :], ssq[i], ssq[i][:, :], 1.0 / D, EPS)
            S.op("act", lambda e: e.activation(out=ab[i][:, :], in_=xtile[:, :], func=AF.Copy, scale=rstd[i][:, 0:1]),
                 r=[xtile, rstd[i]], w=[ab[i]])
            for c in range(8):
                S.op("pe", lambda e: e.transpose(out=pT[i][:, c, :], in_=ab[i][:, c * 128:(c + 1) * 128], identity=k.ident_bf[:, :]),
                     r=[ab[i], k.ident_bf], w=[pT[i]])
            ctr["ev"] += 1
            evac(S, ctr["ev"], aTt, aTt[:, :, col0:col0 + 128], pT[i], pT[i][:, :, :])

        def fm_chunk(aTt, nt, col):
            ps = pacc[ctr["acc"] % 2]
            ctr["acc"] += 1
            for c in range(8):
                S.op("pe", lambda e: e.matmul(ps[:, 0:nt], lhsT=wbf[:, c, col:col + 128], rhs=aTt[:, c, 0:nt],
                                              start=(c == 0), stop=(c == 7)), r=[wbf, aTt], w=[ps])
            return ps

        def tm_chunk(aTt, t, col, ncol, ps=None):
            if ps is None:
                ps = ptm[ctr["tm"] % 2]
                ctr["tm"] += 1
            for c in range(8):
                S.op("pe", lambda e: e.matmul(ps[:, 0:ncol], lhsT=aTt[:, c, t * 128:(t + 1) * 128], rhs=wbf[:, c, col:col + ncol],
                                              start=(c == 0), stop=(c == 7)), r=[wbf, aTt], w=[ps])
            return ps

        def dn_post(ps, nt, ci, src_t, src_ap, dst_fn):
            j = ctr["ob"] % 2
            S.op("act", lambda e: e.activation(out=act_[j][:, 0:nt], in_=src_ap, func=AF.Silu), r=[src_t], w=[act_[j]])
            ob = outb[ctr["ob"] % 3]
            ctr["ob"] += 1
            if ci < 8:
                S.op("act", lambda e: e.activation(out=sq2[j][:, 0:nt], in_=act_[j][:, 0:nt], func=AF.Square), r=[act_[j]], w=[sq2[j]])
                S.op("pe", lambda e: e.matmul(pl2[:, 0:nt], lhsT=ones_bf[:, :], rhs=sq2[j][:, 0:nt], start=True, stop=True),
                     r=[ones_bf, sq2[j]], w=[pl2])
                rsqrt(S, rn[j], rn[j][:, 0:nt], pl2, pl2[:, 0:nt], 1.0, EPS)
                if ci < 4:
                    S.op("dve", lambda e: e.scalar_tensor_tensor(out=ob[:, 0:nt], in0=act_[j][:, 0:nt], scalar=128 ** -0.5,
                                                                 in1=rn[j][:, 0:nt], op0=ALU.mult, op1=ALU.mult),
                         r=[act_[j], rn[j]], w=[ob])
                else:
                    S.op("dve", lambda e: e.tensor_tensor(out=ob[:, 0:nt], in0=act_[j][:, 0:nt], in1=rn[j][:, 0:nt], op=ALU.mult),
                         r=[act_[j], rn[j]], w=[ob])
            else:
                S.op("dve", lambda e: e.tensor_copy(out=ob[:, 0:nt], in_=act_[j][:, 0:nt]), r=[act_[j]], w=[ob])
            dst_fn(ob)

        def gates(ps_g, rows, dst_t, dst_ap):
            g0, g1, g2, g3 = gtmp
            S.op("act", lambda e: e.activation(out=dst_ap[:, 4:8], in_=ps_g[0:rows, 4:8], func=AF.Sigmoid), r=[ps_g], w=[dst_t])
            S.op("dve", lambda e: e.tensor_tensor(out=g0[0:rows, :], in0=ps_g[0:rows, 0:4], in1=dtb[0:rows, :], op=ALU.add),
                 r=[ps_g, dtb], w=[g0])
            S.op("act", lambda e: e.activation(out=g1[0:rows, :], in_=g0[0:rows, :], func=AF.Abs), r=[g0], w=[g1])
            S.op("act", lambda e: e.activation(out=g2[0:rows, :], in_=g1[0:rows, :], func=AF.Exp, scale=-1.0), r=[g1], w=[g2])
            S.op("act", lambda e: e.activation(out=g3[0:rows, :], in_=g2[0:rows, :], func=AF.Ln, bias=1.0), r=[g2], w=[g3])
            S.op("dve", lambda e: e.scalar_tensor_tensor(out=g1[0:rows, :], in0=g0[0:rows, :], scalar=0.0, in1=g3[0:rows, :],
                                                         op0=ALU.max, op1=ALU.add), r=[g0, g3], w=[g1])
            S.op("dve", lambda e: e.tensor_tensor(out=dst_ap[:, 0:4], in0=g1[0:rows, :], in1=nega[0:rows, :], op=ALU.mult),
                 r=[g1, nega], w=[dst_t])

        nsup = EXT // 512
        import os
        sups = range(nsup) if "P1SUP" not in os.environ else [int(x) for x in os.environ["P1SUP"].split(",") if x]
        for s in sups:
            aTt = aT[s % 2]
            in_main = s * 512 >= PRE
            in_kr = s * 512 >= PRE - HALO
            for t in range(4):
                x_ = xt[t % 2]
                r0 = s * 512 + t * 128
                S.dma("sp", x_, lambda e: e.dma_start(out=x_[:, :], in_=k.xe[r0:r0 + 128, :]), w=[x_])
                norm_transpose(x_, t % 2, aTt, t * 128)
            for ci in range(12):
                ps = fm_chunk(aTt, 512, ci * 128)
                xp_ = xp[ci % 2]
                S.op("act", lambda e: e.activation(out=xp_[:, 3:515], in_=ps[:, :], func=AF.Copy), r=[ps], w=[xp_])
                S.op("pool", lambda e: e.tensor_copy(out=xp_[:, 0:3], in_=carry[:, ci, :]), r=[carry], w=[xp_])
                S.op("pool", lambda e: e.tensor_copy(out=carry[:, ci, :], in_=xp_[:, 512:515]), r=[xp_], w=[carry])
                if s == nsup - 1 and not os.environ.get("SKIP_PCONV"):
                    S.dma("sp", xp_, lambda e: e.dma_start(out=k.p_conv.ap()[:, ci * 128:(ci + 1) * 128].rearrange("j p -> p j"),
                                                             in_=xp_[:, 512:515], allow_slow_non_contiguous=True), r=[xp_])
                ac = acc[ci % 2]
                S.op("dve", lambda e: e.tensor_scalar(out=ac[:, :], in0=xp_[:, 0:512], scalar1=cw[:, ci, 0:1], scalar2=None, op0=ALU.mult),
                     r=[xp_, cw], w=[ac])
                for j in range(1, 4):
                    S.op("dve", lambda e: e.scalar_tensor_tensor(out=ac[:, :], in0=xp_[:, j:j + 512], scalar=cw[:, ci, j:j + 1], in1=ac[:, :],
                                                                 op0=ALU.mult, op1=ALU.add), r=[xp_, cw, ac], w=[ac])
                dst = (k.dnq, k.dnk, k.dnv)[ci // 4]
                h = ci % 4
                dn_post(None, 512, ci, ac, ac[:, :],
                        lambda ob: S.dma("sp", ob, lambda e: e.dma_start(out=dst[h, :, s * 512:(s + 1) * 512], in_=ob[:, :]), r=[ob]))
            for t in range(4):
                r0 = s * 512 + t * 128
                for c in range(8):
                    S.op("pe", lambda e: e.matmul(pg[:, :], lhsT=aTt[:, c, t * 128:(t + 1) * 128], rhs=wbf[:, c, CAG:CAG + 8],
                                                  start=(c == 0), stop=(c == 7)), r=[wbf, aTt], w=[pg])
                g_ = gb[t % 2]
                gates(pg, 128, g_, g_[:, :])
                S.dma("sp", g_, lambda e: e.dma_start(out=k.gbs[r0:r0 + 128, :], in_=g_[:, :]), r=[g_])
                if in_main:
                    ps = tm_chunk(aTt, t, CZ, 512)
                    z_ = zb[t % 2]
                    ctr["ev"] += 1
                    evac(S, ctr["ev"], z_, z_[:, :], ps, ps[:, :])
                    S.dma("sp", z_, lambda e: e.dma_start(out=k.zs[r0 - PRE:r0 - PRE + 128, :], in_=z_[:, :]), r=[z_])
            if not in_kr:
                continue
            sk = s - (PRE - HALO) // 512
            sm = s - PRE // 512
            for g in range(3):
                d = DILS[g]
                if False:
                    continue
                for which in range(2):
                    if which == 0 and not in_main:
                        continue
                    for pair in range(2):
                        ps = fm_chunk(aTt, 512, CSW + 768 * g + 256 * which + 128 * pair)
                        q_ = qkp[ctr["ob"] % 3]
                        ctr["ob"] += 1
                        ctr["ev"] += 1
                        evac(S, ctr["ev"], q_, q_[:, :], ps, ps[:, :])
                        if which == 0:
                            dst = k.qts[g][pair, :, sm * 512:(sm + 1) * 512]
                        else:
                            dst = k.kts[g][pair, :, sk * 512:(sk + 1) * 512]
                        S.dma("sp", q_, lambda e: e.dma_start(out=dst, in_=q_[:, :]), r=[q_])
            for t in range(4):
                r0 = s * 512 + t * 128
                if os.environ.get("SKIP_KV") == "1":
                    continue
                for g in range(3):
                    ps = tm_chunk(aTt, t, CSW + 768 * g + 256, 512)
                    v_ = vsb[g]
                    S.op("dve", lambda e: e.tensor_copy(out=v_[:, :], in_=ps[:, 256:512]), r=[ps], w=[v_])
                    S.dma("sp", v_, lambda e: e.dma_start(out=k.vss[g][r0 - (PRE - HALO):r0 - (PRE - HALO) + 128, :], in_=v_[:, :]), r=[v_])
                    wlen = 128 * DILS[g]
                    if r0 >= EXT - wlen:
                        f_ = kvf[g]
                        S.op("dve", lambda e: e.tensor_copy(out=f_[:, :], in_=ps[:, :]), r=[ps], w=[f_])
                        o0 = r0 - (EXT - wlen)
                        S.dma("sp", f_, lambda e: e.dma_start(out=k.p_win[g][o0:o0 + 128, :], in_=f_[:, :]), r=[f_])

        NT = NS * TS
        if os.environ.get("P1NOSMP"):
            k.end_phase()
            return
        S.dma("sp", xsm, lambda e: e.dma_start(out=xsm[0:NT, :], in_=k.xs[:, :]), w=[xsm])
        aTt = aT[0]
        norm_transpose(xsm, 0, aTt, 0)
        for s_ in range(NS):
            for j in range(3):
                S.dma("sp", xps, lambda e: e.dma_start(out=xps[:, :, s_, j], in_=k.st_conv[s_, j, :].rearrange("(c p) -> p c", p=128),
                                                       allow_slow_non_contiguous=True), w=[xps])
        for ci in range(12):
            ps = fm_chunk(aTt, NT, ci * 128)
            S.op("dve", lambda e: e.tensor_copy(out=xps[:, ci, :, 3:7], in_=ps[:, 0:NT].rearrange("p (s t) -> p s t", s=NS)),
                 r=[ps], w=[xps])
        for ci in range(12):
            for s_ in range(NS):
                S.dma("sp", xps, lambda e: e.dma_start(out=k.s_conv[s_, :, ci * 128:(ci + 1) * 128].rearrange("j p -> p j"),
                                                       in_=xps[:, ci, s_, 4:7], allow_slow_non_contiguous=True), r=[xps])
        for ci in range(12):
            ac = acc[ci % 2]
            av = ac[:, 0:NT].rearrange("p (s t) -> p s t", s=NS)
            S.op("dve", lambda e: e.tensor_scalar(out=av, in0=xps[:, ci, :, 0:4], scalar1=cw[:, ci, 0:1], scalar2=None, op0=ALU.mult),
                 r=[xps, cw], w=[ac])
            for j in range(1, 4):
                S.op("dve", lambda e: e.scalar_tensor_tensor(out=av, in0=xps[:, ci, :, j:j + 4], scalar=cw[:, ci, j:j + 1], in1=av,
                                                             op0=ALU.mult, op1=ALU.add), r=[xps, cw, ac], w=[ac])
            dn_post(None, NT, ci, ac, ac[:, 0:NT],
                    lambda ob: S.op("pool", lambda e: e.tensor_copy(out=k.smp_dn[:, ci, :], in_=ob[:, 0:NT]), r=[ob], w=[k.smp_dn]))
        for c in range(8):
            S.op("pe", lambda e: e.matmul(pg[:, :], lhsT=aTt[:, c, 0:128], rhs=wbf[:, c, CAG:CAG + 8],
                                          start=(c == 0), stop=(c == 7)), r=[wbf, aTt], w=[pg])
        gates(pg, NT, k.smp_gb, k.smp_gb[:, :])
        ps = tm_chunk(aTt, 0, CZ, 512)
        S.op("act", lambda e: e.activation(out=k.smp_z[:, :], in_=ps[0:NT, :], func=AF.Copy), r=[ps], w=[k.smp_z])
        for g in range(3):
            for which in range(2):
                for pair in range(2):
                    ps = fm_chunk(aTt, NT, CSW + 768 * g + 256 * which + 128 * pair)
                    S.op("act", lambda e: e.activation(out=k.smp_qk[:, g, which, pair, :], in_=ps[:, 0:NT], func=AF.Copy),
                         r=[ps], w=[k.smp_qk])
            ps = tm_chunk(aTt, 0, CSW + 768 * g + 256, 512)
            S.op("act", lambda e: e.activation(out=k.smp_kv[:, g, :], in_=ps[0:NT, :], func=AF.Copy), r=[ps], w=[k.smp_kv])
        k.end_phase()


def build_nc(debug=False, phases=(1, 5, 2, 3, 4)):
    nc = bass.Bass("TRN2", target_bir_lowering=False)
    k = K()
    k.nc = nc
    k.S = Sched(nc)
    k.debug = debug
    declare_io(k, debug)
    S = k.S
    with ExitStack() as es:
        A = lambda name, shape, dt: TL(es.enter_context(nc.sbuf_tensor(name, shape, dt)), name)
        k.cst = {}
        for i, n in enumerate(CONST_NAMES):
            t = A("c_" + n, [128, 128], F32)
            S.dma("sp", t, lambda e: e.dma_start(out=t[:, :], in_=k.consts[i, :, :]), w=[t])
            k.cst[n] = t
        k.ident_bf = A("ident_bf", [128, 128], BF16)
        S.op("dve", lambda e: e.tensor_copy(out=k.ident_bf[:, :], in_=k.cst["ident"][:, :]), r=[k.cst["ident"]], w=[k.ident_bf])
        k.ones_bf = A("ones_bf", [128, 128], BF16)
        S.op("dve", lambda e: e.tensor_copy(out=k.ones_bf[:, :], in_=k.cst["ones"][:, :]), r=[k.cst["ones"]], w=[k.ones_bf])
        NT = NS * TS
        k.smp_dn = A("smp_dn", [128, 12, NT], BF16)
        k.smp_gb = A("smp_gb", [NT, 8], F32)
        k.smp_z = A("smp_z", [NT, 512], BF16)
        k.smp_qk = A("smp_qk", [128, 3, 2, 2, NT], BF16)
        k.smp_kv = A("smp_kv", [NT, 3, 512], F32)
        k.smp_cat = A("smp_cat", [NT, 512], BF16)
        k.smp_swo = A("smp_swo", [NT, 3, 260], F32)
        if _os.environ.get("P2DBG"):
            k.dbg = nc.dram_tensor("dbg", [128, 4096], F32, kind="ExternalOutput")
        if 1 in phases:
            phase1(k)
        if 5 in phases:
            phase_mem_and_windows(k)
        if 2 in phases:
            phase2a(k)
        if 3 in phases:
            phase2b(k)
        if 4 in phases:
            phase3(k)
        S.barrier()
        S.final_wait()
        S.emit()
    k.ninst = S.ninst
    return nc, k


def shard_inputs(inp):
    consts = np.stack([make_consts()[n] for n in CONST_NAMES]).astype(np.float32)
    maps = []
    c0 = make_consts()
    for c in range(NCORES):
        b, j = c // 2, c % 2
        xe = np.zeros((EXT, D), np.float32)
        if j == 0:
            xe[PRE:] = inp["x_prompt"][b, :MAIN]
            halo = np.full((128, 128), NEG, np.float32)
        else:
            xe[:] = inp["x_prompt"][b]
            halo = c0["swprev"]
        sl = slice(c * NS, (c + 1) * NS)
        m = {
            "xe": xe,
            "xs": np.ascontiguousarray(inp["x_sample"][sl].reshape(NS * TS, D)),
            "st_delta": np.ascontiguousarray(inp["state_delta"][0, sl]),
            "st_conv": np.ascontiguousarray(inp["state_conv"][0, sl]),
            "cwin0": np.ascontiguousarray(inp["cache_win1"][0, sl].reshape(NS, 128, 512)),
            "cwin1": np.ascontiguousarray(inp["cache_win2"][0, sl].reshape(NS, 512, 512)),
            "cwin2": np.ascontiguousarray(inp["cache_win3"][0, sl].reshape(NS, 2048, 512)),
            "cmem": np.ascontiguousarray(inp["cache_mem_kv"][0, sl].reshape(NS, 256, 2048)),
            "mem": np.ascontiguousarray(inp["mem_prompt"][b]),
            "halo": halo,
            "consts": consts,
            "g_mix": inp["g_mix"][0], "w_in": inp["w_in"][0], "conv_w": inp["conv_w"][0], "a_log": inp["a_log"][0],
            "dt_bias": inp["dt_bias"][0], "g_onorm": inp["g_onorm"][0], "w_out": inp["w_out"][0], "g_memq": inp["g_memq"][0],
            "g_memkv": inp["g_memkv"][0], "w_mq": inp["w_mq"][0], "w_mkv": inp["w_mkv"][0], "w_mo": inp["w_mo"][0],
            "g_ffn": inp["g_ffn"][0], "w_pq": inp["w_pq"][0], "sub_keys": inp["sub_keys"][0].reshape(16, 128, 128),
            "expert_u": inp["expert_u"][0][:EXPROWS], "expert_v": inp["expert_v"][0][:EXPROWS], "g_final": inp["g_final"],
        }
        maps.append({kk: np.ascontiguousarray(np.asarray(v, dtype=np.float32)) for kk, v in m.items()})
    return maps


def assemble(res):
    B, SEQ, DB = 4, 8192, 32
    y_prompt = np.zeros((B, SEQ, D), np.float32)
    y_sample = np.zeros((DB, TS, D), np.float32)
    p_delta = np.zeros((1, B, 4, 128, 128), np.float32)
    p_conv = np.zeros((1, B, 3, 1536), np.float32)
    p_win = [np.zeros((1, B, 128 * d, 2, 4, 64), np.float32) for d in DILS]
    p_mem = np.zeros((1, B, 256, 2, 4, 256), np.float32)
    s_delta = np.zeros((1, DB, 4, 128, 128), np.float32)
    s_conv = np.zeros((1, DB, 3, 1536), np.float32)
    s_win = [np.zeros((1, DB, 128 * d, 2, 4, 64), np.float32) for d in DILS]
    for c in range(NCORES):
        r = res[c]
        b, j = c // 2, c % 2
        y_prompt[b, j * MAIN:(j + 1) * MAIN] = r["y_main"]
        sl = slice(c * NS, (c + 1) * NS)
        y_sample[sl] = r["y_smp"].reshape(NS, TS, D)
        s_delta[0, sl] = r["s_delta"]
        s_conv[0, sl] = r["s_conv"]
        for g in range(3):
            s_win[g][0, sl] = r[f"s_win{g}"].reshape(NS, 128 * DILS[g], 2, 4, 64)
        if j == 1:
            p_delta[0, b] = r["p_delta"]
            p_conv[0, b] = r["p_conv"]
            for g in range(3):
                p_win[g][0, b] = r[f"p_win{g}"].reshape(128 * DILS[g], 2, 4, 64)
            p_mem[0, b] = r["p_mem"].reshape(256, 2, 4, 256)
    return (y_prompt, y_sample, p_delta, p_conv, p_win[0], p_win[1], p_win[2], p_mem,
            s_delta, s_conv, s_win[0], s_win[1], s_win[2])


def kernel(**inputs):
    inp = {kk: np.asarray(v) for kk, v in inputs.items()}
    nc, k = build_nc()
    maps = shard_inputs(inp)
    res = run_bass_kernel_spmd(nc, maps, core_ids=list(range(NCORES)))
    return assemble(res.results)


def phase_mem_and_windows(k):
    nc, S = k.nc, k.S
    d2d = k.track(TL(None, "d2d"))
    for g in range(3):
        wb = 128 * DILS[g]
        for s_ in range(NS):
            S.dma("sp", d2d, lambda e: e.dma_start(out=k.s_win[g][s_, 0:wb - TS, :], in_=k.cwin[g][s_, TS:wb, :]))
            S.dma("sp", k.smp_kv, lambda e: e.dma_start(out=k.s_win[g][s_, wb - TS:wb, :], in_=k.smp_kv[s_ * TS:(s_ + 1) * TS, g, :]),
                  r=[k.smp_kv])
    with ExitStack() as es:
        A = lambda name, shape, dt: k.track(TL(es.enter_context(nc.sbuf_tensor(name, shape, dt)), name))
        P = lambda name, shape, dt: TL(es.enter_context(nc.psum_tensor(name, shape, dt)), name)
        wbf = load_weight_bf16(k, es, "w_mkv", k.w_mkv, D, 2048, gdram=k.g_memkv, col_chunk=2048)
        xt = [A(f"mxt{i}", [128, D], F32) for i in range(2)]
        sqj = A("msqj", [128, D], BF16)
        ssq = A("mssq", [128, 1], F32)
        rstd = A("mrstd", [128, 1], F32)
        ab = A("mab", [128, D], BF16)
        aT = A("maT", [128, 8, 256], BF16)
        ob = [A(f"mob{i}", [128, 512], F32) for i in range(2)]
        pT = P("mpT", [128, 8, 128], BF16)
        ps_ = [P(f"mps{i}", [128, 512], F32) for i in range(2)]
        for t in range(2):
            x_ = xt[t]
            S.dma("sp", x_, lambda e: e.dma_start(out=x_[:, :], in_=k.mem[t * 128:(t + 1) * 128, :]), w=[x_])
            S.op("act", lambda e: e.activation(out=sqj[:, :], in_=x_[:, :], func=AF.Square, accum_out=ssq[:, 0:1]), r=[x_], w=[sqj, ssq])
            rsqrt(S, rstd, rstd[:, :], ssq, ssq[:, :], 1.0 / D, EPS)
            S.op("act", lambda e: e.activation(out=ab[:, :], in_=x_[:, :], func=AF.Copy, scale=rstd[:, 0:1]), r=[x_, rstd], w=[ab])
            for c in range(8):
                S.op("pe", lambda e: e.transpose(out=pT[:, c, :], in_=ab[:, c * 128:(c + 1) * 128], identity=k.ident_bf[:, :]),
                     r=[ab, k.ident_bf], w=[pT])
            S.op("dve", lambda e: e.tensor_copy(out=aT[:, :, t * 128:(t + 1) * 128], in_=pT[:, :, :]), r=[pT], w=[aT])
        n = 0
        for t in range(2):
            for nb in range(4):
                ps = ps_[n % 2]
                o_ = ob[n % 2]
                for c in range(8):
                    S.op("pe", lambda e: e.matmul(ps[:, :], lhsT=aT[:, c, t * 128:(t + 1) * 128], rhs=wbf[:, c, nb * 512:(nb + 1) * 512],
                                                  start=(c == 0), stop=(c == 7)), r=[aT, wbf], w=[ps])
                evac(S, n, o_, o_[:, :], ps, ps[:, :])
                S.dma("sp", o_, lambda e: e.dma_start(out=k.p_mem[t * 128:(t + 1) * 128, nb * 512:(nb + 1) * 512], in_=o_[:, :]), r=[o_])
                n += 1
        k.end_phase()


def phase2a(k):
    import os
    nc, S = k.nc, k.S
    with ExitStack() as es:
        A = lambda name, shape, dt: k.track(TL(es.enter_context(nc.sbuf_tensor(name, shape, dt)), name))
        P = lambda name, shape, dt: TL(es.enter_context(nc.psum_tensor(name, shape, dt)), name)
        cst = k.cst
        ident, trit, blk, mTneg, mSpos, ones = (cst[n] for n in ("ident", "trit", "blk", "mTneg", "mSpos", "ones"))
        identb = k.ident_bf
        cbf = {}
        for n_ in ("trit", "blk", "mTneg", "mSpos"):
            cbf[n_] = A("cbf_" + n_, [128, 128], BF16)
            S.op("dve", lambda e: e.tensor_copy(out=cbf[n_][:, :], in_=cst[n_][:, :]), r=[cst[n_]], w=[cbf[n_]])
        tritb, blkb, mTnegb, mSposb = (cbf[n_] for n_ in ("trit", "blk", "mTneg", "mSpos"))
        onesb = k.ones_bf
        hones = [A(f"hones{c_}", [128, 128], BF16) for c_ in range(2)]
        for c_ in range(2):
            S.op("dve", lambda e: e.tensor_copy(out=hones[c_][:, :], in_=cst["blk"][:, 127 * c_:127 * c_ + 1].to_broadcast([128, 128])),
                 r=[cst["blk"]], w=[hones[c_]])
        ghl = A("ghl", [128, 8], BF16)
        Gall_h = A("Gall_h", [128, 4, 128], BF16)
        Gall_l = A("Gall_l", [128, 4, 128], BF16)
        gon = A("gon", [128, 128], F32)
        S.dma("sp", gon, lambda e: e.dma_start(out=gon[:, :], in_=k.g_onorm.ap().partition_broadcast(128)), w=[gon])
        qT = [A(f"qT{i}", [128, 4, 128], BF16) for i in range(2)]
        kT = [A(f"kT{i}", [128, 4, 128], BF16) for i in range(2)]
        vT = [A(f"vT{i}", [128, 4, 128], BF16) for i in range(2)]
        gbt = [A(f"gbt{i}", [128, 8], F32) for i in range(2)]
        zt = [A(f"zt{i}", [128, 512], BF16) for i in range(2)]
        gc = A("gc", [128, 4], F32); ngc = A("ngc", [128, 4], F32); egc = A("egc", [128, 4], F32)
        negegc = A("negegc", [128, 4], F32); ekd = A("ekd", [128, 4], F32); dl = A("dl", [128, 8], F32)
        dif = A("dif", [128, 4], F32)
        dlraw = A("dlraw", [128, 8], F32)
        Gall = A("Gall", [128, 4, 128], F32)
        egcb = [A(f"egcb{i}", [128, 128], F32) for i in range(2)]
        gamT = [A(f"gamT{i}", [128, 128], F32) for i in range(2)]
        gamS = [A(f"gamS{i}", [128, 128], F32) for i in range(2)]
        Lx = [A(f"Lx{i}", [128, 128], BF16) for i in range(3)]
        Ly = [A(f"Ly{i}", [128, 128], BF16) for i in range(3)]
        Rr = [A(f"Rr{i}", [128, 128], BF16) for i in range(2)]
        AqkT = [[A(f"AqkT{i}_{h}", [128, 128], BF16) for h in range(4)] for i in range(2)]
        qgT = [[A(f"qgT{i}_{h}", [128, 128], BF16) for h in range(4)] for i in range(2)]
        TbT = [[A(f"TbT{i}_{h}", [128, 128], BF16) for h in range(4)] for i in range(2)]
        kd = [[A(f"kd{i}_{h}", [128, 128], BF16) for h in range(4)] for i in range(2)]
        vtok = [[A(f"vtok{i}_{h}", [128, 128], F32) for h in range(4)] for i in range(2)]
        sc_neg = [A(f"scneg{i}", [128, 4], F32) for i in range(2)]
        sc_dl = [A(f"scdl{i}", [128, 8], F32) for i in range(2)]
        rt = [A(f"rt{h}", [128, 128], BF16) for h in range(4)]
        ut = [A(f"ut{h}", [128, 128], BF16) for h in range(4)]
        St = [A(f"St{h}", [128, 128], F32) for h in range(4)]
        Sb = [A(f"Sb{h}", [128, 128], BF16) for h in range(4)]
        ot = [A(f"ot{i}", [128, 512], F32) for i in range(2)]
        ssq = A("ossq", [128, 4], F32); orstd = A("orstd", [128, 4], F32)
        sqj = A("osqj", [128, 128], F32)
        szt = A("szt", [128, 512], F32)
        t1 = A("t1", [128, 512], F32)
        og = [A(f"og{i}", [128, 512], BF16) for i in range(2)]
        pb = PsumBlocks(nc, es, "p2a_")
        P = lambda name, dt, bank: pb.get(name, 128, dt, bank)
        pKS = P("pKS", F32, "scan"); pU = P("pU", F32, "scan"); pO = P("pO", F32, "scan"); pdS = P("pdS", F32, "scan")
        pA = [P(f"pA{i}", F32, f"g{i}") for i in range(2)]
        pB = [P(f"pB{i}", F32, f"g{i}") for i in range(2)]
        pC = [P(f"pC{i}", F32, f"g{i}") for i in range(2)]
        pKK = [P(f"pKK{i}", F32, f"k{i}") for i in range(2)]
        pQK = [P(f"pQK{i}", F32, f"k{i}") for i in range(2)]
        pX = P("pX", F32, "nX"); pY = P("pY", F32, "nY"); pP = P("pP", F32, "nP")
        pgt = pb.get("pgt", 16, F32, "k0")
        pLT = P("pLT", F32, "nY"); pkt = P("pkt", F32, "nP"); pvt = P("pvt", F32, "nY")
        def mm(ps, out_ap, lt, lap, rt_, rap, start=True, stop=True):
            S.op("pe", lambda e: e.matmul(out_ap, lhsT=lap, rhs=rap, start=start, stop=stop), r=[lt, rt_], w=[ps])

        def prep(i, q_, k_, v_, g_):
            S.op("dve", lambda e: e.tensor_copy(out=ghl[:, 0:4], in_=g_[:, 0:4]), r=[g_], w=[ghl])
            S.op("dve", lambda e: e.tensor_tensor(out=ghl[:, 4:8], in0=g_[:, 0:4], in1=ghl[:, 0:4], op=ALU.subtract), r=[g_, ghl], w=[ghl])
            for (c0, lt_, lap) in ((0, tritb, tritb[:, :]), (4, blkb, blkb[:, :])):
                mm(pgt, pgt[:, c0:c0 + 4], lt_, lap, ghl, ghl[:, 0:4], True, False)
                mm(pgt, pgt[:, c0:c0 + 4], lt_, lap, ghl, ghl[:, 4:8], False, True)
            for c_ in range(2):
                mm(pgt, pgt[:, 8 + 4 * c_:12 + 4 * c_], hones[c_], hones[c_][:, :], ghl, ghl[:, 0:4], True, False)
                mm(pgt, pgt[:, 8 + 4 * c_:12 + 4 * c_], hones[c_], hones[c_][:, :], ghl, ghl[:, 4:8], False, True)
            S.op("dve", lambda e: e.tensor_copy(out=gc[:, :], in_=pgt[:, 0:4]), r=[pgt], w=[gc])
            S.op("dve", lambda e: e.tensor_scalar(out=ngc[:, :], in0=gc[:, :], scalar1=-1.0, scalar2=None, op0=ALU.mult), r=[gc], w=[ngc])
            S.op("act", lambda e: e.activation(out=egc[:, :], in_=gc[:, :], func=AF.Exp), r=[gc], w=[egc])
            S.op("dve", lambda e: e.tensor_scalar(out=sc_neg[i][:, :], in0=egc[:, :], scalar1=-1.0, scalar2=None, op0=ALU.mult),
                 r=[egc], w=[sc_neg[i]])
            S.op("dve", lambda e: e.tensor_tensor(out=dif[:, :], in0=pgt[:, 4:8], in1=gc[:, :], op=ALU.subtract), r=[pgt, gc], w=[dif])
            S.op("act", lambda e: e.activation(out=ekd[:, :], in_=dif[:, :], func=AF.Exp), r=[dif], w=[ekd])
            S.op("dve", lambda e: e.tensor_copy(out=dlraw[:, :], in_=pgt[:, 8:16]), r=[pgt], w=[dlraw])
            S.op("act", lambda e: e.activation(out=sc_dl[i][:, :], in_=dlraw[:, :], func=AF.Exp), r=[dlraw], w=[sc_dl[i]])
            stg = int(os.environ.get("P2PREP", "9"))
            if stg < 1:
                return
            for h in range(4):
                S.op("dve", lambda e: e.tensor_copy(out=Gall_h[:, h, :], in_=ghl[:, h:h + 1].to_broadcast([128, 128])), r=[ghl], w=[Gall_h])
                S.op("dve", lambda e: e.tensor_copy(out=Gall_l[:, h, :], in_=ghl[:, 4 + h:5 + h].to_broadcast([128, 128])), r=[ghl], w=[Gall_l])
            for h in range(4):
                j = h % 2
                if stg < 2:
                    continue
                mm(pA[j], pA[j][:, :], Gall_h, Gall_h[:, h, :], tritb, tritb[:, :], True, False)
                mm(pA[j], pA[j][:, :], Gall_l, Gall_l[:, h, :], tritb, tritb[:, :], False, True)
                mm(pB[j], pB[j][:, :], Gall_h, Gall_h[:, h, :], tritb, tritb[:, :], True, False)
                mm(pB[j], pB[j][:, :], Gall_l, Gall_l[:, h, :], tritb, tritb[:, :], False, False)
                mm(pB[j], pB[j][:, :], identb, identb[:, :], mTnegb, mTnegb[:, :], False, True)
                mm(pC[j], pC[j][:, :], Gall_h, Gall_h[:, h, :], tritb, tritb[:, :], True, False)
                mm(pC[j], pC[j][:, :], Gall_l, Gall_l[:, h, :], tritb, tritb[:, :], False, False)
                mm(pC[j], pC[j][:, :], identb, identb[:, :], mSposb, mSposb[:, :], False, True)
                S.op("act", lambda e: e.activation(out=egcb[j][:, :], in_=pA[j][:, :], func=AF.Exp), r=[pA[j]], w=[egcb[j]])
                S.op("act", lambda e: e.activation(out=gamT[j][:, :], in_=pB[j][:, :], func=AF.Exp, bias=ngc[:, h:h + 1]),
                     r=[pB[j], ngc], w=[gamT[j]])
                S.op("act", lambda e: e.activation(out=gamS[j][:, :], in_=pC[j][:, :], func=AF.Exp, bias=gc[:, h:h + 1], scale=-1.0),
                     r=[pC[j], gc], w=[gamS[j]])
                if stg < 3:
                    continue
                mm(pKK[j], pKK[j][:, :], k_, k_[:, h, :], k_, k_[:, h, :])
                mm(pQK[j], pQK[j][:, :], k_, k_[:, h, :], q_, q_[:, h, :])
                X, Y = Lx[0], Ly[0]
                S.op("dve", lambda e: e.scalar_tensor_tensor(out=X[:, :], in0=pKK[j][:, :], scalar=g_[:, 4 + h:5 + h], in1=gamS[j][:, :],
                                                             op0=ALU.mult, op1=ALU.mult), r=[pKK[j], g_, gamS[j]], w=[X])
                S.op("dve", lambda e: e.tensor_tensor(out=AqkT[i][h][:, :], in0=pQK[j][:, :], in1=gamT[j][:, :], op=ALU.mult),
                     r=[pQK[j], gamT[j]], w=[AqkT[i][h]])
                S.op("pool", lambda e: e.tensor_tensor(out=qgT[i][h][:, :], in0=q_[:, h, :], in1=egcb[j][:, :], op=ALU.mult),
                     r=[q_, egcb[j]], w=[qgT[i][h]])
                if stg < 4:
                    continue
                exp_ = os.environ.get("P2EXP", "")
                if exp_ == "A":
                    mm(pLT, pLT[:, :], identb, identb[:, :], identb, identb[:, :])
                else:
                    mm(pLT, pLT[:, :], X, X[:, :], identb, identb[:, :])
                if exp_ == "D":
                    S.op("act", lambda e: e.activation(out=Y[:, :], in_=pLT[:, :], func=AF.Copy), r=[pLT], w=[Y])
                elif exp_ != "B":
                    S.op("dve", lambda e: e.tensor_copy(out=Y[:, :], in_=pLT[:, :]), r=[pLT], w=[Y])
                sub = os.environ.get("P2SUB", "z")
                if sub == "a":
                    continue
                R = Rr[0]
                S.op("pool", lambda e: e.tensor_tensor(out=R[:, :], in0=identb[:, :], in1=Y[:, :], op=ALU.subtract), r=[identb, Y], w=[R])
                if sub == "b":
                    continue
                for it in range(1, 6):
                    Xn, Yn = Lx[it % 3], Ly[it % 3]
                    mm(pX, pX[:, :], Y, Y[:, :], X, X[:, :])
                    if sub == "c":
                        break
                    if it < 5:
                        mm(pY, pY[:, :], X, X[:, :], Y, Y[:, :])
                    S.op("act", lambda e: e.activation(out=Xn[:, :], in_=pX[:, :], func=AF.Copy), r=[pX], w=[Xn])
                    if it < 5:
                        S.op("dve", lambda e: e.tensor_copy(out=Yn[:, :], in_=pY[:, :]), r=[pY], w=[Yn])
                    mm(pP, pP[:, :], Xn, Xn[:, :], R, R[:, :])
                    Rn = Rr[it % 2]
                    S.op("dve", lambda e: e.tensor_tensor(out=Rn[:, :], in0=pP[:, :], in1=R[:, :], op=ALU.add), r=[pP, R], w=[Rn])
                    X, Y, R = Xn, Yn, Rn
                if stg < 5:
                    continue
                S.op("pool", lambda e: e.tensor_scalar(out=TbT[i][h][:, :], in0=R[:, :], scalar1=g_[:, 4 + h:5 + h], scalar2=None, op0=ALU.mult),
                     r=[R, g_], w=[TbT[i][h]])
                mm(pkt, pkt[:, :], k_, k_[:, h, :], identb, identb[:, :])
                S.op("dve", lambda e: e.tensor_scalar(out=kd[i][h][:, :], in0=pkt[:, :], scalar1=ekd[:, h:h + 1], scalar2=None, op0=ALU.mult),
                     r=[pkt, ekd], w=[kd[i][h]])
                mm(pvt, pvt[:, :], v_, v_[:, h, :], identb, identb[:, :])
                S.op("dve", lambda e: e.tensor_copy(out=vtok[i][h][:, :], in_=pvt[:, :]), r=[pvt], w=[vtok[i][h]])

        def scan(i, k_, o_, pre=None, post=None):
            for c in range(2):
                ps_ = slice(64 * c, 64 * c + 64)
                if pre:
                    pre(c)
                for h in range(4):
                    mm(pKS, pKS[:, :], k_, k_[:, h, :], Sb[h], Sb[h][:, :])
                    S.op("dve", lambda e: e.scalar_tensor_tensor(out=rt[h][ps_, :], in0=pKS[ps_, :], scalar=sc_neg[i][ps_, h:h + 1],
                                                                 in1=vtok[i][h][ps_, :], op0=ALU.mult, op1=ALU.add),
                         r=[pKS, sc_neg[i], vtok[i][h]], w=[rt[h]])
                    mm(pU, pU[:, :], TbT[i][h], TbT[i][h][ps_, :], rt[h], rt[h][ps_, :])
                    S.op("act", lambda e: e.activation(out=ut[h][ps_, :], in_=pU[ps_, :], func=AF.Copy), r=[pU], w=[ut[h]])
                    mm(pO, pO[:, :], qgT[i][h], qgT[i][h][:, :], Sb[h], Sb[h][:, :], True, False)
                    mm(pO, pO[:, :], AqkT[i][h], AqkT[i][h][ps_, :], ut[h], ut[h][ps_, :], False, True)
                    S.op("act", lambda e: e.activation(out=o_[ps_, h * 128:(h + 1) * 128], in_=pO[ps_, :], func=AF.Copy), r=[pO], w=[o_])
                    mm(pdS, pdS[:, :], kd[i][h], kd[i][h][ps_, :], ut[h], ut[h][ps_, :])
                    S.op("dve", lambda e: e.scalar_tensor_tensor(out=St[h][:, :], in0=St[h][:, :], scalar=sc_dl[i][:, 4 * c + h:4 * c + h + 1],
                                                                 in1=pdS[:, :], op0=ALU.mult, op1=ALU.add),
                         r=[St[h], sc_dl[i], pdS], w=[St[h]])
                    S.op("act", lambda e: e.activation(out=Sb[h][:, :], in_=St[h][:, :], func=AF.Copy), r=[St[h]], w=[Sb[h]])
                if post:
                    post(c)

        def post_out(o_, z_, j, dst_fn):
            for h in range(4):
                S.op("act", lambda e: e.activation(out=sqj[:, :], in_=o_[:, h * 128:(h + 1) * 128], func=AF.Square, accum_out=ssq[:, h:h + 1]),
                     r=[o_], w=[sqj, ssq])
            rsqrt(S, orstd, orstd[:, :], ssq, ssq[:, :], 1.0 / 128, EPS)
            S.op("act", lambda e: e.activation(out=szt[:, :], in_=z_[:, :], func=AF.Silu), r=[z_], w=[szt])
            for h in range(4):
                S.op("dve", lambda e: e.scalar_tensor_tensor(out=t1[:, h * 128:(h + 1) * 128], in0=o_[:, h * 128:(h + 1) * 128],
                                                             scalar=orstd[:, h:h + 1], in1=gon[:, :], op0=ALU.mult, op1=ALU.mult),
                     r=[o_, orstd, gon], w=[t1])
            S.op("dve", lambda e: e.tensor_tensor(out=og[j][:, :], in0=t1[:, :], in1=szt[:, :], op=ALU.mult), r=[t1, szt], w=[og[j]])
            dst_fn(og[j])

        for h in range(4):
            S.op("dve", lambda e: e.memset(St[h][:, :], 0.0), w=[St[h]])
            S.op("dve", lambda e: e.memset(Sb[h][:, :], 0.0), w=[Sb[h]])
        ntile = EXT // 128
        tiles = range(int(os.environ.get("P2START", "0")), int(os.environ.get("P2TILES", str(ntile))))
        for tau in tiles:
            i = tau % 2
            t0 = tau * 128
            for h in range(4):
                S.dma("sp", qT[i], lambda e: e.dma_start(out=qT[i][:, h, :], in_=k.dnq[h, :, t0:t0 + 128]), w=[qT[i]])
                S.dma("sp", kT[i], lambda e: e.dma_start(out=kT[i][:, h, :], in_=k.dnk[h, :, t0:t0 + 128]), w=[kT[i]])
                S.dma("sp", vT[i], lambda e: e.dma_start(out=vT[i][:, h, :], in_=k.dnv[h, :, t0:t0 + 128]), w=[vT[i]])
            S.dma("sp", gbt[i], lambda e: e.dma_start(out=gbt[i][:, :], in_=k.gbs[t0:t0 + 128, :]), w=[gbt[i]])
            main = t0 >= PRE
            if main:
                S.dma("sp", zt[i], lambda e: e.dma_start(out=zt[i][:, :], in_=k.zs[t0 - PRE:t0 - PRE + 128, :]), w=[zt[i]])
            mode = os.environ.get("P2MODE", "all")
            if mode == "load":
                continue
            prep(i, qT[i], kT[i], vT[i], gbt[i])
            if mode == "prep":
                continue
            scan(i, kT[i], ot[i])
            if main:
                post_out(ot[i], zt[i], i,
                         lambda o_: S.dma("sp", o_, lambda e: e.dma_start(out=k.cat[t0 - PRE:t0 - PRE + 128, :], in_=o_[:, :]), r=[o_]))
        for h in range(4):
            S.dma("sp", St[h], lambda e: e.dma_start(out=k.p_delta[h, :, :], in_=St[h][:, :]), r=[St[h]])

        for tb in range(2 if os.environ.get("P2MODE", "all") == "all" else 0):
            i = tb
            for tt in (qT[i], kT[i], vT[i]):
                S.op("pool", lambda e: e.memset(tt[:, :, :], 0.0), w=[tt])
            S.op("pool", lambda e: e.memset(gbt[i][:, :], 0.0), w=[gbt[i]])
            S.op("pool", lambda e: e.memset(zt[i][:, :], 0.0), w=[zt[i]])
            for c in range(2):
                sq = tb * 2 + c
                for h in range(4):
                    for which, tt in enumerate((qT[i], kT[i], vT[i])):
                        S.op("pool", lambda e: e.tensor_copy(out=tt[:, h, 64 * c:64 * c + TS], in_=k.smp_dn[:, which * 4 + h, sq * TS:(sq + 1) * TS]),
                             r=[k.smp_dn], w=[tt])
                S.dma("sp", gbt[i], lambda e: e.dma_start(out=gbt[i][64 * c:64 * c + TS, :], in_=k.smp_gb[sq * TS:(sq + 1) * TS, :]),
                      r=[k.smp_gb], w=[gbt[i]])
                S.dma("sp", zt[i], lambda e: e.dma_start(out=zt[i][64 * c:64 * c + TS, :], in_=k.smp_z[sq * TS:(sq + 1) * TS, :]),
                      r=[k.smp_z], w=[zt[i]])
            prep(i, qT[i], kT[i], vT[i], gbt[i])

            def pre(c, tb=tb):
                sq = tb * 2 + c
                for h in range(4):
                    S.dma("sp", St[h], lambda e: e.dma_start(out=St[h][:, :], in_=k.st_delta[sq, h, :, :]), w=[St[h]])
                    S.op("act", lambda e: e.activation(out=Sb[h][:, :], in_=St[h][:, :], func=AF.Copy), r=[St[h]], w=[Sb[h]])

            def post(c, tb=tb):
                sq = tb * 2 + c
                for h in range(4):
                    S.dma("sp", St[h], lambda e: e.dma_start(out=k.s_delta[sq, h, :, :], in_=St[h][:, :]), r=[St[h]])

            scan(i, kT[i], ot[i], pre, post)
            if tb == 0 and os.environ.get("P2DBG"):
                col = 0
                dbgt = TL(None, "dbgsem")
                for tt, wdt in ((TbT[0][0], 128), (AqkT[0][0], 128), (kd[0][0], 128), (qgT[0][0], 128), (vtok[0][0], 128),
                                (sc_neg[0], 4), (sc_dl[0], 8), (ot[0], 512), (rt[0], 128), (ut[0], 128), (St[0], 128), (gbt[0], 8)):
                    S.dma("pool", TL(None, f"dbg{col}"), lambda e: e.dma_start(out=k.dbg[:, col:col + wdt], in_=tt[:, 0:wdt]), r=[tt])
                    col += wdt

            def dst(o_, tb=tb):
                for c in range(2):
                    sq = tb * 2 + c
                    S.dma("sp", o_, lambda e: e.dma_start(out=k.smp_cat[sq * TS:(sq + 1) * TS, :], in_=o_[64 * c:64 * c + TS, :]),
                          r=[o_], w=[k.smp_cat])
            post_out(ot[i], zt[i], i, dst)
        k.end_phase()


def phase2b(k):
    import os
    nc, S = k.nc, k.S
    with ExitStack() as es:
        A = lambda name, shape, dt: k.track(TL(es.enter_context(nc.sbuf_tensor(name, shape, dt)), name))
        pb = PsumBlocks(nc, es, "p2b_")
        identb, onesb = k.ident_bf, k.ones_bf
        mprev = A("mprev", [128, 128], BF16); mcur = A("mcur", [128, 128], BF16); mhalo = A("mhalo", [128, 128], BF16)
        halo_f = A("halo_f", [128, 128], F32)
        S.dma("sp", halo_f, lambda e: e.dma_start(out=halo_f[:, :], in_=k.halo[:, :]), w=[halo_f])
        S.op("dve", lambda e: e.tensor_copy(out=mprev[:, :], in_=k.cst["swprev"][:, :]), r=[k.cst["swprev"]], w=[mprev])
        S.op("dve", lambda e: e.tensor_copy(out=mcur[:, :], in_=k.cst["swcur"][:, :]), r=[k.cst["swcur"]], w=[mcur])
        S.op("dve", lambda e: e.tensor_copy(out=mhalo[:, :], in_=halo_f[:, :]), r=[halo_f], w=[mhalo])
        QT = [A(f"QT{i}", [128, 2, 2048], BF16) for i in range(2)]
        KT = [A(f"KT{i}", [128, 2, 4096], BF16) for i in range(2)]
        Vb = [A(f"Vb{i}", [128, 2, 256], BF16) for i in range(2)]
        PT = [A(f"PT{i}", [128, 256], BF16) for i in range(2)]
        ot = [A(f"swot{i}", [128, 260], F32) for i in range(2)]
        psS = [pb.get(f"psS{i}", 256, F32, f"s{i}") for i in range(2)]
        psO = [pb.get(f"psO{i}", 128, F32, f"o{i}") for i in range(2)]
        def core(qf, kf, vf, msk0, o_, deps):
            qt_, kt_, vt_ = deps
            for head in range(4):
                pair, hh = head // 2, head % 2
                ph = slice(64 * hh, 64 * hh + 64)
                sS, sO, p_ = psS[head % 2], psO[head % 2], PT[head % 2]
                for kc in range(2):
                    msk = msk0 if kc == 0 else mcur
                    S.op("pe", lambda e: e.matmul(sS[:, kc * 128:(kc + 1) * 128], lhsT=identb[:, :], rhs=msk[:, :], start=True, stop=False),
                         r=[identb, msk], w=[sS])
                    S.op("pe", lambda e: e.matmul(sS[:, kc * 128:(kc + 1) * 128], lhsT=kf(kc, pair, ph), rhs=qf(pair, ph), start=False, stop=True),
                         r=[kt_, qt_], w=[sS])
                S.op("act", lambda e: e.activation(out=p_[:, :], in_=sS[:, :], func=AF.Exp, scale=0.125), r=[sS], w=[p_])
                for kc in range(2):
                    S.op("pe", lambda e: e.matmul(sO[:, 0:64], lhsT=p_[:, kc * 128:(kc + 1) * 128], rhs=vf(kc, head),
                                                  start=(kc == 0), stop=(kc == 1)), r=[p_, vt_], w=[sO])
                for kc in range(2):
                    S.op("pe", lambda e: e.matmul(sO[:, 64:65], lhsT=p_[:, kc * 128:(kc + 1) * 128], rhs=onesb[:, 0:1],
                                                  start=(kc == 0), stop=(kc == 1)), r=[p_, onesb], w=[sO])
                S.op("dve", lambda e: e.tensor_copy(out=o_[:, head * 65:(head + 1) * 65], in_=sO[:, 0:65]), r=[sO], w=[o_])

        groups = [int(x) for x in os.environ.get("P2BG", "0,1,2").split(",")]
        nblk_lim = int(os.environ.get("P2BN", "999"))
        u = 0
        for g in groups:
            d = DILS[g]
            span = 128 * d
            for n in range(min(MAIN // span, nblk_lim)):
                bi = (g * 64 + n) % 2
                q_, k_ = QT[bi], KT[bi]
                for pair in range(2):
                    S.dma("sp", q_, lambda e: e.dma_start(out=q_[:, pair, 0:span], in_=k.qts[g][pair, :, n * span:(n + 1) * span]), w=[q_])
                    k0 = HALO + (n - 1) * span
                    S.dma("sp", k_, lambda e: e.dma_start(out=k_[:, pair, 0:2 * span], in_=k.kts[g][pair, :, k0:k0 + 2 * span]), w=[k_])
                for r in range(d):
                    v_ = Vb[u % 2]
                    o_ = ot[u % 2]
                    v0 = HALO + (n - 1) * span + r
                    for kc in range(2):
                        S.dma("sp", v_, lambda e: e.dma_start(out=v_[:, kc, :],
                                                             in_=k.vss[g][v0 + kc * span:v0 + kc * span + 127 * d + 1:d, :]), w=[v_])
                    core(lambda pair, ph: q_[ph, pair, r:span:d],
                         lambda kc, pair, ph: k_[ph, pair, kc * span + r:(kc + 1) * span:d],
                         lambda kc, head: v_[:, kc, head * 64:(head + 1) * 64],
                         mhalo if n == 0 else mprev, o_, (q_, k_, v_))
                    t0 = n * span + r
                    S.dma("sp", o_, lambda e: e.dma_start(out=k.swo[g][t0:t0 + 127 * d + 1:d, :], in_=o_[:, :]), r=[o_])
                    u += 1
        if not os.environ.get("P2BNOSMP"):
            sq = A("sq", [128, 2, 128], BF16); sk = A("sk", [128, 2, 256], BF16); sv = A("sv", [128, 2, 256], BF16)
            ck = A("ck", [128, 512], F32); ckb = A("ckb", [128, 512], BF16); vst = A("vst", [128, 256], F32)
            pt = pb.get("ptr", 128, F32, "ptr")
            for s_ in range(NS):
                for g in range(3):
                    d = DILS[g]
                    for r in range(1 if d == 1 else TS):
                        nq = TS if d == 1 else 1
                        tok0 = s_ * TS + (0 if d == 1 else r)
                        o_ = ot[u % 2]
                        for tt in (sq, sk, sv):
                            S.op("pool", lambda e: e.memset(tt[:, :, :], 0.0), w=[tt])
                        S.op("pool", lambda e: e.memset(vst[:, :], 0.0), w=[vst])
                        S.dma("sp", ck, lambda e: e.dma_start(out=ck[:, :], in_=k.cwin[g][s_, r:r + 127 * d + 1:d, :]), w=[ck])
                        S.op("dve", lambda e: e.tensor_copy(out=ckb[:, :], in_=ck[:, :]), r=[ck], w=[ckb])
                        for pair in range(2):
                            S.op("pool", lambda e: e.tensor_copy(out=sq[:, pair, 0:nq], in_=k.smp_qk[:, g, 0, pair, tok0:tok0 + nq]), r=[k.smp_qk], w=[sq])
                            S.op("pool", lambda e: e.tensor_copy(out=sk[:, pair, 128:128 + nq], in_=k.smp_qk[:, g, 1, pair, tok0:tok0 + nq]),
                                 r=[k.smp_qk], w=[sk])
                            S.op("pe", lambda e: e.matmul(pt[:, :], lhsT=ckb[:, pair * 128:(pair + 1) * 128], rhs=identb[:, :], start=True, stop=True),
                                 r=[ckb, identb], w=[pt])
                            S.op("dve", lambda e: e.tensor_copy(out=sk[:, pair, 0:128], in_=pt[:, :]), r=[pt], w=[sk])
                        S.op("pool", lambda e: e.tensor_copy(out=sv[:, 0, :], in_=ckb[:, 256:512]), r=[ckb], w=[sv])
                        S.dma("sp", vst, lambda e: e.dma_start(out=vst[0:nq, :], in_=k.smp_kv[tok0:tok0 + nq, g, 256:512]), r=[k.smp_kv], w=[vst])
                        S.op("dve", lambda e: e.tensor_copy(out=sv[:, 1, :], in_=vst[:, :]), r=[vst], w=[sv])
                        core(lambda pair, ph: sq[ph, pair, :], lambda kc, pair, ph: sk[ph, pair, kc * 128:(kc + 1) * 128],
                             lambda kc, head: sv[:, kc, head * 64:(head + 1) * 64], mprev, o_, (sq, sk, sv))
                        S.dma("sp", o_, lambda e: e.dma_start(out=k.smp_swo[tok0:tok0 + nq, g, :], in_=o_[0:nq, :]), r=[o_], w=[k.smp_swo])
                        u += 1
        k.end_phase()


def phase3(k):
    import os
    nc, S = k.nc, k.S
    NT = NS * TS
    with ExitStack() as es:
        A = lambda name, shape, dt: k.track(TL(es.enter_context(nc.sbuf_tensor(name, shape, dt)), name))
        pb = PsumBlocks(nc, es, "p3_")
        identb, onesb = k.ident_bf, k.ones_bf
        iota = k.cst["iota"]
        KmT = A("KmT", [128, 8, 256], BF16); Vm = A("Vm", [128, 2, 1024], BF16)
        KsT = A("KsT", [128, NS, 8, 256], BF16); Vs = A("Vs", [128, NS, 2, 1024], BF16)
        pbig = [pb.get(f"pbig{i}", 512, F32, f"big{i}") for i in range(4)]
        with ExitStack() as es2:
            A2 = lambda name, shape, dt: k.track(TL(es2.enter_context(nc.sbuf_tensor(name, shape, dt)), name))
            wkv = load_weight_bf16(k, es2, "w_mkv3", k.w_mkv, D, 2048, gdram=k.g_memkv, col_chunk=2048)
            mx = A2("m3x", [128, D], F32); msq = A2("m3sq", [128, D], BF16); mss = A2("m3ss", [128, 1], F32)
            mrs = A2("m3rs", [128, 1], F32); mab = A2("m3ab", [128, D], BF16); maT = A2("m3aT", [128, 8, 256], BF16)
            cs = A2("m3cs", [128, 2048], F32); csb = A2("m3csb", [128, 2048], BF16)
            for t in range(2):
                S.dma("sp", mx, lambda e: e.dma_start(out=mx[:, :], in_=k.mem[t * 128:(t + 1) * 128, :]), w=[mx])
                S.op("act", lambda e: e.activation(out=msq[:, :], in_=mx[:, :], func=AF.Square, accum_out=mss[:, 0:1]), r=[mx], w=[msq, mss])
                rsqrt(S, mrs, mrs[:, :], mss, mss[:, :], 1.0 / D, EPS)
                S.op("act", lambda e: e.activation(out=mab[:, :], in_=mx[:, :], func=AF.Copy, scale=mrs[:, 0:1]), r=[mx, mrs], w=[mab])
                for half in range(2):
                    for c in range(4):
                        cc = half * 4 + c
                        S.op("pe", lambda e: e.matmul(pbig[0][:, c * 128:(c + 1) * 128], lhsT=mab[:, cc * 128:(cc + 1) * 128], rhs=identb[:, :],
                                                      start=True, stop=True), r=[mab, identb], w=[pbig[0]])
                    for c in range(4):
                        cc = half * 4 + c
                        S.op("dve", lambda e: e.tensor_copy(out=maT[:, cc, t * 128:(t + 1) * 128], in_=pbig[0][:, c * 128:(c + 1) * 128]),
                             r=[pbig[0]], w=[maT])
            for hc in range(8):
                for c in range(8):
                    S.op("pe", lambda e: e.matmul(pbig[1][:, 0:256], lhsT=wkv[:, c, hc * 128:(hc + 1) * 128], rhs=maT[:, c, :],
                                                  start=(c == 0), stop=(c == 7)), r=[wkv, maT], w=[pbig[1]])
                S.op("dve", lambda e: e.tensor_copy(out=KmT[:, hc, :], in_=pbig[1][:, 0:256]), r=[pbig[1]], w=[KmT])
            for kc in range(2):
                for nb in range(2):
                    for c in range(8):
                        S.op("pe", lambda e: e.matmul(pbig[2][:, :], lhsT=maT[:, c, kc * 128:(kc + 1) * 128], rhs=wkv[:, c, 1024 + nb * 512:1024 + (nb + 1) * 512],
                                                      start=(c == 0), stop=(c == 7)), r=[wkv, maT], w=[pbig[2]])
                    S.op("dve", lambda e: e.tensor_copy(out=Vm[:, kc, nb * 512:(nb + 1) * 512], in_=pbig[2][:, :]), r=[pbig[2]], w=[Vm])
            for s_ in range(NS):
                for kc in range(2):
                    S.dma("sp", cs, lambda e: e.dma_start(out=cs[:, :], in_=k.cmem[s_, kc * 128:(kc + 1) * 128, :]), w=[cs])
                    S.op("dve", lambda e: e.tensor_copy(out=csb[:, :], in_=cs[:, :]), r=[cs], w=[csb])
                    S.op("pool", lambda e: e.tensor_copy(out=Vs[:, s_, kc, :], in_=csb[:, 1024:2048]), r=[csb], w=[Vs])
                    for half in range(2):
                        for c in range(4):
                            hc = half * 4 + c
                            S.op("pe", lambda e: e.matmul(pbig[3][:, c * 128:(c + 1) * 128], lhsT=csb[:, hc * 128:(hc + 1) * 128], rhs=identb[:, :],
                                                          start=True, stop=True), r=[csb, identb], w=[pbig[3]])
                        for c in range(4):
                            hc = half * 4 + c
                            S.op("dve", lambda e: e.tensor_copy(out=KsT[:, s_, hc, kc * 128:(kc + 1) * 128], in_=pbig[3][:, c * 128:(c + 1) * 128]),
                                 r=[pbig[3]], w=[KsT])
            k.S.barrier()
        wout = load_weight_bf16(k, es, "w_out3", k.w_out, 768, D, col_chunk=1024)
        wmq = load_weight_bf16(k, es, "w_mq3", k.w_mq, D, D, gdram=k.g_memq, col_chunk=1024)
        wmo = load_weight_bf16(k, es, "w_mo3", k.w_mo, D, D, col_chunk=1024)
        wpq = load_weight_bf16(k, es, "w_pq3", k.w_pq, D, 2048, col_chunk=2048)
        skT = A("skT", [128, 16, 128], BF16)
        with ExitStack() as es2:
            A2 = lambda name, shape, dt: k.track(TL(es2.enter_context(nc.sbuf_tensor(name, shape, dt)), name))
            skf = A2("skf", [128, 128], F32); skb = A2("skb", [128, 128], BF16)
            for hp in range(16):
                S.dma("sp", skf, lambda e: e.dma_start(out=skf[:, :], in_=k.sub_keys[hp, :, :]), w=[skf])
                S.op("dve", lambda e: e.tensor_copy(out=skb[:, :], in_=skf[:, :]), r=[skf], w=[skb])
                S.op("pe", lambda e: e.matmul(pbig[0][:, 0:128], lhsT=skb[:, :], rhs=identb[:, :], start=True, stop=True), r=[skb, identb], w=[pbig[0]])
                S.op("dve", lambda e: e.tensor_copy(out=skT[:, hp, :], in_=pbig[0][:, 0:128]), r=[pbig[0]], w=[skT])
            k.S.barrier()
        gffn = A("gffn", [128, D], F32); gfin = A("gfin", [128, D], F32)
        S.dma("sp", gffn, lambda e: e.dma_start(out=gffn[:, :], in_=k.g_ffn.ap().partition_broadcast(128)), w=[gffn])
        S.dma("sp", gfin, lambda e: e.dma_start(out=gfin[:, :], in_=k.g_final.ap().partition_broadcast(128)), w=[gfin])
        LOHI_INIT = True
        xt = A("x3", [128, D], F32); cat = A("cat3", [128, 768], BF16); sw = [A(f"sw3_{g}", [128, 260], F32) for g in range(3)]
        rden = A("rden3", [128, 4], F32); catT = A("catT3", [128, 6, 128], BF16)
        h = A("h3", [128, D], F32); sqj = A("sqj3", [128, D], BF16); ssq = A("ssq3", [128, 1], F32); rstd = A("rstd3", [128, 1], F32)
        cb = A("cb3", [128, D], BF16); cT = A("cT3", [128, 8, 128], BF16)
        qmT = A("qmT3", [128, 8, 128], BF16); PTm = A("PTm3", [128, 256], BF16); rdm = A("rdm3", [128, 128], F32)
        attT = A("attT3", [128, 8, 128], BF16)
        fb = A("fb3", [128, D], BF16); fT = A("fT3", [128, 8, 128], BF16)
        qpT = A("qpT3", [128, 16, 128], BF16); sc = A("sc3", [128, 16, 128], F32); sc2 = A("sc23", [128, 128], F32)
        mv = A("mv3", [128, 16, 16], F32); mi = A("mi3", [128, 16, 16], U32); mif = A("mif3", [128, 16, 16], F32)
        cand = A("cand3", [128, 256], F32); cand2 = A("cand23", [128, 256], F32)
        cv = A("cv3", [128, 8, 16], F32); ci = A("ci3", [128, 8, 16], U32); cif = A("cif3", [128, 8, 16], F32)
        ia = A("ia3", [128, 8, 16], F32); ib = A("ib3", [128, 8, 16], F32)
        oh = A("oh3", [128, 16, 16], F32); lo16 = A("lo163", [128, 16], F32); hi16 = A("hi163", [128, 16], F32); i1 = A("i13", [128, 8, 16], F32); i2 = A("i23", [128, 8, 16], F32)
        eidf = A("eidf3", [128, 128], F32); eid = A("eid3", [128, 128], I32)
        gate = A("gate3", [128, 8, 16], F32); gsum = A("gsum3", [128, 8], F32)
        hid = A("hid3", [128, 128], F32); hx = A("hx3", [128, 128], F32); wgt = A("wgt3", [128, 128], F32)
        NB = 2
        Gu = [A(f"Gu{i}", [128, D], BF16) for i in range(NB)]; Gv = [A(f"Gv{i}", [128, D], BF16) for i in range(NB)]
        flat = lambda t_, a0, a1: (lambda: t_[:, a0:a1, :].rearrange("p a b -> p (a b)"))
        Gu = Gu + [TLview(qpT, flat(qpT, 0, 8), "GuV0"), TLview(qpT, flat(qpT, 8, 16), "GuV1"), TLview(cT, flat(cT, 0, 8), "GuV2")]
        Gv = Gv + [TLview(qmT, flat(qmT, 0, 8), "GvV0"), TLview(attT, flat(attT, 0, 8), "GvV1"), TLview(fT, flat(fT, 0, 8), "GvV2")]
        NBU, NBV = len(Gu), len(Gv)
        par = lambda t_: [t_.p] if isinstance(t_, TLview) else []
        prod = [sqj, cb]
        junk = sqj; dg = [A(f"dg{i}", [128, 128], BF16) for i in range(2)]
        yo = xt
        pout = [pb.get(f"pout{i}", 512, F32, f"out{i}") for i in range(2)]
        psm = pb.get("psm", 256, F32, "sm"); pden = pb.get("pden", 128, F32, "den")

        S.op("dve", lambda e: e.tensor_scalar(out=lo16[:, :], in0=iota[:, 0:16], scalar1=16.0, scalar2=None, op0=ALU.mult), r=[iota], w=[lo16])
        S.op("dve", lambda e: e.tensor_scalar(out=hi16[:, :], in0=iota[:, 0:16], scalar1=16.0, scalar2=16.0, op0=ALU.mult, op1=ALU.add), r=[iota], w=[hi16])

        def transposes(src_t, nchunk, dstT):
            for c0 in range(0, nchunk, 4):
                n_ = min(4, nchunk - c0)
                for c in range(n_):
                    S.op("pe", lambda e: e.matmul(pbig[0][:, c * 128:(c + 1) * 128], lhsT=src_t[:, (c0 + c) * 128:(c0 + c + 1) * 128], rhs=identb[:, :],
                                                  start=True, stop=True), r=[src_t, identb], w=[pbig[0]])
                for c in range(n_):
                    S.op("dve", lambda e: e.tensor_copy(out=dstT[:, c0 + c, :], in_=pbig[0][:, c * 128:(c + 1) * 128]), r=[pbig[0]], w=[dstT])

        def rmsn(src, out_bf, gvec=None):
            S.op("act", lambda e: e.activation(out=sqj[:, :], in_=src[:, :], func=AF.Square, accum_out=ssq[:, 0:1]), r=[src], w=[sqj, ssq])
            rsqrt(S, rstd, rstd[:, :], ssq, ssq[:, :], 1.0 / D, EPS)
            if gvec is None:
                S.op("act", lambda e: e.activation(out=out_bf[:, :], in_=src[:, :], func=AF.Copy, scale=rstd[:, 0:1]), r=[src, rstd], w=[out_bf])
            else:
                S.op("dve", lambda e: e.scalar_tensor_tensor(out=out_bf[:, :], in0=src[:, :], scalar=rstd[:, 0:1], in1=gvec[:, :],
                                                             op0=ALU.mult, op1=ALU.mult), r=[src, rstd, gvec], w=[out_bf])

        def top16(vals_t, vals_ap, scratch_t, scratch_ap, mv_ap, mi_ap, mv_t, mi_t):
            S.op("dve", lambda e: e.max(out=mv_ap[:, 0:8], in_=vals_ap), r=[vals_t], w=[mv_t])
            S.op("dve", lambda e: e.max_index(out=mi_ap[:, 0:8], in_max=mv_ap[:, 0:8], in_values=vals_ap), r=[vals_t, mv_t], w=[mi_t])
            S.op("dve", lambda e: e.match_replace(out=scratch_ap, in_to_replace=mv_ap[:, 0:8], in_values=vals_ap, imm_value=-1e30),
                 r=[vals_t, mv_t], w=[scratch_t])
            S.op("dve", lambda e: e.max(out=mv_ap[:, 8:16], in_=scratch_ap), r=[scratch_t], w=[mv_t])
            S.op("dve", lambda e: e.max_index(out=mi_ap[:, 8:16], in_max=mv_ap[:, 8:16], in_values=scratch_ap), r=[scratch_t, mv_t], w=[mi_t])

        tiles = list(range(int(os.environ.get("P3TILES", str(MAIN // 128))))) + ([] if os.environ.get("P3NOSMP") else ["smp"])
        for tau in tiles:
            smp = tau == "smp"
            if smp:
                S.op("dve", lambda e: e.memset(xt[:, :], 0.0), w=[xt])
                S.op("pool", lambda e: e.memset(cat[:, :], 0.0), w=[cat])
                for g in range(3):
                    S.op("pool", lambda e: e.memset(sw[g][:, :], 1.0), w=[sw[g]])
                    S.dma("sp", sw[g], lambda e: e.dma_start(out=sw[g][0:NT, :], in_=k.smp_swo[:, g, :]), r=[k.smp_swo], w=[sw[g]])
                S.dma("sp", xt, lambda e: e.dma_start(out=xt[0:NT, :], in_=k.xs[:, :]), w=[xt])
                S.dma("sp", cat, lambda e: e.dma_start(out=cat[0:NT, 0:512], in_=k.smp_cat[:, :]), r=[k.smp_cat], w=[cat])
            else:
                t0 = tau * 128
                S.dma("sp", xt, lambda e: e.dma_start(out=xt[:, :], in_=k.xe[PRE + t0:PRE + t0 + 128, :]), w=[xt])
                S.dma("sp", cat, lambda e: e.dma_start(out=cat[:, 0:512], in_=k.cat[t0:t0 + 128, :]), w=[cat])
                for g in range(3):
                    S.dma("sp", sw[g], lambda e: e.dma_start(out=sw[g][:, :], in_=k.swo[g][t0:t0 + 128, :]), w=[sw[g]])
            S.op("dve", lambda e: e.tensor_tensor(out=sw[0][:, :], in0=sw[0][:, :], in1=sw[1][:, :], op=ALU.add), r=[sw[0], sw[1]], w=[sw[0]])
            S.op("dve", lambda e: e.tensor_tensor(out=sw[0][:, :], in0=sw[0][:, :], in1=sw[2][:, :], op=ALU.add), r=[sw[0], sw[2]], w=[sw[0]])
            for hd in range(4):
                S.op("dve", lambda e: e.reciprocal(out=rden[:, hd:hd + 1], in_=sw[0][:, hd * 65 + 64:hd * 65 + 65]), r=[sw[0]], w=[rden])
                S.op("dve", lambda e: e.tensor_scalar(out=cat[:, 512 + hd * 64:512 + (hd + 1) * 64], in0=sw[0][:, hd * 65:hd * 65 + 64],
                                                      scalar1=rden[:, hd:hd + 1], scalar2=None, op0=ALU.mult), r=[sw[0], rden], w=[cat])
            transposes(cat, 6, catT)
            for nb in range(2):
                for c in range(6):
                    S.op("pe", lambda e: e.matmul(pout[nb][:, :], lhsT=catT[:, c, :], rhs=wout[:, c, nb * 512:(nb + 1) * 512],
                                                  start=(c == 0), stop=(c == 5)), r=[catT, wout], w=[pout[nb]])
                S.op("dve", lambda e: e.tensor_tensor(out=h[:, nb * 512:(nb + 1) * 512], in0=pout[nb][:, :], in1=xt[:, nb * 512:(nb + 1) * 512], op=ALU.add),
                     r=[pout[nb], xt], w=[h])
            rmsn(h, cb)
            transposes(cb, 8, cT)
            for half in range(2):
                for c4 in range(4):
                    hc = half * 4 + c4
                    for c in range(8):
                        S.op("pe", lambda e: e.matmul(pbig[1][:, c4 * 128:(c4 + 1) * 128], lhsT=wmq[:, c, hc * 128:(hc + 1) * 128], rhs=cT[:, c, :],
                                                      start=(c == 0), stop=(c == 7)), r=[wmq, cT], w=[pbig[1]])
                for c4 in range(4):
                    hc = half * 4 + c4
                    S.op("dve", lambda e: e.tensor_copy(out=qmT[:, hc, :], in_=pbig[1][:, c4 * 128:(c4 + 1) * 128]), r=[pbig[1]], w=[qmT])
            segs = [(s_ * TS, TS, s_) for s_ in range(NS)] if smp else [(0, 128, None)]
            if smp:
                S.op("pool", lambda e: e.memset(attT[:, :, :], 0.0), w=[attT])
            for hd in range(4):
                for (q0, qn, s_) in segs:
                    kT_ap = (lambda hc, kc: KmT[:, hc, kc * 128:(kc + 1) * 128]) if s_ is None else (lambda hc, kc: KsT[:, s_, hc, kc * 128:(kc + 1) * 128])
                    v_ap = (lambda kc, col: Vm[:, kc, col:col + 128]) if s_ is None else (lambda kc, col: Vs[:, s_, kc, col:col + 128])
                    kt_t, v_t = (KmT, Vm) if s_ is None else (KsT, Vs)
                    for kc in range(2):
                        for cc in range(2):
                            S.op("pe", lambda e: e.matmul(psm[:, kc * 128:kc * 128 + qn], lhsT=kT_ap(hd * 2 + cc, kc), rhs=qmT[:, hd * 2 + cc, q0:q0 + qn],
                                                          start=(cc == 0), stop=(cc == 1)), r=[kt_t, qmT], w=[psm])
                    for kc in range(2):
                        S.op("act", lambda e: e.activation(out=PTm[:, kc * 128:kc * 128 + qn], in_=psm[:, kc * 128:kc * 128 + qn], func=AF.Exp, scale=1.0 / 16),
                             r=[psm], w=[PTm])
                    for kc in range(2):
                        S.op("pe", lambda e: e.matmul(pden[:, 0:qn], lhsT=onesb[:, :], rhs=PTm[:, kc * 128:kc * 128 + qn], start=(kc == 0), stop=(kc == 1)),
                             r=[onesb, PTm], w=[pden])
                    S.op("dve", lambda e: e.reciprocal(out=rdm[:, 0:qn], in_=pden[:, 0:qn]), r=[pden], w=[rdm])
                    for cc in range(2):
                        for kc in range(2):
                            S.op("pe", lambda e: e.matmul(pbig[2][:, cc * 128:cc * 128 + qn], lhsT=v_ap(kc, hd * 256 + cc * 128), rhs=PTm[:, kc * 128:kc * 128 + qn],
                                                          start=(kc == 0), stop=(kc == 1)), r=[v_t, PTm], w=[pbig[2]])
                    for cc in range(2):
                        S.op("dve", lambda e: e.tensor_tensor(out=attT[:, hd * 2 + cc, q0:q0 + qn], in0=pbig[2][:, cc * 128:cc * 128 + qn], in1=rdm[:, 0:qn], op=ALU.mult),
                             r=[pbig[2], rdm], w=[attT])
            for nb in range(2):
                for c in range(8):
                    S.op("pe", lambda e: e.matmul(pout[nb][:, :], lhsT=attT[:, c, :], rhs=wmo[:, c, nb * 512:(nb + 1) * 512],
                                                  start=(c == 0), stop=(c == 7)), r=[attT, wmo], w=[pout[nb]])
                S.op("dve", lambda e: e.tensor_tensor(out=h[:, nb * 512:(nb + 1) * 512], in0=pout[nb][:, :], in1=h[:, nb * 512:(nb + 1) * 512], op=ALU.add),
                     r=[pout[nb], h], w=[h])
            rmsn(h, fb, gffn)
            transposes(fb, 8, fT)
            for q4 in range(4):
                for c4 in range(4):
                    hp = q4 * 4 + c4
                    for c in range(8):
                        S.op("pe", lambda e: e.matmul(pbig[1][:, c4 * 128:(c4 + 1) * 128], lhsT=wpq[:, c, hp * 128:(hp + 1) * 128], rhs=fT[:, c, :],
                                                      start=(c == 0), stop=(c == 7)), r=[wpq, fT], w=[pbig[1]])
                for c4 in range(4):
                    hp = q4 * 4 + c4
                    S.op("dve", lambda e: e.tensor_copy(out=qpT[:, hp, :], in_=pbig[1][:, c4 * 128:(c4 + 1) * 128]), r=[pbig[1]], w=[qpT])
            for q4 in range(4):
                for c4 in range(4):
                    hp = q4 * 4 + c4
                    S.op("pe", lambda e: e.matmul(pbig[3][:, c4 * 128:(c4 + 1) * 128], lhsT=qpT[:, hp, :], rhs=skT[:, hp, :], start=True, stop=True),
                         r=[qpT, skT], w=[pbig[3]])
                for c4 in range(4):
                    hp = q4 * 4 + c4
                    S.op("dve", lambda e: e.tensor_copy(out=sc[:, hp, :], in_=pbig[3][:, c4 * 128:(c4 + 1) * 128]), r=[pbig[3]], w=[sc])
            for hp in range(16):
                top16(sc, sc[:, hp, :], sc2, sc2[:, :], mv[:, hp, :], mi[:, hp, :], mv, mi)
            S.op("dve", lambda e: e.tensor_copy(out=mif[:, :, :], in_=mi[:, :, :]), r=[mi], w=[mif])
            for hd in range(8):
                S.op("dve", lambda e: e.tensor_tensor(out=cand[:, :].rearrange("p (a b) -> p a b", a=16),
                                                      in0=mv[:, 2 * hd, :].unsqueeze(2).to_broadcast([128, 16, 16]),
                                                      in1=mv[:, 2 * hd + 1, :].unsqueeze(1).to_broadcast([128, 16, 16]), op=ALU.add), r=[mv], w=[cand])
                top16(cand, cand[:, :], cand2, cand2[:, :], cv[:, hd, :], ci[:, hd, :], cv, ci)
            S.op("dve", lambda e: e.tensor_copy(out=cif[:, :, :], in_=ci[:, :, :]), r=[ci], w=[cif])
            for hd in range(8):
                cb_ = cif[:, hd, :].unsqueeze(2).to_broadcast([128, 16, 16])
                S.op("dve", lambda e: e.tensor_tensor(out=oh[:, :, :], in0=cb_, in1=lo16[:, :].unsqueeze(1).to_broadcast([128, 16, 16]), op=ALU.is_ge),
                     r=[cif, lo16], w=[oh])
                S.op("dve", lambda e: e.tensor_tensor(out=cand2[:, :].rearrange("p (a b) -> p a b", a=16), in0=cb_, in1=hi16[:, :].unsqueeze(1).to_broadcast([128, 16, 16]), op=ALU.is_lt),
                     r=[cif, hi16], w=[cand2])
                S.op("dve", lambda e: e.tensor_tensor(out=oh[:, :, :], in0=oh[:, :, :], in1=cand2[:, :].rearrange("p (a b) -> p a b", a=16), op=ALU.mult), r=[oh, cand2], w=[oh])
                S.op("dve", lambda e: e.tensor_tensor(out=cand2[:, :].rearrange("p (a b) -> p a b", a=16), in0=oh[:, :, :], in1=mif[:, 2 * hd, :].unsqueeze(1).to_broadcast([128, 16, 16]), op=ALU.mult),
                     r=[oh, mif], w=[cand2])
                S.op("dve", lambda e: e.tensor_reduce(out=i1[:, hd, :], in_=cand2[:, :].rearrange("p (a b) -> p a b", a=16), axis=AX.X, op=ALU.add), r=[cand2], w=[i1])
                S.op("dve", lambda e: e.tensor_tensor(out=cand2[:, :].rearrange("p (a b) -> p a b", a=16), in0=oh[:, :, :], in1=lo16[:, :].unsqueeze(1).to_broadcast([128, 16, 16]), op=ALU.mult),
                     r=[oh, lo16], w=[cand2])
                S.op("dve", lambda e: e.tensor_reduce(out=ia[:, hd, :], in_=cand2[:, :].rearrange("p (a b) -> p a b", a=16), axis=AX.X, op=ALU.add), r=[cand2], w=[ia])
            S.op("dve", lambda e: e.tensor_tensor(out=ib[:, :, :], in0=cif[:, :, :], in1=ia[:, :, :], op=ALU.subtract), r=[cif, ia], w=[ib])
            for hd in range(8):
                S.op("dve", lambda e: e.tensor_tensor(out=oh[:, :, :], in0=ib[:, hd, :].unsqueeze(2).to_broadcast([128, 16, 16]),
                                                      in1=iota[:, 0:16].unsqueeze(1).to_broadcast([128, 16, 16]), op=ALU.is_equal), r=[ib, iota], w=[oh])
                S.op("dve", lambda e: e.tensor_tensor(out=oh[:, :, :], in0=oh[:, :, :], in1=mif[:, 2 * hd + 1, :].unsqueeze(1).to_broadcast([128, 16, 16]), op=ALU.mult),
                     r=[oh, mif], w=[oh])
                S.op("dve", lambda e: e.tensor_reduce(out=i2[:, hd, :], in_=oh[:, :, :], axis=AX.X, op=ALU.add), r=[oh], w=[i2])
            S.op("dve", lambda e: e.scalar_tensor_tensor(out=eidf[:, :], in0=i1[:, :, :].rearrange("p a b -> p (a b)"), scalar=128.0,
                                                         in1=i2[:, :, :].rearrange("p a b -> p (a b)"), op0=ALU.mult, op1=ALU.add), r=[i1, i2], w=[eidf])
            S.op("dve", lambda e: e.tensor_copy(out=eid[:, :], in_=eidf[:, :]), r=[eidf], w=[eid])
            for hd in range(8):
                S.op("dve", lambda e: e.tensor_scalar(out=gate[:, hd, :], in0=cv[:, hd, :], scalar1=cv[:, hd, 0:1], scalar2=None, op0=ALU.subtract),
                     r=[cv], w=[gate])
            S.op("act", lambda e: e.activation(out=gate[:, :, :].rearrange("p a b -> p (a b)"), in_=gate[:, :, :].rearrange("p a b -> p (a b)"), func=AF.Exp),
                 r=[gate], w=[gate])
            S.op("dve", lambda e: e.tensor_reduce(out=gsum[:, :], in_=gate[:, :, :], axis=AX.X, op=ALU.add), r=[gate], w=[gsum])
            S.op("dve", lambda e: e.reciprocal(out=gsum[:, :], in_=gsum[:, :]), r=[gsum], w=[gsum])
            for hd in range(8):
                S.op("dve", lambda e: e.tensor_scalar(out=gate[:, hd, :], in0=gate[:, hd, :], scalar1=gsum[:, hd:hd + 1], scalar2=None, op0=ALU.mult),
                     r=[gate, gsum], w=[gate])
            for sl in range(128):
                g_ = Gu[sl % NBU]
                pr_ = prod[sl % 2]
                S.dma("pool", g_, lambda e: e.indirect_dma_start(out=g_[:, :], out_offset=None, in_=k.expert_u.ap(),
                                                                 in_offset=bass.IndirectOffsetOnAxis(ap=eid[:, sl:sl + 1], axis=0)),
                      r=[eid] + par(g_), w=[g_])
                S.op("dve", lambda e: e.tensor_tensor(out=pr_[:, :], in0=g_[:, :], in1=fb[:, :], op=ALU.mult), r=[g_, fb] + par(g_), w=[pr_])
                S.op("act", lambda e: e.activation(out=xt[:, :], in_=pr_[:, :], func=AF.Copy, accum_out=hid[:, sl:sl + 1]), r=[pr_], w=[xt, hid])
            S.op("dve", lambda e: e.tensor_tensor(out=hx[:, :], in0=hid[:, :], in1=hid[:, :], op=ALU.mult), r=[hid], w=[hx])
            S.op("dve", lambda e: e.tensor_scalar(out=hx[:, :], in0=hx[:, :], scalar1=0.044715, scalar2=1.0, op0=ALU.mult, op1=ALU.add), r=[hx], w=[hx])
            S.op("dve", lambda e: e.tensor_tensor(out=hx[:, :], in0=hx[:, :], in1=hid[:, :], op=ALU.mult), r=[hx, hid], w=[hx])
            S.op("act", lambda e: e.activation(out=hx[:, :], in_=hx[:, :], func=AF.Tanh, scale=0.7978845608028654), r=[hx], w=[hx])
            S.op("dve", lambda e: e.tensor_scalar(out=hx[:, :], in0=hx[:, :], scalar1=1.0, scalar2=0.5, op0=ALU.add, op1=ALU.mult), r=[hx], w=[hx])
            S.op("dve", lambda e: e.tensor_tensor(out=hx[:, :], in0=hx[:, :], in1=hid[:, :], op=ALU.mult), r=[hx, hid], w=[hx])
            S.op("dve", lambda e: e.tensor_tensor(out=wgt[:, :], in0=hx[:, :], in1=gate[:, :, :].rearrange("p a b -> p (a b)"), op=ALU.mult), r=[hx, gate], w=[wgt])
            for sl in range(128):
                g_ = Gv[sl % NBV]
                d_ = dg[sl % 2]
                S.dma("pool", g_, lambda e: e.indirect_dma_start(out=g_[:, :], out_offset=None, in_=k.expert_v.ap(),
                                                                 in_offset=bass.IndirectOffsetOnAxis(ap=eid[:, sl:sl + 1], axis=0)),
                      r=[eid] + par(g_), w=[g_])
                S.op("dve", lambda e: e.tensor_scalar(out=d_[:, :], in0=identb[:, :], scalar1=wgt[:, sl:sl + 1], scalar2=None, op0=ALU.mult),
                     r=[identb, wgt], w=[d_])
                for nb in range(2):
                    S.op("pe", lambda e: e.matmul(pout[nb][:, :], lhsT=d_[:, :], rhs=g_[:, nb * 512:(nb + 1) * 512], start=(sl == 0), stop=(sl == 127)),
                         r=[d_, g_] + par(g_), w=[pout[nb]])
            for nb in range(2):
                S.op("dve", lambda e: e.tensor_tensor(out=h[:, nb * 512:(nb + 1) * 512], in0=pout[nb][:, :], in1=h[:, nb * 512:(nb + 1) * 512], op=ALU.add),
                     r=[pout[nb], h], w=[h])
            S.op("act", lambda e: e.activation(out=sqj[:, :], in_=h[:, :], func=AF.Square, accum_out=ssq[:, 0:1]), r=[h], w=[sqj, ssq])
            rsqrt(S, rstd, rstd[:, :], ssq, ssq[:, :], 1.0 / D, EPS)
            S.op("dve", lambda e: e.scalar_tensor_tensor(out=yo[:, :], in0=h[:, :], scalar=rstd[:, 0:1], in1=gfin[:, :], op0=ALU.mult, op1=ALU.mult),
                 r=[h, rstd, gfin], w=[yo])
            if smp:
                S.dma("sp", yo, lambda e: e.dma_start(out=k.y_smp[:, :], in_=yo[0:NT, :]), r=[yo])
            else:
                S.dma("sp", yo, lambda e: e.dma_start(out=k.y_main[tau * 128:(tau + 1) * 128, :], in_=yo[:, :]), r=[yo])
        k.end_phase()
```

```python
import numpy as np
from contextlib import ExitStack
import concourse.bass as bass
import concourse.mybir as mybir
from concourse.bass_utils import run_bass_kernel_spmd

F32 = mybir.dt.float32
BF16 = mybir.dt.bfloat16
I32 = mybir.dt.int32
U32 = mybir.dt.uint32
AF = mybir.ActivationFunctionType
ALU = mybir.AluOpType
AX = mybir.AxisListType

NCORES = 8
D = 1024
PRE = 4096
MAIN = 4096
EXT = PRE + MAIN
HALO = 2048
KR = HALO + MAIN
NS = 4
TS = 4
IN_DIM = 4360
CQ, CK, CV, CAG, CBG, CZ, CSW = 0, 512, 1024, 1536, 1540, 1544, 2056
DILS = (1, 4, 16)
EPS = 1e-6
NEG = -30000.0
SEM_EPOCH = 20000
import os as _os
EXPROWS = int(_os.environ.get("EXPROWS", "16384"))


class TL:
    def __init__(self, t, name):
        self.t = t
        self.name = name
        self.lw = None
        self.rd = []
        self.ds = None

    def __getitem__(self, k):
        return self.t[k]


class TLsub(TL):
    def __init__(self, bank, off, width, name):
        self.bank = bank
        self.t = bank.t
        self.name = name
        self.off = off
        self.width = width
        self.ds = None

    lw = property(lambda self: self.bank.lw, lambda self, v: setattr(self.bank, "lw", v))
    rd = property(lambda self: self.bank.rd, lambda self, v: setattr(self.bank, "rd", v))

    def __getitem__(self, key):
        r, c = key
        if isinstance(c, slice):
            a = self.off + (c.start or 0)
            b_ = self.off + (self.width if c.stop is None else c.stop)
            return self.t[r, a:b_]
        return self.t[r, self.off + c]


class TLview(TL):
    def __init__(self, parent, ap_fn, name):
        super().__init__(None, name)
        self.p = parent
        self.ap_fn = ap_fn

    def __getitem__(self, key):
        return self.ap_fn()[key]


class PsumBlocks:
    def __init__(self, nc, es, prefix):
        self.nc, self.es, self.prefix = nc, es, prefix
        self.banks = {}

    def get(self, name, width, dt, bank):
        per = 2048 // (4 if dt == F32 else 2)
        if bank not in self.banks:
            t = self.es.enter_context(self.nc.psum_tensor(f"{self.prefix}{bank}", [128, per], dt))
            self.banks[bank] = [TL(t, f"{self.prefix}{bank}"), 0]
        b = self.banks[bank]
        assert b[1] + width <= per
        off = b[1]
        b[1] += width
        return TLsub(b[0], off, width, name)


class _Rec:
    def __init__(self):
        self.call = None

    def __getattr__(self, name):
        def f(*a, **kw):
            self.call = (name, a, kw)
            return self
        return f


def _capture(fn):
    r = _Rec()
    fn(r)
    name, a, kw = r.call
    return lambda e: getattr(e, name)(*a, **kw)


class Sched:
    ENG = ("pe", "act", "dve", "pool", "sp")

    def __init__(self, nc):
        self.nc = nc
        self.eng = {"pe": nc.tensor, "act": nc.scalar, "dve": nc.vector, "pool": nc.gpsimd, "sp": nc.sync}
        self.rec = []
        self.dsems = []
        self.dsem_pool = []
        self.ninst = 0
        self.last = {e: None for e in self.ENG}

    def dsem(self, name):
        s = [self.nc.alloc_semaphore("d_" + name), 0, False]
        self.dsems.append(s)
        return s

    def _collect(self, e, r, w, is_dma):
        deps = []
        for t in r:
            if t.lw is not None:
                deps.append(t.lw)
        for t in w:
            if t.lw is not None:
                p = self.rec[t.lw]
                if is_dma or not (p["kind"] == "op" and p["e"] == e and e == "pe"):
                    deps.append(t.lw)
            for d in t.rd:
                p = self.rec[d]
                if is_dma or p["kind"] == "dma" or p["e"] != e or e != "pe":
                    deps.append(d)
        return deps

    def op(self, e, fn, r=(), w=()):
        deps = self._collect(e, r, w, False)
        idx = len(self.rec)
        self.rec.append({"kind": "op", "e": e, "fn": _capture(fn), "deps": deps})
        self.last[e] = idx
        for t in w:
            t.lw = idx
            t.rd = []
        for t in r:
            t.rd.append(idx)
        self.ninst += 1

    def dma(self, q, t, fn, r=(), w=()):
        if q == "pool" and (t.ds is None or not t.ds[2]):
            t.ds = self.dsem(t.name + "_sw")
            t.ds[2] = True
        if t.ds is None:
            t.ds = self.dsem_pool.pop() if self.dsem_pool else self.dsem(t.name)
        ds = t.ds
        deps = self._collect(q, r, w, True)
        if ds[1] + 16 >= 32000:
            sw_ = ds[2]
            t.ds = self.dsem(t.name + f"_r{len(self.dsems)}")
            t.ds[2] = sw_
            ds = t.ds
        ds[1] += 16
        idx = len(self.rec)
        self.rec.append({"kind": "dma", "e": q, "fn": _capture(fn), "deps": deps, "sem": ds[0], "val": ds[1]})
        for x in w:
            x.lw = idx
            x.rd = []
        for x in r:
            x.rd.append(idx)
        self.ninst += 1

    def release(self, tiles):
        for t in tiles:
            if t.ds is not None:
                if not t.ds[2]:
                    self.dsem_pool.append(t.ds)
                t.ds = None

    def barrier(self, engines=None):
        deps = [v for v in self.last.values() if v is not None]
        dm = [(s[0], s[1]) for s in self.dsems if s[1]]
        self.rec.append({"kind": "bar", "deps": deps, "dm": dm, "engines": engines or self.ENG})

    def final_wait(self):
        self.barrier(engines=("sp",))

    def emit(self):
        rec = self.rec
        seq = {}
        cnt = {e: 0 for e in self.ENG}
        for i, r in enumerate(rec):
            if r["kind"] == "op":
                cnt[r["e"]] += 1
                seq[i] = cnt[r["e"]]
        awaited = set()

        def sweep(do_emit, ordv=None, sems=None):
            wseq = {e: {p: 0 for p in self.ENG} for e in self.ENG}
            wdma = {e: {} for e in self.ENG}
            for i, r in enumerate(rec):
                targets = r["engines"] if r["kind"] == "bar" else (r["e"],)
                for e in targets:
                    need = {}
                    for d in r["deps"]:
                        p = rec[d]
                        if p["kind"] == "op":
                            if seq[d] > wseq[e][p["e"]] and seq[d] > need.get(p["e"], (0, None))[0]:
                                need[p["e"]] = (seq[d], d)
                        else:
                            key = id(p["sem"])
                            if wdma[e].get(key, 0) < p["val"]:
                                wdma[e][key] = p["val"]
                                if do_emit:
                                    self.eng[e].wait_ge(p["sem"], p["val"])
                    for (sem, val) in r.get("dm", ()):
                        key = id(sem)
                        if wdma[e].get(key, 0) < val:
                            wdma[e][key] = val
                            if do_emit:
                                self.eng[e].wait_ge(sem, val)
                    for pe_, (sq, d) in need.items():
                        wseq[e][pe_] = sq
                        if do_emit:
                            o = ordv[d]
                            self.eng[e].wait_ge(sems[pe_][(o - 1) // SEM_EPOCH], (o - 1) % SEM_EPOCH + 1)
                        else:
                            awaited.add(d)
                if do_emit and r["kind"] != "bar":
                    ins = r["fn"](self.eng[r["e"]])
                    if r["kind"] == "dma":
                        ins.then_inc(r["sem"], 16)
                    elif i in awaited:
                        o = ordv[i]
                        ins.then_inc(sems[r["e"]][(o - 1) // SEM_EPOCH], 1)

        sweep(False)
        ordv = {}
        oc = {e: 0 for e in self.ENG}
        for i, r in enumerate(rec):
            if r["kind"] == "op" and i in awaited:
                oc[r["e"]] += 1
                ordv[i] = oc[r["e"]]
        sems = {e: [self.nc.alloc_semaphore(f"s_{e}_{j}") for j in range((oc[e] + SEM_EPOCH - 1) // SEM_EPOCH)] for e in self.ENG}
        self.nsig = dict(oc)
        sweep(True, ordv, sems)


class K:
    def __init__(self):
        self.tiles = []

    def track(self, t):
        self.tiles.append(t)
        return t

    def end_phase(self):
        self.S.barrier()
        self.S.release(self.tiles)
        self.tiles = []


def make_consts():
    c = {}
    idx = np.arange(128)
    same = (idx[:, None] // 64) == (idx[None, :] // 64)
    c["ident"] = np.eye(128, dtype=np.float32)
    c["trit"] = ((idx[:, None] <= idx[None, :]) & same).astype(np.float32)
    c["blk"] = same.astype(np.float32)
    c["mTneg"] = np.where((idx[None, :] >= idx[:, None]) & same, 0.0, NEG).astype(np.float32)
    c["mSpos"] = np.where((idx[:, None] > idx[None, :]) & same, 0.0, -NEG).astype(np.float32)
    c["swprev"] = np.where(idx[:, None] >= idx[None, :], 0.0, NEG).astype(np.float32)
    c["swcur"] = np.where(idx[:, None] <= idx[None, :], 0.0, NEG).astype(np.float32)
    c["ones"] = np.ones((128, 128), np.float32)
    c["iota"] = np.tile(np.arange(128, dtype=np.float32), (128, 1))
    return c


CONST_NAMES = ("ident", "trit", "blk", "mTneg", "mSpos", "swprev", "swcur", "ones", "iota")


def declare_io(k, debug):
    nc = k.nc
    I = lambda n, s, dt=F32: nc.dram_tensor(n, list(s), dt, kind="ExternalInput")
    O = lambda n, s, dt=F32: nc.dram_tensor(n, list(s), dt, kind="ExternalOutput")
    SCR = (lambda n, s, dt: nc.dram_tensor(n, list(s), dt, kind="ExternalOutput")) if debug else \
          (lambda n, s, dt: nc.dram_tensor(n, list(s), dt, kind="Internal"))
    k.xe = I("xe", [EXT, D])
    k.xs = I("xs", [NS * TS, D])
    k.st_delta = I("st_delta", [NS, 4, 128, 128])
    k.st_conv = I("st_conv", [NS, 3, 1536])
    k.cwin = [I(f"cwin{g}", [NS, 128 * DILS[g], 512]) for g in range(3)]
    k.cmem = I("cmem", [NS, 256, 2048])
    k.mem = I("mem", [256, D])
    k.halo = I("halo", [128, 128])
    k.consts = I("consts", [len(CONST_NAMES), 128, 128])
    for n, s in (("g_mix", [D]), ("w_in", [D, IN_DIM]), ("conv_w", [4, 1536]), ("a_log", [4]), ("dt_bias", [4]),
                 ("g_onorm", [128]), ("w_out", [768, D]), ("g_memq", [D]), ("g_memkv", [D]), ("w_mq", [D, D]),
                 ("w_mkv", [D, 2048]), ("w_mo", [D, D]), ("g_ffn", [D]), ("w_pq", [D, 2048]),
                 ("sub_keys", [16, 128, 128]), ("expert_u", [EXPROWS, D]), ("expert_v", [EXPROWS, D]), ("g_final", [D])):
        setattr(k, n, I(n, s))
    k.y_main = O("y_main", [MAIN, D])
    k.y_smp = O("y_smp", [NS * TS, D])
    k.p_delta = O("p_delta", [4, 128, 128])
    k.p_conv = O("p_conv", [3, 1536])
    k.p_win = [O(f"p_win{g}", [128 * DILS[g], 512]) for g in range(3)]
    k.p_mem = O("p_mem", [256, 2048])
    k.s_delta = O("s_delta", [NS, 4, 128, 128])
    k.s_conv = O("s_conv", [NS, 3, 1536])
    k.s_win = [O(f"s_win{g}", [NS, 128 * DILS[g], 512]) for g in range(3)]
    k.dnq = SCR("dnq", [4, 128, EXT], BF16)
    k.dnk = SCR("dnk", [4, 128, EXT], BF16)
    k.dnv = SCR("dnv", [4, 128, EXT], BF16)
    k.gbs = SCR("gbs", [EXT, 8], F32)
    k.zs = SCR("zs", [MAIN, 512], BF16)
    k.qts = [SCR(f"qts{g}", [2, 128, MAIN], BF16) for g in range(3)]
    k.kts = [SCR(f"kts{g}", [2, 128, KR], BF16) for g in range(3)]
    k.vss = [SCR(f"vss{g}", [KR, 256], BF16) for g in range(3)]
    SWO = (lambda n, s, dt: nc.dram_tensor(n, list(s), dt, kind="ExternalOutput")) if _os.environ.get("SWO_OUT") else SCR
    k.swo = [SWO(f"swo{g}", [MAIN, 260], F32) for g in range(3)]
    k.cat = SCR("cat", [MAIN, 512], BF16)
    k.eu_bf = nc.dram_tensor("eu_bf", [EXPROWS, D], BF16, kind="Internal")
    k.ev_bf = nc.dram_tensor("ev_bf", [EXPROWS, D], BF16, kind="Internal")


def evac(S, i, out_t, out_ap, in_t, in_ap, rx=(), **kw):
    if i % 2 == 0 and len(out_ap.shape) == 2 and len(in_ap.shape) == 2:
        S.op("act", lambda e: e.activation(out=out_ap, in_=in_ap, func=AF.Copy, **kw), r=[in_t, *rx], w=[out_t])
    else:
        if "scale" in kw:
            S.op("dve", lambda e: e.tensor_scalar(out=out_ap, in0=in_ap, scalar1=kw["scale"], scalar2=None, op0=ALU.mult),
                 r=[in_t, *rx], w=[out_t])
        else:
            S.op("dve", lambda e: e.tensor_copy(out=out_ap, in_=in_ap), r=[in_t, *rx], w=[out_t])


def rsqrt(S, out_t, out_ap, in_t, in_ap, mul, add):
    S.op("act", lambda e: e.activation(out=out_ap, in_=in_ap, func=AF.Sqrt, scale=mul, bias=add), r=[in_t], w=[out_t])
    S.op("dve", lambda e: e.reciprocal(out=out_ap, in_=out_ap), r=[out_t], w=[out_t])


def load_weight_bf16(k, es, name, wdram, rows, cols, gdram=None, col_chunk=None):
    nc, S = k.nc, k.S
    nch = rows // 128
    wbf = TL(es.enter_context(nc.sbuf_tensor(name + "_bf", [128, nch, cols], BF16)), name)
    gcol = None
    if gdram is not None:
        gcol = k.track(TL(es.enter_context(nc.sbuf_tensor(name + "_g", [128, nch], F32)), name + "_g"))
        S.dma("sp", gcol, lambda e: e.dma_start(out=gcol[:, :], in_=gdram.ap().rearrange("(c p) -> p c", p=128),
                                                 allow_slow_non_contiguous=True), w=[gcol])
    cc = col_chunk or cols
    with ExitStack() as es2:
        st = [k.track(TL(es2.enter_context(nc.sbuf_tensor(f"{name}_st{i}", [128, cc], F32)), f"{name}_st{i}")) for i in range(2)]
        n = 0
        for c in range(nch):
            for c0 in range(0, cols, cc):
                w_ = min(cc, cols - c0)
                s_ = st[n % 2]
                S.dma("sp", s_, lambda e: e.dma_start(out=s_[:, 0:w_], in_=wdram[c * 128:(c + 1) * 128, c0:c0 + w_]), w=[s_])
                if gcol is not None:
                    evac(S, n, wbf, wbf[:, c, c0:c0 + w_], s_, s_[:, 0:w_], rx=[gcol], scale=gcol[:, c:c + 1])
                else:
                    evac(S, n, wbf, wbf[:, c, c0:c0 + w_], s_, s_[:, 0:w_])
                n += 1
        S.barrier()
    return wbf


def phase1(k):
    nc, S = k.nc, k.S
    with ExitStack() as es:
        A = lambda name, shape, dt: k.track(TL(es.enter_context(nc.sbuf_tensor(name, shape, dt)), name))
        P = lambda name, shape, dt: TL(es.enter_context(nc.psum_tensor(name, shape, dt)), name)
        wbf = load_weight_bf16(k, es, "w_in", k.w_in, D, IN_DIM, gdram=k.g_mix, col_chunk=2180)
        cw = A("cw", [128, 12, 4], F32)
        for j in range(4):
            S.dma("sp", cw, lambda e: e.dma_start(out=cw[:, :, j], in_=k.conv_w[j, :].rearrange("(c p) -> p c", p=128),
                                                  allow_slow_non_contiguous=True), w=[cw])
        dtb = A("dtb", [128, 4], F32)
        S.dma("sp", dtb, lambda e: e.dma_start(out=dtb[:, :], in_=k.dt_bias.ap().partition_broadcast(128)), w=[dtb])
        nega = A("nega", [128, 4], F32)
        S.dma("sp", nega, lambda e: e.dma_start(out=nega[:, :], in_=k.a_log.ap().partition_broadcast(128)), w=[nega])
        S.op("act", lambda e: e.activation(out=nega[:, :], in_=nega[:, :], func=AF.Exp), r=[nega], w=[nega])
        S.op("dve", lambda e: e.tensor_scalar(out=nega[:, :], in0=nega[:, :], scalar1=-1.0, scalar2=None, op0=ALU.mult), r=[nega], w=[nega])
        xt = [A(f"xt{i}", [128, D], F32) for i in range(2)]
        xsm = A("xsm", [128, D], F32)
        sqj = A("sqj", [128, D], BF16)
        ssq = [A(f"ssq{i}", [128, 1], F32) for i in range(2)]
        rstd = [A(f"rstd{i}", [128, 1], F32) for i in range(2)]
        ab = [A(f"ab{i}", [128, D], BF16) for i in range(2)]
        aT = [A(f"aT{i}", [128, 8, 512], BF16) for i in range(2)]
        xp = [A(f"xp{i}", [128, 515], F32) for i in range(2)]
        carry = A("carry", [128, 12, 3], F32)
        acc = [A(f"acc{i}", [128, 512], F32) for i in range(2)]
        act_ = [A(f"actt{i}", [128, 512], F32) for i in range(2)]
        sq2 = [A(f"sq2{i}", [128, 512], BF16) for i in range(2)]
        rn = [A(f"rn{i}", [128, 512], F32) for i in range(2)]
        outb = [A(f"outb{i}", [128, 512], BF16) for i in range(3)]
        qkp = [A(f"qkp{i}", [128, 512], BF16) for i in range(3)]
        gb = [A(f"gb{i}", [128, 8], F32) for i in range(2)]
        gtmp = [A(f"gtmp{i}", [128, 4], F32) for i in range(4)]
        zb = [A(f"zb{i}", [128, 512], BF16) for i in range(2)]
        vsb = [A(f"vsb{i}", [128, 256], BF16) for i in range(3)]
        kvf = [A(f"kvf{i}", [128, 512], F32) for i in range(3)]
        xps = A("xps", [128, 12, NS, 7], F32)
        ones_bf = k.ones_bf
        pT = [P(f"pT{i}", [128, 8, 128], BF16) for i in range(2)]
        pacc = [P(f"pacc{i}", [128, 512], F32) for i in range(2)]
        pl2 = P("pl2", [128, 512], F32)
        ptm = [P(f"ptm{i}", [128, 512], F32) for i in range(2)]
        pg = P("pg", [128, 8], F32)
        S.op("dve", lambda e: e.memset(carry[:, :, :], 0.0), w=[carry])
        S.op("dve", lambda e: e.memset(xsm[:, :], 0.0), w=[xsm])
        ctr = {"ev": 0, "acc": 0, "tm": 0, "ob": 0}

        def norm_transpose(xtile, i, aTt, col0):
            S.op("act", lambda e: e.activation(out=sqj[:, :], in_=xtile[:, :], func=AF.Square, accum_out=ssq[i][:, 0:1]),
                 r=[xtile], w=[sqj, ssq[i]])
            rsqrt(S, rstd[i], rstd[i][:, :], ssq[i], ssq[i][:, :], 1.0 / D, EPS)
            S.op("act", lambda e: e.activation(out=ab[i][:, :], in_=xtile[:, :], func=AF.Copy, scale=rstd[i][:, 0:1]),
                 r=[xtile, rstd[i]], w=[ab[i]])
            for c in range(8):
                S.op("pe", lambda e: e.transpose(out=pT[i][:, c, :], in_=ab[i][:, c * 128:(c + 1) * 128], identity=k.ident_bf[:, :]),
                     r=[ab[i], k.ident_bf], w=[pT[i]])
            ctr["ev"] += 1
            evac(S, ctr["ev"], aTt, aTt[:, :, col0:col0 + 128], pT[i], pT[i][:, :, :])

        def fm_chunk(aTt, nt, col):
            ps = pacc[ctr["acc"] % 2]
            ctr["acc"] += 1
            for c in range(8):
                S.op("pe", lambda e: e.matmul(ps[:, 0:nt], lhsT=wbf[:, c, col:col + 128], rhs=aTt[:, c, 0:nt],
                                              start=(c == 0), stop=(c == 7)), r=[wbf, aTt], w=[ps])
            return ps

        def tm_chunk(aTt, t, col, ncol, ps=None):
            if ps is None:
                ps = ptm[ctr["tm"] % 2]
                ctr["tm"] += 1
            for c in range(8):
                S.op("pe", lambda e: e.matmul(ps[:, 0:ncol], lhsT=aTt[:, c, t * 128:(t + 1) * 128], rhs=wbf[:, c, col:col + ncol],
                                              start=(c == 0), stop=(c == 7)), r=[wbf, aTt], w=[ps])
            return ps

        def dn_post(ps, nt, ci, src_t, src_ap, dst_fn):
            j = ctr["ob"] % 2
            S.op("act", lambda e: e.activation(out=act_[j][:, 0:nt], in_=src_ap, func=AF.Silu), r=[src_t], w=[act_[j]])
            ob = outb[ctr["ob"] % 3]
            ctr["ob"] += 1
            if ci < 8:
                S.op("act", lambda e: e.activation(out=sq2[j][:, 0:nt], in_=act_[j][:, 0:nt], func=AF.Square), r=[act_[j]], w=[sq2[j]])
                S.op("pe", lambda e: e.matmul(pl2[:, 0:nt], lhsT=ones_bf[:, :], rhs=sq2[j][:, 0:nt], start=True, stop=True),
                     r=[ones_bf, sq2[j]], w=[pl2])
                rsqrt(S, rn[j], rn[j][:, 0:nt], pl2, pl2[:, 0:nt], 1.0, EPS)
                if ci < 4:
                    S.op("dve", lambda e: e.scalar_tensor_tensor(out=ob[:, 0:nt], in0=act_[j][:, 0:nt], scalar=128 ** -0.5,
                                                                 in1=rn[j][:, 0:nt], op0=ALU.mult, op1=ALU.mult),
                         r=[act_[j], rn[j]], w=[ob])
                else:
                    S.op("dve", lambda e: e.tensor_tensor(out=ob[:, 0:nt], in0=act_[j][:, 0:nt], in1=rn[j][:, 0:nt], op=ALU.mult),
                         r=[act_[j], rn[j]], w=[ob])
            else:
                S.op("dve", lambda e: e.tensor_copy(out=ob[:, 0:nt], in_=act_[j][:, 0:nt]), r=[act_[j]], w=[ob])
            dst_fn(ob)

        def gates(ps_g, rows, dst_t, dst_ap):
            g0, g1, g2, g3 = gtmp
            S.op("act", lambda e: e.activation(out=dst_ap[:, 4:8], in_=ps_g[0:rows, 4:8], func=AF.Sigmoid), r=[ps_g], w=[dst_t])
            S.op("dve", lambda e: e.tensor_tensor(out=g0[0:rows, :], in0=ps_g[0:rows, 0:4], in1=dtb[0:rows, :], op=ALU.add),
                 r=[ps_g, dtb], w=[g0])
            S.op("act", lambda e: e.activation(out=g1[0:rows, :], in_=g0[0:rows, :], func=AF.Abs), r=[g0], w=[g1])
            S.op("act", lambda e: e.activation(out=g2[0:rows, :], in_=g1[0:rows, :], func=AF.Exp, scale=-1.0), r=[g1], w=[g2])
            S.op("act", lambda e: e.activation(out=g3[0:rows, :], in_=g2[0:rows, :], func=AF.Ln, bias=1.0), r=[g2], w=[g3])
            S.op("dve", lambda e: e.scalar_tensor_tensor(out=g1[0:rows, :], in0=g0[0:rows, :], scalar=0.0, in1=g3[0:rows, :],
                                                         op0=ALU.max, op1=ALU.add), r=[g0, g3], w=[g1])
            S.op("dve", lambda e: e.tensor_tensor(out=dst_ap[:, 0:4], in0=g1[0:rows, :], in1=nega[0:rows, :], op=ALU.mult),
                 r=[g1, nega], w=[dst_t])

        nsup = EXT // 512
        import os
        sups = range(nsup) if "P1SUP" not in os.environ else [int(x) for x in os.environ["P1SUP"].split(",") if x]
        for s in sups:
            aTt = aT[s % 2]
            in_main = s * 512 >= PRE
            in_kr = s * 512 >= PRE - HALO
            for t in range(4):
                x_ = xt[t % 2]
                r0 = s * 512 + t * 128
                S.dma("sp", x_, lambda e: e.dma_start(out=x_[:, :], in_=k.xe[r0:r0 + 128, :]), w=[x_])
                norm_transpose(x_, t % 2, aTt, t * 128)
            for ci in range(12):
                ps = fm_chunk(aTt, 512, ci * 128)
                xp_ = xp[ci % 2]
                S.op("act", lambda e: e.activation(out=xp_[:, 3:515], in_=ps[:, :], func=AF.Copy), r=[ps], w=[xp_])
                S.op("pool", lambda e: e.tensor_copy(out=xp_[:, 0:3], in_=carry[:, ci, :]), r=[carry], w=[xp_])
                S.op("pool", lambda e: e.tensor_copy(out=carry[:, ci, :], in_=xp_[:, 512:515]), r=[xp_], w=[carry])
                if s == nsup - 1 and not os.environ.get("SKIP_PCONV"):
                    S.dma("sp", xp_, lambda e: e.dma_start(out=k.p_conv.ap()[:, ci * 128:(ci + 1) * 128].rearrange("j p -> p j"),
                                                             in_=xp_[:, 512:515], allow_slow_non_contiguous=True), r=[xp_])
                ac = acc[ci % 2]
                S.op("dve", lambda e: e.tensor_scalar(out=ac[:, :], in0=xp_[:, 0:512], scalar1=cw[:, ci, 0:1], scalar2=None, op0=ALU.mult),
                     r=[xp_, cw], w=[ac])
                for j in range(1, 4):
                    S.op("dve", lambda e: e.scalar_tensor_tensor(out=ac[:, :], in0=xp_[:, j:j + 512], scalar=cw[:, ci, j:j + 1], in1=ac[:, :],
                                                                 op0=ALU.mult, op1=ALU.add), r=[xp_, cw, ac], w=[ac])
                dst = (k.dnq, k.dnk, k.dnv)[ci // 4]
                h = ci % 4
                dn_post(None, 512, ci, ac, ac[:, :],
                        lambda ob: S.dma("sp", ob, lambda e: e.dma_start(out=dst[h, :, s * 512:(s + 1) * 512], in_=ob[:, :]), r=[ob]))
            for t in range(4):
                r0 = s * 512 + t * 128
                for c in range(8):
                    S.op("pe", lambda e: e.matmul(pg[:, :], lhsT=aTt[:, c, t * 128:(t + 1) * 128], rhs=wbf[:, c, CAG:CAG + 8],
                                                  start=(c == 0), stop=(c == 7)), r=[wbf, aTt], w=[pg])
                g_ = gb[t % 2]
                gates(pg, 128, g_, g_[:, :])
                S.dma("sp", g_, lambda e: e.dma_start(out=k.gbs[r0:r0 + 128, :], in_=g_[:, :]), r=[g_])
                if in_main:
                    ps = tm_chunk(aTt, t, CZ, 512)
                    z_ = zb[t % 2]
                    ctr["ev"] += 1
                    evac(S, ctr["ev"], z_, z_[:, :], ps, ps[:, :])
                    S.dma("sp", z_, lambda e: e.dma_start(out=k.zs[r0 - PRE:r0 - PRE + 128, :], in_=z_[:, :]), r=[z_])
            if not in_kr:
                continue
            sk = s - (PRE - HALO) // 512
            sm = s - PRE // 512
            for g in range(3):
                d = DILS[g]
                if False:
                    continue
                for which in range(2):
                    if which == 0 and not in_main:
                        continue
                    for pair in range(2):
                        ps = fm_chunk(aTt, 512, CSW + 768 * g + 256 * which + 128 * pair)
                        q_ = qkp[ctr["ob"] % 3]
                        ctr["ob"] += 1
                        ctr["ev"] += 1
                        evac(S, ctr["ev"], q_, q_[:, :], ps, ps[:, :])
                        if which == 0:
                            dst = k.qts[g][pair, :, sm * 512:(sm + 1) * 512]
                        else:
                            dst = k.kts[g][pair, :, sk * 512:(sk + 1) * 512]
                        S.dma("sp", q_, lambda e: e.dma_start(out=dst, in_=q_[:, :]), r=[q_])
            for t in range(4):
                r0 = s * 512 + t * 128
                if os.environ.get("SKIP_KV") == "1":
                    continue
                for g in range(3):
                    ps = tm_chunk(aTt, t, CSW + 768 * g + 256, 512)
                    v_ = vsb[g]
                    S.op("dve", lambda e: e.tensor_copy(out=v_[:, :], in_=ps[:, 256:512]), r=[ps], w=[v_])
                    S.dma("sp", v_, lambda e: e.dma_start(out=k.vss[g][r0 - (PRE - HALO):r0 - (PRE - HALO) + 128, :], in_=v_[:, :]), r=[v_])
                    wlen = 128 * DILS[g]
                    if r0 >= EXT - wlen:
                        f_ = kvf[g]
                        S.op("dve", lambda e: e.tensor_copy(out=f_[:, :], in_=ps[:, :]), r=[ps], w=[f_])
                        o0 = r0 - (EXT - wlen)
                        S.dma("sp", f_, lambda e: e.dma_start(out=k.p_win[g][o0:o0 + 128, :], in_=f_[:, :]), r=[f_])

        NT = NS * TS
        if os.environ.get("P1NOSMP"):
            k.end_phase()
            return
        S.dma("sp", xsm, lambda e: e.dma_start(out=xsm[0:NT, :], in_=k.xs[:, :]), w=[xsm])
        aTt = aT[0]
        norm_transpose(xsm, 0, aTt, 0)
        for s_ in range(NS):
            for j in range(3):
                S.dma("sp", xps, lambda e: e.dma_start(out=xps[:, :, s_, j], in_=k.st_conv[s_, j, :].rearrange("(c p) -> p c", p=128),
                                                       allow_slow_non_contiguous=True), w=[xps])
        for ci in range(12):
            ps = fm_chunk(aTt, NT, ci * 128)
            S.op("dve", lambda e: e.tensor_copy(out=xps[:, ci, :, 3:7], in_=ps[:, 0:NT].rearrange("p (s t) -> p s t", s=NS)),
                 r=[ps], w=[xps])
        for ci in range(12):
            for s_ in range(NS):
                S.dma("sp", xps, lambda e: e.dma_start(out=k.s_conv[s_, :, ci * 128:(ci + 1) * 128].rearrange("j p -> p j"),
                                                       in_=xps[:, ci, s_, 4:7], allow_slow_non_contiguous=True), r=[xps])
        for ci in range(12):
            ac = acc[ci % 2]
            av = ac[:, 0:NT].rearrange("p (s t) -> p s t", s=NS)
            S.op("dve", lambda e: e.tensor_scalar(out=av, in0=xps[:, ci, :, 0:4], scalar1=cw[:, ci, 0:1], scalar2=None, op0=ALU.mult),
                 r=[xps, cw], w=[ac])
            for j in range(1, 4):
                S.op("dve", lambda e: e.scalar_tensor_tensor(out=av, in0=xps[:, ci, :, j:j + 4], scalar=cw[:, ci, j:j + 1], in1=av,
                                                             op0=ALU.mult, op1=ALU.add), r=[xps, cw, ac], w=[ac])
            dn_post(None, NT, ci, ac, ac[:, 0:NT],
                    lambda ob: S.op("pool", lambda e: e.tensor_copy(out=k.smp_dn[:, ci, :], in_=ob[:, 0:NT]), r=[ob], w=[k.smp_dn]))
        for c in range(8):
            S.op("pe", lambda e: e.matmul(pg[:, :], lhsT=aTt[:, c, 0:128], rhs=wbf[:, c, CAG:CAG + 8],
                                          start=(c == 0), stop=(c == 7)), r=[wbf, aTt], w=[pg])
        gates(pg, NT, k.smp_gb, k.smp_gb[:, :])
        ps = tm_chunk(aTt, 0, CZ, 512)
        S.op("act", lambda e: e.activation(out=k.smp_z[:, :], in_=ps[0:NT, :], func=AF.Copy), r=[ps], w=[k.smp_z])
        for g in range(3):
            for which in range(2):
                for pair in range(2):
                    ps = fm_chunk(aTt, NT, CSW + 768 * g + 256 * which + 128 * pair)
                    S.op("act", lambda e: e.activation(out=k.smp_qk[:, g, which, pair, :], in_=ps[:, 0:NT], func=AF.Copy),
                         r=[ps], w=[k.smp_qk])
            ps = tm_chunk(aTt, 0, CSW + 768 * g + 256, 512)
            S.op("act", lambda e: e.activation(out=k.smp_kv[:, g, :], in_=ps[0:NT, :], func=AF.Copy), r=[ps], w=[k.smp_kv])
        k.end_phase()


def build_nc(debug=False, phases=(1, 5, 2, 3, 4)):
    nc = bass.Bass("TRN2", target_bir_lowering=False)
    k = K()
    k.nc = nc
    k.S = Sched(nc)
    k.debug = debug
    declare_io(k, debug)
    S = k.S
    with ExitStack() as es:
        A = lambda name, shape, dt: TL(es.enter_context(nc.sbuf_tensor(name, shape, dt)), name)
        k.cst = {}
        for i, n in enumerate(CONST_NAMES):
            t = A("c_" + n, [128, 128], F32)
            S.dma("sp", t, lambda e: e.dma_start(out=t[:, :], in_=k.consts[i, :, :]), w=[t])
            k.cst[n] = t
        k.ident_bf = A("ident_bf", [128, 128], BF16)
        S.op("dve", lambda e: e.tensor_copy(out=k.ident_bf[:, :], in_=k.cst["ident"][:, :]), r=[k.cst["ident"]], w=[k.ident_bf])
        k.ones_bf = A("ones_bf", [128, 128], BF16)
        S.op("dve", lambda e: e.tensor_copy(out=k.ones_bf[:, :], in_=k.cst["ones"][:, :]), r=[k.cst["ones"]], w=[k.ones_bf])
        NT = NS * TS
        k.smp_dn = A("smp_dn", [128, 12, NT], BF16)
        k.smp_gb = A("smp_gb", [NT, 8], F32)
        k.smp_z = A("smp_z", [NT, 512], BF16)
        k.smp_qk = A("smp_qk", [128, 3, 2, 2, NT], BF16)
        k.smp_kv = A("smp_kv", [NT, 3, 512], F32)
        k.smp_cat = A("smp_cat", [NT, 512], BF16)
        k.smp_swo = A("smp_swo", [NT, 3, 260], F32)
        if _os.environ.get("P2DBG"):
            k.dbg = nc.dram_tensor("dbg", [128, 4096], F32, kind="ExternalOutput")
        if 4 in phases and EXPROWS >= 1024:
            phase0_convert(k)
        if 1 in phases:
            phase1(k)
        if 5 in phases:
            phase_mem_and_windows(k)
        if 2 in phases:
            phase2a(k)
        if 3 in phases:
            phase2b(k)
        if 4 in phases:
            phase3(k)
        S.barrier()
        S.final_wait()
        S.emit()
    k.ninst = S.ninst
    return nc, k


def shard_inputs(inp):
    consts = np.stack([make_consts()[n] for n in CONST_NAMES]).astype(np.float32)
    maps = []
    c0 = make_consts()
    for c in range(NCORES):
        b, j = c // 2, c % 2
        xe = np.zeros((EXT, D), np.float32)
        if j == 0:
            xe[PRE:] = inp["x_prompt"][b, :MAIN]
            halo = np.full((128, 128), NEG, np.float32)
        else:
            xe[:] = inp["x_prompt"][b]
            halo = c0["swprev"]
        sl = slice(c * NS, (c + 1) * NS)
        m = {
            "xe": xe,
            "xs": np.ascontiguousarray(inp["x_sample"][sl].reshape(NS * TS, D)),
            "st_delta": np.ascontiguousarray(inp["state_delta"][0, sl]),
            "st_conv": np.ascontiguousarray(inp["state_conv"][0, sl]),
            "cwin0": np.ascontiguousarray(inp["cache_win1"][0, sl].reshape(NS, 128, 512)),
            "cwin1": np.ascontiguousarray(inp["cache_win2"][0, sl].reshape(NS, 512, 512)),
            "cwin2": np.ascontiguousarray(inp["cache_win3"][0, sl].reshape(NS, 2048, 512)),
            "cmem": np.ascontiguousarray(inp["cache_mem_kv"][0, sl].reshape(NS, 256, 2048)),
            "mem": np.ascontiguousarray(inp["mem_prompt"][b]),
            "halo": halo,
            "consts": consts,
            "g_mix": inp["g_mix"][0], "w_in": inp["w_in"][0], "conv_w": inp["conv_w"][0], "a_log": inp["a_log"][0],
            "dt_bias": inp["dt_bias"][0], "g_onorm": inp["g_onorm"][0], "w_out": inp["w_out"][0], "g_memq": inp["g_memq"][0],
            "g_memkv": inp["g_memkv"][0], "w_mq": inp["w_mq"][0], "w_mkv": inp["w_mkv"][0], "w_mo": inp["w_mo"][0],
            "g_ffn": inp["g_ffn"][0], "w_pq": inp["w_pq"][0], "sub_keys": inp["sub_keys"][0].reshape(16, 128, 128),
            "expert_u": inp["expert_u"][0][:EXPROWS], "expert_v": inp["expert_v"][0][:EXPROWS], "g_final": inp["g_final"],
        }
        maps.append({kk: np.ascontiguousarray(np.asarray(v, dtype=np.float32)) for kk, v in m.items()})
    return maps


def assemble(res):
    B, SEQ, DB = 4, 8192, 32
    y_prompt = np.zeros((B, SEQ, D), np.float32)
    y_sample = np.zeros((DB, TS, D), np.float32)
    p_delta = np.zeros((1, B, 4, 128, 128), np.float32)
    p_conv = np.zeros((1, B, 3, 1536), np.float32)
    p_win = [np.zeros((1, B, 128 * d, 2, 4, 64), np.float32) for d in DILS]
    p_mem = np.zeros((1, B, 256, 2, 4, 256), np.float32)
    s_delta = np.zeros((1, DB, 4, 128, 128), np.float32)
    s_conv = np.zeros((1, DB, 3, 1536), np.float32)
    s_win = [np.zeros((1, DB, 128 * d, 2, 4, 64), np.float32) for d in DILS]
    for c in range(NCORES):
        r = res[c]
        b, j = c // 2, c % 2
        y_prompt[b, j * MAIN:(j + 1) * MAIN] = r["y_main"]
        sl = slice(c * NS, (c + 1) * NS)
        y_sample[sl] = r["y_smp"].reshape(NS, TS, D)
        s_delta[0, sl] = r["s_delta"]
        s_conv[0, sl] = r["s_conv"]
        for g in range(3):
            s_win[g][0, sl] = r[f"s_win{g}"].reshape(NS, 128 * DILS[g], 2, 4, 64)
        if j == 1:
            p_delta[0, b] = r["p_delta"]
            p_conv[0, b] = r["p_conv"]
            for g in range(3):
                p_win[g][0, b] = r[f"p_win{g}"].reshape(128 * DILS[g], 2, 4, 64)
            p_mem[0, b] = r["p_mem"].reshape(256, 2, 4, 256)
    return (y_prompt, y_sample, p_delta, p_conv, p_win[0], p_win[1], p_win[2], p_mem,
            s_delta, s_conv, s_win[0], s_win[1], s_win[2])


def kernel(**inputs):
    inp = {kk: np.asarray(v) for kk, v in inputs.items()}
    nc, k = build_nc()
    maps = shard_inputs(inp)
    res = run_bass_kernel_spmd(nc, maps, core_ids=list(range(NCORES)))
    return assemble(res.results)


def phase0_convert(k):
    nc, S = k.nc, k.S
    with ExitStack() as es:
        A = lambda name, shape, dt: k.track(TL(es.enter_context(nc.sbuf_tensor(name, shape, dt)), name))
        st = [A(f"cv_f{i}", [128, 8192], F32) for i in range(2)]
        oa = [A(f"cv_a{i}", [128, 4096], BF16) for i in range(2)]
        ob = [A(f"cv_b{i}", [128, 4096], BF16) for i in range(2)]
        n = 0
        for (src_, dst_) in ((k.expert_u, k.eu_bf), (k.expert_v, k.ev_bf)):
            for ps_ in range(EXPROWS // 1024):
                s_, a_, b_ = st[n % 2], oa[n % 2], ob[n % 2]
                r0 = ps_ * 1024
                S.dma("sp", s_, lambda e: e.dma_start(out=s_[:, :], in_=src_[r0:r0 + 1024, :].rearrange("(p j) d -> p (j d)", j=8)), w=[s_])
                S.op("act", lambda e: e.activation(out=a_[:, :], in_=s_[:, 0:4096], func=AF.Copy), r=[s_], w=[a_])
                S.op("dve", lambda e: e.tensor_copy(out=b_[:, :], in_=s_[:, 4096:8192]), r=[s_], w=[b_])
                dview = dst_[r0:r0 + 1024, :].rearrange("(p j) d -> p j d", j=8)
                S.dma("pool", a_, lambda e: e.dma_start(out=dview[:, 0:4, :], in_=a_[:, :].rearrange("p (j d) -> p j d", j=4)), r=[a_])
                S.dma("pool", b_, lambda e: e.dma_start(out=dview[:, 4:8, :], in_=b_[:, :].rearrange("p (j d) -> p j d", j=4)), r=[b_])
                n += 1
        k.end_phase()


def phase_mem_and_windows(k):
    nc, S = k.nc, k.S
    d2d = k.track(TL(None, "d2d"))
    for g in range(3):
        wb = 128 * DILS[g]
        for s_ in range(NS):
            S.dma("sp", d2d, lambda e: e.dma_start(out=k.s_win[g][s_, 0:wb - TS, :], in_=k.cwin[g][s_, TS:wb, :]))
            S.dma("sp", k.smp_kv, lambda e: e.dma_start(out=k.s_win[g][s_, wb - TS:wb, :], in_=k.smp_kv[s_ * TS:(s_ + 1) * TS, g, :]),
                  r=[k.smp_kv])
    with ExitStack() as es:
        A = lambda name, shape, dt: k.track(TL(es.enter_context(nc.sbuf_tensor(name, shape, dt)), name))
        P = lambda name, shape, dt: TL(es.enter_context(nc.psum_tensor(name, shape, dt)), name)
        wbf = load_weight_bf16(k, es, "w_mkv", k.w_mkv, D, 2048, gdram=k.g_memkv, col_chunk=2048)
        xt = [A(f"mxt{i}", [128, D], F32) for i in range(2)]
        sqj = A("msqj", [128, D], BF16)
        ssq = A("mssq", [128, 1], F32)
        rstd = A("mrstd", [128, 1], F32)
        ab = A("mab", [128, D], BF16)
        aT = A("maT", [128, 8, 256], BF16)
        ob = [A(f"mob{i}", [128, 512], F32) for i in range(2)]
        pT = P("mpT", [128, 8, 128], BF16)
        ps_ = [P(f"mps{i}", [128, 512], F32) for i in range(2)]
        for t in range(2):
            x_ = xt[t]
            S.dma("sp", x_, lambda e: e.dma_start(out=x_[:, :], in_=k.mem[t * 128:(t + 1) * 128, :]), w=[x_])
            S.op("act", lambda e: e.activation(out=sqj[:, :], in_=x_[:, :], func=AF.Square, accum_out=ssq[:, 0:1]), r=[x_], w=[sqj, ssq])
            rsqrt(S, rstd, rstd[:, :], ssq, ssq[:, :], 1.0 / D, EPS)
            S.op("act", lambda e: e.activation(out=ab[:, :], in_=x_[:, :], func=AF.Copy, scale=rstd[:, 0:1]), r=[x_, rstd], w=[ab])
            for c in range(8):
                S.op("pe", lambda e: e.transpose(out=pT[:, c, :], in_=ab[:, c * 128:(c + 1) * 128], identity=k.ident_bf[:, :]),
                     r=[ab, k.ident_bf], w=[pT])
            S.op("dve", lambda e: e.tensor_copy(out=aT[:, :, t * 128:(t + 1) * 128], in_=pT[:, :, :]), r=[pT], w=[aT])
        n = 0
        for t in range(2):
            for nb in range(4):
                ps = ps_[n % 2]
                o_ = ob[n % 2]
                for c in range(8):
                    S.op("pe", lambda e: e.matmul(ps[:, :], lhsT=aT[:, c, t * 128:(t + 1) * 128], rhs=wbf[:, c, nb * 512:(nb + 1) * 512],
                                                  start=(c == 0), stop=(c == 7)), r=[aT, wbf], w=[ps])
                evac(S, n, o_, o_[:, :], ps, ps[:, :])
                S.dma("sp", o_, lambda e: e.dma_start(out=k.p_mem[t * 128:(t + 1) * 128, nb * 512:(nb + 1) * 512], in_=o_[:, :]), r=[o_])
                n += 1
        k.end_phase()


def phase2a(k):
    import os
    nc, S = k.nc, k.S
    with ExitStack() as es:
        A = lambda name, shape, dt: k.track(TL(es.enter_context(nc.sbuf_tensor(name, shape, dt)), name))
        P = lambda name, shape, dt: TL(es.enter_context(nc.psum_tensor(name, shape, dt)), name)
        cst = k.cst
        ident, trit, blk, mTneg, mSpos, ones = (cst[n] for n in ("ident", "trit", "blk", "mTneg", "mSpos", "ones"))
        identb = k.ident_bf
        cbf = {}
        for n_ in ("trit", "blk", "mTneg", "mSpos"):
            cbf[n_] = A("cbf_" + n_, [128, 128], BF16)
            S.op("dve", lambda e: e.tensor_copy(out=cbf[n_][:, :], in_=cst[n_][:, :]), r=[cst[n_]], w=[cbf[n_]])
        tritb, blkb, mTnegb, mSposb = (cbf[n_] for n_ in ("trit", "blk", "mTneg", "mSpos"))
        onesb = k.ones_bf
        hones = [A(f"hones{c_}", [128, 128], BF16) for c_ in range(2)]
        for c_ in range(2):
            S.op("dve", lambda e: e.tensor_copy(out=hones[c_][:, :], in_=cst["blk"][:, 127 * c_:127 * c_ + 1].to_broadcast([128, 128])),
                 r=[cst["blk"]], w=[hones[c_]])
        ghl = A("ghl", [128, 8], BF16)
        Gall_h = A("Gall_h", [128, 4, 128], BF16)
        Gall_l = A("Gall_l", [128, 4, 128], BF16)
        gon = A("gon", [128, 128], F32)
        S.dma("sp", gon, lambda e: e.dma_start(out=gon[:, :], in_=k.g_onorm.ap().partition_broadcast(128)), w=[gon])
        qT = [A(f"qT{i}", [128, 4, 128], BF16) for i in range(2)]
        kT = [A(f"kT{i}", [128, 4, 128], BF16) for i in range(2)]
        vT = [A(f"vT{i}", [128, 4, 128], BF16) for i in range(2)]
        gbt = [A(f"gbt{i}", [128, 8], F32) for i in range(2)]
        zt = [A(f"zt{i}", [128, 512], BF16) for i in range(2)]
        gc = A("gc", [128, 4], F32); ngc = A("ngc", [128, 4], F32); egc = A("egc", [128, 4], F32)
        negegc = A("negegc", [128, 4], F32); ekd = A("ekd", [128, 4], F32); dl = A("dl", [128, 8], F32)
        dif = A("dif", [128, 4], F32)
        dlraw = A("dlraw", [128, 8], F32)
        Gall = A("Gall", [128, 4, 128], F32)
        egcb = [A(f"egcb{i}", [128, 128], F32) for i in range(2)]
        gamT = [A(f"gamT{i}", [128, 128], F32) for i in range(2)]
        gamS = [A(f"gamS{i}", [128, 128], F32) for i in range(2)]
        Lx = [A(f"Lx{i}", [128, 128], BF16) for i in range(3)]
        Ly = [A(f"Ly{i}", [128, 128], BF16) for i in range(3)]
        Rr = [A(f"Rr{i}", [128, 128], BF16) for i in range(2)]
        AqkT = [[A(f"AqkT{i}_{h}", [128, 128], BF16) for h in range(4)] for i in range(2)]
        qgT = [[A(f"qgT{i}_{h}", [128, 128], BF16) for h in range(4)] for i in range(2)]
        TbT = [[A(f"TbT{i}_{h}", [128, 128], BF16) for h in range(4)] for i in range(2)]
        kd = [[A(f"kd{i}_{h}", [128, 128], BF16) for h in range(4)] for i in range(2)]
        vtok = [[A(f"vtok{i}_{h}", [128, 128], F32) for h in range(4)] for i in range(2)]
        sc_neg = [A(f"scneg{i}", [128, 4], F32) for i in range(2)]
        sc_dl = [A(f"scdl{i}", [128, 8], F32) for i in range(2)]
        rt = [A(f"rt{h}", [128, 128], BF16) for h in range(4)]
        ut = [A(f"ut{h}", [128, 128], BF16) for h in range(4)]
        St = [A(f"St{h}", [128, 128], F32) for h in range(4)]
        Sb = [A(f"Sb{h}", [128, 128], BF16) for h in range(4)]
        ot = [A(f"ot{i}", [128, 512], F32) for i in range(2)]
        ssq = A("ossq", [128, 4], F32); orstd = A("orstd", [128, 4], F32)
        sqj = A("osqj", [128, 128], F32)
        szt = A("szt", [128, 512], F32)
        t1 = A("t1", [128, 512], F32)
        og = [A(f"og{i}", [128, 512], BF16) for i in range(2)]
        pb = PsumBlocks(nc, es, "p2a_")
        P = lambda name, dt, bank: pb.get(name, 128, dt, bank)
        pKS = P("pKS", F32, "scan"); pU = P("pU", F32, "scan"); pO = P("pO", F32, "scan"); pdS = P("pdS", F32, "scan")
        pA = [P(f"pA{i}", F32, f"g{i}") for i in range(2)]
        pB = [P(f"pB{i}", F32, f"g{i}") for i in range(2)]
        pC = [P(f"pC{i}", F32, f"g{i}") for i in range(2)]
        pKK = [P(f"pKK{i}", F32, f"k{i}") for i in range(2)]
        pQK = [P(f"pQK{i}", F32, f"k{i}") for i in range(2)]
        pX = P("pX", F32, "nX"); pY = P("pY", F32, "nY"); pP = P("pP", F32, "nP")
        pgt = pb.get("pgt", 16, F32, "k0")
        pLT = P("pLT", F32, "nY"); pkt = P("pkt", F32, "nP"); pvt = P("pvt", F32, "nY")
        def mm(ps, out_ap, lt, lap, rt_, rap, start=True, stop=True):
            S.op("pe", lambda e: e.matmul(out_ap, lhsT=lap, rhs=rap, start=start, stop=stop), r=[lt, rt_], w=[ps])

        def prep(i, q_, k_, v_, g_):
            S.op("dve", lambda e: e.tensor_copy(out=ghl[:, 0:4], in_=g_[:, 0:4]), r=[g_], w=[ghl])
            S.op("dve", lambda e: e.tensor_tensor(out=ghl[:, 4:8], in0=g_[:, 0:4], in1=ghl[:, 0:4], op=ALU.subtract), r=[g_, ghl], w=[ghl])
            for (c0, lt_, lap) in ((0, tritb, tritb[:, :]), (4, blkb, blkb[:, :])):
                mm(pgt, pgt[:, c0:c0 + 4], lt_, lap, ghl, ghl[:, 0:4], True, False)
                mm(pgt, pgt[:, c0:c0 + 4], lt_, lap, ghl, ghl[:, 4:8], False, True)
            for c_ in range(2):
                mm(pgt, pgt[:, 8 + 4 * c_:12 + 4 * c_], hones[c_], hones[c_][:, :], ghl, ghl[:, 0:4], True, False)
                mm(pgt, pgt[:, 8 + 4 * c_:12 + 4 * c_], hones[c_], hones[c_][:, :], ghl, ghl[:, 4:8], False, True)
            S.op("dve", lambda e: e.tensor_copy(out=gc[:, :], in_=pgt[:, 0:4]), r=[pgt], w=[gc])
            S.op("dve", lambda e: e.tensor_scalar(out=ngc[:, :], in0=gc[:, :], scalar1=-1.0, scalar2=None, op0=ALU.mult), r=[gc], w=[ngc])
            S.op("act", lambda e: e.activation(out=egc[:, :], in_=gc[:, :], func=AF.Exp), r=[gc], w=[egc])
            S.op("dve", lambda e: e.tensor_scalar(out=sc_neg[i][:, :], in0=egc[:, :], scalar1=-1.0, scalar2=None, op0=ALU.mult),
                 r=[egc], w=[sc_neg[i]])
            S.op("dve", lambda e: e.tensor_tensor(out=dif[:, :], in0=pgt[:, 4:8], in1=gc[:, :], op=ALU.subtract), r=[pgt, gc], w=[dif])
            S.op("act", lambda e: e.activation(out=ekd[:, :], in_=dif[:, :], func=AF.Exp), r=[dif], w=[ekd])
            S.op("dve", lambda e: e.tensor_copy(out=dlraw[:, :], in_=pgt[:, 8:16]), r=[pgt], w=[dlraw])
            S.op("act", lambda e: e.activation(out=sc_dl[i][:, :], in_=dlraw[:, :], func=AF.Exp), r=[dlraw], w=[sc_dl[i]])
            stg = int(os.environ.get("P2PREP", "9"))
            if stg < 1:
                return
            for h in range(4):
                S.op("dve", lambda e: e.tensor_copy(out=Gall_h[:, h, :], in_=ghl[:, h:h + 1].to_broadcast([128, 128])), r=[ghl], w=[Gall_h])
                S.op("dve", lambda e: e.tensor_copy(out=Gall_l[:, h, :], in_=ghl[:, 4 + h:5 + h].to_broadcast([128, 128])), r=[ghl], w=[Gall_l])
            for h in range(4):
                j = h % 2
                if stg < 2:
                    continue
                mm(pA[j], pA[j][:, :], Gall_h, Gall_h[:, h, :], tritb, tritb[:, :], True, False)
                mm(pA[j], pA[j][:, :], Gall_l, Gall_l[:, h, :], tritb, tritb[:, :], False, True)
                mm(pB[j], pB[j][:, :], Gall_h, Gall_h[:, h, :], tritb, tritb[:, :], True, False)
                mm(pB[j], pB[j][:, :], Gall_l, Gall_l[:, h, :], tritb, tritb[:, :], False, False)
                mm(pB[j], pB[j][:, :], identb, identb[:, :], mTnegb, mTnegb[:, :], False, True)
                mm(pC[j], pC[j][:, :], Gall_h, Gall_h[:, h, :], tritb, tritb[:, :], True, False)
                mm(pC[j], pC[j][:, :], Gall_l, Gall_l[:, h, :], tritb, tritb[:, :], False, False)
                mm(pC[j], pC[j][:, :], identb, identb[:, :], mSposb, mSposb[:, :], False, True)
                S.op("act", lambda e: e.activation(out=egcb[j][:, :], in_=pA[j][:, :], func=AF.Exp), r=[pA[j]], w=[egcb[j]])
                S.op("act", lambda e: e.activation(out=gamT[j][:, :], in_=pB[j][:, :], func=AF.Exp, bias=ngc[:, h:h + 1]),
                     r=[pB[j], ngc], w=[gamT[j]])
                S.op("act", lambda e: e.activation(out=gamS[j][:, :], in_=pC[j][:, :], func=AF.Exp, bias=gc[:, h:h + 1], scale=-1.0),
                     r=[pC[j], gc], w=[gamS[j]])
                if stg < 3:
                    continue
                mm(pKK[j], pKK[j][:, :], k_, k_[:, h, :], k_, k_[:, h, :])
                mm(pQK[j], pQK[j][:, :], k_, k_[:, h, :], q_, q_[:, h, :])
                X, Y = Lx[0], Ly[0]
                S.op("dve", lambda e: e.scalar_tensor_tensor(out=X[:, :], in0=pKK[j][:, :], scalar=g_[:, 4 + h:5 + h], in1=gamS[j][:, :],
                                                             op0=ALU.mult, op1=ALU.mult), r=[pKK[j], g_, gamS[j]], w=[X])
                S.op("dve", lambda e: e.tensor_tensor(out=AqkT[i][h][:, :], in0=pQK[j][:, :], in1=gamT[j][:, :], op=ALU.mult),
                     r=[pQK[j], gamT[j]], w=[AqkT[i][h]])
                S.op("pool", lambda e: e.tensor_tensor(out=qgT[i][h][:, :], in0=q_[:, h, :], in1=egcb[j][:, :], op=ALU.mult),
                     r=[q_, egcb[j]], w=[qgT[i][h]])
                if stg < 4:
                    continue
                exp_ = os.environ.get("P2EXP", "")
                if exp_ == "A":
                    mm(pLT, pLT[:, :], identb, identb[:, :], identb, identb[:, :])
                else:
                    mm(pLT, pLT[:, :], X, X[:, :], identb, identb[:, :])
                if exp_ == "D":
                    S.op("act", lambda e: e.activation(out=Y[:, :], in_=pLT[:, :], func=AF.Copy), r=[pLT], w=[Y])
                elif exp_ != "B":
                    S.op("dve", lambda e: e.tensor_copy(out=Y[:, :], in_=pLT[:, :]), r=[pLT], w=[Y])
                sub = os.environ.get("P2SUB", "z")
                if sub == "a":
                    continue
                R = Rr[0]
                S.op("pool", lambda e: e.tensor_tensor(out=R[:, :], in0=identb[:, :], in1=Y[:, :], op=ALU.subtract), r=[identb, Y], w=[R])
                if sub == "b":
                    continue
                for it in range(1, 6):
                    Xn, Yn = Lx[it % 3], Ly[it % 3]
                    mm(pX, pX[:, :], Y, Y[:, :], X, X[:, :])
                    if sub == "c":
                        break
                    if it < 5:
                        mm(pY, pY[:, :], X, X[:, :], Y, Y[:, :])
                    S.op("act", lambda e: e.activation(out=Xn[:, :], in_=pX[:, :], func=AF.Copy), r=[pX], w=[Xn])
                    if it < 5:
                        S.op("dve", lambda e: e.tensor_copy(out=Yn[:, :], in_=pY[:, :]), r=[pY], w=[Yn])
                    mm(pP, pP[:, :], Xn, Xn[:, :], R, R[:, :])
                    Rn = Rr[it % 2]
                    S.op("dve", lambda e: e.tensor_tensor(out=Rn[:, :], in0=pP[:, :], in1=R[:, :], op=ALU.add), r=[pP, R], w=[Rn])
                    X, Y, R = Xn, Yn, Rn
                if stg < 5:
                    continue
                S.op("pool", lambda e: e.tensor_scalar(out=TbT[i][h][:, :], in0=R[:, :], scalar1=g_[:, 4 + h:5 + h], scalar2=None, op0=ALU.mult),
                     r=[R, g_], w=[TbT[i][h]])
                mm(pkt, pkt[:, :], k_, k_[:, h, :], identb, identb[:, :])
                S.op("dve", lambda e: e.tensor_scalar(out=kd[i][h][:, :], in0=pkt[:, :], scalar1=ekd[:, h:h + 1], scalar2=None, op0=ALU.mult),
                     r=[pkt, ekd], w=[kd[i][h]])
                mm(pvt, pvt[:, :], v_, v_[:, h, :], identb, identb[:, :])
                S.op("dve", lambda e: e.tensor_copy(out=vtok[i][h][:, :], in_=pvt[:, :]), r=[pvt], w=[vtok[i][h]])

        def scan(i, k_, o_, pre=None, post=None):
            for c in range(2):
                ps_ = slice(64 * c, 64 * c + 64)
                if pre:
                    pre(c)
                for h in range(4):
                    mm(pKS, pKS[:, :], k_, k_[:, h, :], Sb[h], Sb[h][:, :])
                    S.op("dve", lambda e: e.scalar_tensor_tensor(out=rt[h][ps_, :], in0=pKS[ps_, :], scalar=sc_neg[i][ps_, h:h + 1],
                                                                 in1=vtok[i][h][ps_, :], op0=ALU.mult, op1=ALU.add),
                         r=[pKS, sc_neg[i], vtok[i][h]], w=[rt[h]])
                    mm(pU, pU[:, :], TbT[i][h], TbT[i][h][ps_, :], rt[h], rt[h][ps_, :])
                    S.op("act", lambda e: e.activation(out=ut[h][ps_, :], in_=pU[ps_, :], func=AF.Copy), r=[pU], w=[ut[h]])
                    mm(pO, pO[:, :], qgT[i][h], qgT[i][h][:, :], Sb[h], Sb[h][:, :], True, False)
                    mm(pO, pO[:, :], AqkT[i][h], AqkT[i][h][ps_, :], ut[h], ut[h][ps_, :], False, True)
                    S.op("act", lambda e: e.activation(out=o_[ps_, h * 128:(h + 1) * 128], in_=pO[ps_, :], func=AF.Copy), r=[pO], w=[o_])
                    mm(pdS, pdS[:, :], kd[i][h], kd[i][h][ps_, :], ut[h], ut[h][ps_, :])
                    S.op("dve", lambda e: e.scalar_tensor_tensor(out=St[h][:, :], in0=St[h][:, :], scalar=sc_dl[i][:, 4 * c + h:4 * c + h + 1],
                                                                 in1=pdS[:, :], op0=ALU.mult, op1=ALU.add),
                         r=[St[h], sc_dl[i], pdS], w=[St[h]])
                    S.op("act", lambda e: e.activation(out=Sb[h][:, :], in_=St[h][:, :], func=AF.Copy), r=[St[h]], w=[Sb[h]])
                if post:
                    post(c)

        def post_out(o_, z_, j, dst_fn):
            for h in range(4):
                S.op("act", lambda e: e.activation(out=sqj[:, :], in_=o_[:, h * 128:(h + 1) * 128], func=AF.Square, accum_out=ssq[:, h:h + 1]),
                     r=[o_], w=[sqj, ssq])
            rsqrt(S, orstd, orstd[:, :], ssq, ssq[:, :], 1.0 / 128, EPS)
            S.op("act", lambda e: e.activation(out=szt[:, :], in_=z_[:, :], func=AF.Silu), r=[z_], w=[szt])
            for h in range(4):
                S.op("dve", lambda e: e.scalar_tensor_tensor(out=t1[:, h * 128:(h + 1) * 128], in0=o_[:, h * 128:(h + 1) * 128],
                                                             scalar=orstd[:, h:h + 1], in1=gon[:, :], op0=ALU.mult, op1=ALU.mult),
                     r=[o_, orstd, gon], w=[t1])
            S.op("dve", lambda e: e.tensor_tensor(out=og[j][:, :], in0=t1[:, :], in1=szt[:, :], op=ALU.mult), r=[t1, szt], w=[og[j]])
            dst_fn(og[j])

        for h in range(4):
            S.op("dve", lambda e: e.memset(St[h][:, :], 0.0), w=[St[h]])
            S.op("dve", lambda e: e.memset(Sb[h][:, :], 0.0), w=[Sb[h]])
        ntile = EXT // 128
        tiles = range(int(os.environ.get("P2START", "0")), int(os.environ.get("P2TILES", str(ntile))))
        for tau in tiles:
            i = tau % 2
            t0 = tau * 128
            for h in range(4):
                S.dma("sp", qT[i], lambda e: e.dma_start(out=qT[i][:, h, :], in_=k.dnq[h, :, t0:t0 + 128]), w=[qT[i]])
                S.dma("sp", kT[i], lambda e: e.dma_start(out=kT[i][:, h, :], in_=k.dnk[h, :, t0:t0 + 128]), w=[kT[i]])
                S.dma("sp", vT[i], lambda e: e.dma_start(out=vT[i][:, h, :], in_=k.dnv[h, :, t0:t0 + 128]), w=[vT[i]])
            S.dma("sp", gbt[i], lambda e: e.dma_start(out=gbt[i][:, :], in_=k.gbs[t0:t0 + 128, :]), w=[gbt[i]])
            main = t0 >= PRE
            if main:
                S.dma("sp", zt[i], lambda e: e.dma_start(out=zt[i][:, :], in_=k.zs[t0 - PRE:t0 - PRE + 128, :]), w=[zt[i]])
            mode = os.environ.get("P2MODE", "all")
            if mode == "load":
                continue
            prep(i, qT[i], kT[i], vT[i], gbt[i])
            if mode == "prep":
                continue
            scan(i, kT[i], ot[i])
            if main:
                post_out(ot[i], zt[i], i,
                         lambda o_: S.dma("sp", o_, lambda e: e.dma_start(out=k.cat[t0 - PRE:t0 - PRE + 128, :], in_=o_[:, :]), r=[o_]))
        for h in range(4):
            S.dma("sp", St[h], lambda e: e.dma_start(out=k.p_delta[h, :, :], in_=St[h][:, :]), r=[St[h]])

        for tb in range(2 if os.environ.get("P2MODE", "all") == "all" else 0):
            i = tb
            for tt in (qT[i], kT[i], vT[i]):
                S.op("pool", lambda e: e.memset(tt[:, :, :], 0.0), w=[tt])
            S.op("pool", lambda e: e.memset(gbt[i][:, :], 0.0), w=[gbt[i]])
            S.op("pool", lambda e: e.memset(zt[i][:, :], 0.0), w=[zt[i]])
            for c in range(2):
                sq = tb * 2 + c
                for h in range(4):
                    for which, tt in enumerate((qT[i], kT[i], vT[i])):
                        S.op("pool", lambda e: e.tensor_copy(out=tt[:, h, 64 * c:64 * c + TS], in_=k.smp_dn[:, which * 4 + h, sq * TS:(sq + 1) * TS]),
                             r=[k.smp_dn], w=[tt])
                S.dma("sp", gbt[i], lambda e: e.dma_start(out=gbt[i][64 * c:64 * c + TS, :], in_=k.smp_gb[sq * TS:(sq + 1) * TS, :]),
                      r=[k.smp_gb], w=[gbt[i]])
                S.dma("sp", zt[i], lambda e: e.dma_start(out=zt[i][64 * c:64 * c + TS, :], in_=k.smp_z[sq * TS:(sq + 1) * TS, :]),
                      r=[k.smp_z], w=[zt[i]])
            prep(i, qT[i], kT[i], vT[i], gbt[i])

            def pre(c, tb=tb):
                sq = tb * 2 + c
                for h in range(4):
                    S.dma("sp", St[h], lambda e: e.dma_start(out=St[h][:, :], in_=k.st_delta[sq, h, :, :]), w=[St[h]])
                    S.op("act", lambda e: e.activation(out=Sb[h][:, :], in_=St[h][:, :], func=AF.Copy), r=[St[h]], w=[Sb[h]])

            def post(c, tb=tb):
                sq = tb * 2 + c
                for h in range(4):
                    S.dma("sp", St[h], lambda e: e.dma_start(out=k.s_delta[sq, h, :, :], in_=St[h][:, :]), r=[St[h]])

            scan(i, kT[i], ot[i], pre, post)
            if tb == 0 and os.environ.get("P2DBG"):
                col = 0
                dbgt = TL(None, "dbgsem")
                for tt, wdt in ((TbT[0][0], 128), (AqkT[0][0], 128), (kd[0][0], 128), (qgT[0][0], 128), (vtok[0][0], 128),
                                (sc_neg[0], 4), (sc_dl[0], 8), (ot[0], 512), (rt[0], 128), (ut[0], 128), (St[0], 128), (gbt[0], 8)):
                    S.dma("pool", TL(None, f"dbg{col}"), lambda e: e.dma_start(out=k.dbg[:, col:col + wdt], in_=tt[:, 0:wdt]), r=[tt])
                    col += wdt

            def dst(o_, tb=tb):
                for c in range(2):
                    sq = tb * 2 + c
                    S.dma("sp", o_, lambda e: e.dma_start(out=k.smp_cat[sq * TS:(sq + 1) * TS, :], in_=o_[64 * c:64 * c + TS, :]),
                          r=[o_], w=[k.smp_cat])
            post_out(ot[i], zt[i], i, dst)
        k.end_phase()


def phase2b(k):
    import os
    nc, S = k.nc, k.S
    with ExitStack() as es:
        A = lambda name, shape, dt: k.track(TL(es.enter_context(nc.sbuf_tensor(name, shape, dt)), name))
        pb = PsumBlocks(nc, es, "p2b_")
        identb, onesb = k.ident_bf, k.ones_bf
        mprev = A("mprev", [128, 128], BF16); mcur = A("mcur", [128, 128], BF16); mhalo = A("mhalo", [128, 128], BF16)
        halo_f = A("halo_f", [128, 128], F32)
        S.dma("sp", halo_f, lambda e: e.dma_start(out=halo_f[:, :], in_=k.halo[:, :]), w=[halo_f])
        S.op("dve", lambda e: e.tensor_copy(out=mprev[:, :], in_=k.cst["swprev"][:, :]), r=[k.cst["swprev"]], w=[mprev])
        S.op("dve", lambda e: e.tensor_copy(out=mcur[:, :], in_=k.cst["swcur"][:, :]), r=[k.cst["swcur"]], w=[mcur])
        S.op("dve", lambda e: e.tensor_copy(out=mhalo[:, :], in_=halo_f[:, :]), r=[halo_f], w=[mhalo])
        QT = [A(f"QT{i}", [128, 2, 2048], BF16) for i in range(2)]
        KT = [A(f"KT{i}", [128, 2, 4096], BF16) for i in range(2)]
        Vb = [A(f"Vb{i}", [128, 2, 256], BF16) for i in range(2)]
        PT = [A(f"PT{i}", [128, 256], BF16) for i in range(2)]
        ot = [A(f"swot{i}", [128, 260], F32) for i in range(2)]
        psS = [pb.get(f"psS{i}", 256, F32, f"s{i}") for i in range(2)]
        psO = [pb.get(f"psO{i}", 128, F32, f"o{i}") for i in range(2)]
        def core(qf, kf, vf, msk0, o_, deps):
            qt_, kt_, vt_ = deps
            for head in range(4):
                pair, hh = head // 2, head % 2
                ph = slice(64 * hh, 64 * hh + 64)
                sS, sO, p_ = psS[head % 2], psO[head % 2], PT[head % 2]
                for kc in range(2):
                    msk = msk0 if kc == 0 else mcur
                    S.op("pe", lambda e: e.matmul(sS[:, kc * 128:(kc + 1) * 128], lhsT=identb[:, :], rhs=msk[:, :], start=True, stop=False),
                         r=[identb, msk], w=[sS])
                    S.op("pe", lambda e: e.matmul(sS[:, kc * 128:(kc + 1) * 128], lhsT=kf(kc, pair, ph), rhs=qf(pair, ph), start=False, stop=True),
                         r=[kt_, qt_], w=[sS])
                S.op("act", lambda e: e.activation(out=p_[:, :], in_=sS[:, :], func=AF.Exp, scale=0.125), r=[sS], w=[p_])
                for kc in range(2):
                    S.op("pe", lambda e: e.matmul(sO[:, 0:64], lhsT=p_[:, kc * 128:(kc + 1) * 128], rhs=vf(kc, head),
                                                  start=(kc == 0), stop=(kc == 1)), r=[p_, vt_], w=[sO])
                for kc in range(2):
                    S.op("pe", lambda e: e.matmul(sO[:, 64:65], lhsT=p_[:, kc * 128:(kc + 1) * 128], rhs=onesb[:, 0:1],
                                                  start=(kc == 0), stop=(kc == 1)), r=[p_, onesb], w=[sO])
                S.op("dve", lambda e: e.tensor_copy(out=o_[:, head * 65:(head + 1) * 65], in_=sO[:, 0:65]), r=[sO], w=[o_])

        groups = [int(x) for x in os.environ.get("P2BG", "0,1,2").split(",")]
        nblk_lim = int(os.environ.get("P2BN", "999"))
        u = 0
        for g in groups:
            d = DILS[g]
            span = 128 * d
            for n in range(min(MAIN // span, nblk_lim)):
                bi = (g * 64 + n) % 2
                q_, k_ = QT[bi], KT[bi]
                for pair in range(2):
                    S.dma("sp", q_, lambda e: e.dma_start(out=q_[:, pair, 0:span], in_=k.qts[g][pair, :, n * span:(n + 1) * span]), w=[q_])
                    k0 = HALO + (n - 1) * span
                    S.dma("sp", k_, lambda e: e.dma_start(out=k_[:, pair, 0:2 * span], in_=k.kts[g][pair, :, k0:k0 + 2 * span]), w=[k_])
                for r in range(d):
                    v_ = Vb[u % 2]
                    o_ = ot[u % 2]
                    v0 = HALO + (n - 1) * span + r
                    for kc in range(2):
                        S.dma("sp", v_, lambda e: e.dma_start(out=v_[:, kc, :],
                                                             in_=k.vss[g][v0 + kc * span:v0 + kc * span + 127 * d + 1:d, :]), w=[v_])
                    core(lambda pair, ph: q_[ph, pair, r:span:d],
                         lambda kc, pair, ph: k_[ph, pair, kc * span + r:(kc + 1) * span:d],
                         lambda kc, head: v_[:, kc, head * 64:(head + 1) * 64],
                         mhalo if n == 0 else mprev, o_, (q_, k_, v_))
                    t0 = n * span + r
                    S.dma("sp", o_, lambda e: e.dma_start(out=k.swo[g][t0:t0 + 127 * d + 1:d, :], in_=o_[:, :]), r=[o_])
                    u += 1
        if not os.environ.get("P2BNOSMP"):
            sq = A("sq", [128, 2, 128], BF16); sk = A("sk", [128, 2, 256], BF16); sv = A("sv", [128, 2, 256], BF16)
            ck = A("ck", [128, 512], F32); ckb = A("ckb", [128, 512], BF16); vst = A("vst", [128, 256], F32)
            pt = pb.get("ptr", 128, F32, "ptr")
            for s_ in range(NS):
                for g in range(3):
                    d = DILS[g]
                    for r in range(1 if d == 1 else TS):
                        nq = TS if d == 1 else 1
                        tok0 = s_ * TS + (0 if d == 1 else r)
                        o_ = ot[u % 2]
                        for tt in (sq, sk, sv):
                            S.op("pool", lambda e: e.memset(tt[:, :, :], 0.0), w=[tt])
                        S.op("pool", lambda e: e.memset(vst[:, :], 0.0), w=[vst])
                        S.dma("sp", ck, lambda e: e.dma_start(out=ck[:, :], in_=k.cwin[g][s_, r:r + 127 * d + 1:d, :]), w=[ck])
                        S.op("dve", lambda e: e.tensor_copy(out=ckb[:, :], in_=ck[:, :]), r=[ck], w=[ckb])
                        for pair in range(2):
                            S.op("pool", lambda e: e.tensor_copy(out=sq[:, pair, 0:nq], in_=k.smp_qk[:, g, 0, pair, tok0:tok0 + nq]), r=[k.smp_qk], w=[sq])
                            S.op("pool", lambda e: e.tensor_copy(out=sk[:, pair, 128:128 + nq], in_=k.smp_qk[:, g, 1, pair, tok0:tok0 + nq]),
                                 r=[k.smp_qk], w=[sk])
                            S.op("pe", lambda e: e.matmul(pt[:, :], lhsT=ckb[:, pair * 128:(pair + 1) * 128], rhs=identb[:, :], start=True, stop=True),
                                 r=[ckb, identb], w=[pt])
                            S.op("dve", lambda e: e.tensor_copy(out=sk[:, pair, 0:128], in_=pt[:, :]), r=[pt], w=[sk])
                        S.op("pool", lambda e: e.tensor_copy(out=sv[:, 0, :], in_=ckb[:, 256:512]), r=[ckb], w=[sv])
                        S.dma("sp", vst, lambda e: e.dma_start(out=vst[0:nq, :], in_=k.smp_kv[tok0:tok0 + nq, g, 256:512]), r=[k.smp_kv], w=[vst])
                        S.op("dve", lambda e: e.tensor_copy(out=sv[:, 1, :], in_=vst[:, :]), r=[vst], w=[sv])
                        core(lambda pair, ph: sq[ph, pair, :], lambda kc, pair, ph: sk[ph, pair, kc * 128:(kc + 1) * 128],
                             lambda kc, head: sv[:, kc, head * 64:(head + 1) * 64], mprev, o_, (sq, sk, sv))
                        S.dma("sp", o_, lambda e: e.dma_start(out=k.smp_swo[tok0:tok0 + nq, g, :], in_=o_[0:nq, :]), r=[o_], w=[k.smp_swo])
                        u += 1
        k.end_phase()


def phase3(k):
    import os
    nc, S = k.nc, k.S
    NT = NS * TS
    with ExitStack() as es:
        A = lambda name, shape, dt: k.track(TL(es.enter_context(nc.sbuf_tensor(name, shape, dt)), name))
        pb = PsumBlocks(nc, es, "p3_")
        identb, onesb = k.ident_bf, k.ones_bf
        iota = k.cst["iota"]
        KmT = A("KmT", [128, 8, 256], BF16); Vm = A("Vm", [128, 2, 1024], BF16)
        KsT = A("KsT", [128, NS, 8, 256], BF16); Vs = A("Vs", [128, NS, 2, 1024], BF16)
        pbig = [pb.get(f"pbig{i}", 512, F32, f"big{i}") for i in range(4)]
        with ExitStack() as es2:
            A2 = lambda name, shape, dt: k.track(TL(es2.enter_context(nc.sbuf_tensor(name, shape, dt)), name))
            wkv = load_weight_bf16(k, es2, "w_mkv3", k.w_mkv, D, 2048, gdram=k.g_memkv, col_chunk=2048)
            mx = A2("m3x", [128, D], F32); msq = A2("m3sq", [128, D], BF16); mss = A2("m3ss", [128, 1], F32)
            mrs = A2("m3rs", [128, 1], F32); mab = A2("m3ab", [128, D], BF16); maT = A2("m3aT", [128, 8, 256], BF16)
            cs = A2("m3cs", [128, 2048], F32); csb = A2("m3csb", [128, 2048], BF16)
            for t in range(2):
                S.dma("sp", mx, lambda e: e.dma_start(out=mx[:, :], in_=k.mem[t * 128:(t + 1) * 128, :]), w=[mx])
                S.op("act", lambda e: e.activation(out=msq[:, :], in_=mx[:, :], func=AF.Square, accum_out=mss[:, 0:1]), r=[mx], w=[msq, mss])
                rsqrt(S, mrs, mrs[:, :], mss, mss[:, :], 1.0 / D, EPS)
                S.op("act", lambda e: e.activation(out=mab[:, :], in_=mx[:, :], func=AF.Copy, scale=mrs[:, 0:1]), r=[mx, mrs], w=[mab])
                for half in range(2):
                    for c in range(4):
                        cc = half * 4 + c
                        S.op("pe", lambda e: e.matmul(pbig[0][:, c * 128:(c + 1) * 128], lhsT=mab[:, cc * 128:(cc + 1) * 128], rhs=identb[:, :],
                                                      start=True, stop=True), r=[mab, identb], w=[pbig[0]])
                    for c in range(4):
                        cc = half * 4 + c
                        S.op("dve", lambda e: e.tensor_copy(out=maT[:, cc, t * 128:(t + 1) * 128], in_=pbig[0][:, c * 128:(c + 1) * 128]),
                             r=[pbig[0]], w=[maT])
            for hc in range(8):
                for c in range(8):
                    S.op("pe", lambda e: e.matmul(pbig[1][:, 0:256], lhsT=wkv[:, c, hc * 128:(hc + 1) * 128], rhs=maT[:, c, :],
                                                  start=(c == 0), stop=(c == 7)), r=[wkv, maT], w=[pbig[1]])
                S.op("dve", lambda e: e.tensor_copy(out=KmT[:, hc, :], in_=pbig[1][:, 0:256]), r=[pbig[1]], w=[KmT])
            for kc in range(2):
                for nb in range(2):
                    for c in range(8):
                        S.op("pe", lambda e: e.matmul(pbig[2][:, :], lhsT=maT[:, c, kc * 128:(kc + 1) * 128], rhs=wkv[:, c, 1024 + nb * 512:1024 + (nb + 1) * 512],
                                                      start=(c == 0), stop=(c == 7)), r=[wkv, maT], w=[pbig[2]])
                    S.op("dve", lambda e: e.tensor_copy(out=Vm[:, kc, nb * 512:(nb + 1) * 512], in_=pbig[2][:, :]), r=[pbig[2]], w=[Vm])
            for s_ in range(NS):
                for kc in range(2):
                    S.dma("sp", cs, lambda e: e.dma_start(out=cs[:, :], in_=k.cmem[s_, kc * 128:(kc + 1) * 128, :]), w=[cs])
                    S.op("dve", lambda e: e.tensor_copy(out=csb[:, :], in_=cs[:, :]), r=[cs], w=[csb])
                    S.op("pool", lambda e: e.tensor_copy(out=Vs[:, s_, kc, :], in_=csb[:, 1024:2048]), r=[csb], w=[Vs])
                    for half in range(2):
                        for c in range(4):
                            hc = half * 4 + c
                            S.op("pe", lambda e: e.matmul(pbig[3][:, c * 128:(c + 1) * 128], lhsT=csb[:, hc * 128:(hc + 1) * 128], rhs=identb[:, :],
                                                          start=True, stop=True), r=[csb, identb], w=[pbig[3]])
                        for c in range(4):
                            hc = half * 4 + c
                            S.op("dve", lambda e: e.tensor_copy(out=KsT[:, s_, hc, kc * 128:(kc + 1) * 128], in_=pbig[3][:, c * 128:(c + 1) * 128]),
                                 r=[pbig[3]], w=[KsT])
            k.S.barrier()
        wout = load_weight_bf16(k, es, "w_out3", k.w_out, 768, D, col_chunk=1024)
        wmq = load_weight_bf16(k, es, "w_mq3", k.w_mq, D, D, gdram=k.g_memq, col_chunk=1024)
        wmo = load_weight_bf16(k, es, "w_mo3", k.w_mo, D, D, col_chunk=1024)
        wpq = load_weight_bf16(k, es, "w_pq3", k.w_pq, D, 2048, col_chunk=2048)
        skT = A("skT", [128, 16, 128], BF16)
        with ExitStack() as es2:
            A2 = lambda name, shape, dt: k.track(TL(es2.enter_context(nc.sbuf_tensor(name, shape, dt)), name))
            skf = A2("skf", [128, 128], F32); skb = A2("skb", [128, 128], BF16)
            for hp in range(16):
                S.dma("sp", skf, lambda e: e.dma_start(out=skf[:, :], in_=k.sub_keys[hp, :, :]), w=[skf])
                S.op("dve", lambda e: e.tensor_copy(out=skb[:, :], in_=skf[:, :]), r=[skf], w=[skb])
                S.op("pe", lambda e: e.matmul(pbig[0][:, 0:128], lhsT=skb[:, :], rhs=identb[:, :], start=True, stop=True), r=[skb, identb], w=[pbig[0]])
                S.op("dve", lambda e: e.tensor_copy(out=skT[:, hp, :], in_=pbig[0][:, 0:128]), r=[pbig[0]], w=[skT])
            k.S.barrier()
        gffn = A("gffn", [128, D], F32); gfin = A("gfin", [128, D], F32)
        S.dma("sp", gffn, lambda e: e.dma_start(out=gffn[:, :], in_=k.g_ffn.ap().partition_broadcast(128)), w=[gffn])
        S.dma("sp", gfin, lambda e: e.dma_start(out=gfin[:, :], in_=k.g_final.ap().partition_broadcast(128)), w=[gfin])
        LOHI_INIT = True
        xt = A("x3", [128, D], F32); cat = A("cat3", [128, 768], BF16); sw = [A(f"sw3_{g}", [128, 260], F32) for g in range(3)]
        rden = A("rden3", [128, 4], F32); catT = A("catT3", [128, 6, 128], BF16)
        h = A("h3", [128, D], F32); sqj = A("sqj3", [128, D], BF16); ssq = A("ssq3", [128, 1], F32); rstd = A("rstd3", [128, 1], F32)
        cb = A("cb3", [128, D], BF16); cT = A("cT3", [128, 8, 128], BF16)
        qmT = A("qmT3", [128, 8, 128], BF16); PTm = A("PTm3", [128, 256], BF16); rdm = A("rdm3", [128, 128], F32)
        attT = A("attT3", [128, 8, 128], BF16)
        fb = A("fb3", [128, D], BF16); fT = A("fT3", [128, 8, 128], BF16)
        qpT = A("qpT3", [128, 16, 128], BF16); sc = A("sc3", [128, 16, 128], F32); sc2 = A("sc23", [128, 128], F32)
        mv = A("mv3", [128, 16, 16], F32); mi = A("mi3", [128, 16, 16], U32); mif = A("mif3", [128, 16, 16], F32)
        cand = A("cand3", [128, 256], F32); cand2 = A("cand23", [128, 256], F32)
        cv = A("cv3", [128, 8, 16], F32); ci = A("ci3", [128, 8, 16], U32); cif = A("cif3", [128, 8, 16], F32)
        ia = A("ia3", [128, 8, 16], F32); ib = A("ib3", [128, 8, 16], F32)
        oh = A("oh3", [128, 16, 16], F32); lo16 = A("lo163", [128, 16], F32); hi16 = A("hi163", [128, 16], F32); i1 = A("i13", [128, 8, 16], F32); i2 = A("i23", [128, 8, 16], F32)
        eidf = A("eidf3", [128, 128], F32); eid = A("eid3", [128, 128], I32)
        gate = A("gate3", [128, 8, 16], F32); gsum = A("gsum3", [128, 8], F32)
        hid = A("hid3", [128, 128], F32); hx = A("hx3", [128, 128], F32); wgt = A("wgt3", [128, 128], F32)
        NB = 2
        Gu = [A(f"Gu{i}", [128, D], BF16) for i in range(NB)]; Gv = [A(f"Gv{i}", [128, D], BF16) for i in range(NB)]
        flat = lambda t_, a0, a1: (lambda: t_[:, a0:a1, :].rearrange("p a b -> p (a b)"))
        Gu = Gu + [TLview(qpT, flat(qpT, 0, 8), "GuV0"), TLview(qpT, flat(qpT, 8, 16), "GuV1"), TLview(cT, flat(cT, 0, 8), "GuV2")]
        Gv = Gv + [TLview(qmT, flat(qmT, 0, 8), "GvV0"), TLview(attT, flat(attT, 0, 8), "GvV1"), TLview(fT, flat(fT, 0, 8), "GvV2")]
        NBU, NBV = len(Gu), len(Gv)
        par = lambda t_: [t_.p] if isinstance(t_, TLview) else []
        prod = [sqj, cb]
        junk = sqj; dg = [A(f"dg{i}", [128, 128], BF16) for i in range(2)]
        yo = xt
        pout = [pb.get(f"pout{i}", 512, F32, f"out{i}") for i in range(2)]
        psm = pb.get("psm", 256, F32, "sm"); pden = pb.get("pden", 128, F32, "den")

        S.op("dve", lambda e: e.tensor_scalar(out=lo16[:, :], in0=iota[:, 0:16], scalar1=16.0, scalar2=None, op0=ALU.mult), r=[iota], w=[lo16])
        S.op("dve", lambda e: e.tensor_scalar(out=hi16[:, :], in0=iota[:, 0:16], scalar1=16.0, scalar2=16.0, op0=ALU.mult, op1=ALU.add), r=[iota], w=[hi16])

        def transposes(src_t, nchunk, dstT):
            for c0 in range(0, nchunk, 4):
                n_ = min(4, nchunk - c0)
                for c in range(n_):
                    S.op("pe", lambda e: e.matmul(pbig[0][:, c * 128:(c + 1) * 128], lhsT=src_t[:, (c0 + c) * 128:(c0 + c + 1) * 128], rhs=identb[:, :],
                                                  start=True, stop=True), r=[src_t, identb], w=[pbig[0]])
                for c in range(n_):
                    S.op("dve", lambda e: e.tensor_copy(out=dstT[:, c0 + c, :], in_=pbig[0][:, c * 128:(c + 1) * 128]), r=[pbig[0]], w=[dstT])

        def rmsn(src, out_bf, gvec=None):
            S.op("act", lambda e: e.activation(out=sqj[:, :], in_=src[:, :], func=AF.Square, accum_out=ssq[:, 0:1]), r=[src], w=[sqj, ssq])
            rsqrt(S, rstd, rstd[:, :], ssq, ssq[:, :], 1.0 / D, EPS)
            if gvec is None:
                S.op("act", lambda e: e.activation(out=out_bf[:, :], in_=src[:, :], func=AF.Copy, scale=rstd[:, 0:1]), r=[src, rstd], w=[out_bf])
            else:
                S.op("dve", lambda e: e.scalar_tensor_tensor(out=out_bf[:, :], in0=src[:, :], scalar=rstd[:, 0:1], in1=gvec[:, :],
                                                             op0=ALU.mult, op1=ALU.mult), r=[src, rstd, gvec], w=[out_bf])

        def top16(vals_t, vals_ap, scratch_t, scratch_ap, mv_ap, mi_ap, mv_t, mi_t):
            S.op("dve", lambda e: e.max(out=mv_ap[:, 0:8], in_=vals_ap), r=[vals_t], w=[mv_t])
            S.op("dve", lambda e: e.max_index(out=mi_ap[:, 0:8], in_max=mv_ap[:, 0:8], in_values=vals_ap), r=[vals_t, mv_t], w=[mi_t])
            S.op("dve", lambda e: e.match_replace(out=scratch_ap, in_to_replace=mv_ap[:, 0:8], in_values=vals_ap, imm_value=-1e30),
                 r=[vals_t, mv_t], w=[scratch_t])
            S.op("dve", lambda e: e.max(out=mv_ap[:, 8:16], in_=scratch_ap), r=[scratch_t], w=[mv_t])
            S.op("dve", lambda e: e.max_index(out=mi_ap[:, 8:16], in_max=mv_ap[:, 8:16], in_values=scratch_ap), r=[scratch_t, mv_t], w=[mi_t])

        tiles = list(range(int(os.environ.get("P3TILES", str(MAIN // 128))))) + ([] if os.environ.get("P3NOSMP") else ["smp"])
        for tau in tiles:
            smp = tau == "smp"
            if smp:
                S.op("dve", lambda e: e.memset(xt[:, :], 0.0), w=[xt])
                S.op("pool", lambda e: e.memset(cat[:, :], 0.0), w=[cat])
                for g in range(3):
                    S.op("pool", lambda e: e.memset(sw[g][:, :], 1.0), w=[sw[g]])
                    S.dma("sp", sw[g], lambda e: e.dma_start(out=sw[g][0:NT, :], in_=k.smp_swo[:, g, :]), r=[k.smp_swo], w=[sw[g]])
                S.dma("sp", xt, lambda e: e.dma_start(out=xt[0:NT, :], in_=k.xs[:, :]), w=[xt])
                S.dma("sp", cat, lambda e: e.dma_start(out=cat[0:NT, 0:512], in_=k.smp_cat[:, :]), r=[k.smp_cat], w=[cat])
            else:
                t0 = tau * 128
                S.dma("sp", xt, lambda e: e.dma_start(out=xt[:, :], in_=k.xe[PRE + t0:PRE + t0 + 128, :]), w=[xt])
                S.dma("sp", cat, lambda e: e.dma_start(out=cat[:, 0:512], in_=k.cat[t0:t0 + 128, :]), w=[cat])
                for g in range(3):
                    S.dma("sp", sw[g], lambda e: e.dma_start(out=sw[g][:, :], in_=k.swo[g][t0:t0 + 128, :]), w=[sw[g]])
            S.op("dve", lambda e: e.tensor_tensor(out=sw[0][:, :], in0=sw[0][:, :], in1=sw[1][:, :], op=ALU.add), r=[sw[0], sw[1]], w=[sw[0]])
            S.op("dve", lambda e: e.tensor_tensor(out=sw[0][:, :], in0=sw[0][:, :], in1=sw[2][:, :], op=ALU.add), r=[sw[0], sw[2]], w=[sw[0]])
            for hd in range(4):
                S.op("dve", lambda e: e.reciprocal(out=rden[:, hd:hd + 1], in_=sw[0][:, hd * 65 + 64:hd * 65 + 65]), r=[sw[0]], w=[rden])
                S.op("dve", lambda e: e.tensor_scalar(out=cat[:, 512 + hd * 64:512 + (hd + 1) * 64], in0=sw[0][:, hd * 65:hd * 65 + 64],
                                                      scalar1=rden[:, hd:hd + 1], scalar2=None, op0=ALU.mult), r=[sw[0], rden], w=[cat])
            transposes(cat, 6, catT)
            for nb in range(2):
                for c in range(6):
                    S.op("pe", lambda e: e.matmul(pout[nb][:, :], lhsT=catT[:, c, :], rhs=wout[:, c, nb * 512:(nb + 1) * 512],
                                                  start=(c == 0), stop=(c == 5)), r=[catT, wout], w=[pout[nb]])
                S.op("dve", lambda e: e.tensor_tensor(out=h[:, nb * 512:(nb + 1) * 512], in0=pout[nb][:, :], in1=xt[:, nb * 512:(nb + 1) * 512], op=ALU.add),
                     r=[pout[nb], xt], w=[h])
            rmsn(h, cb)
            transposes(cb, 8, cT)
            for half in range(2):
                for c4 in range(4):
                    hc = half * 4 + c4
                    for c in range(8):
                        S.op("pe", lambda e: e.matmul(pbig[1][:, c4 * 128:(c4 + 1) * 128], lhsT=wmq[:, c, hc * 128:(hc + 1) * 128], rhs=cT[:, c, :],
                                                      start=(c == 0), stop=(c == 7)), r=[wmq, cT], w=[pbig[1]])
                for c4 in range(4):
                    hc = half * 4 + c4
                    S.op("dve", lambda e: e.tensor_copy(out=qmT[:, hc, :], in_=pbig[1][:, c4 * 128:(c4 + 1) * 128]), r=[pbig[1]], w=[qmT])
            segs = [(s_ * TS, TS, s_) for s_ in range(NS)] if smp else [(0, 128, None)]
            if smp:
                S.op("pool", lambda e: e.memset(attT[:, :, :], 0.0), w=[attT])
            for hd in range(4):
                for (q0, qn, s_) in segs:
                    kT_ap = (lambda hc, kc: KmT[:, hc, kc * 128:(kc + 1) * 128]) if s_ is None else (lambda hc, kc: KsT[:, s_, hc, kc * 128:(kc + 1) * 128])
                    v_ap = (lambda kc, col: Vm[:, kc, col:col + 128]) if s_ is None else (lambda kc, col: Vs[:, s_, kc, col:col + 128])
                    kt_t, v_t = (KmT, Vm) if s_ is None else (KsT, Vs)
                    for kc in range(2):
                        for cc in range(2):
                            S.op("pe", lambda e: e.matmul(psm[:, kc * 128:kc * 128 + qn], lhsT=kT_ap(hd * 2 + cc, kc), rhs=qmT[:, hd * 2 + cc, q0:q0 + qn],
                                                          start=(cc == 0), stop=(cc == 1)), r=[kt_t, qmT], w=[psm])
                    for kc in range(2):
                        S.op("act", lambda e: e.activation(out=PTm[:, kc * 128:kc * 128 + qn], in_=psm[:, kc * 128:kc * 128 + qn], func=AF.Exp, scale=1.0 / 16),
                             r=[psm], w=[PTm])
                    for kc in range(2):
                        S.op("pe", lambda e: e.matmul(pden[:, 0:qn], lhsT=onesb[:, :], rhs=PTm[:, kc * 128:kc * 128 + qn], start=(kc == 0), stop=(kc == 1)),
                             r=[onesb, PTm], w=[pden])
                    S.op("dve", lambda e: e.reciprocal(out=rdm[:, 0:qn], in_=pden[:, 0:qn]), r=[pden], w=[rdm])
                    for cc in range(2):
                        for kc in range(2):
                            S.op("pe", lambda e: e.matmul(pbig[2][:, cc * 128:cc * 128 + qn], lhsT=v_ap(kc, hd * 256 + cc * 128), rhs=PTm[:, kc * 128:kc * 128 + qn],
                                                          start=(kc == 0), stop=(kc == 1)), r=[v_t, PTm], w=[pbig[2]])
                    for cc in range(2):
                        S.op("dve", lambda e: e.tensor_tensor(out=attT[:, hd * 2 + cc, q0:q0 + qn], in0=pbig[2][:, cc * 128:cc * 128 + qn], in1=rdm[:, 0:qn], op=ALU.mult),
                             r=[pbig[2], rdm], w=[attT])
            for nb in range(2):
                for c in range(8):
                    S.op("pe", lambda e: e.matmul(pout[nb][:, :], lhsT=attT[:, c, :], rhs=wmo[:, c, nb * 512:(nb + 1) * 512],
                                                  start=(c == 0), stop=(c == 7)), r=[attT, wmo], w=[pout[nb]])
                S.op("dve", lambda e: e.tensor_tensor(out=h[:, nb * 512:(nb + 1) * 512], in0=pout[nb][:, :], in1=h[:, nb * 512:(nb + 1) * 512], op=ALU.add),
                     r=[pout[nb], h], w=[h])
            rmsn(h, fb, gffn)
            transposes(fb, 8, fT)
            for q4 in range(4):
                for c4 in range(4):
                    hp = q4 * 4 + c4
                    for c in range(8):
                        S.op("pe", lambda e: e.matmul(pbig[1][:, c4 * 128:(c4 + 1) * 128], lhsT=wpq[:, c, hp * 128:(hp + 1) * 128], rhs=fT[:, c, :],
                                                      start=(c == 0), stop=(c == 7)), r=[wpq, fT], w=[pbig[1]])
                for c4 in range(4):
                    hp = q4 * 4 + c4
                    S.op("dve", lambda e: e.tensor_copy(out=qpT[:, hp, :], in_=pbig[1][:, c4 * 128:(c4 + 1) * 128]), r=[pbig[1]], w=[qpT])
            for q4 in range(4):
                for c4 in range(4):
                    hp = q4 * 4 + c4
                    S.op("pe", lambda e: e.matmul(pbig[3][:, c4 * 128:(c4 + 1) * 128], lhsT=qpT[:, hp, :], rhs=skT[:, hp, :], start=True, stop=True),
                         r=[qpT, skT], w=[pbig[3]])
                for c4 in range(4):
                    hp = q4 * 4 + c4
                    S.op("dve", lambda e: e.tensor_copy(out=sc[:, hp, :], in_=pbig[3][:, c4 * 128:(c4 + 1) * 128]), r=[pbig[3]], w=[sc])
            for hp in range(16):
                top16(sc, sc[:, hp, :], sc2, sc2[:, :], mv[:, hp, :], mi[:, hp, :], mv, mi)
            S.op("dve", lambda e: e.tensor_copy(out=mif[:, :, :], in_=mi[:, :, :]), r=[mi], w=[mif])
            for hd in range(8):
                S.op("dve", lambda e: e.tensor_tensor(out=cand[:, :].rearrange("p (a b) -> p a b", a=16),
                                                      in0=mv[:, 2 * hd, :].unsqueeze(2).to_broadcast([128, 16, 16]),
                                                      in1=mv[:, 2 * hd + 1, :].unsqueeze(1).to_broadcast([128, 16, 16]), op=ALU.add), r=[mv], w=[cand])
                top16(cand, cand[:, :], cand2, cand2[:, :], cv[:, hd, :], ci[:, hd, :], cv, ci)
            S.op("dve", lambda e: e.tensor_copy(out=cif[:, :, :], in_=ci[:, :, :]), r=[ci], w=[cif])
            for hd in range(8):
                cb_ = cif[:, hd, :].unsqueeze(2).to_broadcast([128, 16, 16])
                S.op("dve", lambda e: e.tensor_tensor(out=oh[:, :, :], in0=cb_, in1=lo16[:, :].unsqueeze(1).to_broadcast([128, 16, 16]), op=ALU.is_ge),
                     r=[cif, lo16], w=[oh])
                S.op("dve", lambda e: e.tensor_tensor(out=cand2[:, :].rearrange("p (a b) -> p a b", a=16), in0=cb_, in1=hi16[:, :].unsqueeze(1).to_broadcast([128, 16, 16]), op=ALU.is_lt),
                     r=[cif, hi16], w=[cand2])
                S.op("dve", lambda e: e.tensor_tensor(out=oh[:, :, :], in0=oh[:, :, :], in1=cand2[:, :].rearrange("p (a b) -> p a b", a=16), op=ALU.mult), r=[oh, cand2], w=[oh])
                S.op("dve", lambda e: e.tensor_tensor(out=cand2[:, :].rearrange("p (a b) -> p a b", a=16), in0=oh[:, :, :], in1=mif[:, 2 * hd, :].unsqueeze(1).to_broadcast([128, 16, 16]), op=ALU.mult),
                     r=[oh, mif], w=[cand2])
                S.op("dve", lambda e: e.tensor_reduce(out=i1[:, hd, :], in_=cand2[:, :].rearrange("p (a b) -> p a b", a=16), axis=AX.X, op=ALU.add), r=[cand2], w=[i1])
                S.op("dve", lambda e: e.tensor_tensor(out=cand2[:, :].rearrange("p (a b) -> p a b", a=16), in0=oh[:, :, :], in1=lo16[:, :].unsqueeze(1).to_broadcast([128, 16, 16]), op=ALU.mult),
                     r=[oh, lo16], w=[cand2])
                S.op("dve", lambda e: e.tensor_reduce(out=ia[:, hd, :], in_=cand2[:, :].rearrange("p (a b) -> p a b", a=16), axis=AX.X, op=ALU.add), r=[cand2], w=[ia])
            S.op("dve", lambda e: e.tensor_tensor(out=ib[:, :, :], in0=cif[:, :, :], in1=ia[:, :, :], op=ALU.subtract), r=[cif, ia], w=[ib])
            for hd in range(8):
                S.op("dve", lambda e: e.tensor_tensor(out=oh[:, :, :], in0=ib[:, hd, :].unsqueeze(2).to_broadcast([128, 16, 16]),
                                                      in1=iota[:, 0:16].unsqueeze(1).to_broadcast([128, 16, 16]), op=ALU.is_equal), r=[ib, iota], w=[oh])
                S.op("dve", lambda e: e.tensor_tensor(out=oh[:, :, :], in0=oh[:, :, :], in1=mif[:, 2 * hd + 1, :].unsqueeze(1).to_broadcast([128, 16, 16]), op=ALU.mult),
                     r=[oh, mif], w=[oh])
                S.op("dve", lambda e: e.tensor_reduce(out=i2[:, hd, :], in_=oh[:, :, :], axis=AX.X, op=ALU.add), r=[oh], w=[i2])
            S.op("dve", lambda e: e.scalar_tensor_tensor(out=eidf[:, :], in0=i1[:, :, :].rearrange("p a b -> p (a b)"), scalar=128.0,
                                                         in1=i2[:, :, :].rearrange("p a b -> p (a b)"), op0=ALU.mult, op1=ALU.add), r=[i1, i2], w=[eidf])
            S.op("dve", lambda e: e.tensor_copy(out=eid[:, :], in_=eidf[:, :]), r=[eidf], w=[eid])
            for hd in range(8):
                S.op("dve", lambda e: e.tensor_scalar(out=gate[:, hd, :], in0=cv[:, hd, :], scalar1=cv[:, hd, 0:1], scalar2=None, op0=ALU.subtract),
                     r=[cv], w=[gate])
            S.op("act", lambda e: e.activation(out=gate[:, :, :].rearrange("p a b -> p (a b)"), in_=gate[:, :, :].rearrange("p a b -> p (a b)"), func=AF.Exp),
                 r=[gate], w=[gate])
            S.op("dve", lambda e: e.tensor_reduce(out=gsum[:, :], in_=gate[:, :, :], axis=AX.X, op=ALU.add), r=[gate], w=[gsum])
            S.op("dve", lambda e: e.reciprocal(out=gsum[:, :], in_=gsum[:, :]), r=[gsum], w=[gsum])
            for hd in range(8):
                S.op("dve", lambda e: e.tensor_scalar(out=gate[:, hd, :], in0=gate[:, hd, :], scalar1=gsum[:, hd:hd + 1], scalar2=None, op0=ALU.mult),
                     r=[gate, gsum], w=[gate])
            for sl in range(128):
                g_ = Gu[sl % NBU]
                pr_ = prod[sl % 2]
                S.dma("pool", g_, lambda e: e.indirect_dma_start(out=g_[:, :], out_offset=None, in_=k.eu_bf.ap(),
                                                                 in_offset=bass.IndirectOffsetOnAxis(ap=eid[:, sl:sl + 1], axis=0)),
                      r=[eid] + par(g_), w=[g_])
                S.op("dve", lambda e: e.tensor_tensor(out=pr_[:, :], in0=g_[:, :], in1=fb[:, :], op=ALU.mult), r=[g_, fb] + par(g_), w=[pr_])
                S.op("act", lambda e: e.activation(out=xt[:, :], in_=pr_[:, :], func=AF.Copy, accum_out=hid[:, sl:sl + 1]), r=[pr_], w=[xt, hid])
            S.op("dve", lambda e: e.tensor_tensor(out=hx[:, :], in0=hid[:, :], in1=hid[:, :], op=ALU.mult), r=[hid], w=[hx])
            S.op("dve", lambda e: e.tensor_scalar(out=hx[:, :], in0=hx[:, :], scalar1=0.044715, scalar2=1.0, op0=ALU.mult, op1=ALU.add), r=[hx], w=[hx])
            S.op("dve", lambda e: e.tensor_tensor(out=hx[:, :], in0=hx[:, :], in1=hid[:, :], op=ALU.mult), r=[hx, hid], w=[hx])
            S.op("act", lambda e: e.activation(out=hx[:, :], in_=hx[:, :], func=AF.Tanh, scale=0.7978845608028654), r=[hx], w=[hx])
            S.op("dve", lambda e: e.tensor_scalar(out=hx[:, :], in0=hx[:, :], scalar1=1.0, scalar2=0.5, op0=ALU.add, op1=ALU.mult), r=[hx], w=[hx])
            S.op("dve", lambda e: e.tensor_tensor(out=hx[:, :], in0=hx[:, :], in1=hid[:, :], op=ALU.mult), r=[hx, hid], w=[hx])
            S.op("dve", lambda e: e.tensor_tensor(out=wgt[:, :], in0=hx[:, :], in1=gate[:, :, :].rearrange("p a b -> p (a b)"), op=ALU.mult), r=[hx, gate], w=[wgt])
            for sl in range(128):
                g_ = Gv[sl % NBV]
                d_ = dg[sl % 2]
                S.dma("pool", g_, lambda e: e.indirect_dma_start(out=g_[:, :], out_offset=None, in_=k.ev_bf.ap(),
                                                                 in_offset=bass.IndirectOffsetOnAxis(ap=eid[:, sl:sl + 1], axis=0)),
                      r=[eid] + par(g_), w=[g_])
                S.op("dve", lambda e: e.tensor_scalar(out=d_[:, :], in0=identb[:, :], scalar1=wgt[:, sl:sl + 1], scalar2=None, op0=ALU.mult),
                     r=[identb, wgt], w=[d_])
                for nb in range(2):
                    S.op("pe", lambda e: e.matmul(pout[nb][:, :], lhsT=d_[:, :], rhs=g_[:, nb * 512:(nb + 1) * 512], start=(sl == 0), stop=(sl == 127)),
                         r=[d_, g_] + par(g_), w=[pout[nb]])
            for nb in range(2):
                S.op("dve", lambda e: e.tensor_tensor(out=h[:, nb * 512:(nb + 1) * 512], in0=pout[nb][:, :], in1=h[:, nb * 512:(nb + 1) * 512], op=ALU.add),
                     r=[pout[nb], h], w=[h])
            S.op("act", lambda e: e.activation(out=sqj[:, :], in_=h[:, :], func=AF.Square, accum_out=ssq[:, 0:1]), r=[h], w=[sqj, ssq])
            rsqrt(S, rstd, rstd[:, :], ssq, ssq[:, :], 1.0 / D, EPS)
            S.op("dve", lambda e: e.scalar_tensor_tensor(out=yo[:, :], in0=h[:, :], scalar=rstd[:, 0:1], in1=gfin[:, :], op0=ALU.mult, op1=ALU.mult),
                 r=[h, rstd, gfin], w=[yo])
            if smp:
                S.dma("sp", yo, lambda e: e.dma_start(out=k.y_smp[:, :], in_=yo[0:NT, :]), r=[yo])
            else:
                S.dma("sp", yo, lambda e: e.dma_start(out=k.y_main[tau * 128:(tau + 1) * 128, :], in_=yo[:, :]), r=[yo])
        k.end_phase()
```

```python
import numpy as np
from contextlib import ExitStack
import concourse.bass as bass
import concourse.mybir as mybir
from concourse.bass_utils import run_bass_kernel_spmd

F32 = mybir.dt.float32
BF16 = mybir.dt.bfloat16
I32 = mybir.dt.int32
U32 = mybir.dt.uint32
AF = mybir.ActivationFunctionType
ALU = mybir.AluOpType
AX = mybir.AxisListType

NCORES = 8
D = 1024
PRE = 4096
MAIN = 4096
EXT = PRE + MAIN
HALO = 2048
KR = HALO + MAIN
NS = 4
TS = 4
IN_DIM = 4360
CQ, CK, CV, CAG, CBG, CZ, CSW = 0, 512, 1024, 1536, 1540, 1544, 2056
DILS = (1, 4, 16)
EPS = 1e-6
NEG = -30000.0
SEM_EPOCH = 20000
import os as _os
EXPROWS = int(_os.environ.get("EXPROWS", "16384"))


class TL:
    def __init__(self, t, name):
        self.t = t
        self.name = name
        self.lw = None
        self.rd = []
        self.ds = None

    def __getitem__(self, k):
        return self.t[k]


class TLsub(TL):
    def __init__(self, bank, off, width, name):
        self.bank = bank
        self.t = bank.t
        self.name = name
        self.off = off
        self.width = width
        self.ds = None

    lw = property(lambda self: self.bank.lw, lambda self, v: setattr(self.bank, "lw", v))
    rd = property(lambda self: self.bank.rd, lambda self, v: setattr(self.bank, "rd", v))

    def __getitem__(self, key):
        r, c = key
        if isinstance(c, slice):
            a = self.off + (c.start or 0)
            b_ = self.off + (self.width if c.stop is None else c.stop)
            return self.t[r, a:b_]
        return self.t[r, self.off + c]


class TLview(TL):
    def __init__(self, parent, ap_fn, name):
        super().__init__(None, name)
        self.p = parent
        self.ap_fn = ap_fn

    def __getitem__(self, key):
        return self.ap_fn()[key]


class PsumBlocks:
    def __init__(self, nc, es, prefix):
        self.nc, self.es, self.prefix = nc, es, prefix
        self.banks = {}

    def get(self, name, width, dt, bank):
        per = 2048 // (4 if dt == F32 else 2)
        if bank not in self.banks:
            t = self.es.enter_context(self.nc.psum_tensor(f"{self.prefix}{bank}", [128, per], dt))
            self.banks[bank] = [TL(t, f"{self.prefix}{bank}"), 0]
        b = self.banks[bank]
        assert b[1] + width <= per
        off = b[1]
        b[1] += width
        return TLsub(b[0], off, width, name)


class _Rec:
    def __init__(self):
        self.call = None

    def __getattr__(self, name):
        def f(*a, **kw):
            self.call = (name, a, kw)
            return self
        return f


def _capture(fn):
    r = _Rec()
    fn(r)
    name, a, kw = r.call
    return lambda e: getattr(e, name)(*a, **kw)


class Sched:
    ENG = ("pe", "act", "dve", "pool", "sp")

    def __init__(self, nc):
        self.nc = nc
        self.eng = {"pe": nc.tensor, "act": nc.scalar, "dve": nc.vector, "pool": nc.gpsimd, "sp": nc.sync}
        self.rec = []
        self.dsems = []
        self.dsem_pool = []
        self.ninst = 0
        self.last = {e: None for e in self.ENG}

    def dsem(self, name):
        s = [self.nc.alloc_semaphore("d_" + name), 0, False]
        self.dsems.append(s)
        return s

    def _collect(self, e, r, w, is_dma):
        deps = []
        for t in r:
            if t.lw is not None:
                deps.append(t.lw)
        for t in w:
            if t.lw is not None:
                p = self.rec[t.lw]
                if is_dma or not (p["kind"] == "op" and p["e"] == e and e == "pe"):
                    deps.append(t.lw)
            for d in t.rd:
                p = self.rec[d]
                if is_dma or p["kind"] == "dma" or p["e"] != e or e != "pe":
                    deps.append(d)
        return deps

    def op(self, e, fn, r=(), w=()):
        deps = self._collect(e, r, w, False)
        idx = len(self.rec)
        self.rec.append({"kind": "op", "e": e, "fn": _capture(fn), "deps": deps})
        self.last[e] = idx
        for t in w:
            t.lw = idx
            t.rd = []
        for t in r:
            t.rd.append(idx)
        self.ninst += 1

    def dma(self, q, t, fn, r=(), w=()):
        if q == "pool" and (t.ds is None or not t.ds[2]):
            t.ds = self.dsem(t.name + "_sw")
            t.ds[2] = True
        if t.ds is None:
            t.ds = self.dsem_pool.pop() if self.dsem_pool else self.dsem(t.name)
        ds = t.ds
        deps = self._collect(q, r, w, True)
        if ds[1] + 16 >= 32000:
            sw_ = ds[2]
            t.ds = self.dsem(t.name + f"_r{len(self.dsems)}")
            t.ds[2] = sw_
            ds = t.ds
        ds[1] += 16
        idx = len(self.rec)
        self.rec.append({"kind": "dma", "e": q, "fn": _capture(fn), "deps": deps, "sem": ds[0], "val": ds[1]})
        for x in w:
            x.lw = idx
            x.rd = []
        for x in r:
            x.rd.append(idx)
        self.ninst += 1

    def release(self, tiles):
        for t in tiles:
            if t.ds is not None:
                if not t.ds[2]:
                    self.dsem_pool.append(t.ds)
                t.ds = None

    def barrier(self, engines=None):
        deps = [v for v in self.last.values() if v is not None]
        dm = [(s[0], s[1]) for s in self.dsems if s[1]]
        self.rec.append({"kind": "bar", "deps": deps, "dm": dm, "engines": engines or self.ENG})

    def final_wait(self):
        self.barrier(engines=("sp",))

    def emit(self):
        rec = self.rec
        seq = {}
        cnt = {e: 0 for e in self.ENG}
        for i, r in enumerate(rec):
            if r["kind"] == "op":
                cnt[r["e"]] += 1
                seq[i] = cnt[r["e"]]
        awaited = set()

        def sweep(do_emit, ordv=None, sems=None):
            wseq = {e: {p: 0 for p in self.ENG} for e in self.ENG}
            wdma = {e: {} for e in self.ENG}
            for i, r in enumerate(rec):
                targets = r["engines"] if r["kind"] == "bar" else (r["e"],)
                for e in targets:
                    need = {}
                    for d in r["deps"]:
                        p = rec[d]
                        if p["kind"] == "op":
                            if seq[d] > wseq[e][p["e"]] and seq[d] > need.get(p["e"], (0, None))[0]:
                                need[p["e"]] = (seq[d], d)
                        else:
                            key = id(p["sem"])
                            if wdma[e].get(key, 0) < p["val"]:
                                wdma[e][key] = p["val"]
                                if do_emit:
                                    self.eng[e].wait_ge(p["sem"], p["val"])
                    for (sem, val) in r.get("dm", ()):
                        key = id(sem)
                        if wdma[e].get(key, 0) < val:
                            wdma[e][key] = val
                            if do_emit:
                                self.eng[e].wait_ge(sem, val)
                    for pe_, (sq, d) in need.items():
                        wseq[e][pe_] = sq
                        if do_emit:
                            o = ordv[d]
                            self.eng[e].wait_ge(sems[pe_][(o - 1) // SEM_EPOCH], (o - 1) % SEM_EPOCH + 1)
                        else:
                            awaited.add(d)
                if do_emit and r["kind"] != "bar":
                    ins = r["fn"](self.eng[r["e"]])
                    if r["kind"] == "dma":
                        ins.then_inc(r["sem"], 16)
                    elif i in awaited:
                        o = ordv[i]
                        ins.then_inc(sems[r["e"]][(o - 1) // SEM_EPOCH], 1)

        sweep(False)
        ordv = {}
        oc = {e: 0 for e in self.ENG}
        for i, r in enumerate(rec):
            if r["kind"] == "op" and i in awaited:
                oc[r["e"]] += 1
                ordv[i] = oc[r["e"]]
        sems = {e: [self.nc.alloc_semaphore(f"s_{e}_{j}") for j in range((oc[e] + SEM_EPOCH - 1) // SEM_EPOCH)] for e in self.ENG}
        self.nsig = dict(oc)
        sweep(True, ordv, sems)


class K:
    def __init__(self):
        self.tiles = []

    def track(self, t):
        self.tiles.append(t)
        return t

    def end_phase(self):
        self.S.barrier()
        self.S.release(self.tiles)
        self.tiles = []


def make_consts():
    c = {}
    idx = np.arange(128)
    same = (idx[:, None] // 64) == (idx[None, :] // 64)
    c["ident"] = np.eye(128, dtype=np.float32)
    c["trit"] = ((idx[:, None] <= idx[None, :]) & same).astype(np.float32)
    c["blk"] = same.astype(np.float32)
    c["mTneg"] = np.where((idx[None, :] >= idx[:, None]) & same, 0.0, NEG).astype(np.float32)
    c["mSpos"] = np.where((idx[:, None] > idx[None, :]) & same, 0.0, -NEG).astype(np.float32)
    c["swprev"] = np.where(idx[:, None] >= idx[None, :], 0.0, NEG).astype(np.float32)
    c["swcur"] = np.where(idx[:, None] <= idx[None, :], 0.0, NEG).astype(np.float32)
    c["ones"] = np.ones((128, 128), np.float32)
    c["iota"] = np.tile(np.arange(128, dtype=np.float32), (128, 1))
    return c


CONST_NAMES = ("ident", "trit", "blk", "mTneg", "mSpos", "swprev", "swcur", "ones", "iota")


def declare_io(k, debug):
    nc = k.nc
    I = lambda n, s, dt=F32: nc.dram_tensor(n, list(s), dt, kind="ExternalInput")
    O = lambda n, s, dt=F32: nc.dram_tensor(n, list(s), dt, kind="ExternalOutput")
    SCR = (lambda n, s, dt: nc.dram_tensor(n, list(s), dt, kind="ExternalOutput")) if debug else \
          (lambda n, s, dt: nc.dram_tensor(n, list(s), dt, kind="Internal"))
    k.xe = I("xe", [EXT, D])
    k.xs = I("xs", [NS * TS, D])
    k.st_delta = I("st_delta", [NS, 4, 128, 128])
    k.st_conv = I("st_conv", [NS, 3, 1536])
    k.cwin = [I(f"cwin{g}", [NS, 128 * DILS[g], 512]) for g in range(3)]
    k.cmem = I("cmem", [NS, 256, 2048])
    k.mem = I("mem", [256, D])
    k.halo = I("halo", [128, 128])
    k.consts = I("consts", [len(CONST_NAMES), 128, 128])
    for n, s in (("g_mix", [D]), ("w_in", [D, IN_DIM]), ("conv_w", [4, 1536]), ("a_log", [4]), ("dt_bias", [4]),
                 ("g_onorm", [128]), ("w_out", [768, D]), ("g_memq", [D]), ("g_memkv", [D]), ("w_mq", [D, D]),
                 ("w_mkv", [D, 2048]), ("w_mo", [D, D]), ("g_ffn", [D]), ("w_pq", [D, 2048]),
                 ("sub_keys", [16, 128, 128]), ("expert_u", [EXPROWS, D]), ("expert_v", [EXPROWS, D]), ("g_final", [D])):
        setattr(k, n, I(n, s))
    k.y_main = O("y_main", [MAIN, D])
    k.y_smp = O("y_smp", [NS * TS, D])
    k.p_delta = O("p_delta", [4, 128, 128])
    k.p_conv = O("p_conv", [3, 1536])
    k.p_win = [O(f"p_win{g}", [128 * DILS[g], 512]) for g in range(3)]
    k.p_mem = O("p_mem", [256, 2048])
    k.s_delta = O("s_delta", [NS, 4, 128, 128])
    k.s_conv = O("s_conv", [NS, 3, 1536])
    k.s_win = [O(f"s_win{g}", [NS, 128 * DILS[g], 512]) for g in range(3)]
    k.dnq = SCR("dnq", [4, 128, EXT], BF16)
    k.dnk = SCR("dnk", [4, 128, EXT], BF16)
    k.dnv = SCR("dnv", [4, 128, EXT], BF16)
    k.gbs = SCR("gbs", [EXT, 8], F32)
    k.zs = SCR("zs", [MAIN, 512], BF16)
    k.qts = [SCR(f"qts{g}", [2, 128, MAIN], BF16) for g in range(3)]
    k.kts = [SCR(f"kts{g}", [2, 128, KR], BF16) for g in range(3)]
    k.vss = [SCR(f"vss{g}", [KR, 256], BF16) for g in range(3)]
    SWO = (lambda n, s, dt: nc.dram_tensor(n, list(s), dt, kind="ExternalOutput")) if _os.environ.get("SWO_OUT") else SCR
    k.swo = [SWO(f"swo{g}", [MAIN, 260], F32) for g in range(3)]
    k.cat = SCR("cat", [MAIN, 512], BF16)
    k.euv_bf = nc.dram_tensor("euv_bf", [EXPROWS, 2 * D], BF16, kind="Internal")


def evac(S, i, out_t, out_ap, in_t, in_ap, rx=(), **kw):
    if i % 2 == 0 and len(out_ap.shape) == 2 and len(in_ap.shape) == 2:
        S.op("act", lambda e: e.activation(out=out_ap, in_=in_ap, func=AF.Copy, **kw), r=[in_t, *rx], w=[out_t])
    else:
        if "scale" in kw:
            S.op("dve", lambda e: e.tensor_scalar(out=out_ap, in0=in_ap, scalar1=kw["scale"], scalar2=None, op0=ALU.mult),
                 r=[in_t, *rx], w=[out_t])
        else:
            S.op("dve", lambda e: e.tensor_copy(out=out_ap, in_=in_ap), r=[in_t, *rx], w=[out_t])


def rsqrt(S, out_t, out_ap, in_t, in_ap, mul, add):
    S.op("act", lambda e: e.activation(out=out_ap, in_=in_ap, func=AF.Sqrt, scale=mul, bias=add), r=[in_t], w=[out_t])
    S.op("dve", lambda e: e.reciprocal(out=out_ap, in_=out_ap), r=[out_t], w=[out_t])


def load_weight_bf16(k, es, name, wdram, rows, cols, gdram=None, col_chunk=None):
    nc, S = k.nc, k.S
    nch = rows // 128
    wbf = TL(es.enter_context(nc.sbuf_tensor(name + "_bf", [128, nch, cols], BF16)), name)
    gcol = None
    if gdram is not None:
        gcol = k.track(TL(es.enter_context(nc.sbuf_tensor(name + "_g", [128, nch], F32)), name + "_g"))
        S.dma("sp", gcol, lambda e: e.dma_start(out=gcol[:, :], in_=gdram.ap().rearrange("(c p) -> p c", p=128),
                                                 allow_slow_non_contiguous=True), w=[gcol])
    cc = col_chunk or cols
    with ExitStack() as es2:
        st = [k.track(TL(es2.enter_context(nc.sbuf_tensor(f"{name}_st{i}", [128, cc], F32)), f"{name}_st{i}")) for i in range(2)]
        n = 0
        for c in range(nch):
            for c0 in range(0, cols, cc):
                w_ = min(cc, cols - c0)
                s_ = st[n % 2]
                S.dma("sp", s_, lambda e: e.dma_start(out=s_[:, 0:w_], in_=wdram[c * 128:(c + 1) * 128, c0:c0 + w_]), w=[s_])
                if gcol is not None:
                    evac(S, n, wbf, wbf[:, c, c0:c0 + w_], s_, s_[:, 0:w_], rx=[gcol], scale=gcol[:, c:c + 1])
                else:
                    evac(S, n, wbf, wbf[:, c, c0:c0 + w_], s_, s_[:, 0:w_])
                n += 1
        S.barrier()
    return wbf


def phase1(k):
    nc, S = k.nc, k.S
    with ExitStack() as es:
        A = lambda name, shape, dt: k.track(TL(es.enter_context(nc.sbuf_tensor(name, shape, dt)), name))
        P = lambda name, shape, dt: TL(es.enter_context(nc.psum_tensor(name, shape, dt)), name)
        wbf = load_weight_bf16(k, es, "w_in", k.w_in, D, IN_DIM, gdram=k.g_mix, col_chunk=2180)
        cw = A("cw", [128, 12, 4], F32)
        for j in range(4):
            S.dma("sp", cw, lambda e: e.dma_start(out=cw[:, :, j], in_=k.conv_w[j, :].rearrange("(c p) -> p c", p=128),
                                                  allow_slow_non_contiguous=True), w=[cw])
        dtb = A("dtb", [128, 4], F32)
        S.dma("sp", dtb, lambda e: e.dma_start(out=dtb[:, :], in_=k.dt_bias.ap().partition_broadcast(128)), w=[dtb])
        nega = A("nega", [128, 4], F32)
        S.dma("sp", nega, lambda e: e.dma_start(out=nega[:, :], in_=k.a_log.ap().partition_broadcast(128)), w=[nega])
        S.op("act", lambda e: e.activation(out=nega[:, :], in_=nega[:, :], func=AF.Exp), r=[nega], w=[nega])
        S.op("dve", lambda e: e.tensor_scalar(out=nega[:, :], in0=nega[:, :], scalar1=-1.0, scalar2=None, op0=ALU.mult), r=[nega], w=[nega])
        xt = [A(f"xt{i}", [128, D], F32) for i in range(2)]
        xsm = A("xsm", [128, D], F32)
        sqj = A("sqj", [128, D], BF16)
        ssq = [A(f"ssq{i}", [128, 1], F32) for i in range(2)]
        rstd = [A(f"rstd{i}", [128, 1], F32) for i in range(2)]
        ab = [A(f"ab{i}", [128, D], BF16) for i in range(2)]
        aT = [A(f"aT{i}", [128, 8, 512], BF16) for i in range(2)]
        xp = [A(f"xp{i}", [128, 515], F32) for i in range(2)]
        carry = A("carry", [128, 12, 3], F32)
        acc = [A(f"acc{i}", [128, 512], F32) for i in range(2)]
        act_ = [A(f"actt{i}", [128, 512], F32) for i in range(2)]
        sq2 = [A(f"sq2{i}", [128, 512], BF16) for i in range(2)]
        rn = [A(f"rn{i}", [128, 512], F32) for i in range(2)]
        outb = [A(f"outb{i}", [128, 512], BF16) for i in range(3)]
        qkp = [A(f"qkp{i}", [128, 512], BF16) for i in range(3)]
        gb = [A(f"gb{i}", [128, 8], F32) for i in range(2)]
        gtmp = [A(f"gtmp{i}", [128, 4], F32) for i in range(4)]
        zb = [A(f"zb{i}", [128, 512], BF16) for i in range(2)]
        vsb = [A(f"vsb{i}", [128, 256], BF16) for i in range(3)]
        kvf = [A(f"kvf{i}", [128, 512], F32) for i in range(3)]
        xps = A("xps", [128, 12, NS, 7], F32)
        ones_bf = k.ones_bf
        pT = [P(f"pT{i}", [128, 8, 128], BF16) for i in range(2)]
        pacc = [P(f"pacc{i}", [128, 512], F32) for i in range(2)]
        pl2 = P("pl2", [128, 512], F32)
        ptm = [P(f"ptm{i}", [128, 512], F32) for i in range(2)]
        pg = P("pg", [128, 8], F32)
        S.op("dve", lambda e: e.memset(carry[:, :, :], 0.0), w=[carry])
        S.op("dve", lambda e: e.memset(xsm[:, :], 0.0), w=[xsm])
        ctr = {"ev": 0, "acc": 0, "tm": 0, "ob": 0}

        def norm_transpose(xtile, i, aTt, col0):
            S.op("act", lambda e: e.activation(out=sqj[:, :], in_=xtile[:, :], func=AF.Square, accum_out=ssq[i][:, 0:1]),
                 r=[xtile], w=[sqj, ssq[i]])
            rsqrt(S, rstd[i], rstd[i][:, :], ssq[i], ssq[i][:, :], 1.0 / D, EPS)
            S.op("act", lambda e: e.activation(out=ab[i][:, :], in_=xtile[:, :], func=AF.Copy, scale=rstd[i][:, 0:1]),
                 r=[xtile, rstd[i]], w=[ab[i]])
            for c in range(8):
                S.op("pe", lambda e: e.transpose(out=pT[i][:, c, :], in_=ab[i][:, c * 128:(c + 1) * 128], identity=k.ident_bf[:, :]),
                     r=[ab[i], k.ident_bf], w=[pT[i]])
            ctr["ev"] += 1
            evac(S, ctr["ev"], aTt, aTt[:, :, col0:col0 + 128], pT[i], pT[i][:, :, :])

        def fm_chunk(aTt, nt, col):
            ps = pacc[ctr["acc"] % 2]
            ctr["acc"] += 1
            for c in range(8):
                S.op("pe", lambda e: e.matmul(ps[:, 0:nt], lhsT=wbf[:, c, col:col + 128], rhs=aTt[:, c, 0:nt],
                                              start=(c == 0), stop=(c == 7)), r=[wbf, aTt], w=[ps])
            return ps

        def tm_chunk(aTt, t, col, ncol, ps=None):
            if ps is None:
                ps = ptm[ctr["tm"] % 2]
                ctr["tm"] += 1
            for c in range(8):
                S.op("pe", lambda e: e.matmul(ps[:, 0:ncol], lhsT=aTt[:, c, t * 128:(t + 1) * 128], rhs=wbf[:, c, col:col + ncol],
                                              start=(c == 0), stop=(c == 7)), r=[wbf, aTt], w=[ps])
            return ps

        def dn_post(ps, nt, ci, src_t, src_ap, dst_fn):
            j = ctr["ob"] % 2
            S.op("act", lambda e: e.activation(out=act_[j][:, 0:nt], in_=src_ap, func=AF.Silu), r=[src_t], w=[act_[j]])
            ob = outb[ctr["ob"] % 3]
            ctr["ob"] += 1
            if ci < 8:
                S.op("act", lambda e: e.activation(out=sq2[j][:, 0:nt], in_=act_[j][:, 0:nt], func=AF.Square), r=[act_[j]], w=[sq2[j]])
                S.op("pe", lambda e: e.matmul(pl2[:, 0:nt], lhsT=ones_bf[:, :], rhs=sq2[j][:, 0:nt], start=True, stop=True),
                     r=[ones_bf, sq2[j]], w=[pl2])
                rsqrt(S, rn[j], rn[j][:, 0:nt], pl2, pl2[:, 0:nt], 1.0, EPS)
                if ci < 4:
                    S.op("dve", lambda e: e.scalar_tensor_tensor(out=ob[:, 0:nt], in0=act_[j][:, 0:nt], scalar=128 ** -0.5,
                                                                 in1=rn[j][:, 0:nt], op0=ALU.mult, op1=ALU.mult),
                         r=[act_[j], rn[j]], w=[ob])
                else:
                    S.op("dve", lambda e: e.tensor_tensor(out=ob[:, 0:nt], in0=act_[j][:, 0:nt], in1=rn[j][:, 0:nt], op=ALU.mult),
                         r=[act_[j], rn[j]], w=[ob])
            else:
                S.op("dve", lambda e: e.tensor_copy(out=ob[:, 0:nt], in_=act_[j][:, 0:nt]), r=[act_[j]], w=[ob])
            dst_fn(ob)

        def gates(ps_g, rows, dst_t, dst_ap):
            g0, g1, g2, g3 = gtmp
            S.op("act", lambda e: e.activation(out=dst_ap[:, 4:8], in_=ps_g[0:rows, 4:8], func=AF.Sigmoid), r=[ps_g], w=[dst_t])
            S.op("dve", lambda e: e.tensor_tensor(out=g0[0:rows, :], in0=ps_g[0:rows, 0:4], in1=dtb[0:rows, :], op=ALU.add),
                 r=[ps_g, dtb], w=[g0])
            S.op("act", lambda e: e.activation(out=g1[0:rows, :], in_=g0[0:rows, :], func=AF.Abs), r=[g0], w=[g1])
            S.op("act", lambda e: e.activation(out=g2[0:rows, :], in_=g1[0:rows, :], func=AF.Exp, scale=-1.0), r=[g1], w=[g2])
            S.op("act", lambda e: e.activation(out=g3[0:rows, :], in_=g2[0:rows, :], func=AF.Ln, bias=1.0), r=[g2], w=[g3])
            S.op("dve", lambda e: e.scalar_tensor_tensor(out=g1[0:rows, :], in0=g0[0:rows, :], scalar=0.0, in1=g3[0:rows, :],
                                                         op0=ALU.max, op1=ALU.add), r=[g0, g3], w=[g1])
            S.op("dve", lambda e: e.tensor_tensor(out=dst_ap[:, 0:4], in0=g1[0:rows, :], in1=nega[0:rows, :], op=ALU.mult),
                 r=[g1, nega], w=[dst_t])

        nsup = EXT // 512
        import os
        sups = range(nsup) if "P1SUP" not in os.environ else [int(x) for x in os.environ["P1SUP"].split(",") if x]
        for s in sups:
            aTt = aT[s % 2]
            in_main = s * 512 >= PRE
            in_kr = s * 512 >= PRE - HALO
            for t in range(4):
                x_ = xt[t % 2]
                r0 = s * 512 + t * 128
                S.dma("sp", x_, lambda e: e.dma_start(out=x_[:, :], in_=k.xe[r0:r0 + 128, :]), w=[x_])
                norm_transpose(x_, t % 2, aTt, t * 128)
            for ci in range(12):
                ps = fm_chunk(aTt, 512, ci * 128)
                xp_ = xp[ci % 2]
                S.op("act", lambda e: e.activation(out=xp_[:, 3:515], in_=ps[:, :], func=AF.Copy), r=[ps], w=[xp_])
                S.op("pool", lambda e: e.tensor_copy(out=xp_[:, 0:3], in_=carry[:, ci, :]), r=[carry], w=[xp_])
                S.op("pool", lambda e: e.tensor_copy(out=carry[:, ci, :], in_=xp_[:, 512:515]), r=[xp_], w=[carry])
                if s == nsup - 1 and not os.environ.get("SKIP_PCONV"):
                    S.dma("sp", xp_, lambda e: e.dma_start(out=k.p_conv.ap()[:, ci * 128:(ci + 1) * 128].rearrange("j p -> p j"),
                                                             in_=xp_[:, 512:515], allow_slow_non_contiguous=True), r=[xp_])
                ac = acc[ci % 2]
                S.op("dve", lambda e: e.tensor_scalar(out=ac[:, :], in0=xp_[:, 0:512], scalar1=cw[:, ci, 0:1], scalar2=None, op0=ALU.mult),
                     r=[xp_, cw], w=[ac])
                for j in range(1, 4):
                    S.op("dve", lambda e: e.scalar_tensor_tensor(out=ac[:, :], in0=xp_[:, j:j + 512], scalar=cw[:, ci, j:j + 1], in1=ac[:, :],
                                                                 op0=ALU.mult, op1=ALU.add), r=[xp_, cw, ac], w=[ac])
                dst = (k.dnq, k.dnk, k.dnv)[ci // 4]
                h = ci % 4
                dn_post(None, 512, ci, ac, ac[:, :],
                        lambda ob: S.dma("sp", ob, lambda e: e.dma_start(out=dst[h, :, s * 512:(s + 1) * 512], in_=ob[:, :]), r=[ob]))
            for t in range(4):
                r0 = s * 512 + t * 128
                for c in range(8):
                    S.op("pe", lambda e: e.matmul(pg[:, :], lhsT=aTt[:, c, t * 128:(t + 1) * 128], rhs=wbf[:, c, CAG:CAG + 8],
                                                  start=(c == 0), stop=(c == 7)), r=[wbf, aTt], w=[pg])
                g_ = gb[t % 2]
                gates(pg, 128, g_, g_[:, :])
                S.dma("sp", g_, lambda e: e.dma_start(out=k.gbs[r0:r0 + 128, :], in_=g_[:, :]), r=[g_])
                if in_main:
                    ps = tm_chunk(aTt, t, CZ, 512)
                    z_ = zb[t % 2]
                    ctr["ev"] += 1
                    evac(S, ctr["ev"], z_, z_[:, :], ps, ps[:, :])
                    S.dma("sp", z_, lambda e: e.dma_start(out=k.zs[r0 - PRE:r0 - PRE + 128, :], in_=z_[:, :]), r=[z_])
            if not in_kr:
                continue
            sk = s - (PRE - HALO) // 512
            sm = s - PRE // 512
            for g in range(3):
                d = DILS[g]
                if False:
                    continue
                for which in range(2):
                    if which == 0 and not in_main:
                        continue
                    for pair in range(2):
                        ps = fm_chunk(aTt, 512, CSW + 768 * g + 256 * which + 128 * pair)
                        q_ = qkp[ctr["ob"] % 3]
                        ctr["ob"] += 1
                        ctr["ev"] += 1
                        evac(S, ctr["ev"], q_, q_[:, :], ps, ps[:, :])
                        if which == 0:
                            dst = k.qts[g][pair, :, sm * 512:(sm + 1) * 512]
                        else:
                            dst = k.kts[g][pair, :, sk * 512:(sk + 1) * 512]
                        S.dma("sp", q_, lambda e: e.dma_start(out=dst, in_=q_[:, :]), r=[q_])
            for t in range(4):
                r0 = s * 512 + t * 128
                if os.environ.get("SKIP_KV") == "1":
                    continue
                for g in range(3):
                    ps = tm_chunk(aTt, t, CSW + 768 * g + 256, 512)
                    v_ = vsb[g]
                    S.op("dve", lambda e: e.tensor_copy(out=v_[:, :], in_=ps[:, 256:512]), r=[ps], w=[v_])
                    S.dma("sp", v_, lambda e: e.dma_start(out=k.vss[g][r0 - (PRE - HALO):r0 - (PRE - HALO) + 128, :], in_=v_[:, :]), r=[v_])
                    wlen = 128 * DILS[g]
                    if r0 >= EXT - wlen:
                        f_ = kvf[g]
                        S.op("dve", lambda e: e.tensor_copy(out=f_[:, :], in_=ps[:, :]), r=[ps], w=[f_])
                        o0 = r0 - (EXT - wlen)
                        S.dma("sp", f_, lambda e: e.dma_start(out=k.p_win[g][o0:o0 + 128, :], in_=f_[:, :]), r=[f_])

        NT = NS * TS
        if os.environ.get("P1NOSMP"):
            k.end_phase()
            return
        S.dma("sp", xsm, lambda e: e.dma_start(out=xsm[0:NT, :], in_=k.xs[:, :]), w=[xsm])
        aTt = aT[0]
        norm_transpose(xsm, 0, aTt, 0)
        for s_ in range(NS):
            for j in range(3):
                S.dma("sp", xps, lambda e: e.dma_start(out=xps[:, :, s_, j], in_=k.st_conv[s_, j, :].rearrange("(c p) -> p c", p=128),
                                                       allow_slow_non_contiguous=True), w=[xps])
        for ci in range(12):
            ps = fm_chunk(aTt, NT, ci * 128)
            S.op("dve", lambda e: e.tensor_copy(out=xps[:, ci, :, 3:7], in_=ps[:, 0:NT].rearrange("p (s t) -> p s t", s=NS)),
                 r=[ps], w=[xps])
        for ci in range(12):
            for s_ in range(NS):
                S.dma("sp", xps, lambda e: e.dma_start(out=k.s_conv[s_, :, ci * 128:(ci + 1) * 128].rearrange("j p -> p j"),
                                                       in_=xps[:, ci, s_, 4:7], allow_slow_non_contiguous=True), r=[xps])
        for ci in range(12):
            ac = acc[ci % 2]
            av = ac[:, 0:NT].rearrange("p (s t) -> p s t", s=NS)
            S.op("dve", lambda e: e.tensor_scalar(out=av, in0=xps[:, ci, :, 0:4], scalar1=cw[:, ci, 0:1], scalar2=None, op0=ALU.mult),
                 r=[xps, cw], w=[ac])
            for j in range(1, 4):
                S.op("dve", lambda e: e.scalar_tensor_tensor(out=av, in0=xps[:, ci, :, j:j + 4], scalar=cw[:, ci, j:j + 1], in1=av,
                                                             op0=ALU.mult, op1=ALU.add), r=[xps, cw, ac], w=[ac])
            dn_post(None, NT, ci, ac, ac[:, 0:NT],
                    lambda ob: S.op("pool", lambda e: e.tensor_copy(out=k.smp_dn[:, ci, :], in_=ob[:, 0:NT]), r=[ob], w=[k.smp_dn]))
        for c in range(8):
            S.op("pe", lambda e: e.matmul(pg[:, :], lhsT=aTt[:, c, 0:128], rhs=wbf[:, c, CAG:CAG + 8],
                                          start=(c == 0), stop=(c == 7)), r=[wbf, aTt], w=[pg])
        gates(pg, NT, k.smp_gb, k.smp_gb[:, :])
        ps = tm_chunk(aTt, 0, CZ, 512)
        S.op("act", lambda e: e.activation(out=k.smp_z[:, :], in_=ps[0:NT, :], func=AF.Copy), r=[ps], w=[k.smp_z])
        for g in range(3):
            for which in range(2):
                for pair in range(2):
                    ps = fm_chunk(aTt, NT, CSW + 768 * g + 256 * which + 128 * pair)
                    S.op("act", lambda e: e.activation(out=k.smp_qk[:, g, which, pair, :], in_=ps[:, 0:NT], func=AF.Copy),
                         r=[ps], w=[k.smp_qk])
            ps = tm_chunk(aTt, 0, CSW + 768 * g + 256, 512)
            S.op("act", lambda e: e.activation(out=k.smp_kv[:, g, :], in_=ps[0:NT, :], func=AF.Copy), r=[ps], w=[k.smp_kv])
        k.end_phase()


def build_nc(debug=False, phases=(1, 5, 2, 3, 4)):
    nc = bass.Bass("TRN2", target_bir_lowering=False)
    k = K()
    k.nc = nc
    k.S = Sched(nc)
    k.debug = debug
    declare_io(k, debug)
    S = k.S
    with ExitStack() as es:
        A = lambda name, shape, dt: TL(es.enter_context(nc.sbuf_tensor(name, shape, dt)), name)
        k.cst = {}
        for i, n in enumerate(CONST_NAMES):
            t = A("c_" + n, [128, 128], F32)
            S.dma("sp", t, lambda e: e.dma_start(out=t[:, :], in_=k.consts[i, :, :]), w=[t])
            k.cst[n] = t
        k.ident_bf = A("ident_bf", [128, 128], BF16)
        S.op("dve", lambda e: e.tensor_copy(out=k.ident_bf[:, :], in_=k.cst["ident"][:, :]), r=[k.cst["ident"]], w=[k.ident_bf])
        k.ones_bf = A("ones_bf", [128, 128], BF16)
        S.op("dve", lambda e: e.tensor_copy(out=k.ones_bf[:, :], in_=k.cst["ones"][:, :]), r=[k.cst["ones"]], w=[k.ones_bf])
        NT = NS * TS
        k.smp_dn = A("smp_dn", [128, 12, NT], BF16)
        k.smp_gb = A("smp_gb", [NT, 8], F32)
        k.smp_z = A("smp_z", [NT, 512], BF16)
        k.smp_qk = A("smp_qk", [128, 3, 2, 2, NT], BF16)
        k.smp_kv = A("smp_kv", [NT, 3, 512], F32)
        k.smp_cat = A("smp_cat", [NT, 512], BF16)
        k.smp_swo = A("smp_swo", [NT, 3, 260], F32)
        if _os.environ.get("P2DBG"):
            k.dbg = nc.dram_tensor("dbg", [128, 4096], F32, kind="ExternalOutput")
        if 4 in phases and EXPROWS >= 1024:
            phase0_convert(k)
        if 1 in phases:
            phase1(k)
        if 5 in phases:
            phase_mem_and_windows(k)
        if 2 in phases:
            phase2a(k)
        if 3 in phases:
            phase2b(k)
        if 4 in phases:
            phase3(k)
        S.barrier()
        S.final_wait()
        S.emit()
    k.ninst = S.ninst
    return nc, k


def shard_inputs(inp):
    consts = np.stack([make_consts()[n] for n in CONST_NAMES]).astype(np.float32)
    maps = []
    c0 = make_consts()
    for c in range(NCORES):
        b, j = c // 2, c % 2
        xe = np.zeros((EXT, D), np.float32)
        if j == 0:
            xe[PRE:] = inp["x_prompt"][b, :MAIN]
            halo = np.full((128, 128), NEG, np.float32)
        else:
            xe[:] = inp["x_prompt"][b]
            halo = c0["swprev"]
        sl = slice(c * NS, (c + 1) * NS)
        m = {
            "xe": xe,
            "xs": np.ascontiguousarray(inp["x_sample"][sl].reshape(NS * TS, D)),
            "st_delta": np.ascontiguousarray(inp["state_delta"][0, sl]),
            "st_conv": np.ascontiguousarray(inp["state_conv"][0, sl]),
            "cwin0": np.ascontiguousarray(inp["cache_win1"][0, sl].reshape(NS, 128, 512)),
            "cwin1": np.ascontiguousarray(inp["cache_win2"][0, sl].reshape(NS, 512, 512)),
            "cwin2": np.ascontiguousarray(inp["cache_win3"][0, sl].reshape(NS, 2048, 512)),
            "cmem": np.ascontiguousarray(inp["cache_mem_kv"][0, sl].reshape(NS, 256, 2048)),
            "mem": np.ascontiguousarray(inp["mem_prompt"][b]),
            "halo": halo,
            "consts": consts,
            "g_mix": inp["g_mix"][0], "w_in": inp["w_in"][0], "conv_w": inp["conv_w"][0], "a_log": inp["a_log"][0],
            "dt_bias": inp["dt_bias"][0], "g_onorm": inp["g_onorm"][0], "w_out": inp["w_out"][0], "g_memq": inp["g_memq"][0],
            "g_memkv": inp["g_memkv"][0], "w_mq": inp["w_mq"][0], "w_mkv": inp["w_mkv"][0], "w_mo": inp["w_mo"][0],
            "g_ffn": inp["g_ffn"][0], "w_pq": inp["w_pq"][0], "sub_keys": inp["sub_keys"][0].reshape(16, 128, 128),
            "expert_u": inp["expert_u"][0][:EXPROWS], "expert_v": inp["expert_v"][0][:EXPROWS], "g_final": inp["g_final"],
        }
        maps.append({kk: np.ascontiguousarray(np.asarray(v, dtype=np.float32)) for kk, v in m.items()})
    return maps


def assemble(res):
    B, SEQ, DB = 4, 8192, 32
    y_prompt = np.zeros((B, SEQ, D), np.float32)
    y_sample = np.zeros((DB, TS, D), np.float32)
    p_delta = np.zeros((1, B, 4, 128, 128), np.float32)
    p_conv = np.zeros((1, B, 3, 1536), np.float32)
    p_win = [np.zeros((1, B, 128 * d, 2, 4, 64), np.float32) for d in DILS]
    p_mem = np.zeros((1, B, 256, 2, 4, 256), np.float32)
    s_delta = np.zeros((1, DB, 4, 128, 128), np.float32)
    s_conv = np.zeros((1, DB, 3, 1536), np.float32)
    s_win = [np.zeros((1, DB, 128 * d, 2, 4, 64), np.float32) for d in DILS]
    for c in range(NCORES):
        r = res[c]
        b, j = c // 2, c % 2
        y_prompt[b, j * MAIN:(j + 1) * MAIN] = r["y_main"]
        sl = slice(c * NS, (c + 1) * NS)
        y_sample[sl] = r["y_smp"].reshape(NS, TS, D)
        s_delta[0, sl] = r["s_delta"]
        s_conv[0, sl] = r["s_conv"]
        for g in range(3):
            s_win[g][0, sl] = r[f"s_win{g}"].reshape(NS, 128 * DILS[g], 2, 4, 64)
        if j == 1:
            p_delta[0, b] = r["p_delta"]
            p_conv[0, b] = r["p_conv"]
            for g in range(3):
                p_win[g][0, b] = r[f"p_win{g}"].reshape(128 * DILS[g], 2, 4, 64)
            p_mem[0, b] = r["p_mem"].reshape(256, 2, 4, 256)
    return (y_prompt, y_sample, p_delta, p_conv, p_win[0], p_win[1], p_win[2], p_mem,
            s_delta, s_conv, s_win[0], s_win[1], s_win[2])


def kernel(**inputs):
    inp = {kk: np.asarray(v) for kk, v in inputs.items()}
    nc, k = build_nc()
    maps = shard_inputs(inp)
    res = run_bass_kernel_spmd(nc, maps, core_ids=list(range(NCORES)))
    return assemble(res.results)


def phase0_convert(k):
    nc, S = k.nc, k.S
    with ExitStack() as es:
        A = lambda name, shape, dt: k.track(TL(es.enter_context(nc.sbuf_tensor(name, shape, dt)), name))
        st = [A(f"cv_f{i}", [128, 8192], F32) for i in range(2)]
        oa = [A(f"cv_a{i}", [128, 4096], BF16) for i in range(2)]
        ob = [A(f"cv_b{i}", [128, 4096], BF16) for i in range(2)]
        n = 0
        for (src_, c0_) in ((k.expert_u, 0), (k.expert_v, D)):
            for ps_ in range(EXPROWS // 1024):
                s_, a_, b_ = st[n % 2], oa[n % 2], ob[n % 2]
                r0 = ps_ * 1024
                S.dma("sp", s_, lambda e: e.dma_start(out=s_[:, :], in_=src_[r0:r0 + 1024, :].rearrange("(p j) d -> p (j d)", j=8)), w=[s_])
                S.op("act", lambda e: e.activation(out=a_[:, :], in_=s_[:, 0:4096], func=AF.Copy), r=[s_], w=[a_])
                S.op("dve", lambda e: e.tensor_copy(out=b_[:, :], in_=s_[:, 4096:8192]), r=[s_], w=[b_])
                dview = k.euv_bf[r0:r0 + 1024, c0_:c0_ + D].rearrange("(p j) d -> p j d", j=8)
                S.dma("pool", a_, lambda e: e.dma_start(out=dview[:, 0:4, :], in_=a_[:, :].rearrange("p (j d) -> p j d", j=4)), r=[a_])
                S.dma("pool", b_, lambda e: e.dma_start(out=dview[:, 4:8, :], in_=b_[:, :].rearrange("p (j d) -> p j d", j=4)), r=[b_])
                n += 1
        k.end_phase()


def phase_mem_and_windows(k):
    nc, S = k.nc, k.S
    d2d = k.track(TL(None, "d2d"))
    for g in range(3):
        wb = 128 * DILS[g]
        for s_ in range(NS):
            S.dma("sp", d2d, lambda e: e.dma_start(out=k.s_win[g][s_, 0:wb - TS, :], in_=k.cwin[g][s_, TS:wb, :]))
            S.dma("sp", k.smp_kv, lambda e: e.dma_start(out=k.s_win[g][s_, wb - TS:wb, :], in_=k.smp_kv[s_ * TS:(s_ + 1) * TS, g, :]),
                  r=[k.smp_kv])
    with ExitStack() as es:
        A = lambda name, shape, dt: k.track(TL(es.enter_context(nc.sbuf_tensor(name, shape, dt)), name))
        P = lambda name, shape, dt: TL(es.enter_context(nc.psum_tensor(name, shape, dt)), name)
        wbf = load_weight_bf16(k, es, "w_mkv", k.w_mkv, D, 2048, gdram=k.g_memkv, col_chunk=2048)
        xt = [A(f"mxt{i}", [128, D], F32) for i in range(2)]
        sqj = A("msqj", [128, D], BF16)
        ssq = A("mssq", [128, 1], F32)
        rstd = A("mrstd", [128, 1], F32)
        ab = A("mab", [128, D], BF16)
        aT = A("maT", [128, 8, 256], BF16)
        ob = [A(f"mob{i}", [128, 512], F32) for i in range(2)]
        pT = P("mpT", [128, 8, 128], BF16)
        ps_ = [P(f"mps{i}", [128, 512], F32) for i in range(2)]
        for t in range(2):
            x_ = xt[t]
            S.dma("sp", x_, lambda e: e.dma_start(out=x_[:, :], in_=k.mem[t * 128:(t + 1) * 128, :]), w=[x_])
            S.op("act", lambda e: e.activation(out=sqj[:, :], in_=x_[:, :], func=AF.Square, accum_out=ssq[:, 0:1]), r=[x_], w=[sqj, ssq])
            rsqrt(S, rstd, rstd[:, :], ssq, ssq[:, :], 1.0 / D, EPS)
            S.op("act", lambda e: e.activation(out=ab[:, :], in_=x_[:, :], func=AF.Copy, scale=rstd[:, 0:1]), r=[x_, rstd], w=[ab])
            for c in range(8):
                S.op("pe", lambda e: e.transpose(out=pT[:, c, :], in_=ab[:, c * 128:(c + 1) * 128], identity=k.ident_bf[:, :]),
                     r=[ab, k.ident_bf], w=[pT])
            S.op("dve", lambda e: e.tensor_copy(out=aT[:, :, t * 128:(t + 1) * 128], in_=pT[:, :, :]), r=[pT], w=[aT])
        n = 0
        for t in range(2):
            for nb in range(4):
                ps = ps_[n % 2]
                o_ = ob[n % 2]
                for c in range(8):
                    S.op("pe", lambda e: e.matmul(ps[:, :], lhsT=aT[:, c, t * 128:(t + 1) * 128], rhs=wbf[:, c, nb * 512:(nb + 1) * 512],
                                                  start=(c == 0), stop=(c == 7)), r=[aT, wbf], w=[ps])
                evac(S, n, o_, o_[:, :], ps, ps[:, :])
                S.dma("sp", o_, lambda e: e.dma_start(out=k.p_mem[t * 128:(t + 1) * 128, nb * 512:(nb + 1) * 512], in_=o_[:, :]), r=[o_])
                n += 1
        k.end_phase()


def phase2a(k):
    import os
    nc, S = k.nc, k.S
    with ExitStack() as es:
        A = lambda name, shape, dt: k.track(TL(es.enter_context(nc.sbuf_tensor(name, shape, dt)), name))
        P = lambda name, shape, dt: TL(es.enter_context(nc.psum_tensor(name, shape, dt)), name)
        cst = k.cst
        ident, trit, blk, mTneg, mSpos, ones = (cst[n] for n in ("ident", "trit", "blk", "mTneg", "mSpos", "ones"))
        identb = k.ident_bf
        cbf = {}
        for n_ in ("trit", "blk", "mTneg", "mSpos"):
            cbf[n_] = A("cbf_" + n_, [128, 128], BF16)
            S.op("dve", lambda e: e.tensor_copy(out=cbf[n_][:, :], in_=cst[n_][:, :]), r=[cst[n_]], w=[cbf[n_]])
        tritb, blkb, mTnegb, mSposb = (cbf[n_] for n_ in ("trit", "blk", "mTneg", "mSpos"))
        onesb = k.ones_bf
        hones = [A(f"hones{c_}", [128, 128], BF16) for c_ in range(2)]
        for c_ in range(2):
            S.op("dve", lambda e: e.tensor_copy(out=hones[c_][:, :], in_=cst["blk"][:, 127 * c_:127 * c_ + 1].to_broadcast([128, 128])),
                 r=[cst["blk"]], w=[hones[c_]])
        ghl = A("ghl", [128, 8], BF16)
        Gall_h = A("Gall_h", [128, 4, 128], BF16)
        Gall_l = A("Gall_l", [128, 4, 128], BF16)
        gon = A("gon", [128, 128], F32)
        S.dma("sp", gon, lambda e: e.dma_start(out=gon[:, :], in_=k.g_onorm.ap().partition_broadcast(128)), w=[gon])
        qT = [A(f"qT{i}", [128, 4, 128], BF16) for i in range(2)]
        kT = [A(f"kT{i}", [128, 4, 128], BF16) for i in range(2)]
        vT = [A(f"vT{i}", [128, 4, 128], BF16) for i in range(2)]
        gbt = [A(f"gbt{i}", [128, 8], F32) for i in range(2)]
        zt = [A(f"zt{i}", [128, 512], BF16) for i in range(2)]
        gc = A("gc", [128, 4], F32); ngc = A("ngc", [128, 4], F32); egc = A("egc", [128, 4], F32)
        negegc = A("negegc", [128, 4], F32); ekd = A("ekd", [128, 4], F32); dl = A("dl", [128, 8], F32)
        dif = A("dif", [128, 4], F32)
        dlraw = A("dlraw", [128, 8], F32)
        Gall = A("Gall", [128, 4, 128], F32)
        egcb = [A(f"egcb{i}", [128, 128], F32) for i in range(2)]
        gamT = [A(f"gamT{i}", [128, 128], F32) for i in range(2)]
        gamS = [A(f"gamS{i}", [128, 128], F32) for i in range(2)]
        Lx = [A(f"Lx{i}", [128, 128], BF16) for i in range(3)]
        Ly = [A(f"Ly{i}", [128, 128], BF16) for i in range(3)]
        Rr = [A(f"Rr{i}", [128, 128], BF16) for i in range(2)]
        AqkT = [[A(f"AqkT{i}_{h}", [128, 128], BF16) for h in range(4)] for i in range(2)]
        qgT = [[A(f"qgT{i}_{h}", [128, 128], BF16) for h in range(4)] for i in range(2)]
        TbT = [[A(f"TbT{i}_{h}", [128, 128], BF16) for h in range(4)] for i in range(2)]
        kd = [[A(f"kd{i}_{h}", [128, 128], BF16) for h in range(4)] for i in range(2)]
        vtok = [[A(f"vtok{i}_{h}", [128, 128], F32) for h in range(4)] for i in range(2)]
        sc_neg = [A(f"scneg{i}", [128, 4], F32) for i in range(2)]
        sc_dl = [A(f"scdl{i}", [128, 8], F32) for i in range(2)]
        rt = [A(f"rt{h}", [128, 128], BF16) for h in range(4)]
        ut = [A(f"ut{h}", [128, 128], BF16) for h in range(4)]
        St = [A(f"St{h}", [128, 128], F32) for h in range(4)]
        Sb = [A(f"Sb{h}", [128, 128], BF16) for h in range(4)]
        ot = [A(f"ot{i}", [128, 512], F32) for i in range(2)]
        ssq = A("ossq", [128, 4], F32); orstd = A("orstd", [128, 4], F32)
        sqj = A("osqj", [128, 128], F32)
        szt = A("szt", [128, 512], F32)
        t1 = A("t1", [128, 512], F32)
        og = [A(f"og{i}", [128, 512], BF16) for i in range(2)]
        pb = PsumBlocks(nc, es, "p2a_")
        P = lambda name, dt, bank: pb.get(name, 128, dt, bank)
        pKS = P("pKS", F32, "scan"); pU = P("pU", F32, "scan"); pO = P("pO", F32, "scan"); pdS = P("pdS", F32, "scan")
        pA = [P(f"pA{i}", F32, f"g{i}") for i in range(2)]
        pB = [P(f"pB{i}", F32, f"g{i}") for i in range(2)]
        pC = [P(f"pC{i}", F32, f"g{i}") for i in range(2)]
        pKK = [P(f"pKK{i}", F32, f"k{i}") for i in range(2)]
        pQK = [P(f"pQK{i}", F32, f"k{i}") for i in range(2)]
        pX = P("pX", F32, "nX"); pY = P("pY", F32, "nY"); pP = P("pP", F32, "nP")
        pgt = pb.get("pgt", 16, F32, "k0")
        pLT = P("pLT", F32, "nY"); pkt = P("pkt", F32, "nP"); pvt = P("pvt", F32, "nY")
        def mm(ps, out_ap, lt, lap, rt_, rap, start=True, stop=True):
            S.op("pe", lambda e: e.matmul(out_ap, lhsT=lap, rhs=rap, start=start, stop=stop), r=[lt, rt_], w=[ps])

        def prep(i, q_, k_, v_, g_):
            S.op("dve", lambda e: e.tensor_copy(out=ghl[:, 0:4], in_=g_[:, 0:4]), r=[g_], w=[ghl])
            S.op("dve", lambda e: e.tensor_tensor(out=ghl[:, 4:8], in0=g_[:, 0:4], in1=ghl[:, 0:4], op=ALU.subtract), r=[g_, ghl], w=[ghl])
            for (c0, lt_, lap) in ((0, tritb, tritb[:, :]), (4, blkb, blkb[:, :])):
                mm(pgt, pgt[:, c0:c0 + 4], lt_, lap, ghl, ghl[:, 0:4], True, False)
                mm(pgt, pgt[:, c0:c0 + 4], lt_, lap, ghl, ghl[:, 4:8], False, True)
            for c_ in range(2):
                mm(pgt, pgt[:, 8 + 4 * c_:12 + 4 * c_], hones[c_], hones[c_][:, :], ghl, ghl[:, 0:4], True, False)
                mm(pgt, pgt[:, 8 + 4 * c_:12 + 4 * c_], hones[c_], hones[c_][:, :], ghl, ghl[:, 4:8], False, True)
            S.op("dve", lambda e: e.tensor_copy(out=gc[:, :], in_=pgt[:, 0:4]), r=[pgt], w=[gc])
            S.op("dve", lambda e: e.tensor_scalar(out=ngc[:, :], in0=gc[:, :], scalar1=-1.0, scalar2=None, op0=ALU.mult), r=[gc], w=[ngc])
            S.op("act", lambda e: e.activation(out=egc[:, :], in_=gc[:, :], func=AF.Exp), r=[gc], w=[egc])
            S.op("dve", lambda e: e.tensor_scalar(out=sc_neg[i][:, :], in0=egc[:, :], scalar1=-1.0, scalar2=None, op0=ALU.mult),
                 r=[egc], w=[sc_neg[i]])
            S.op("dve", lambda e: e.tensor_tensor(out=dif[:, :], in0=pgt[:, 4:8], in1=gc[:, :], op=ALU.subtract), r=[pgt, gc], w=[dif])
            S.op("act", lambda e: e.activation(out=ekd[:, :], in_=dif[:, :], func=AF.Exp), r=[dif], w=[ekd])
            S.op("dve", lambda e: e.tensor_copy(out=dlraw[:, :], in_=pgt[:, 8:16]), r=[pgt], w=[dlraw])
            S.op("act", lambda e: e.activation(out=sc_dl[i][:, :], in_=dlraw[:, :], func=AF.Exp), r=[dlraw], w=[sc_dl[i]])
            stg = int(os.environ.get("P2PREP", "9"))
            if stg < 1:
                return
            for h in range(4):
                S.op("dve", lambda e: e.tensor_copy(out=Gall_h[:, h, :], in_=ghl[:, h:h + 1].to_broadcast([128, 128])), r=[ghl], w=[Gall_h])
                S.op("dve", lambda e: e.tensor_copy(out=Gall_l[:, h, :], in_=ghl[:, 4 + h:5 + h].to_broadcast([128, 128])), r=[ghl], w=[Gall_l])
            for h in range(4):
                j = h % 2
                if stg < 2:
                    continue
                mm(pA[j], pA[j][:, :], Gall_h, Gall_h[:, h, :], tritb, tritb[:, :], True, False)
                mm(pA[j], pA[j][:, :], Gall_l, Gall_l[:, h, :], tritb, tritb[:, :], False, True)
                mm(pB[j], pB[j][:, :], Gall_h, Gall_h[:, h, :], tritb, tritb[:, :], True, False)
                mm(pB[j], pB[j][:, :], Gall_l, Gall_l[:, h, :], tritb, tritb[:, :], False, False)
                mm(pB[j], pB[j][:, :], identb, identb[:, :], mTnegb, mTnegb[:, :], False, True)
                mm(pC[j], pC[j][:, :], Gall_h, Gall_h[:, h, :], tritb, tritb[:, :], True, False)
                mm(pC[j], pC[j][:, :], Gall_l, Gall_l[:, h, :], tritb, tritb[:, :], False, False)
                mm(pC[j], pC[j][:, :], identb, identb[:, :], mSposb, mSposb[:, :], False, True)
                S.op("act", lambda e: e.activation(out=egcb[j][:, :], in_=pA[j][:, :], func=AF.Exp), r=[pA[j]], w=[egcb[j]])
                S.op("act", lambda e: e.activation(out=gamT[j][:, :], in_=pB[j][:, :], func=AF.Exp, bias=ngc[:, h:h + 1]),
                     r=[pB[j], ngc], w=[gamT[j]])
                S.op("act", lambda e: e.activation(out=gamS[j][:, :], in_=pC[j][:, :], func=AF.Exp, bias=gc[:, h:h + 1], scale=-1.0),
                     r=[pC[j], gc], w=[gamS[j]])
                if stg < 3:
                    continue
                mm(pKK[j], pKK[j][:, :], k_, k_[:, h, :], k_, k_[:, h, :])
                mm(pQK[j], pQK[j][:, :], k_, k_[:, h, :], q_, q_[:, h, :])
                X, Y = Lx[0], Ly[0]
                S.op("dve", lambda e: e.scalar_tensor_tensor(out=X[:, :], in0=pKK[j][:, :], scalar=g_[:, 4 + h:5 + h], in1=gamS[j][:, :],
                                                             op0=ALU.mult, op1=ALU.mult), r=[pKK[j], g_, gamS[j]], w=[X])
                S.op("dve", lambda e: e.tensor_tensor(out=AqkT[i][h][:, :], in0=pQK[j][:, :], in1=gamT[j][:, :], op=ALU.mult),
                     r=[pQK[j], gamT[j]], w=[AqkT[i][h]])
                S.op("pool", lambda e: e.tensor_tensor(out=qgT[i][h][:, :], in0=q_[:, h, :], in1=egcb[j][:, :], op=ALU.mult),
                     r=[q_, egcb[j]], w=[qgT[i][h]])
                if stg < 4:
                    continue
                exp_ = os.environ.get("P2EXP", "")
                if exp_ == "A":
                    mm(pLT, pLT[:, :], identb, identb[:, :], identb, identb[:, :])
                else:
                    mm(pLT, pLT[:, :], X, X[:, :], identb, identb[:, :])
                if exp_ == "D":
                    S.op("act", lambda e: e.activation(out=Y[:, :], in_=pLT[:, :], func=AF.Copy), r=[pLT], w=[Y])
                elif exp_ != "B":
                    S.op("dve", lambda e: e.tensor_copy(out=Y[:, :], in_=pLT[:, :]), r=[pLT], w=[Y])
                sub = os.environ.get("P2SUB", "z")
                if sub == "a":
                    continue
                R = Rr[0]
                S.op("pool", lambda e: e.tensor_tensor(out=R[:, :], in0=identb[:, :], in1=Y[:, :], op=ALU.subtract), r=[identb, Y], w=[R])
                if sub == "b":
                    continue
                for it in range(1, 6):
                    Xn, Yn = Lx[it % 3], Ly[it % 3]
                    mm(pX, pX[:, :], Y, Y[:, :], X, X[:, :])
                    if sub == "c":
                        break
                    if it < 5:
                        mm(pY, pY[:, :], X, X[:, :], Y, Y[:, :])
                    S.op("act", lambda e: e.activation(out=Xn[:, :], in_=pX[:, :], func=AF.Copy), r=[pX], w=[Xn])
                    if it < 5:
                        S.op("dve", lambda e: e.tensor_copy(out=Yn[:, :], in_=pY[:, :]), r=[pY], w=[Yn])
                    mm(pP, pP[:, :], Xn, Xn[:, :], R, R[:, :])
                    Rn = Rr[it % 2]
                    S.op("dve", lambda e: e.tensor_tensor(out=Rn[:, :], in0=pP[:, :], in1=R[:, :], op=ALU.add), r=[pP, R], w=[Rn])
                    X, Y, R = Xn, Yn, Rn
                if stg < 5:
                    continue
                S.op("pool", lambda e: e.tensor_scalar(out=TbT[i][h][:, :], in0=R[:, :], scalar1=g_[:, 4 + h:5 + h], scalar2=None, op0=ALU.mult),
                     r=[R, g_], w=[TbT[i][h]])
                mm(pkt, pkt[:, :], k_, k_[:, h, :], identb, identb[:, :])
                S.op("dve", lambda e: e.tensor_scalar(out=kd[i][h][:, :], in0=pkt[:, :], scalar1=ekd[:, h:h + 1], scalar2=None, op0=ALU.mult),
                     r=[pkt, ekd], w=[kd[i][h]])
                mm(pvt, pvt[:, :], v_, v_[:, h, :], identb, identb[:, :])
                S.op("dve", lambda e: e.tensor_copy(out=vtok[i][h][:, :], in_=pvt[:, :]), r=[pvt], w=[vtok[i][h]])

        def scan(i, k_, o_, pre=None, post=None):
            for c in range(2):
                ps_ = slice(64 * c, 64 * c + 64)
                if pre:
                    pre(c)
                for h in range(4):
                    mm(pKS, pKS[:, :], k_, k_[:, h, :], Sb[h], Sb[h][:, :])
                    S.op("dve", lambda e: e.scalar_tensor_tensor(out=rt[h][ps_, :], in0=pKS[ps_, :], scalar=sc_neg[i][ps_, h:h + 1],
                                                                 in1=vtok[i][h][ps_, :], op0=ALU.mult, op1=ALU.add),
                         r=[pKS, sc_neg[i], vtok[i][h]], w=[rt[h]])
                    mm(pU, pU[:, :], TbT[i][h], TbT[i][h][ps_, :], rt[h], rt[h][ps_, :])
                    S.op("act", lambda e: e.activation(out=ut[h][ps_, :], in_=pU[ps_, :], func=AF.Copy), r=[pU], w=[ut[h]])
                    mm(pO, pO[:, :], qgT[i][h], qgT[i][h][:, :], Sb[h], Sb[h][:, :], True, False)
                    mm(pO, pO[:, :], AqkT[i][h], AqkT[i][h][ps_, :], ut[h], ut[h][ps_, :], False, True)
                    S.op("act", lambda e: e.activation(out=o_[ps_, h * 128:(h + 1) * 128], in_=pO[ps_, :], func=AF.Copy), r=[pO], w=[o_])
                    mm(pdS, pdS[:, :], kd[i][h], kd[i][h][ps_, :], ut[h], ut[h][ps_, :])
                    S.op("dve", lambda e: e.scalar_tensor_tensor(out=St[h][:, :], in0=St[h][:, :], scalar=sc_dl[i][:, 4 * c + h:4 * c + h + 1],
                                                                 in1=pdS[:, :], op0=ALU.mult, op1=ALU.add),
                         r=[St[h], sc_dl[i], pdS], w=[St[h]])
                    S.op("act", lambda e: e.activation(out=Sb[h][:, :], in_=St[h][:, :], func=AF.Copy), r=[St[h]], w=[Sb[h]])
                if post:
                    post(c)

        def post_out(o_, z_, j, dst_fn):
            for h in range(4):
                S.op("act", lambda e: e.activation(out=sqj[:, :], in_=o_[:, h * 128:(h + 1) * 128], func=AF.Square, accum_out=ssq[:, h:h + 1]),
                     r=[o_], w=[sqj, ssq])
            rsqrt(S, orstd, orstd[:, :], ssq, ssq[:, :], 1.0 / 128, EPS)
            S.op("act", lambda e: e.activation(out=szt[:, :], in_=z_[:, :], func=AF.Silu), r=[z_], w=[szt])
            for h in range(4):
                S.op("dve", lambda e: e.scalar_tensor_tensor(out=t1[:, h * 128:(h + 1) * 128], in0=o_[:, h * 128:(h + 1) * 128],
                                                             scalar=orstd[:, h:h + 1], in1=gon[:, :], op0=ALU.mult, op1=ALU.mult),
                     r=[o_, orstd, gon], w=[t1])
            S.op("dve", lambda e: e.tensor_tensor(out=og[j][:, :], in0=t1[:, :], in1=szt[:, :], op=ALU.mult), r=[t1, szt], w=[og[j]])
            dst_fn(og[j])

        for h in range(4):
            S.op("dve", lambda e: e.memset(St[h][:, :], 0.0), w=[St[h]])
            S.op("dve", lambda e: e.memset(Sb[h][:, :], 0.0), w=[Sb[h]])
        ntile = EXT // 128
        tiles = range(int(os.environ.get("P2START", "0")), int(os.environ.get("P2TILES", str(ntile))))
        for tau in tiles:
            i = tau % 2
            t0 = tau * 128
            for h in range(4):
                S.dma("sp", qT[i], lambda e: e.dma_start(out=qT[i][:, h, :], in_=k.dnq[h, :, t0:t0 + 128]), w=[qT[i]])
                S.dma("sp", kT[i], lambda e: e.dma_start(out=kT[i][:, h, :], in_=k.dnk[h, :, t0:t0 + 128]), w=[kT[i]])
                S.dma("sp", vT[i], lambda e: e.dma_start(out=vT[i][:, h, :], in_=k.dnv[h, :, t0:t0 + 128]), w=[vT[i]])
            S.dma("sp", gbt[i], lambda e: e.dma_start(out=gbt[i][:, :], in_=k.gbs[t0:t0 + 128, :]), w=[gbt[i]])
            main = t0 >= PRE
            if main:
                S.dma("sp", zt[i], lambda e: e.dma_start(out=zt[i][:, :], in_=k.zs[t0 - PRE:t0 - PRE + 128, :]), w=[zt[i]])
            mode = os.environ.get("P2MODE", "all")
            if mode == "load":
                continue
            prep(i, qT[i], kT[i], vT[i], gbt[i])
            if mode == "prep":
                continue
            scan(i, kT[i], ot[i])
            if main:
                post_out(ot[i], zt[i], i,
                         lambda o_: S.dma("sp", o_, lambda e: e.dma_start(out=k.cat[t0 - PRE:t0 - PRE + 128, :], in_=o_[:, :]), r=[o_]))
        for h in range(4):
            S.dma("sp", St[h], lambda e: e.dma_start(out=k.p_delta[h, :, :], in_=St[h][:, :]), r=[St[h]])

        for tb in range(2 if os.environ.get("P2MODE", "all") == "all" else 0):
            i = tb
            for tt in (qT[i], kT[i], vT[i]):
                S.op("pool", lambda e: e.memset(tt[:, :, :], 0.0), w=[tt])
            S.op("pool", lambda e: e.memset(gbt[i][:, :], 0.0), w=[gbt[i]])
            S.op("pool", lambda e: e.memset(zt[i][:, :], 0.0), w=[zt[i]])
            for c in range(2):
                sq = tb * 2 + c
                for h in range(4):
                    for which, tt in enumerate((qT[i], kT[i], vT[i])):
                        S.op("pool", lambda e: e.tensor_copy(out=tt[:, h, 64 * c:64 * c + TS], in_=k.smp_dn[:, which * 4 + h, sq * TS:(sq + 1) * TS]),
                             r=[k.smp_dn], w=[tt])
                S.dma("sp", gbt[i], lambda e: e.dma_start(out=gbt[i][64 * c:64 * c + TS, :], in_=k.smp_gb[sq * TS:(sq + 1) * TS, :]),
                      r=[k.smp_gb], w=[gbt[i]])
                S.dma("sp", zt[i], lambda e: e.dma_start(out=zt[i][64 * c:64 * c + TS, :], in_=k.smp_z[sq * TS:(sq + 1) * TS, :]),
                      r=[k.smp_z], w=[zt[i]])
            prep(i, qT[i], kT[i], vT[i], gbt[i])

            def pre(c, tb=tb):
                sq = tb * 2 + c
                for h in range(4):
                    S.dma("sp", St[h], lambda e: e.dma_start(out=St[h][:, :], in_=k.st_delta[sq, h, :, :]), w=[St[h]])
                    S.op("act", lambda e: e.activation(out=Sb[h][:, :], in_=St[h][:, :], func=AF.Copy), r=[St[h]], w=[Sb[h]])

            def post(c, tb=tb):
                sq = tb * 2 + c
                for h in range(4):
                    S.dma("sp", St[h], lambda e: e.dma_start(out=k.s_delta[sq, h, :, :], in_=St[h][:, :]), r=[St[h]])

            scan(i, kT[i], ot[i], pre, post)
            if tb == 0 and os.environ.get("P2DBG"):
                col = 0
                dbgt = TL(None, "dbgsem")
                for tt, wdt in ((TbT[0][0], 128), (AqkT[0][0], 128), (kd[0][0], 128), (qgT[0][0], 128), (vtok[0][0], 128),
                                (sc_neg[0], 4), (sc_dl[0], 8), (ot[0], 512), (rt[0], 128), (ut[0], 128), (St[0], 128), (gbt[0], 8)):
                    S.dma("pool", TL(None, f"dbg{col}"), lambda e: e.dma_start(out=k.dbg[:, col:col + wdt], in_=tt[:, 0:wdt]), r=[tt])
                    col += wdt

            def dst(o_, tb=tb):
                for c in range(2):
                    sq = tb * 2 + c
                    S.dma("sp", o_, lambda e: e.dma_start(out=k.smp_cat[sq * TS:(sq + 1) * TS, :], in_=o_[64 * c:64 * c + TS, :]),
                          r=[o_], w=[k.smp_cat])
            post_out(ot[i], zt[i], i, dst)
        k.end_phase()


def phase2b(k):
    import os
    nc, S = k.nc, k.S
    with ExitStack() as es:
        A = lambda name, shape, dt: k.track(TL(es.enter_context(nc.sbuf_tensor(name, shape, dt)), name))
        pb = PsumBlocks(nc, es, "p2b_")
        identb, onesb = k.ident_bf, k.ones_bf
        mprev = A("mprev", [128, 128], BF16); mcur = A("mcur", [128, 128], BF16); mhalo = A("mhalo", [128, 128], BF16)
        halo_f = A("halo_f", [128, 128], F32)
        S.dma("sp", halo_f, lambda e: e.dma_start(out=halo_f[:, :], in_=k.halo[:, :]), w=[halo_f])
        S.op("dve", lambda e: e.tensor_copy(out=mprev[:, :], in_=k.cst["swprev"][:, :]), r=[k.cst["swprev"]], w=[mprev])
        S.op("dve", lambda e: e.tensor_copy(out=mcur[:, :], in_=k.cst["swcur"][:, :]), r=[k.cst["swcur"]], w=[mcur])
        S.op("dve", lambda e: e.tensor_copy(out=mhalo[:, :], in_=halo_f[:, :]), r=[halo_f], w=[mhalo])
        QT = [A(f"QT{i}", [128, 2, 2048], BF16) for i in range(2)]
        KT = [A(f"KT{i}", [128, 2, 4096], BF16) for i in range(2)]
        Vb = [A(f"Vb{i}", [128, 2, 256], BF16) for i in range(2)]
        PT = [A(f"PT{i}", [128, 256], BF16) for i in range(2)]
        ot = [A(f"swot{i}", [128, 260], F32) for i in range(2)]
        psS = [pb.get(f"psS{i}", 256, F32, f"s{i}") for i in range(2)]
        psO = [pb.get(f"psO{i}", 128, F32, f"o{i}") for i in range(2)]
        def core(qf, kf, vf, msk0, o_, deps):
            qt_, kt_, vt_ = deps
            for head in range(4):
                pair, hh = head // 2, head % 2
                ph = slice(64 * hh, 64 * hh + 64)
                sS, sO, p_ = psS[head % 2], psO[head % 2], PT[head % 2]
                for kc in range(2):
                    msk = msk0 if kc == 0 else mcur
                    S.op("pe", lambda e: e.matmul(sS[:, kc * 128:(kc + 1) * 128], lhsT=identb[:, :], rhs=msk[:, :], start=True, stop=False),
                         r=[identb, msk], w=[sS])
                    S.op("pe", lambda e: e.matmul(sS[:, kc * 128:(kc + 1) * 128], lhsT=kf(kc, pair, ph), rhs=qf(pair, ph), start=False, stop=True),
                         r=[kt_, qt_], w=[sS])
                S.op("act", lambda e: e.activation(out=p_[:, :], in_=sS[:, :], func=AF.Exp, scale=0.125), r=[sS], w=[p_])
                for kc in range(2):
                    S.op("pe", lambda e: e.matmul(sO[:, 0:64], lhsT=p_[:, kc * 128:(kc + 1) * 128], rhs=vf(kc, head),
                                                  start=(kc == 0), stop=(kc == 1)), r=[p_, vt_], w=[sO])
                for kc in range(2):
                    S.op("pe", lambda e: e.matmul(sO[:, 64:65], lhsT=p_[:, kc * 128:(kc + 1) * 128], rhs=onesb[:, 0:1],
                                                  start=(kc == 0), stop=(kc == 1)), r=[p_, onesb], w=[sO])
                S.op("dve", lambda e: e.tensor_copy(out=o_[:, head * 65:(head + 1) * 65], in_=sO[:, 0:65]), r=[sO], w=[o_])

        groups = [int(x) for x in os.environ.get("P2BG", "0,1,2").split(",")]
        nblk_lim = int(os.environ.get("P2BN", "999"))
        u = 0
        for g in groups:
            d = DILS[g]
            span = 128 * d
            for n in range(min(MAIN // span, nblk_lim)):
                bi = (g * 64 + n) % 2
                q_, k_ = QT[bi], KT[bi]
                for pair in range(2):
                    S.dma("sp", q_, lambda e: e.dma_start(out=q_[:, pair, 0:span], in_=k.qts[g][pair, :, n * span:(n + 1) * span]), w=[q_])
                    k0 = HALO + (n - 1) * span
                    S.dma("sp", k_, lambda e: e.dma_start(out=k_[:, pair, 0:2 * span], in_=k.kts[g][pair, :, k0:k0 + 2 * span]), w=[k_])
                for r in range(d):
                    v_ = Vb[u % 2]
                    o_ = ot[u % 2]
                    v0 = HALO + (n - 1) * span + r
                    for kc in range(2):
                        S.dma("sp", v_, lambda e: e.dma_start(out=v_[:, kc, :],
                                                             in_=k.vss[g][v0 + kc * span:v0 + kc * span + 127 * d + 1:d, :]), w=[v_])
                    core(lambda pair, ph: q_[ph, pair, r:span:d],
                         lambda kc, pair, ph: k_[ph, pair, kc * span + r:(kc + 1) * span:d],
                         lambda kc, head: v_[:, kc, head * 64:(head + 1) * 64],
                         mhalo if n == 0 else mprev, o_, (q_, k_, v_))
                    t0 = n * span + r
                    S.dma("sp", o_, lambda e: e.dma_start(out=k.swo[g][t0:t0 + 127 * d + 1:d, :], in_=o_[:, :]), r=[o_])
                    u += 1
        if not os.environ.get("P2BNOSMP"):
            sq = A("sq", [128, 2, 128], BF16); sk = A("sk", [128, 2, 256], BF16); sv = A("sv", [128, 2, 256], BF16)
            ck = A("ck", [128, 512], F32); ckb = A("ckb", [128, 512], BF16); vst = A("vst", [128, 256], F32)
            pt = pb.get("ptr", 128, F32, "ptr")
            for s_ in range(NS):
                for g in range(3):
                    d = DILS[g]
                    for r in range(1 if d == 1 else TS):
                        nq = TS if d == 1 else 1
                        tok0 = s_ * TS + (0 if d == 1 else r)
                        o_ = ot[u % 2]
                        for tt in (sq, sk, sv):
                            S.op("pool", lambda e: e.memset(tt[:, :, :], 0.0), w=[tt])
                        S.op("pool", lambda e: e.memset(vst[:, :], 0.0), w=[vst])
                        S.dma("sp", ck, lambda e: e.dma_start(out=ck[:, :], in_=k.cwin[g][s_, r:r + 127 * d + 1:d, :]), w=[ck])
                        S.op("dve", lambda e: e.tensor_copy(out=ckb[:, :], in_=ck[:, :]), r=[ck], w=[ckb])
                        for pair in range(2):
                            S.op("pool", lambda e: e.tensor_copy(out=sq[:, pair, 0:nq], in_=k.smp_qk[:, g, 0, pair, tok0:tok0 + nq]), r=[k.smp_qk], w=[sq])
                            S.op("pool", lambda e: e.tensor_copy(out=sk[:, pair, 128:128 + nq], in_=k.smp_qk[:, g, 1, pair, tok0:tok0 + nq]),
                                 r=[k.smp_qk], w=[sk])
                            S.op("pe", lambda e: e.matmul(pt[:, :], lhsT=ckb[:, pair * 128:(pair + 1) * 128], rhs=identb[:, :], start=True, stop=True),
                                 r=[ckb, identb], w=[pt])
                            S.op("dve", lambda e: e.tensor_copy(out=sk[:, pair, 0:128], in_=pt[:, :]), r=[pt], w=[sk])
                        S.op("pool", lambda e: e.tensor_copy(out=sv[:, 0, :], in_=ckb[:, 256:512]), r=[ckb], w=[sv])
                        S.dma("sp", vst, lambda e: e.dma_start(out=vst[0:nq, :], in_=k.smp_kv[tok0:tok0 + nq, g, 256:512]), r=[k.smp_kv], w=[vst])
                        S.op("dve", lambda e: e.tensor_copy(out=sv[:, 1, :], in_=vst[:, :]), r=[vst], w=[sv])
                        core(lambda pair, ph: sq[ph, pair, :], lambda kc, pair, ph: sk[ph, pair, kc * 128:(kc + 1) * 128],
                             lambda kc, head: sv[:, kc, head * 64:(head + 1) * 64], mprev, o_, (sq, sk, sv))
                        S.dma("sp", o_, lambda e: e.dma_start(out=k.smp_swo[tok0:tok0 + nq, g, :], in_=o_[0:nq, :]), r=[o_], w=[k.smp_swo])
                        u += 1
        k.end_phase()


def phase3(k):
    import os
    nc, S = k.nc, k.S
    NT = NS * TS
    with ExitStack() as es:
        A = lambda name, shape, dt: k.track(TL(es.enter_context(nc.sbuf_tensor(name, shape, dt)), name))
        pb = PsumBlocks(nc, es, "p3_")
        identb, onesb = k.ident_bf, k.ones_bf
        iota = k.cst["iota"]
        KmT = A("KmT", [128, 8, 256], BF16); Vm = A("Vm", [128, 2, 1024], BF16)
        KsT = A("KsT", [128, NS, 8, 256], BF16); Vs = A("Vs", [128, NS, 2, 1024], BF16)
        pbig = [pb.get(f"pbig{i}", 512, F32, f"big{i}") for i in range(4)]
        with ExitStack() as es2:
            A2 = lambda name, shape, dt: k.track(TL(es2.enter_context(nc.sbuf_tensor(name, shape, dt)), name))
            wkv = load_weight_bf16(k, es2, "w_mkv3", k.w_mkv, D, 2048, gdram=k.g_memkv, col_chunk=2048)
            mx = A2("m3x", [128, D], F32); msq = A2("m3sq", [128, D], BF16); mss = A2("m3ss", [128, 1], F32)
            mrs = A2("m3rs", [128, 1], F32); mab = A2("m3ab", [128, D], BF16); maT = A2("m3aT", [128, 8, 256], BF16)
            cs = A2("m3cs", [128, 2048], F32); csb = A2("m3csb", [128, 2048], BF16)
            for t in range(2):
                S.dma("sp", mx, lambda e: e.dma_start(out=mx[:, :], in_=k.mem[t * 128:(t + 1) * 128, :]), w=[mx])
                S.op("act", lambda e: e.activation(out=msq[:, :], in_=mx[:, :], func=AF.Square, accum_out=mss[:, 0:1]), r=[mx], w=[msq, mss])
                rsqrt(S, mrs, mrs[:, :], mss, mss[:, :], 1.0 / D, EPS)
                S.op("act", lambda e: e.activation(out=mab[:, :], in_=mx[:, :], func=AF.Copy, scale=mrs[:, 0:1]), r=[mx, mrs], w=[mab])
                for half in range(2):
                    for c in range(4):
                        cc = half * 4 + c
                        S.op("pe", lambda e: e.matmul(pbig[0][:, c * 128:(c + 1) * 128], lhsT=mab[:, cc * 128:(cc + 1) * 128], rhs=identb[:, :],
                                                      start=True, stop=True), r=[mab, identb], w=[pbig[0]])
                    for c in range(4):
                        cc = half * 4 + c
                        S.op("dve", lambda e: e.tensor_copy(out=maT[:, cc, t * 128:(t + 1) * 128], in_=pbig[0][:, c * 128:(c + 1) * 128]),
                             r=[pbig[0]], w=[maT])
            for hc in range(8):
                for c in range(8):
                    S.op("pe", lambda e: e.matmul(pbig[1][:, 0:256], lhsT=wkv[:, c, hc * 128:(hc + 1) * 128], rhs=maT[:, c, :],
                                                  start=(c == 0), stop=(c == 7)), r=[wkv, maT], w=[pbig[1]])
                S.op("dve", lambda e: e.tensor_copy(out=KmT[:, hc, :], in_=pbig[1][:, 0:256]), r=[pbig[1]], w=[KmT])
            for kc in range(2):
                for nb in range(2):
                    for c in range(8):
                        S.op("pe", lambda e: e.matmul(pbig[2][:, :], lhsT=maT[:, c, kc * 128:(kc + 1) * 128], rhs=wkv[:, c, 1024 + nb * 512:1024 + (nb + 1) * 512],
                                                      start=(c == 0), stop=(c == 7)), r=[wkv, maT], w=[pbig[2]])
                    S.op("dve", lambda e: e.tensor_copy(out=Vm[:, kc, nb * 512:(nb + 1) * 512], in_=pbig[2][:, :]), r=[pbig[2]], w=[Vm])
            for s_ in range(NS):
                for kc in range(2):
                    S.dma("sp", cs, lambda e: e.dma_start(out=cs[:, :], in_=k.cmem[s_, kc * 128:(kc + 1) * 128, :]), w=[cs])
                    S.op("dve", lambda e: e.tensor_copy(out=csb[:, :], in_=cs[:, :]), r=[cs], w=[csb])
                    S.op("pool", lambda e: e.tensor_copy(out=Vs[:, s_, kc, :], in_=csb[:, 1024:2048]), r=[csb], w=[Vs])
                    for half in range(2):
                        for c in range(4):
                            hc = half * 4 + c
                            S.op("pe", lambda e: e.matmul(pbig[3][:, c * 128:(c + 1) * 128], lhsT=csb[:, hc * 128:(hc + 1) * 128], rhs=identb[:, :],
                                                          start=True, stop=True), r=[csb, identb], w=[pbig[3]])
                        for c in range(4):
                            hc = half * 4 + c
                            S.op("dve", lambda e: e.tensor_copy(out=KsT[:, s_, hc, kc * 128:(kc + 1) * 128], in_=pbig[3][:, c * 128:(c + 1) * 128]),
                                 r=[pbig[3]], w=[KsT])
            k.S.barrier()
        wout = load_weight_bf16(k, es, "w_out3", k.w_out, 768, D, col_chunk=1024)
        wmq = load_weight_bf16(k, es, "w_mq3", k.w_mq, D, D, gdram=k.g_memq, col_chunk=1024)
        wmo = load_weight_bf16(k, es, "w_mo3", k.w_mo, D, D, col_chunk=1024)
        wpq = load_weight_bf16(k, es, "w_pq3", k.w_pq, D, 2048, col_chunk=2048)
        skT = A("skT", [128, 16, 128], BF16)
        with ExitStack() as es2:
            A2 = lambda name, shape, dt: k.track(TL(es2.enter_context(nc.sbuf_tensor(name, shape, dt)), name))
            skf = A2("skf", [128, 128], F32); skb = A2("skb", [128, 128], BF16)
            for hp in range(16):
                S.dma("sp", skf, lambda e: e.dma_start(out=skf[:, :], in_=k.sub_keys[hp, :, :]), w=[skf])
                S.op("dve", lambda e: e.tensor_copy(out=skb[:, :], in_=skf[:, :]), r=[skf], w=[skb])
                S.op("pe", lambda e: e.matmul(pbig[0][:, 0:128], lhsT=skb[:, :], rhs=identb[:, :], start=True, stop=True), r=[skb, identb], w=[pbig[0]])
                S.op("dve", lambda e: e.tensor_copy(out=skT[:, hp, :], in_=pbig[0][:, 0:128]), r=[pbig[0]], w=[skT])
            k.S.barrier()
        gffn = A("gffn", [128, D], F32); gfin = A("gfin", [128, D], F32)
        S.dma("sp", gffn, lambda e: e.dma_start(out=gffn[:, :], in_=k.g_ffn.ap().partition_broadcast(128)), w=[gffn])
        S.dma("sp", gfin, lambda e: e.dma_start(out=gfin[:, :], in_=k.g_final.ap().partition_broadcast(128)), w=[gfin])
        LOHI_INIT = True
        xt = A("x3", [128, D], F32); cat = A("cat3", [128, 768], BF16); sw = [A(f"sw3_{g}", [128, 260], F32) for g in range(3)]
        rden = A("rden3", [128, 4], F32); catT = A("catT3", [128, 6, 128], BF16)
        h = A("h3", [128, D], F32); sqj = A("sqj3", [128, D], BF16); ssq = A("ssq3", [128, 1], F32); rstd = A("rstd3", [128, 1], F32)
        cb = A("cb3", [128, D], BF16); cT = A("cT3", [128, 8, 128], BF16)
        qmT = A("qmT3", [128, 8, 128], BF16); PTm = A("PTm3", [128, 256], BF16); rdm = A("rdm3", [128, 128], F32)
        attT = A("attT3", [128, 8, 128], BF16)
        fb = A("fb3", [128, D], BF16); fT = A("fT3", [128, 8, 128], BF16)
        qpT = A("qpT3", [128, 16, 128], BF16); sc = A("sc3", [128, 16, 128], F32); sc2 = A("sc23", [128, 128], F32)
        mv = A("mv3", [128, 16, 16], F32); mi = A("mi3", [128, 16, 16], U32); mif = A("mif3", [128, 16, 16], F32)
        cand = A("cand3", [128, 256], F32); cand2 = A("cand23", [128, 256], F32)
        cv = A("cv3", [128, 8, 16], F32); ci = A("ci3", [128, 8, 16], U32); cif = A("cif3", [128, 8, 16], F32)
        ia = A("ia3", [128, 8, 16], F32); ib = A("ib3", [128, 8, 16], F32)
        oh = A("oh3", [128, 16, 16], F32); lo16 = A("lo163", [128, 16], F32); hi16 = A("hi163", [128, 16], F32); i1 = A("i13", [128, 8, 16], F32); i2 = A("i23", [128, 8, 16], F32)
        eidf = A("eidf3", [128, 128], F32); eid = A("eid3", [128, 128], I32)
        gate = A("gate3", [128, 8, 16], F32); gsum = A("gsum3", [128, 8], F32)
        hid = A("hid3", [128, 128], F32); hx = A("hx3", [128, 128], F32); wgt = A("wgt3", [128, 128], F32)
        NB = 2
        GBr = [A(f"GB{i}", [128, 2 * D], BF16) for i in range(2)]
        GB = list(GBr)
        if not os.environ.get("P3NOSMP"):
            for s_ in range(NS):
                GB.append(TLview(Vs, (lambda s_=s_: Vs[:, s_, :, :].rearrange("p a b -> p (a b)")), f"GBv{s_}"))
                GB.append(TLview(KsT, (lambda s_=s_: KsT[:, s_, :, :].rearrange("p a b -> p (a b)")), f"GBk{s_}"))
        NBG = len(GB)
        par = lambda t_: [t_.p] if isinstance(t_, TLview) else []
        prod = [sqj, cb]
        junk = sqj; dg = [A(f"dg{i}", [128, 128], BF16) for i in range(2)]
        yo = xt
        pout = [pb.get(f"pout{i}", 512, F32, f"out{i}") for i in range(2)]
        psm = pb.get("psm", 256, F32, "sm"); pden = pb.get("pden", 128, F32, "den")

        S.op("dve", lambda e: e.tensor_scalar(out=lo16[:, :], in0=iota[:, 0:16], scalar1=16.0, scalar2=None, op0=ALU.mult), r=[iota], w=[lo16])
        S.op("dve", lambda e: e.tensor_scalar(out=hi16[:, :], in0=iota[:, 0:16], scalar1=16.0, scalar2=16.0, op0=ALU.mult, op1=ALU.add), r=[iota], w=[hi16])

        def transposes(src_t, nchunk, dstT):
            for c0 in range(0, nchunk, 4):
                n_ = min(4, nchunk - c0)
                for c in range(n_):
                    S.op("pe", lambda e: e.matmul(pbig[0][:, c * 128:(c + 1) * 128], lhsT=src_t[:, (c0 + c) * 128:(c0 + c + 1) * 128], rhs=identb[:, :],
                                                  start=True, stop=True), r=[src_t, identb], w=[pbig[0]])
                for c in range(n_):
                    S.op("dve", lambda e: e.tensor_copy(out=dstT[:, c0 + c, :], in_=pbig[0][:, c * 128:(c + 1) * 128]), r=[pbig[0]], w=[dstT])

        def rmsn(src, out_bf, gvec=None):
            S.op("act", lambda e: e.activation(out=sqj[:, :], in_=src[:, :], func=AF.Square, accum_out=ssq[:, 0:1]), r=[src], w=[sqj, ssq])
            rsqrt(S, rstd, rstd[:, :], ssq, ssq[:, :], 1.0 / D, EPS)
            if gvec is None:
                S.op("act", lambda e: e.activation(out=out_bf[:, :], in_=src[:, :], func=AF.Copy, scale=rstd[:, 0:1]), r=[src, rstd], w=[out_bf])
            else:
                S.op("dve", lambda e: e.scalar_tensor_tensor(out=out_bf[:, :], in0=src[:, :], scalar=rstd[:, 0:1], in1=gvec[:, :],
                                                             op0=ALU.mult, op1=ALU.mult), r=[src, rstd, gvec], w=[out_bf])

        def top16(vals_t, vals_ap, scratch_t, scratch_ap, mv_ap, mi_ap, mv_t, mi_t):
            S.op("dve", lambda e: e.max(out=mv_ap[:, 0:8], in_=vals_ap), r=[vals_t], w=[mv_t])
            S.op("dve", lambda e: e.max_index(out=mi_ap[:, 0:8], in_max=mv_ap[:, 0:8], in_values=vals_ap), r=[vals_t, mv_t], w=[mi_t])
            S.op("dve", lambda e: e.match_replace(out=scratch_ap, in_to_replace=mv_ap[:, 0:8], in_values=vals_ap, imm_value=-1e30),
                 r=[vals_t, mv_t], w=[scratch_t])
            S.op("dve", lambda e: e.max(out=mv_ap[:, 8:16], in_=scratch_ap), r=[scratch_t], w=[mv_t])
            S.op("dve", lambda e: e.max_index(out=mi_ap[:, 8:16], in_max=mv_ap[:, 8:16], in_values=scratch_ap), r=[scratch_t, mv_t], w=[mi_t])

        tiles = ([] if os.environ.get("P3NOSMP") else ["smp"]) + list(range(int(os.environ.get("P3TILES", str(MAIN // 128)))))
        for tau in tiles:
            smp = tau == "smp"
            if smp:
                S.op("dve", lambda e: e.memset(xt[:, :], 0.0), w=[xt])
                S.op("pool", lambda e: e.memset(cat[:, :], 0.0), w=[cat])
                for g in range(3):
                    S.op("pool", lambda e: e.memset(sw[g][:, :], 1.0), w=[sw[g]])
                    S.dma("sp", sw[g], lambda e: e.dma_start(out=sw[g][0:NT, :], in_=k.smp_swo[:, g, :]), r=[k.smp_swo], w=[sw[g]])
                S.dma("sp", xt, lambda e: e.dma_start(out=xt[0:NT, :], in_=k.xs[:, :]), w=[xt])
                S.dma("sp", cat, lambda e: e.dma_start(out=cat[0:NT, 0:512], in_=k.smp_cat[:, :]), r=[k.smp_cat], w=[cat])
            else:
                t0 = tau * 128
                S.dma("sp", xt, lambda e: e.dma_start(out=xt[:, :], in_=k.xe[PRE + t0:PRE + t0 + 128, :]), w=[xt])
                S.dma("sp", cat, lambda e: e.dma_start(out=cat[:, 0:512], in_=k.cat[t0:t0 + 128, :]), w=[cat])
                for g in range(3):
                    S.dma("sp", sw[g], lambda e: e.dma_start(out=sw[g][:, :], in_=k.swo[g][t0:t0 + 128, :]), w=[sw[g]])
            S.op("dve", lambda e: e.tensor_tensor(out=sw[0][:, :], in0=sw[0][:, :], in1=sw[1][:, :], op=ALU.add), r=[sw[0], sw[1]], w=[sw[0]])
            S.op("dve", lambda e: e.tensor_tensor(out=sw[0][:, :], in0=sw[0][:, :], in1=sw[2][:, :], op=ALU.add), r=[sw[0], sw[2]], w=[sw[0]])
            for hd in range(4):
                S.op("dve", lambda e: e.reciprocal(out=rden[:, hd:hd + 1], in_=sw[0][:, hd * 65 + 64:hd * 65 + 65]), r=[sw[0]], w=[rden])
                S.op("dve", lambda e: e.tensor_scalar(out=cat[:, 512 + hd * 64:512 + (hd + 1) * 64], in0=sw[0][:, hd * 65:hd * 65 + 64],
                                                      scalar1=rden[:, hd:hd + 1], scalar2=None, op0=ALU.mult), r=[sw[0], rden], w=[cat])
            transposes(cat, 6, catT)
            for nb in range(2):
                for c in range(6):
                    S.op("pe", lambda e: e.matmul(pout[nb][:, :], lhsT=catT[:, c, :], rhs=wout[:, c, nb * 512:(nb + 1) * 512],
                                                  start=(c == 0), stop=(c == 5)), r=[catT, wout], w=[pout[nb]])
                S.op("dve", lambda e: e.tensor_tensor(out=h[:, nb * 512:(nb + 1) * 512], in0=pout[nb][:, :], in1=xt[:, nb * 512:(nb + 1) * 512], op=ALU.add),
                     r=[pout[nb], xt], w=[h])
            rmsn(h, cb)
            transposes(cb, 8, cT)
            for half in range(2):
                for c4 in range(4):
                    hc = half * 4 + c4
                    for c in range(8):
                        S.op("pe", lambda e: e.matmul(pbig[1][:, c4 * 128:(c4 + 1) * 128], lhsT=wmq[:, c, hc * 128:(hc + 1) * 128], rhs=cT[:, c, :],
                                                      start=(c == 0), stop=(c == 7)), r=[wmq, cT], w=[pbig[1]])
                for c4 in range(4):
                    hc = half * 4 + c4
                    S.op("dve", lambda e: e.tensor_copy(out=qmT[:, hc, :], in_=pbig[1][:, c4 * 128:(c4 + 1) * 128]), r=[pbig[1]], w=[qmT])
            segs = [(s_ * TS, TS, s_) for s_ in range(NS)] if smp else [(0, 128, None)]
            if smp:
                S.op("pool", lambda e: e.memset(attT[:, :, :], 0.0), w=[attT])
            for hd in range(4):
                for (q0, qn, s_) in segs:
                    kT_ap = (lambda hc, kc: KmT[:, hc, kc * 128:(kc + 1) * 128]) if s_ is None else (lambda hc, kc: KsT[:, s_, hc, kc * 128:(kc + 1) * 128])
                    v_ap = (lambda kc, col: Vm[:, kc, col:col + 128]) if s_ is None else (lambda kc, col: Vs[:, s_, kc, col:col + 128])
                    kt_t, v_t = (KmT, Vm) if s_ is None else (KsT, Vs)
                    for kc in range(2):
                        for cc in range(2):
                            S.op("pe", lambda e: e.matmul(psm[:, kc * 128:kc * 128 + qn], lhsT=kT_ap(hd * 2 + cc, kc), rhs=qmT[:, hd * 2 + cc, q0:q0 + qn],
                                                          start=(cc == 0), stop=(cc == 1)), r=[kt_t, qmT], w=[psm])
                    for kc in range(2):
                        S.op("act", lambda e: e.activation(out=PTm[:, kc * 128:kc * 128 + qn], in_=psm[:, kc * 128:kc * 128 + qn], func=AF.Exp, scale=1.0 / 16),
                             r=[psm], w=[PTm])
                    for kc in range(2):
                        S.op("pe", lambda e: e.matmul(pden[:, 0:qn], lhsT=onesb[:, :], rhs=PTm[:, kc * 128:kc * 128 + qn], start=(kc == 0), stop=(kc == 1)),
                             r=[onesb, PTm], w=[pden])
                    S.op("dve", lambda e: e.reciprocal(out=rdm[:, 0:qn], in_=pden[:, 0:qn]), r=[pden], w=[rdm])
                    for cc in range(2):
                        for kc in range(2):
                            S.op("pe", lambda e: e.matmul(pbig[2][:, cc * 128:cc * 128 + qn], lhsT=v_ap(kc, hd * 256 + cc * 128), rhs=PTm[:, kc * 128:kc * 128 + qn],
                                                          start=(kc == 0), stop=(kc == 1)), r=[v_t, PTm], w=[pbig[2]])
                    for cc in range(2):
                        S.op("dve", lambda e: e.tensor_tensor(out=attT[:, hd * 2 + cc, q0:q0 + qn], in0=pbig[2][:, cc * 128:cc * 128 + qn], in1=rdm[:, 0:qn], op=ALU.mult),
                             r=[pbig[2], rdm], w=[attT])
            for nb in range(2):
                for c in range(8):
                    S.op("pe", lambda e: e.matmul(pout[nb][:, :], lhsT=attT[:, c, :], rhs=wmo[:, c, nb * 512:(nb + 1) * 512],
                                                  start=(c == 0), stop=(c == 7)), r=[attT, wmo], w=[pout[nb]])
                S.op("dve", lambda e: e.tensor_tensor(out=h[:, nb * 512:(nb + 1) * 512], in0=pout[nb][:, :], in1=h[:, nb * 512:(nb + 1) * 512], op=ALU.add),
                     r=[pout[nb], h], w=[h])
            rmsn(h, fb, gffn)
            transposes(fb, 8, fT)
            for q4 in range(4):
                for c4 in range(4):
                    hp = q4 * 4 + c4
                    for c in range(8):
                        S.op("pe", lambda e: e.matmul(pbig[1][:, c4 * 128:(c4 + 1) * 128], lhsT=wpq[:, c, hp * 128:(hp + 1) * 128], rhs=fT[:, c, :],
                                                      start=(c == 0), stop=(c == 7)), r=[wpq, fT], w=[pbig[1]])
                for c4 in range(4):
                    hp = q4 * 4 + c4
                    S.op("dve", lambda e: e.tensor_copy(out=qpT[:, hp, :], in_=pbig[1][:, c4 * 128:(c4 + 1) * 128]), r=[pbig[1]], w=[qpT])
            for q4 in range(4):
                for c4 in range(4):
                    hp = q4 * 4 + c4
                    S.op("pe", lambda e: e.matmul(pbig[3][:, c4 * 128:(c4 + 1) * 128], lhsT=qpT[:, hp, :], rhs=skT[:, hp, :], start=True, stop=True),
                         r=[qpT, skT], w=[pbig[3]])
                for c4 in range(4):
                    hp = q4 * 4 + c4
                    S.op("dve", lambda e: e.tensor_copy(out=sc[:, hp, :], in_=pbig[3][:, c4 * 128:(c4 + 1) * 128]), r=[pbig[3]], w=[sc])
            for hp in range(16):
                top16(sc, sc[:, hp, :], sc2, sc2[:, :], mv[:, hp, :], mi[:, hp, :], mv, mi)
            S.op("dve", lambda e: e.tensor_copy(out=mif[:, :, :], in_=mi[:, :, :]), r=[mi], w=[mif])
            for hd in range(8):
                S.op("dve", lambda e: e.tensor_tensor(out=cand[:, :].rearrange("p (a b) -> p a b", a=16),
                                                      in0=mv[:, 2 * hd, :].unsqueeze(2).to_broadcast([128, 16, 16]),
                                                      in1=mv[:, 2 * hd + 1, :].unsqueeze(1).to_broadcast([128, 16, 16]), op=ALU.add), r=[mv], w=[cand])
                top16(cand, cand[:, :], cand2, cand2[:, :], cv[:, hd, :], ci[:, hd, :], cv, ci)
            S.op("dve", lambda e: e.tensor_copy(out=cif[:, :, :], in_=ci[:, :, :]), r=[ci], w=[cif])
            for hd in range(8):
                cb_ = cif[:, hd, :].unsqueeze(2).to_broadcast([128, 16, 16])
                S.op("dve", lambda e: e.tensor_tensor(out=oh[:, :, :], in0=cb_, in1=lo16[:, :].unsqueeze(1).to_broadcast([128, 16, 16]), op=ALU.is_ge),
                     r=[cif, lo16], w=[oh])
                S.op("dve", lambda e: e.tensor_tensor(out=cand2[:, :].rearrange("p (a b) -> p a b", a=16), in0=cb_, in1=hi16[:, :].unsqueeze(1).to_broadcast([128, 16, 16]), op=ALU.is_lt),
                     r=[cif, hi16], w=[cand2])
                S.op("dve", lambda e: e.tensor_tensor(out=oh[:, :, :], in0=oh[:, :, :], in1=cand2[:, :].rearrange("p (a b) -> p a b", a=16), op=ALU.mult), r=[oh, cand2], w=[oh])
                S.op("dve", lambda e: e.tensor_tensor(out=cand2[:, :].rearrange("p (a b) -> p a b", a=16), in0=oh[:, :, :], in1=mif[:, 2 * hd, :].unsqueeze(1).to_broadcast([128, 16, 16]), op=ALU.mult),
                     r=[oh, mif], w=[cand2])
                S.op("dve", lambda e: e.tensor_reduce(out=i1[:, hd, :], in_=cand2[:, :].rearrange("p (a b) -> p a b", a=16), axis=AX.X, op=ALU.add), r=[cand2], w=[i1])
                S.op("dve", lambda e: e.tensor_tensor(out=cand2[:, :].rearrange("p (a b) -> p a b", a=16), in0=oh[:, :, :], in1=lo16[:, :].unsqueeze(1).to_broadcast([128, 16, 16]), op=ALU.mult),
                     r=[oh, lo16], w=[cand2])
                S.op("dve", lambda e: e.tensor_reduce(out=ia[:, hd, :], in_=cand2[:, :].rearrange("p (a b) -> p a b", a=16), axis=AX.X, op=ALU.add), r=[cand2], w=[ia])
            S.op("dve", lambda e: e.tensor_tensor(out=ib[:, :, :], in0=cif[:, :, :], in1=ia[:, :, :], op=ALU.subtract), r=[cif, ia], w=[ib])
            for hd in range(8):
                S.op("dve", lambda e: e.tensor_tensor(out=oh[:, :, :], in0=ib[:, hd, :].unsqueeze(2).to_broadcast([128, 16, 16]),
                                                      in1=iota[:, 0:16].unsqueeze(1).to_broadcast([128, 16, 16]), op=ALU.is_equal), r=[ib, iota], w=[oh])
                S.op("dve", lambda e: e.tensor_tensor(out=oh[:, :, :], in0=oh[:, :, :], in1=mif[:, 2 * hd + 1, :].unsqueeze(1).to_broadcast([128, 16, 16]), op=ALU.mult),
                     r=[oh, mif], w=[oh])
                S.op("dve", lambda e: e.tensor_reduce(out=i2[:, hd, :], in_=oh[:, :, :], axis=AX.X, op=ALU.add), r=[oh], w=[i2])
            S.op("dve", lambda e: e.scalar_tensor_tensor(out=eidf[:, :], in0=i1[:, :, :].rearrange("p a b -> p (a b)"), scalar=128.0,
                                                         in1=i2[:, :, :].rearrange("p a b -> p (a b)"), op0=ALU.mult, op1=ALU.add), r=[i1, i2], w=[eidf])
            S.op("dve", lambda e: e.tensor_copy(out=eid[:, :], in_=eidf[:, :]), r=[eidf], w=[eid])
            for hd in range(8):
                S.op("dve", lambda e: e.tensor_scalar(out=gate[:, hd, :], in0=cv[:, hd, :], scalar1=cv[:, hd, 0:1], scalar2=None, op0=ALU.subtract),
                     r=[cv], w=[gate])
            S.op("act", lambda e: e.activation(out=gate[:, :, :].rearrange("p a b -> p (a b)"), in_=gate[:, :, :].rearrange("p a b -> p (a b)"), func=AF.Exp),
                 r=[gate], w=[gate])
            S.op("dve", lambda e: e.tensor_reduce(out=gsum[:, :], in_=gate[:, :, :], axis=AX.X, op=ALU.add), r=[gate], w=[gsum])
            S.op("dve", lambda e: e.reciprocal(out=gsum[:, :], in_=gsum[:, :]), r=[gsum], w=[gsum])
            for hd in range(8):
                S.op("dve", lambda e: e.tensor_scalar(out=gate[:, hd, :], in0=gate[:, hd, :], scalar1=gsum[:, hd:hd + 1], scalar2=None, op0=ALU.mult),
                     r=[gate, gsum], w=[gate])
            GS = 4
            gflat = gate[:, :, :].rearrange("p a b -> p (a b)")
            for grp in range(128 // GS):
                sls = range(grp * GS, (grp + 1) * GS)
                for sl in sls:
                    g_ = GB[sl % NBG]
                    pr_ = prod[sl % 2]
                    S.dma("pool", g_, lambda e: e.indirect_dma_start(out=g_[:, :], out_offset=None, in_=k.euv_bf.ap(),
                                                                     in_offset=bass.IndirectOffsetOnAxis(ap=eid[:, sl:sl + 1], axis=0)),
                          r=[eid] + par(g_), w=[g_])
                    S.op("dve", lambda e: e.tensor_tensor(out=pr_[:, :], in0=g_[:, 0:D], in1=fb[:, :], op=ALU.mult), r=[g_, fb] + par(g_), w=[pr_])
                    S.op("act", lambda e: e.activation(out=xt[:, :], in_=pr_[:, :], func=AF.Copy, accum_out=hid[:, sl:sl + 1]), r=[pr_], w=[xt, hid])
                c0, c1 = grp * GS, (grp + 1) * GS
                hg, xg, wg = hid[:, c0:c1], hx[:, c0:c1], wgt[:, c0:c1]
                S.op("dve", lambda e: e.tensor_tensor(out=xg, in0=hg, in1=hg, op=ALU.mult), r=[hid], w=[hx])
                S.op("dve", lambda e: e.tensor_scalar(out=xg, in0=xg, scalar1=0.044715, scalar2=1.0, op0=ALU.mult, op1=ALU.add), r=[hx], w=[hx])
                S.op("dve", lambda e: e.tensor_tensor(out=xg, in0=xg, in1=hg, op=ALU.mult), r=[hx, hid], w=[hx])
                S.op("act", lambda e: e.activation(out=xg, in_=xg, func=AF.Tanh, scale=0.7978845608028654), r=[hx], w=[hx])
                S.op("dve", lambda e: e.tensor_scalar(out=xg, in0=xg, scalar1=1.0, scalar2=0.5, op0=ALU.add, op1=ALU.mult), r=[hx], w=[hx])
                S.op("dve", lambda e: e.tensor_tensor(out=xg, in0=xg, in1=hg, op=ALU.mult), r=[hx, hid], w=[hx])
                S.op("dve", lambda e: e.tensor_tensor(out=wg, in0=xg, in1=gflat[:, c0:c1], op=ALU.mult), r=[hx, gate], w=[wgt])
                for sl in sls:
                    g_ = GB[sl % NBG]
                    d_ = dg[sl % 2]
                    S.op("dve", lambda e: e.tensor_scalar(out=d_[:, :], in0=identb[:, :], scalar1=wgt[:, sl:sl + 1], scalar2=None, op0=ALU.mult),
                         r=[identb, wgt], w=[d_])
                    for nb in range(2):
                        S.op("pe", lambda e: e.matmul(pout[nb][:, :], lhsT=d_[:, :], rhs=g_[:, D + nb * 512:D + (nb + 1) * 512],
                                                      start=(sl == 0), stop=(sl == 127)), r=[d_, g_] + par(g_), w=[pout[nb]])
            for nb in range(2):
                S.op("dve", lambda e: e.tensor_tensor(out=h[:, nb * 512:(nb + 1) * 512], in0=pout[nb][:, :], in1=h[:, nb * 512:(nb + 1) * 512], op=ALU.add),
                     r=[pout[nb], h], w=[h])
            S.op("act", lambda e: e.activation(out=sqj[:, :], in_=h[:, :], func=AF.Square, accum_out=ssq[:, 0:1]), r=[h], w=[sqj, ssq])
            rsqrt(S, rstd, rstd[:, :], ssq, ssq[:, :], 1.0 / D, EPS)
            S.op("dve", lambda e: e.scalar_tensor_tensor(out=yo[:, :], in0=h[:, :], scalar=rstd[:, 0:1], in1=gfin[:, :], op0=ALU.mult, op1=ALU.mult),
                 r=[h, rstd, gfin], w=[yo])
            if smp:
                S.dma("sp", yo, lambda e: e.dma_start(out=k.y_smp[:, :], in_=yo[0:NT, :]), r=[yo])
            else:
                S.dma("sp", yo, lambda e: e.dma_start(out=k.y_main[tau * 128:(tau + 1) * 128, :], in_=yo[:, :]), r=[yo])
        k.end_phase()
```

```python
import numpy as np
from contextlib import ExitStack
import concourse.bass as bass
import concourse.mybir as mybir
from concourse.bass_utils import run_bass_kernel_spmd

F32 = mybir.dt.float32
BF16 = mybir.dt.bfloat16
I32 = mybir.dt.int32
U32 = mybir.dt.uint32
AF = mybir.ActivationFunctionType
ALU = mybir.AluOpType
AX = mybir.AxisListType

NCORES = 8
D = 1024
PRE = 4096
MAIN = 4096
EXT = PRE + MAIN
HALO = 2048
KR = HALO + MAIN
NS = 4
TS = 4
IN_DIM = 4360
CQ, CK, CV, CAG, CBG, CZ, CSW = 0, 512, 1024, 1536, 1540, 1544, 2056
DILS = (1, 4, 16)
EPS = 1e-6
NEG = -30000.0
SEM_EPOCH = 20000
import os as _os
EXPROWS = int(_os.environ.get("EXPROWS", "16384"))


class TL:
    def __init__(self, t, name):
        self.t = t
        self.name = name
        self.lw = None
        self.rd = []
        self.ds = None

    def __getitem__(self, k):
        return self.t[k]


class TLsub(TL):
    def __init__(self, bank, off, width, name):
        self.bank = bank
        self.t = bank.t
        self.name = name
        self.off = off
        self.width = width
        self.ds = None

    lw = property(lambda self: self.bank.lw, lambda self, v: setattr(self.bank, "lw", v))
    rd = property(lambda self: self.bank.rd, lambda self, v: setattr(self.bank, "rd", v))

    def __getitem__(self, key):
        r, c = key
        if isinstance(c, slice):
            a = self.off + (c.start or 0)
            b_ = self.off + (self.width if c.stop is None else c.stop)
            return self.t[r, a:b_]
        return self.t[r, self.off + c]


class TLview(TL):
    def __init__(self, parent, ap_fn, name):
        super().__init__(None, name)
        self.p = parent
        self.ap_fn = ap_fn

    def __getitem__(self, key):
        return self.ap_fn()[key]


class PsumBlocks:
    def __init__(self, nc, es, prefix):
        self.nc, self.es, self.prefix = nc, es, prefix
        self.banks = {}

    def get(self, name, width, dt, bank):
        per = 2048 // (4 if dt == F32 else 2)
        if bank not in self.banks:
            t = self.es.enter_context(self.nc.psum_tensor(f"{self.prefix}{bank}", [128, per], dt))
            self.banks[bank] = [TL(t, f"{self.prefix}{bank}"), 0]
        b = self.banks[bank]
        assert b[1] + width <= per
        off = b[1]
        b[1] += width
        return TLsub(b[0], off, width, name)


class _Rec:
    def __init__(self):
        self.call = None

    def __getattr__(self, name):
        def f(*a, **kw):
            self.call = (name, a, kw)
            return self
        return f


def _capture(fn):
    r = _Rec()
    fn(r)
    name, a, kw = r.call
    return lambda e: getattr(e, name)(*a, **kw)


class Sched:
    ENG = ("pe", "act", "dve", "pool", "sp")

    def __init__(self, nc):
        self.nc = nc
        self.eng = {"pe": nc.tensor, "act": nc.scalar, "dve": nc.vector, "pool": nc.gpsimd, "sp": nc.sync}
        self.rec = []
        self.dsems = []
        self.dsem_pool = []
        self.ninst = 0
        self.last = {e: None for e in self.ENG}

    def dsem(self, name):
        s = [self.nc.alloc_semaphore("d_" + name), 0, False]
        self.dsems.append(s)
        return s

    def _collect(self, e, r, w, is_dma):
        deps = []
        for t in r:
            if t.lw is not None:
                deps.append(t.lw)
        for t in w:
            if t.lw is not None:
                p = self.rec[t.lw]
                if is_dma or not (p["kind"] == "op" and p["e"] == e and e == "pe"):
                    deps.append(t.lw)
            for d in t.rd:
                p = self.rec[d]
                if is_dma or p["kind"] == "dma" or p["e"] != e or e != "pe":
                    deps.append(d)
        return deps

    def op(self, e, fn, r=(), w=()):
        deps = self._collect(e, r, w, False)
        idx = len(self.rec)
        self.rec.append({"kind": "op", "e": e, "fn": _capture(fn), "deps": deps})
        self.last[e] = idx
        for t in w:
            t.lw = idx
            t.rd = []
        for t in r:
            t.rd.append(idx)
        self.ninst += 1

    def dma(self, q, t, fn, r=(), w=()):
        if q == "pool" and (t.ds is None or not t.ds[2]):
            t.ds = self.dsem(t.name + "_sw")
            t.ds[2] = True
        if t.ds is None:
            t.ds = self.dsem_pool.pop() if self.dsem_pool else self.dsem(t.name)
        ds = t.ds
        deps = self._collect(q, r, w, True)
        if ds[1] + 16 >= 32000:
            sw_ = ds[2]
            t.ds = self.dsem(t.name + f"_r{len(self.dsems)}")
            t.ds[2] = sw_
            ds = t.ds
        ds[1] += 16
        idx = len(self.rec)
        self.rec.append({"kind": "dma", "e": q, "fn": _capture(fn), "deps": deps, "sem": ds[0], "val": ds[1]})
        for x in w:
            x.lw = idx
            x.rd = []
        for x in r:
            x.rd.append(idx)
        self.ninst += 1

    def release(self, tiles):
        for t in tiles:
            if t.ds is not None:
                if not t.ds[2]:
                    self.dsem_pool.append(t.ds)
                t.ds = None

    def barrier(self, engines=None):
        deps = [v for v in self.last.values() if v is not None]
        dm = [(s[0], s[1]) for s in self.dsems if s[1]]
        self.rec.append({"kind": "bar", "deps": deps, "dm": dm, "engines": engines or self.ENG})

    def final_wait(self):
        self.barrier(engines=("sp",))

    def emit(self):
        rec = self.rec
        seq = {}
        cnt = {e: 0 for e in self.ENG}
        for i, r in enumerate(rec):
            if r["kind"] == "op":
                cnt[r["e"]] += 1
                seq[i] = cnt[r["e"]]
        awaited = set()

        def sweep(do_emit, ordv=None, sems=None):
            wseq = {e: {p: 0 for p in self.ENG} for e in self.ENG}
            wdma = {e: {} for e in self.ENG}
            for i, r in enumerate(rec):
                targets = r["engines"] if r["kind"] == "bar" else (r["e"],)
                for e in targets:
                    need = {}
                    for d in r["deps"]:
                        p = rec[d]
                        if p["kind"] == "op":
                            if seq[d] > wseq[e][p["e"]] and seq[d] > need.get(p["e"], (0, None))[0]:
                                need[p["e"]] = (seq[d], d)
                        else:
                            key = id(p["sem"])
                            if wdma[e].get(key, 0) < p["val"]:
                                wdma[e][key] = p["val"]
                                if do_emit:
                                    self.eng[e].wait_ge(p["sem"], p["val"])
                    for (sem, val) in r.get("dm", ()):
                        key = id(sem)
                        if wdma[e].get(key, 0) < val:
                            wdma[e][key] = val
                            if do_emit:
                                self.eng[e].wait_ge(sem, val)
                    for pe_, (sq, d) in need.items():
                        wseq[e][pe_] = sq
                        if do_emit:
                            o = ordv[d]
                            self.eng[e].wait_ge(sems[pe_][(o - 1) // SEM_EPOCH], (o - 1) % SEM_EPOCH + 1)
                        else:
                            awaited.add(d)
                if do_emit and r["kind"] != "bar":
                    ins = r["fn"](self.eng[r["e"]])
                    if r["kind"] == "dma":
                        ins.then_inc(r["sem"], 16)
                    elif i in awaited:
                        o = ordv[i]
                        ins.then_inc(sems[r["e"]][(o - 1) // SEM_EPOCH], 1)

        sweep(False)
        ordv = {}
        oc = {e: 0 for e in self.ENG}
        for i, r in enumerate(rec):
            if r["kind"] == "op" and i in awaited:
                oc[r["e"]] += 1
                ordv[i] = oc[r["e"]]
        sems = {e: [self.nc.alloc_semaphore(f"s_{e}_{j}") for j in range((oc[e] + SEM_EPOCH - 1) // SEM_EPOCH)] for e in self.ENG}
        self.nsig = dict(oc)
        sweep(True, ordv, sems)


class K:
    def __init__(self):
        self.tiles = []

    def track(self, t):
        self.tiles.append(t)
        return t

    def end_phase(self):
        self.S.barrier()
        self.S.release(self.tiles)
        self.tiles = []


def make_consts():
    c = {}
    idx = np.arange(128)
    same = (idx[:, None] // 64) == (idx[None, :] // 64)
    c["ident"] = np.eye(128, dtype=np.float32)
    c["trit"] = ((idx[:, None] <= idx[None, :]) & same).astype(np.float32)
    c["blk"] = same.astype(np.float32)
    c["mTneg"] = np.where((idx[None, :] >= idx[:, None]) & same, 0.0, NEG).astype(np.float32)
    c["mSpos"] = np.where((idx[:, None] > idx[None, :]) & same, 0.0, -NEG).astype(np.float32)
    c["swprev"] = np.where(idx[:, None] >= idx[None, :], 0.0, NEG).astype(np.float32)
    c["swcur"] = np.where(idx[:, None] <= idx[None, :], 0.0, NEG).astype(np.float32)
    c["ones"] = np.ones((128, 128), np.float32)
    c["iota"] = np.tile(np.arange(128, dtype=np.float32), (128, 1))
    return c


CONST_NAMES = ("ident", "trit", "blk", "mTneg", "mSpos", "swprev", "swcur", "ones", "iota")


def declare_io(k, debug):
    nc = k.nc
    I = lambda n, s, dt=F32: nc.dram_tensor(n, list(s), dt, kind="ExternalInput")
    O = lambda n, s, dt=F32: nc.dram_tensor(n, list(s), dt, kind="ExternalOutput")
    SCR = (lambda n, s, dt: nc.dram_tensor(n, list(s), dt, kind="ExternalOutput")) if debug else \
          (lambda n, s, dt: nc.dram_tensor(n, list(s), dt, kind="Internal"))
    k.xe = I("xe", [EXT, D])
    k.xs = I("xs", [NS * TS, D])
    k.st_delta = I("st_delta", [NS, 4, 128, 128])
    k.st_conv = I("st_conv", [NS, 3, 1536])
    k.cwin = [I(f"cwin{g}", [NS, 128 * DILS[g], 512]) for g in range(3)]
    k.cmem = I("cmem", [NS, 256, 2048])
    k.mem = I("mem", [256, D])
    k.halo = I("halo", [128, 128])
    k.consts = I("consts", [len(CONST_NAMES), 128, 128])
    for n, s in (("g_mix", [D]), ("w_in", [D, IN_DIM]), ("conv_w", [4, 1536]), ("a_log", [4]), ("dt_bias", [4]),
                 ("g_onorm", [128]), ("w_out", [768, D]), ("g_memq", [D]), ("g_memkv", [D]), ("w_mq", [D, D]),
                 ("w_mkv", [D, 2048]), ("w_mo", [D, D]), ("g_ffn", [D]), ("w_pq", [D, 2048]),
                 ("sub_keys", [16, 128, 128]), ("expert_u", [EXPROWS, D]), ("expert_v", [EXPROWS, D]), ("g_final", [D])):
        setattr(k, n, I(n, s))
    k.y_main = O("y_main", [MAIN, D])
    k.y_smp = O("y_smp", [NS * TS, D])
    k.p_delta = O("p_delta", [4, 128, 128])
    k.p_conv = O("p_conv", [3, 1536])
    k.p_win = [O(f"p_win{g}", [128 * DILS[g], 512]) for g in range(3)]
    k.p_mem = O("p_mem", [256, 2048])
    k.s_delta = O("s_delta", [NS, 4, 128, 128])
    k.s_conv = O("s_conv", [NS, 3, 1536])
    k.s_win = [O(f"s_win{g}", [NS, 128 * DILS[g], 512]) for g in range(3)]
    k.dnq = SCR("dnq", [4, 128, EXT], BF16)
    k.dnk = SCR("dnk", [4, 128, EXT], BF16)
    k.dnv = SCR("dnv", [4, 128, EXT], BF16)
    k.gbs = SCR("gbs", [EXT, 8], F32)
    k.zs = SCR("zs", [MAIN, 512], BF16)
    k.qts = [SCR(f"qts{g}", [2, 128, MAIN], BF16) for g in range(3)]
    k.kts = [SCR(f"kts{g}", [2, 128, KR], BF16) for g in range(3)]
    k.vss = [SCR(f"vss{g}", [KR, 256], BF16) for g in range(3)]
    SWO = (lambda n, s, dt: nc.dram_tensor(n, list(s), dt, kind="ExternalOutput")) if _os.environ.get("SWO_OUT") else SCR
    k.swo = [SWO(f"swo{g}", [MAIN, 260], F32) for g in range(3)]
    k.cat = SCR("cat", [MAIN, 512], BF16)
    k.euv_bf = nc.dram_tensor("euv_bf", [EXPROWS, 2 * D], BF16, kind="Internal")


def evac(S, i, out_t, out_ap, in_t, in_ap, rx=(), **kw):
    if i % 2 == 0 and len(out_ap.shape) == 2 and len(in_ap.shape) == 2:
        S.op("act", lambda e: e.activation(out=out_ap, in_=in_ap, func=AF.Copy, **kw), r=[in_t, *rx], w=[out_t])
    else:
        if "scale" in kw:
            S.op("dve", lambda e: e.tensor_scalar(out=out_ap, in0=in_ap, scalar1=kw["scale"], scalar2=None, op0=ALU.mult),
                 r=[in_t, *rx], w=[out_t])
        else:
            S.op("dve", lambda e: e.tensor_copy(out=out_ap, in_=in_ap), r=[in_t, *rx], w=[out_t])


def rsqrt(S, out_t, out_ap, in_t, in_ap, mul, add):
    S.op("act", lambda e: e.activation(out=out_ap, in_=in_ap, func=AF.Sqrt, scale=mul, bias=add), r=[in_t], w=[out_t])
    S.op("dve", lambda e: e.reciprocal(out=out_ap, in_=out_ap), r=[out_t], w=[out_t])


def load_weight_bf16(k, es, name, wdram, rows, cols, gdram=None, col_chunk=None):
    nc, S = k.nc, k.S
    nch = rows // 128
    wbf = TL(es.enter_context(nc.sbuf_tensor(name + "_bf", [128, nch, cols], BF16)), name)
    gcol = None
    if gdram is not None:
        gcol = k.track(TL(es.enter_context(nc.sbuf_tensor(name + "_g", [128, nch], F32)), name + "_g"))
        S.dma("sp", gcol, lambda e: e.dma_start(out=gcol[:, :], in_=gdram.ap().rearrange("(c p) -> p c", p=128),
                                                 allow_slow_non_contiguous=True), w=[gcol])
    cc = col_chunk or cols
    with ExitStack() as es2:
        st = [k.track(TL(es2.enter_context(nc.sbuf_tensor(f"{name}_st{i}", [128, cc], F32)), f"{name}_st{i}")) for i in range(2)]
        n = 0
        for c in range(nch):
            for c0 in range(0, cols, cc):
                w_ = min(cc, cols - c0)
                s_ = st[n % 2]
                S.dma("sp", s_, lambda e: e.dma_start(out=s_[:, 0:w_], in_=wdram[c * 128:(c + 1) * 128, c0:c0 + w_]), w=[s_])
                if gcol is not None:
                    evac(S, n, wbf, wbf[:, c, c0:c0 + w_], s_, s_[:, 0:w_], rx=[gcol], scale=gcol[:, c:c + 1])
                else:
                    evac(S, n, wbf, wbf[:, c, c0:c0 + w_], s_, s_[:, 0:w_])
                n += 1
        S.barrier()
    return wbf


def phase1(k):
    nc, S = k.nc, k.S
    with ExitStack() as es:
        A = lambda name, shape, dt: k.track(TL(es.enter_context(nc.sbuf_tensor(name, shape, dt)), name))
        P = lambda name, shape, dt: TL(es.enter_context(nc.psum_tensor(name, shape, dt)), name)
        wbf = load_weight_bf16(k, es, "w_in", k.w_in, D, IN_DIM, gdram=k.g_mix, col_chunk=2180)
        cw = A("cw", [128, 12, 4], F32)
        for j in range(4):
            S.dma("sp", cw, lambda e: e.dma_start(out=cw[:, :, j], in_=k.conv_w[j, :].rearrange("(c p) -> p c", p=128),
                                                  allow_slow_non_contiguous=True), w=[cw])
        dtb = A("dtb", [128, 4], F32)
        S.dma("sp", dtb, lambda e: e.dma_start(out=dtb[:, :], in_=k.dt_bias.ap().partition_broadcast(128)), w=[dtb])
        nega = A("nega", [128, 4], F32)
        S.dma("sp", nega, lambda e: e.dma_start(out=nega[:, :], in_=k.a_log.ap().partition_broadcast(128)), w=[nega])
        S.op("act", lambda e: e.activation(out=nega[:, :], in_=nega[:, :], func=AF.Exp), r=[nega], w=[nega])
        S.op("dve", lambda e: e.tensor_scalar(out=nega[:, :], in0=nega[:, :], scalar1=-1.0, scalar2=None, op0=ALU.mult), r=[nega], w=[nega])
        xt = [A(f"xt{i}", [128, D], F32) for i in range(2)]
        xsm = A("xsm", [128, D], F32)
        sqj = A("sqj", [128, D], BF16)
        ssq = [A(f"ssq{i}", [128, 1], F32) for i in range(2)]
        rstd = [A(f"rstd{i}", [128, 1], F32) for i in range(2)]
        ab = [A(f"ab{i}", [128, D], BF16) for i in range(2)]
        aT = [A(f"aT{i}", [128, 8, 512], BF16) for i in range(2)]
        xp = [A(f"xp{i}", [128, 515], F32) for i in range(2)]
        carry = A("carry", [128, 12, 3], F32)
        acc = [A(f"acc{i}", [128, 512], F32) for i in range(2)]
        act_ = [A(f"actt{i}", [128, 512], F32) for i in range(2)]
        sq2 = [A(f"sq2{i}", [128, 512], BF16) for i in range(2)]
        rn = [A(f"rn{i}", [128, 512], F32) for i in range(2)]
        outb = [A(f"outb{i}", [128, 512], BF16) for i in range(3)]
        qkp = [A(f"qkp{i}", [128, 512], BF16) for i in range(3)]
        gb = [A(f"gb{i}", [128, 8], F32) for i in range(2)]
        gtmp = [A(f"gtmp{i}", [128, 4], F32) for i in range(4)]
        zb = [A(f"zb{i}", [128, 512], BF16) for i in range(2)]
        vsb = [A(f"vsb{i}", [128, 256], BF16) for i in range(3)]
        kvf = [A(f"kvf{i}", [128, 512], F32) for i in range(3)]
        xps = A("xps", [128, 12, NS, 7], F32)
        ones_bf = k.ones_bf
        pT = [P(f"pT{i}", [128, 8, 128], BF16) for i in range(2)]
        pacc = [P(f"pacc{i}", [128, 512], F32) for i in range(2)]
        pl2 = P("pl2", [128, 512], F32)
        ptm = [P(f"ptm{i}", [128, 512], F32) for i in range(2)]
        pg = P("pg", [128, 8], F32)
        S.op("dve", lambda e: e.memset(carry[:, :, :], 0.0), w=[carry])
        S.op("dve", lambda e: e.memset(xsm[:, :], 0.0), w=[xsm])
        ctr = {"ev": 0, "acc": 0, "tm": 0, "ob": 0}

        def norm_transpose(xtile, i, aTt, col0):
            S.op("act", lambda e: e.activation(out=sqj[:, :], in_=xtile[:, :], func=AF.Square, accum_out=ssq[i][:, 0:1]),
                 r=[xtile], w=[sqj, ssq[i]])
            rsqrt(S, rstd[i], rstd[i][:, :], ssq[i], ssq[i][:, :], 1.0 / D, EPS)
            S.op("act", lambda e: e.activation(out=ab[i][:, :], in_=xtile[:, :], func=AF.Copy, scale=rstd[i][:, 0:1]),
                 r=[xtile, rstd[i]], w=[ab[i]])
            for c in range(8):
                S.op("pe", lambda e: e.transpose(out=pT[i][:, c, :], in_=ab[i][:, c * 128:(c + 1) * 128], identity=k.ident_bf[:, :]),
                     r=[ab[i], k.ident_bf], w=[pT[i]])
            ctr["ev"] += 1
            evac(S, ctr["ev"], aTt, aTt[:, :, col0:col0 + 128], pT[i], pT[i][:, :, :])

        def fm_chunk(aTt, nt, col):
            ps = pacc[ctr["acc"] % 2]
            ctr["acc"] += 1
            for c in range(8):
                S.op("pe", lambda e: e.matmul(ps[:, 0:nt], lhsT=wbf[:, c, col:col + 128], rhs=aTt[:, c, 0:nt],
                                              start=(c == 0), stop=(c == 7)), r=[wbf, aTt], w=[ps])
            return ps

        def tm_chunk(aTt, t, col, ncol, ps=None):
            if ps is None:
                ps = ptm[ctr["tm"] % 2]
                ctr["tm"] += 1
            for c in range(8):
                S.op("pe", lambda e: e.matmul(ps[:, 0:ncol], lhsT=aTt[:, c, t * 128:(t + 1) * 128], rhs=wbf[:, c, col:col + ncol],
                                              start=(c == 0), stop=(c == 7)), r=[wbf, aTt], w=[ps])
            return ps

        def dn_post(ps, nt, ci, src_t, src_ap, dst_fn):
            j = ctr["ob"] % 2
            S.op("act", lambda e: e.activation(out=act_[j][:, 0:nt], in_=src_ap, func=AF.Silu), r=[src_t], w=[act_[j]])
            ob = outb[ctr["ob"] % 3]
            ctr["ob"] += 1
            if ci < 8:
                S.op("act", lambda e: e.activation(out=sq2[j][:, 0:nt], in_=act_[j][:, 0:nt], func=AF.Square), r=[act_[j]], w=[sq2[j]])
                S.op("pe", lambda e: e.matmul(pl2[:, 0:nt], lhsT=ones_bf[:, :], rhs=sq2[j][:, 0:nt], start=True, stop=True),
                     r=[ones_bf, sq2[j]], w=[pl2])
                rsqrt(S, rn[j], rn[j][:, 0:nt], pl2, pl2[:, 0:nt], 1.0, EPS)
                if ci < 4:
                    S.op("dve", lambda e: e.scalar_tensor_tensor(out=ob[:, 0:nt], in0=act_[j][:, 0:nt], scalar=128 ** -0.5,
                                                                 in1=rn[j][:, 0:nt], op0=ALU.mult, op1=ALU.mult),
                         r=[act_[j], rn[j]], w=[ob])
                else:
                    S.op("dve", lambda e: e.tensor_tensor(out=ob[:, 0:nt], in0=act_[j][:, 0:nt], in1=rn[j][:, 0:nt], op=ALU.mult),
                         r=[act_[j], rn[j]], w=[ob])
            else:
                S.op("dve", lambda e: e.tensor_copy(out=ob[:, 0:nt], in_=act_[j][:, 0:nt]), r=[act_[j]], w=[ob])
            dst_fn(ob)

        def gates(ps_g, rows, dst_t, dst_ap):
            g0, g1, g2, g3 = gtmp
            S.op("act", lambda e: e.activation(out=dst_ap[:, 4:8], in_=ps_g[0:rows, 4:8], func=AF.Sigmoid), r=[ps_g], w=[dst_t])
            S.op("dve", lambda e: e.tensor_tensor(out=g0[0:rows, :], in0=ps_g[0:rows, 0:4], in1=dtb[0:rows, :], op=ALU.add),
                 r=[ps_g, dtb], w=[g0])
            S.op("act", lambda e: e.activation(out=g1[0:rows, :], in_=g0[0:rows, :], func=AF.Abs), r=[g0], w=[g1])
            S.op("act", lambda e: e.activation(out=g2[0:rows, :], in_=g1[0:rows, :], func=AF.Exp, scale=-1.0), r=[g1], w=[g2])
            S.op("act", lambda e: e.activation(out=g3[0:rows, :], in_=g2[0:rows, :], func=AF.Ln, bias=1.0), r=[g2], w=[g3])
            S.op("dve", lambda e: e.scalar_tensor_tensor(out=g1[0:rows, :], in0=g0[0:rows, :], scalar=0.0, in1=g3[0:rows, :],
                                                         op0=ALU.max, op1=ALU.add), r=[g0, g3], w=[g1])
            S.op("dve", lambda e: e.tensor_tensor(out=dst_ap[:, 0:4], in0=g1[0:rows, :], in1=nega[0:rows, :], op=ALU.mult),
                 r=[g1, nega], w=[dst_t])

        nsup = EXT // 512
        import os
        sups = range(nsup) if "P1SUP" not in os.environ else [int(x) for x in os.environ["P1SUP"].split(",") if x]
        for s in sups:
            aTt = aT[s % 2]
            in_main = s * 512 >= PRE
            in_kr = s * 512 >= PRE - HALO
            for t in range(4):
                x_ = xt[t % 2]
                r0 = s * 512 + t * 128
                S.dma("sp", x_, lambda e: e.dma_start(out=x_[:, :], in_=k.xe[r0:r0 + 128, :]), w=[x_])
                norm_transpose(x_, t % 2, aTt, t * 128)
            for ci in range(12):
                ps = fm_chunk(aTt, 512, ci * 128)
                xp_ = xp[ci % 2]
                S.op("act", lambda e: e.activation(out=xp_[:, 3:515], in_=ps[:, :], func=AF.Copy), r=[ps], w=[xp_])
                S.op("pool", lambda e: e.tensor_copy(out=xp_[:, 0:3], in_=carry[:, ci, :]), r=[carry], w=[xp_])
                S.op("pool", lambda e: e.tensor_copy(out=carry[:, ci, :], in_=xp_[:, 512:515]), r=[xp_], w=[carry])
                if s == nsup - 1 and not os.environ.get("SKIP_PCONV"):
                    S.dma("sp", xp_, lambda e: e.dma_start(out=k.p_conv.ap()[:, ci * 128:(ci + 1) * 128].rearrange("j p -> p j"),
                                                             in_=xp_[:, 512:515], allow_slow_non_contiguous=True), r=[xp_])
                ac = acc[ci % 2]
                S.op("dve", lambda e: e.tensor_scalar(out=ac[:, :], in0=xp_[:, 0:512], scalar1=cw[:, ci, 0:1], scalar2=None, op0=ALU.mult),
                     r=[xp_, cw], w=[ac])
                for j in range(1, 4):
                    S.op("dve", lambda e: e.scalar_tensor_tensor(out=ac[:, :], in0=xp_[:, j:j + 512], scalar=cw[:, ci, j:j + 1], in1=ac[:, :],
                                                                 op0=ALU.mult, op1=ALU.add), r=[xp_, cw, ac], w=[ac])
                dst = (k.dnq, k.dnk, k.dnv)[ci // 4]
                h = ci % 4
                dn_post(None, 512, ci, ac, ac[:, :],
                        lambda ob: S.dma("sp", ob, lambda e: e.dma_start(out=dst[h, :, s * 512:(s + 1) * 512], in_=ob[:, :]), r=[ob]))
            for t in range(4):
                r0 = s * 512 + t * 128
                for c in range(8):
                    S.op("pe", lambda e: e.matmul(pg[:, :], lhsT=aTt[:, c, t * 128:(t + 1) * 128], rhs=wbf[:, c, CAG:CAG + 8],
                                                  start=(c == 0), stop=(c == 7)), r=[wbf, aTt], w=[pg])
                g_ = gb[t % 2]
                gates(pg, 128, g_, g_[:, :])
                S.dma("sp", g_, lambda e: e.dma_start(out=k.gbs[r0:r0 + 128, :], in_=g_[:, :]), r=[g_])
                if in_main:
                    ps = tm_chunk(aTt, t, CZ, 512)
                    z_ = zb[t % 2]
                    ctr["ev"] += 1
                    evac(S, ctr["ev"], z_, z_[:, :], ps, ps[:, :])
                    S.dma("sp", z_, lambda e: e.dma_start(out=k.zs[r0 - PRE:r0 - PRE + 128, :], in_=z_[:, :]), r=[z_])
            if not in_kr:
                continue
            sk = s - (PRE - HALO) // 512
            sm = s - PRE // 512
            for g in range(3):
                d = DILS[g]
                if False:
                    continue
                for which in range(2):
                    if which == 0 and not in_main:
                        continue
                    for pair in range(2):
                        ps = fm_chunk(aTt, 512, CSW + 768 * g + 256 * which + 128 * pair)
                        q_ = qkp[ctr["ob"] % 3]
                        ctr["ob"] += 1
                        ctr["ev"] += 1
                        evac(S, ctr["ev"], q_, q_[:, :], ps, ps[:, :])
                        if which == 0:
                            dst = k.qts[g][pair, :, sm * 512:(sm + 1) * 512]
                        else:
                            dst = k.kts[g][pair, :, sk * 512:(sk + 1) * 512]
                        S.dma("sp", q_, lambda e: e.dma_start(out=dst, in_=q_[:, :]), r=[q_])
            for t in range(4):
                r0 = s * 512 + t * 128
                if os.environ.get("SKIP_KV") == "1":
                    continue
                for g in range(3):
                    ps = tm_chunk(aTt, t, CSW + 768 * g + 256, 512)
                    v_ = vsb[g]
                    S.op("dve", lambda e: e.tensor_copy(out=v_[:, :], in_=ps[:, 256:512]), r=[ps], w=[v_])
                    S.dma("sp", v_, lambda e: e.dma_start(out=k.vss[g][r0 - (PRE - HALO):r0 - (PRE - HALO) + 128, :], in_=v_[:, :]), r=[v_])
                    wlen = 128 * DILS[g]
                    if r0 >= EXT - wlen:
                        f_ = kvf[g]
                        S.op("dve", lambda e: e.tensor_copy(out=f_[:, :], in_=ps[:, :]), r=[ps], w=[f_])
                        o0 = r0 - (EXT - wlen)
                        S.dma("sp", f_, lambda e: e.dma_start(out=k.p_win[g][o0:o0 + 128, :], in_=f_[:, :]), r=[f_])

        NT = NS * TS
        if os.environ.get("P1NOSMP"):
            k.end_phase()
            return
        S.dma("sp", xsm, lambda e: e.dma_start(out=xsm[0:NT, :], in_=k.xs[:, :]), w=[xsm])
        aTt = aT[0]
        norm_transpose(xsm, 0, aTt, 0)
        for s_ in range(NS):
            for j in range(3):
                S.dma("sp", xps, lambda e: e.dma_start(out=xps[:, :, s_, j], in_=k.st_conv[s_, j, :].rearrange("(c p) -> p c", p=128),
                                                       allow_slow_non_contiguous=True), w=[xps])
        for ci in range(12):
            ps = fm_chunk(aTt, NT, ci * 128)
            S.op("dve", lambda e: e.tensor_copy(out=xps[:, ci, :, 3:7], in_=ps[:, 0:NT].rearrange("p (s t) -> p s t", s=NS)),
                 r=[ps], w=[xps])
        for ci in range(12):
            for s_ in range(NS):
                S.dma("sp", xps, lambda e: e.dma_start(out=k.s_conv[s_, :, ci * 128:(ci + 1) * 128].rearrange("j p -> p j"),
                                                       in_=xps[:, ci, s_, 4:7], allow_slow_non_contiguous=True), r=[xps])
        for ci in range(12):
            ac = acc[ci % 2]
            av = ac[:, 0:NT].rearrange("p (s t) -> p s t", s=NS)
            S.op("dve", lambda e: e.tensor_scalar(out=av, in0=xps[:, ci, :, 0:4], scalar1=cw[:, ci, 0:1], scalar2=None, op0=ALU.mult),
                 r=[xps, cw], w=[ac])
            for j in range(1, 4):
                S.op("dve", lambda e: e.scalar_tensor_tensor(out=av, in0=xps[:, ci, :, j:j + 4], scalar=cw[:, ci, j:j + 1], in1=av,
                                                             op0=ALU.mult, op1=ALU.add), r=[xps, cw, ac], w=[ac])
            dn_post(None, NT, ci, ac, ac[:, 0:NT],
                    lambda ob: S.op("pool", lambda e: e.tensor_copy(out=k.smp_dn[:, ci, :], in_=ob[:, 0:NT]), r=[ob], w=[k.smp_dn]))
        for c in range(8):
            S.op("pe", lambda e: e.matmul(pg[:, :], lhsT=aTt[:, c, 0:128], rhs=wbf[:, c, CAG:CAG + 8],
                                          start=(c == 0), stop=(c == 7)), r=[wbf, aTt], w=[pg])
        gates(pg, NT, k.smp_gb, k.smp_gb[:, :])
        ps = tm_chunk(aTt, 0, CZ, 512)
        S.op("act", lambda e: e.activation(out=k.smp_z[:, :], in_=ps[0:NT, :], func=AF.Copy), r=[ps], w=[k.smp_z])
        for g in range(3):
            for which in range(2):
                for pair in range(2):
                    ps = fm_chunk(aTt, NT, CSW + 768 * g + 256 * which + 128 * pair)
                    S.op("act", lambda e: e.activation(out=k.smp_qk[:, g, which, pair, :], in_=ps[:, 0:NT], func=AF.Copy),
                         r=[ps], w=[k.smp_qk])
            ps = tm_chunk(aTt, 0, CSW + 768 * g + 256, 512)
            S.op("act", lambda e: e.activation(out=k.smp_kv[:, g, :], in_=ps[0:NT, :], func=AF.Copy), r=[ps], w=[k.smp_kv])
        k.end_phase()


def build_nc(debug=False, phases=(1, 5, 2, 3, 4)):
    nc = bass.Bass("TRN2", target_bir_lowering=False)
    k = K()
    k.nc = nc
    k.S = Sched(nc)
    k.debug = debug
    declare_io(k, debug)
    S = k.S
    with ExitStack() as es:
        A = lambda name, shape, dt: TL(es.enter_context(nc.sbuf_tensor(name, shape, dt)), name)
        k.cst = {}
        for i, n in enumerate(CONST_NAMES):
            t = A("c_" + n, [128, 128], F32)
            S.dma("sp", t, lambda e: e.dma_start(out=t[:, :], in_=k.consts[i, :, :]), w=[t])
            k.cst[n] = t
        k.ident_bf = A("ident_bf", [128, 128], BF16)
        S.op("dve", lambda e: e.tensor_copy(out=k.ident_bf[:, :], in_=k.cst["ident"][:, :]), r=[k.cst["ident"]], w=[k.ident_bf])
        k.ones_bf = A("ones_bf", [128, 128], BF16)
        S.op("dve", lambda e: e.tensor_copy(out=k.ones_bf[:, :], in_=k.cst["ones"][:, :]), r=[k.cst["ones"]], w=[k.ones_bf])
        NT = NS * TS
        k.smp_dn = A("smp_dn", [128, 12, NT], BF16)
        k.smp_gb = A("smp_gb", [NT, 8], F32)
        k.smp_z = A("smp_z", [NT, 512], BF16)
        k.smp_qk = A("smp_qk", [128, 3, 2, 2, NT], BF16)
        k.smp_kv = A("smp_kv", [NT, 3, 512], F32)
        k.smp_cat = A("smp_cat", [NT, 512], BF16)
        k.smp_swo = A("smp_swo", [NT, 3, 260], F32)
        if _os.environ.get("P2DBG"):
            k.dbg = nc.dram_tensor("dbg", [128, 4096], F32, kind="ExternalOutput")
        if 4 in phases and EXPROWS >= 1024:
            phase0_convert(k)
        if 1 in phases:
            phase1(k)
        if 5 in phases:
            phase_mem_and_windows(k)
        if 2 in phases:
            phase2a(k)
        if 3 in phases:
            phase2b(k)
        if 4 in phases:
            phase3(k)
        S.barrier()
        S.final_wait()
        S.emit()
    k.ninst = S.ninst
    return nc, k


def shard_inputs(inp):
    consts = np.stack([make_consts()[n] for n in CONST_NAMES]).astype(np.float32)
    maps = []
    c0 = make_consts()
    for c in range(NCORES):
        b, j = c // 2, c % 2
        xe = np.zeros((EXT, D), np.float32)
        if j == 0:
            xe[PRE:] = inp["x_prompt"][b, :MAIN]
            halo = np.full((128, 128), NEG, np.float32)
        else:
            xe[:] = inp["x_prompt"][b]
            halo = c0["swprev"]
        sl = slice(c * NS, (c + 1) * NS)
        m = {
            "xe": xe,
            "xs": np.ascontiguousarray(inp["x_sample"][sl].reshape(NS * TS, D)),
            "st_delta": np.ascontiguousarray(inp["state_delta"][0, sl]),
            "st_conv": np.ascontiguousarray(inp["state_conv"][0, sl]),
            "cwin0": np.ascontiguousarray(inp["cache_win1"][0, sl].reshape(NS, 128, 512)),
            "cwin1": np.ascontiguousarray(inp["cache_win2"][0, sl].reshape(NS, 512, 512)),
            "cwin2": np.ascontiguousarray(inp["cache_win3"][0, sl].reshape(NS, 2048, 512)),
            "cmem": np.ascontiguousarray(inp["cache_mem_kv"][0, sl].reshape(NS, 256, 2048)),
            "mem": np.ascontiguousarray(inp["mem_prompt"][b]),
            "halo": halo,
            "consts": consts,
            "g_mix": inp["g_mix"][0], "w_in": inp["w_in"][0], "conv_w": inp["conv_w"][0], "a_log": inp["a_log"][0],
            "dt_bias": inp["dt_bias"][0], "g_onorm": inp["g_onorm"][0], "w_out": inp["w_out"][0], "g_memq": inp["g_memq"][0],
            "g_memkv": inp["g_memkv"][0], "w_mq": inp["w_mq"][0], "w_mkv": inp["w_mkv"][0], "w_mo": inp["w_mo"][0],
            "g_ffn": inp["g_ffn"][0], "w_pq": inp["w_pq"][0], "sub_keys": inp["sub_keys"][0].reshape(16, 128, 128),
            "expert_u": inp["expert_u"][0][:EXPROWS], "expert_v": inp["expert_v"][0][:EXPROWS], "g_final": inp["g_final"],
        }
        maps.append({kk: np.ascontiguousarray(np.asarray(v, dtype=np.float32)) for kk, v in m.items()})
    return maps


def assemble(res):
    B, SEQ, DB = 4, 8192, 32
    y_prompt = np.zeros((B, SEQ, D), np.float32)
    y_sample = np.zeros((DB, TS, D), np.float32)
    p_delta = np.zeros((1, B, 4, 128, 128), np.float32)
    p_conv = np.zeros((1, B, 3, 1536), np.float32)
    p_win = [np.zeros((1, B, 128 * d, 2, 4, 64), np.float32) for d in DILS]
    p_mem = np.zeros((1, B, 256, 2, 4, 256), np.float32)
    s_delta = np.zeros((1, DB, 4, 128, 128), np.float32)
    s_conv = np.zeros((1, DB, 3, 1536), np.float32)
    s_win = [np.zeros((1, DB, 128 * d, 2, 4, 64), np.float32) for d in DILS]
    for c in range(NCORES):
        r = res[c]
        b, j = c // 2, c % 2
        y_prompt[b, j * MAIN:(j + 1) * MAIN] = r["y_main"]
        sl = slice(c * NS, (c + 1) * NS)
        y_sample[sl] = r["y_smp"].reshape(NS, TS, D)
        s_delta[0, sl] = r["s_delta"]
        s_conv[0, sl] = r["s_conv"]
        for g in range(3):
            s_win[g][0, sl] = r[f"s_win{g}"].reshape(NS, 128 * DILS[g], 2, 4, 64)
        if j == 1:
            p_delta[0, b] = r["p_delta"]
            p_conv[0, b] = r["p_conv"]
            for g in range(3):
                p_win[g][0, b] = r[f"p_win{g}"].reshape(128 * DILS[g], 2, 4, 64)
            p_mem[0, b] = r["p_mem"].reshape(256, 2, 4, 256)
    return (y_prompt, y_sample, p_delta, p_conv, p_win[0], p_win[1], p_win[2], p_mem,
            s_delta, s_conv, s_win[0], s_win[1], s_win[2])


def kernel(**inputs):
    inp = {kk: np.asarray(v) for kk, v in inputs.items()}
    nc, k = build_nc()
    maps = shard_inputs(inp)
    res = run_bass_kernel_spmd(nc, maps, core_ids=list(range(NCORES)))
    return assemble(res.results)


def phase0_convert(k):
    nc, S = k.nc, k.S
    with ExitStack() as es:
        A = lambda name, shape, dt: k.track(TL(es.enter_context(nc.sbuf_tensor(name, shape, dt)), name))
        st = [A(f"cv_f{i}", [128, 8192], F32) for i in range(2)]
        oa = [A(f"cv_a{i}", [128, 4096], BF16) for i in range(2)]
        ob = [A(f"cv_b{i}", [128, 4096], BF16) for i in range(2)]
        n = 0
        for (src_, c0_) in ((k.expert_u, 0), (k.expert_v, D)):
            for ps_ in range(EXPROWS // 1024):
                s_, a_, b_ = st[n % 2], oa[n % 2], ob[n % 2]
                r0 = ps_ * 1024
                S.dma("sp", s_, lambda e: e.dma_start(out=s_[:, :], in_=src_[r0:r0 + 1024, :].rearrange("(p j) d -> p (j d)", j=8)), w=[s_])
                S.op("act", lambda e: e.activation(out=a_[:, :], in_=s_[:, 0:4096], func=AF.Copy), r=[s_], w=[a_])
                S.op("dve", lambda e: e.tensor_copy(out=b_[:, :], in_=s_[:, 4096:8192]), r=[s_], w=[b_])
                dview = k.euv_bf[r0:r0 + 1024, c0_:c0_ + D].rearrange("(p j) d -> p j d", j=8)
                S.dma("pool", a_, lambda e: e.dma_start(out=dview[:, 0:4, :], in_=a_[:, :].rearrange("p (j d) -> p j d", j=4)), r=[a_])
                S.dma("pool", b_, lambda e: e.dma_start(out=dview[:, 4:8, :], in_=b_[:, :].rearrange("p (j d) -> p j d", j=4)), r=[b_])
                n += 1
        k.end_phase()


def phase_mem_and_windows(k):
    nc, S = k.nc, k.S
    d2d = k.track(TL(None, "d2d"))
    for g in range(3):
        wb = 128 * DILS[g]
        for s_ in range(NS):
            S.dma("sp", d2d, lambda e: e.dma_start(out=k.s_win[g][s_, 0:wb - TS, :], in_=k.cwin[g][s_, TS:wb, :]))
            S.dma("sp", k.smp_kv, lambda e: e.dma_start(out=k.s_win[g][s_, wb - TS:wb, :], in_=k.smp_kv[s_ * TS:(s_ + 1) * TS, g, :]),
                  r=[k.smp_kv])
    with ExitStack() as es:
        A = lambda name, shape, dt: k.track(TL(es.enter_context(nc.sbuf_tensor(name, shape, dt)), name))
        P = lambda name, shape, dt: TL(es.enter_context(nc.psum_tensor(name, shape, dt)), name)
        wbf = load_weight_bf16(k, es, "w_mkv", k.w_mkv, D, 2048, gdram=k.g_memkv, col_chunk=2048)
        xt = [A(f"mxt{i}", [128, D], F32) for i in range(2)]
        sqj = A("msqj", [128, D], BF16)
        ssq = A("mssq", [128, 1], F32)
        rstd = A("mrstd", [128, 1], F32)
        ab = A("mab", [128, D], BF16)
        aT = A("maT", [128, 8, 256], BF16)
        ob = [A(f"mob{i}", [128, 512], F32) for i in range(2)]
        pT = P("mpT", [128, 8, 128], BF16)
        ps_ = [P(f"mps{i}", [128, 512], F32) for i in range(2)]
        for t in range(2):
            x_ = xt[t]
            S.dma("sp", x_, lambda e: e.dma_start(out=x_[:, :], in_=k.mem[t * 128:(t + 1) * 128, :]), w=[x_])
            S.op("act", lambda e: e.activation(out=sqj[:, :], in_=x_[:, :], func=AF.Square, accum_out=ssq[:, 0:1]), r=[x_], w=[sqj, ssq])
            rsqrt(S, rstd, rstd[:, :], ssq, ssq[:, :], 1.0 / D, EPS)
            S.op("act", lambda e: e.activation(out=ab[:, :], in_=x_[:, :], func=AF.Copy, scale=rstd[:, 0:1]), r=[x_, rstd], w=[ab])
            for c in range(8):
                S.op("pe", lambda e: e.transpose(out=pT[:, c, :], in_=ab[:, c * 128:(c + 1) * 128], identity=k.ident_bf[:, :]),
                     r=[ab, k.ident_bf], w=[pT])
            S.op("dve", lambda e: e.tensor_copy(out=aT[:, :, t * 128:(t + 1) * 128], in_=pT[:, :, :]), r=[pT], w=[aT])
        n = 0
        for t in range(2):
            for nb in range(4):
                ps = ps_[n % 2]
                o_ = ob[n % 2]
                for c in range(8):
                    S.op("pe", lambda e: e.matmul(ps[:, :], lhsT=aT[:, c, t * 128:(t + 1) * 128], rhs=wbf[:, c, nb * 512:(nb + 1) * 512],
                                                  start=(c == 0), stop=(c == 7)), r=[aT, wbf], w=[ps])
                evac(S, n, o_, o_[:, :], ps, ps[:, :])
                S.dma("sp", o_, lambda e: e.dma_start(out=k.p_mem[t * 128:(t + 1) * 128, nb * 512:(nb + 1) * 512], in_=o_[:, :]), r=[o_])
                n += 1
        k.end_phase()


def phase2a(k):
    import os
    nc, S = k.nc, k.S
    with ExitStack() as es:
        A = lambda name, shape, dt: k.track(TL(es.enter_context(nc.sbuf_tensor(name, shape, dt)), name))
        P = lambda name, shape, dt: TL(es.enter_context(nc.psum_tensor(name, shape, dt)), name)
        cst = k.cst
        ident, trit, blk, mTneg, mSpos, ones = (cst[n] for n in ("ident", "trit", "blk", "mTneg", "mSpos", "ones"))
        identb = k.ident_bf
        cbf = {}
        for n_ in ("trit", "blk", "mTneg", "mSpos"):
            cbf[n_] = A("cbf_" + n_, [128, 128], BF16)
            S.op("dve", lambda e: e.tensor_copy(out=cbf[n_][:, :], in_=cst[n_][:, :]), r=[cst[n_]], w=[cbf[n_]])
        tritb, blkb, mTnegb, mSposb = (cbf[n_] for n_ in ("trit", "blk", "mTneg", "mSpos"))
        onesb = k.ones_bf
        hones = [A(f"hones{c_}", [128, 128], BF16) for c_ in range(2)]
        for c_ in range(2):
            S.op("dve", lambda e: e.tensor_copy(out=hones[c_][:, :], in_=cst["blk"][:, 127 * c_:127 * c_ + 1].to_broadcast([128, 128])),
                 r=[cst["blk"]], w=[hones[c_]])
        ghl = A("ghl", [128, 8], BF16)
        Gall_h = A("Gall_h", [128, 4, 128], BF16)
        Gall_l = A("Gall_l", [128, 4, 128], BF16)
        gon = A("gon", [128, 128], F32)
        S.dma("sp", gon, lambda e: e.dma_start(out=gon[:, :], in_=k.g_onorm.ap().partition_broadcast(128)), w=[gon])
        qT = [A(f"qT{i}", [128, 4, 128], BF16) for i in range(2)]
        kT = [A(f"kT{i}", [128, 4, 128], BF16) for i in range(2)]
        vT = [A(f"vT{i}", [128, 4, 128], BF16) for i in range(2)]
        gbt = [A(f"gbt{i}", [128, 8], F32) for i in range(2)]
        zt = [A(f"zt{i}", [128, 512], BF16) for i in range(2)]
        gc = A("gc", [128, 4], F32); ngc = A("ngc", [128, 4], F32); egc = A("egc", [128, 4], F32)
        negegc = A("negegc", [128, 4], F32); ekd = A("ekd", [128, 4], F32); dl = A("dl", [128, 8], F32)
        dif = A("dif", [128, 4], F32)
        dlraw = A("dlraw", [128, 8], F32)
        Gall = A("Gall", [128, 4, 128], F32)
        egcb = [A(f"egcb{i}", [128, 128], F32) for i in range(2)]
        gamT = [A(f"gamT{i}", [128, 128], F32) for i in range(2)]
        gamS = [A(f"gamS{i}", [128, 128], F32) for i in range(2)]
        Lx = [A(f"Lx{i}", [128, 128], BF16) for i in range(3)]
        Ly = [A(f"Ly{i}", [128, 128], BF16) for i in range(3)]
        Rr = [A(f"Rr{i}", [128, 128], BF16) for i in range(2)]
        AqkT = [[A(f"AqkT{i}_{h}", [128, 128], BF16) for h in range(4)] for i in range(2)]
        qgT = [[A(f"qgT{i}_{h}", [128, 128], BF16) for h in range(4)] for i in range(2)]
        TbT = [[A(f"TbT{i}_{h}", [128, 128], BF16) for h in range(4)] for i in range(2)]
        kd = [[A(f"kd{i}_{h}", [128, 128], BF16) for h in range(4)] for i in range(2)]
        vtok = [[A(f"vtok{i}_{h}", [128, 128], F32) for h in range(4)] for i in range(2)]
        sc_neg = [A(f"scneg{i}", [128, 4], F32) for i in range(2)]
        sc_dl = [A(f"scdl{i}", [128, 8], F32) for i in range(2)]
        rt = [A(f"rt{h}", [128, 128], BF16) for h in range(4)]
        ut = [A(f"ut{h}", [128, 128], BF16) for h in range(4)]
        St = [A(f"St{h}", [128, 128], F32) for h in range(4)]
        Sb = [A(f"Sb{h}", [128, 128], BF16) for h in range(4)]
        ot = [A(f"ot{i}", [128, 512], F32) for i in range(2)]
        ssq = A("ossq", [128, 4], F32); orstd = A("orstd", [128, 4], F32)
        sqj = A("osqj", [128, 128], F32)
        szt = A("szt", [128, 512], F32)
        t1 = A("t1", [128, 512], F32)
        og = [A(f"og{i}", [128, 512], BF16) for i in range(2)]
        pb = PsumBlocks(nc, es, "p2a_")
        P = lambda name, dt, bank: pb.get(name, 128, dt, bank)
        pKS = P("pKS", F32, "scan"); pU = P("pU", F32, "scan"); pO = P("pO", F32, "scan"); pdS = P("pdS", F32, "scan")
        pA = [P(f"pA{i}", F32, f"g{i}") for i in range(2)]
        pB = [P(f"pB{i}", F32, f"g{i}") for i in range(2)]
        pC = [P(f"pC{i}", F32, f"g{i}") for i in range(2)]
        pKK = [P(f"pKK{i}", F32, f"k{i}") for i in range(2)]
        pQK = [P(f"pQK{i}", F32, f"k{i}") for i in range(2)]
        pX = P("pX", F32, "nX"); pY = P("pY", F32, "nY"); pP = P("pP", F32, "nP")
        pgt = pb.get("pgt", 16, F32, "k0")
        pLT = P("pLT", F32, "nY"); pkt = P("pkt", F32, "nP"); pvt = P("pvt", F32, "nY")
        def mm(ps, out_ap, lt, lap, rt_, rap, start=True, stop=True):
            S.op("pe", lambda e: e.matmul(out_ap, lhsT=lap, rhs=rap, start=start, stop=stop), r=[lt, rt_], w=[ps])

        def prep(i, q_, k_, v_, g_, need_o=True):
            S.op("dve", lambda e: e.tensor_copy(out=ghl[:, 0:4], in_=g_[:, 0:4]), r=[g_], w=[ghl])
            S.op("dve", lambda e: e.tensor_tensor(out=ghl[:, 4:8], in0=g_[:, 0:4], in1=ghl[:, 0:4], op=ALU.subtract), r=[g_, ghl], w=[ghl])
            for (c0, lt_, lap) in ((0, tritb, tritb[:, :]), (4, blkb, blkb[:, :])):
                mm(pgt, pgt[:, c0:c0 + 4], lt_, lap, ghl, ghl[:, 0:4], True, False)
                mm(pgt, pgt[:, c0:c0 + 4], lt_, lap, ghl, ghl[:, 4:8], False, True)
            for c_ in range(2):
                mm(pgt, pgt[:, 8 + 4 * c_:12 + 4 * c_], hones[c_], hones[c_][:, :], ghl, ghl[:, 0:4], True, False)
                mm(pgt, pgt[:, 8 + 4 * c_:12 + 4 * c_], hones[c_], hones[c_][:, :], ghl, ghl[:, 4:8], False, True)
            S.op("dve", lambda e: e.tensor_copy(out=gc[:, :], in_=pgt[:, 0:4]), r=[pgt], w=[gc])
            S.op("dve", lambda e: e.tensor_scalar(out=ngc[:, :], in0=gc[:, :], scalar1=-1.0, scalar2=None, op0=ALU.mult), r=[gc], w=[ngc])
            S.op("act", lambda e: e.activation(out=egc[:, :], in_=gc[:, :], func=AF.Exp), r=[gc], w=[egc])
            S.op("dve", lambda e: e.tensor_scalar(out=sc_neg[i][:, :], in0=egc[:, :], scalar1=-1.0, scalar2=None, op0=ALU.mult),
                 r=[egc], w=[sc_neg[i]])
            S.op("dve", lambda e: e.tensor_tensor(out=dif[:, :], in0=pgt[:, 4:8], in1=gc[:, :], op=ALU.subtract), r=[pgt, gc], w=[dif])
            S.op("act", lambda e: e.activation(out=ekd[:, :], in_=dif[:, :], func=AF.Exp), r=[dif], w=[ekd])
            S.op("dve", lambda e: e.tensor_copy(out=dlraw[:, :], in_=pgt[:, 8:16]), r=[pgt], w=[dlraw])
            S.op("act", lambda e: e.activation(out=sc_dl[i][:, :], in_=dlraw[:, :], func=AF.Exp), r=[dlraw], w=[sc_dl[i]])
            stg = int(os.environ.get("P2PREP", "9"))
            if stg < 1:
                return
            for h in range(4):
                S.op("dve", lambda e: e.tensor_copy(out=Gall_h[:, h, :], in_=ghl[:, h:h + 1].to_broadcast([128, 128])), r=[ghl], w=[Gall_h])
                S.op("dve", lambda e: e.tensor_copy(out=Gall_l[:, h, :], in_=ghl[:, 4 + h:5 + h].to_broadcast([128, 128])), r=[ghl], w=[Gall_l])
            for h in range(4):
                j = h % 2
                if stg < 2:
                    continue
                if need_o:
                    mm(pA[j], pA[j][:, :], Gall_h, Gall_h[:, h, :], tritb, tritb[:, :], True, False)
                    mm(pA[j], pA[j][:, :], Gall_l, Gall_l[:, h, :], tritb, tritb[:, :], False, True)
                    mm(pB[j], pB[j][:, :], Gall_h, Gall_h[:, h, :], tritb, tritb[:, :], True, False)
                    mm(pB[j], pB[j][:, :], Gall_l, Gall_l[:, h, :], tritb, tritb[:, :], False, False)
                    mm(pB[j], pB[j][:, :], identb, identb[:, :], mTnegb, mTnegb[:, :], False, True)
                mm(pC[j], pC[j][:, :], Gall_h, Gall_h[:, h, :], tritb, tritb[:, :], True, False)
                mm(pC[j], pC[j][:, :], Gall_l, Gall_l[:, h, :], tritb, tritb[:, :], False, False)
                mm(pC[j], pC[j][:, :], identb, identb[:, :], mSposb, mSposb[:, :], False, True)
                if need_o:
                    S.op("act", lambda e: e.activation(out=egcb[j][:, :], in_=pA[j][:, :], func=AF.Exp), r=[pA[j]], w=[egcb[j]])
                    S.op("act", lambda e: e.activation(out=gamT[j][:, :], in_=pB[j][:, :], func=AF.Exp, bias=ngc[:, h:h + 1]),
                         r=[pB[j], ngc], w=[gamT[j]])
                S.op("act", lambda e: e.activation(out=gamS[j][:, :], in_=pC[j][:, :], func=AF.Exp, bias=gc[:, h:h + 1], scale=-1.0),
                     r=[pC[j], gc], w=[gamS[j]])
                if stg < 3:
                    continue
                mm(pKK[j], pKK[j][:, :], k_, k_[:, h, :], k_, k_[:, h, :])
                if need_o:
                    mm(pQK[j], pQK[j][:, :], k_, k_[:, h, :], q_, q_[:, h, :])
                X, Y = Lx[0], Ly[0]
                S.op("dve", lambda e: e.scalar_tensor_tensor(out=X[:, :], in0=pKK[j][:, :], scalar=g_[:, 4 + h:5 + h], in1=gamS[j][:, :],
                                                             op0=ALU.mult, op1=ALU.mult), r=[pKK[j], g_, gamS[j]], w=[X])
                if need_o:
                    S.op("dve", lambda e: e.tensor_tensor(out=AqkT[i][h][:, :], in0=pQK[j][:, :], in1=gamT[j][:, :], op=ALU.mult),
                         r=[pQK[j], gamT[j]], w=[AqkT[i][h]])
                    S.op("pool", lambda e: e.tensor_tensor(out=qgT[i][h][:, :], in0=q_[:, h, :], in1=egcb[j][:, :], op=ALU.mult),
                         r=[q_, egcb[j]], w=[qgT[i][h]])
                if stg < 4:
                    continue
                exp_ = os.environ.get("P2EXP", "")
                if exp_ == "A":
                    mm(pLT, pLT[:, :], identb, identb[:, :], identb, identb[:, :])
                else:
                    mm(pLT, pLT[:, :], X, X[:, :], identb, identb[:, :])
                if exp_ == "D":
                    S.op("act", lambda e: e.activation(out=Y[:, :], in_=pLT[:, :], func=AF.Copy), r=[pLT], w=[Y])
                elif exp_ != "B":
                    S.op("dve", lambda e: e.tensor_copy(out=Y[:, :], in_=pLT[:, :]), r=[pLT], w=[Y])
                sub = os.environ.get("P2SUB", "z")
                if sub == "a":
                    continue
                R = Rr[0]
                S.op("pool", lambda e: e.tensor_tensor(out=R[:, :], in0=identb[:, :], in1=Y[:, :], op=ALU.subtract), r=[identb, Y], w=[R])
                if sub == "b":
                    continue
                for it in range(1, 6):
                    Xn, Yn = Lx[it % 3], Ly[it % 3]
                    mm(pX, pX[:, :], Y, Y[:, :], X, X[:, :])
                    if sub == "c":
                        break
                    if it < 5:
                        mm(pY, pY[:, :], X, X[:, :], Y, Y[:, :])
                    S.op("act", lambda e: e.activation(out=Xn[:, :], in_=pX[:, :], func=AF.Copy), r=[pX], w=[Xn])
                    if it < 5:
                        S.op("dve", lambda e: e.tensor_copy(out=Yn[:, :], in_=pY[:, :]), r=[pY], w=[Yn])
                    mm(pP, pP[:, :], Xn, Xn[:, :], R, R[:, :])
                    Rn = Rr[it % 2]
                    S.op("dve", lambda e: e.tensor_tensor(out=Rn[:, :], in0=pP[:, :], in1=R[:, :], op=ALU.add), r=[pP, R], w=[Rn])
                    X, Y, R = Xn, Yn, Rn
                if stg < 5:
                    continue
                S.op("pool", lambda e: e.tensor_scalar(out=TbT[i][h][:, :], in0=R[:, :], scalar1=g_[:, 4 + h:5 + h], scalar2=None, op0=ALU.mult),
                     r=[R, g_], w=[TbT[i][h]])
                mm(pkt, pkt[:, :], k_, k_[:, h, :], identb, identb[:, :])
                S.op("dve", lambda e: e.tensor_scalar(out=kd[i][h][:, :], in0=pkt[:, :], scalar1=ekd[:, h:h + 1], scalar2=None, op0=ALU.mult),
                     r=[pkt, ekd], w=[kd[i][h]])
                mm(pvt, pvt[:, :], v_, v_[:, h, :], identb, identb[:, :])
                S.op("dve", lambda e: e.tensor_copy(out=vtok[i][h][:, :], in_=pvt[:, :]), r=[pvt], w=[vtok[i][h]])

        def scan(i, k_, o_, pre=None, post=None, need_o=True):
            for c in range(2):
                ps_ = slice(64 * c, 64 * c + 64)
                if pre:
                    pre(c)
                for h in range(4):
                    mm(pKS, pKS[:, :], k_, k_[:, h, :], Sb[h], Sb[h][:, :])
                    S.op("dve", lambda e: e.scalar_tensor_tensor(out=rt[h][ps_, :], in0=pKS[ps_, :], scalar=sc_neg[i][ps_, h:h + 1],
                                                                 in1=vtok[i][h][ps_, :], op0=ALU.mult, op1=ALU.add),
                         r=[pKS, sc_neg[i], vtok[i][h]], w=[rt[h]])
                    mm(pU, pU[:, :], TbT[i][h], TbT[i][h][ps_, :], rt[h], rt[h][ps_, :])
                    S.op("act", lambda e: e.activation(out=ut[h][ps_, :], in_=pU[ps_, :], func=AF.Copy), r=[pU], w=[ut[h]])
                    if need_o:
                        mm(pO, pO[:, :], qgT[i][h], qgT[i][h][:, :], Sb[h], Sb[h][:, :], True, False)
                        mm(pO, pO[:, :], AqkT[i][h], AqkT[i][h][ps_, :], ut[h], ut[h][ps_, :], False, True)
                        S.op("act", lambda e: e.activation(out=o_[ps_, h * 128:(h + 1) * 128], in_=pO[ps_, :], func=AF.Copy), r=[pO], w=[o_])
                    mm(pdS, pdS[:, :], kd[i][h], kd[i][h][ps_, :], ut[h], ut[h][ps_, :])
                    S.op("dve", lambda e: e.scalar_tensor_tensor(out=St[h][:, :], in0=St[h][:, :], scalar=sc_dl[i][:, 4 * c + h:4 * c + h + 1],
                                                                 in1=pdS[:, :], op0=ALU.mult, op1=ALU.add),
                         r=[St[h], sc_dl[i], pdS], w=[St[h]])
                    S.op("act", lambda e: e.activation(out=Sb[h][:, :], in_=St[h][:, :], func=AF.Copy), r=[St[h]], w=[Sb[h]])
                if post:
                    post(c)

        def post_out(o_, z_, j, dst_fn):
            for h in range(4):
                S.op("act", lambda e: e.activation(out=sqj[:, :], in_=o_[:, h * 128:(h + 1) * 128], func=AF.Square, accum_out=ssq[:, h:h + 1]),
                     r=[o_], w=[sqj, ssq])
            rsqrt(S, orstd, orstd[:, :], ssq, ssq[:, :], 1.0 / 128, EPS)
            S.op("act", lambda e: e.activation(out=szt[:, :], in_=z_[:, :], func=AF.Silu), r=[z_], w=[szt])
            for h in range(4):
                S.op("dve", lambda e: e.scalar_tensor_tensor(out=t1[:, h * 128:(h + 1) * 128], in0=o_[:, h * 128:(h + 1) * 128],
                                                             scalar=orstd[:, h:h + 1], in1=gon[:, :], op0=ALU.mult, op1=ALU.mult),
                     r=[o_, orstd, gon], w=[t1])
            S.op("dve", lambda e: e.tensor_tensor(out=og[j][:, :], in0=t1[:, :], in1=szt[:, :], op=ALU.mult), r=[t1, szt], w=[og[j]])
            dst_fn(og[j])

        for h in range(4):
            S.op("dve", lambda e: e.memset(St[h][:, :], 0.0), w=[St[h]])
            S.op("dve", lambda e: e.memset(Sb[h][:, :], 0.0), w=[Sb[h]])
        ntile = EXT // 128
        tiles = range(int(os.environ.get("P2START", "0")), int(os.environ.get("P2TILES", str(ntile))))
        for tau in tiles:
            i = tau % 2
            t0 = tau * 128
            for h in range(4):
                S.dma("sp", qT[i], lambda e: e.dma_start(out=qT[i][:, h, :], in_=k.dnq[h, :, t0:t0 + 128]), w=[qT[i]])
                S.dma("sp", kT[i], lambda e: e.dma_start(out=kT[i][:, h, :], in_=k.dnk[h, :, t0:t0 + 128]), w=[kT[i]])
                S.dma("sp", vT[i], lambda e: e.dma_start(out=vT[i][:, h, :], in_=k.dnv[h, :, t0:t0 + 128]), w=[vT[i]])
            S.dma("sp", gbt[i], lambda e: e.dma_start(out=gbt[i][:, :], in_=k.gbs[t0:t0 + 128, :]), w=[gbt[i]])
            main = t0 >= PRE
            if main:
                S.dma("sp", zt[i], lambda e: e.dma_start(out=zt[i][:, :], in_=k.zs[t0 - PRE:t0 - PRE + 128, :]), w=[zt[i]])
            mode = os.environ.get("P2MODE", "all")
            if mode == "load":
                continue
            prep(i, qT[i], kT[i], vT[i], gbt[i], need_o=main)
            if mode == "prep":
                continue
            scan(i, kT[i], ot[i], need_o=main)
            if main:
                post_out(ot[i], zt[i], i,
                         lambda o_: S.dma("sp", o_, lambda e: e.dma_start(out=k.cat[t0 - PRE:t0 - PRE + 128, :], in_=o_[:, :]), r=[o_]))
        for h in range(4):
            S.dma("sp", St[h], lambda e: e.dma_start(out=k.p_delta[h, :, :], in_=St[h][:, :]), r=[St[h]])

        for tb in range(2 if os.environ.get("P2MODE", "all") == "all" else 0):
            i = tb
            for tt in (qT[i], kT[i], vT[i]):
                S.op("pool", lambda e: e.memset(tt[:, :, :], 0.0), w=[tt])
            S.op("pool", lambda e: e.memset(gbt[i][:, :], 0.0), w=[gbt[i]])
            S.op("pool", lambda e: e.memset(zt[i][:, :], 0.0), w=[zt[i]])
            for c in range(2):
                sq = tb * 2 + c
                for h in range(4):
                    for which, tt in enumerate((qT[i], kT[i], vT[i])):
                        S.op("pool", lambda e: e.tensor_copy(out=tt[:, h, 64 * c:64 * c + TS], in_=k.smp_dn[:, which * 4 + h, sq * TS:(sq + 1) * TS]),
                             r=[k.smp_dn], w=[tt])
                S.dma("sp", gbt[i], lambda e: e.dma_start(out=gbt[i][64 * c:64 * c + TS, :], in_=k.smp_gb[sq * TS:(sq + 1) * TS, :]),
                      r=[k.smp_gb], w=[gbt[i]])
                S.dma("sp", zt[i], lambda e: e.dma_start(out=zt[i][64 * c:64 * c + TS, :], in_=k.smp_z[sq * TS:(sq + 1) * TS, :]),
                      r=[k.smp_z], w=[zt[i]])
            prep(i, qT[i], kT[i], vT[i], gbt[i])

            def pre(c, tb=tb):
                sq = tb * 2 + c
                for h in range(4):
                    S.dma("sp", St[h], lambda e: e.dma_start(out=St[h][:, :], in_=k.st_delta[sq, h, :, :]), w=[St[h]])
                    S.op("act", lambda e: e.activation(out=Sb[h][:, :], in_=St[h][:, :], func=AF.Copy), r=[St[h]], w=[Sb[h]])

            def post(c, tb=tb):
                sq = tb * 2 + c
                for h in range(4):
                    S.dma("sp", St[h], lambda e: e.dma_start(out=k.s_delta[sq, h, :, :], in_=St[h][:, :]), r=[St[h]])

            scan(i, kT[i], ot[i], pre, post)
            if tb == 0 and os.environ.get("P2DBG"):
                col = 0
                dbgt = TL(None, "dbgsem")
                for tt, wdt in ((TbT[0][0], 128), (AqkT[0][0], 128), (kd[0][0], 128), (qgT[0][0], 128), (vtok[0][0], 128),
                                (sc_neg[0], 4), (sc_dl[0], 8), (ot[0], 512), (rt[0], 128), (ut[0], 128), (St[0], 128), (gbt[0], 8)):
                    S.dma("pool", TL(None, f"dbg{col}"), lambda e: e.dma_start(out=k.dbg[:, col:col + wdt], in_=tt[:, 0:wdt]), r=[tt])
                    col += wdt

            def dst(o_, tb=tb):
                for c in range(2):
                    sq = tb * 2 + c
                    S.dma("sp", o_, lambda e: e.dma_start(out=k.smp_cat[sq * TS:(sq + 1) * TS, :], in_=o_[64 * c:64 * c + TS, :]),
                          r=[o_], w=[k.smp_cat])
            post_out(ot[i], zt[i], i, dst)
        k.end_phase()


def phase2b(k):
    import os
    nc, S = k.nc, k.S
    with ExitStack() as es:
        A = lambda name, shape, dt: k.track(TL(es.enter_context(nc.sbuf_tensor(name, shape, dt)), name))
        pb = PsumBlocks(nc, es, "p2b_")
        identb, onesb = k.ident_bf, k.ones_bf
        mprev = A("mprev", [128, 128], BF16); mcur = A("mcur", [128, 128], BF16); mhalo = A("mhalo", [128, 128], BF16)
        halo_f = A("halo_f", [128, 128], F32)
        S.dma("sp", halo_f, lambda e: e.dma_start(out=halo_f[:, :], in_=k.halo[:, :]), w=[halo_f])
        S.op("dve", lambda e: e.tensor_copy(out=mprev[:, :], in_=k.cst["swprev"][:, :]), r=[k.cst["swprev"]], w=[mprev])
        S.op("dve", lambda e: e.tensor_copy(out=mcur[:, :], in_=k.cst["swcur"][:, :]), r=[k.cst["swcur"]], w=[mcur])
        S.op("dve", lambda e: e.tensor_copy(out=mhalo[:, :], in_=halo_f[:, :]), r=[halo_f], w=[mhalo])
        QT = [A(f"QT{i}", [128, 2, 2048], BF16) for i in range(2)]
        KT = [A(f"KT{i}", [128, 2, 4096], BF16) for i in range(2)]
        Vb = [A(f"Vb{i}", [128, 2, 256], BF16) for i in range(2)]
        PT = [A(f"PT{i}", [128, 256], BF16) for i in range(2)]
        ot = [A(f"swot{i}", [128, 260], F32) for i in range(2)]
        psS = [pb.get(f"psS{i}", 256, F32, f"s{i}") for i in range(2)]
        psO = [pb.get(f"psO{i}", 128, F32, f"o{i}") for i in range(2)]
        def core(qf, kf, vf, msk0, o_, deps):
            qt_, kt_, vt_ = deps
            for head in range(4):
                pair, hh = head // 2, head % 2
                ph = slice(64 * hh, 64 * hh + 64)
                sS, sO, p_ = psS[head % 2], psO[head % 2], PT[head % 2]
                for kc in range(2):
                    msk = msk0 if kc == 0 else mcur
                    S.op("pe", lambda e: e.matmul(sS[:, kc * 128:(kc + 1) * 128], lhsT=identb[:, :], rhs=msk[:, :], start=True, stop=False),
                         r=[identb, msk], w=[sS])
                    S.op("pe", lambda e: e.matmul(sS[:, kc * 128:(kc + 1) * 128], lhsT=kf(kc, pair, ph), rhs=qf(pair, ph), start=False, stop=True),
                         r=[kt_, qt_], w=[sS])
                S.op("act", lambda e: e.activation(out=p_[:, :], in_=sS[:, :], func=AF.Exp, scale=0.125), r=[sS], w=[p_])
                for kc in range(2):
                    S.op("pe", lambda e: e.matmul(sO[:, 0:64], lhsT=p_[:, kc * 128:(kc + 1) * 128], rhs=vf(kc, head),
                                                  start=(kc == 0), stop=(kc == 1)), r=[p_, vt_], w=[sO])
                for kc in range(2):
                    S.op("pe", lambda e: e.matmul(sO[:, 64:65], lhsT=p_[:, kc * 128:(kc + 1) * 128], rhs=onesb[:, 0:1],
                                                  start=(kc == 0), stop=(kc == 1)), r=[p_, onesb], w=[sO])
                S.op("dve", lambda e: e.tensor_copy(out=o_[:, head * 65:(head + 1) * 65], in_=sO[:, 0:65]), r=[sO], w=[o_])

        groups = [int(x) for x in os.environ.get("P2BG", "0,1,2").split(",")]
        nblk_lim = int(os.environ.get("P2BN", "999"))
        u = 0
        for g in groups:
            d = DILS[g]
            span = 128 * d
            for n in range(min(MAIN // span, nblk_lim)):
                bi = (g * 64 + n) % 2
                q_, k_ = QT[bi], KT[bi]
                for pair in range(2):
                    S.dma("sp", q_, lambda e: e.dma_start(out=q_[:, pair, 0:span], in_=k.qts[g][pair, :, n * span:(n + 1) * span]), w=[q_])
                    k0 = HALO + (n - 1) * span
                    S.dma("sp", k_, lambda e: e.dma_start(out=k_[:, pair, 0:2 * span], in_=k.kts[g][pair, :, k0:k0 + 2 * span]), w=[k_])
                for r in range(d):
                    v_ = Vb[u % 2]
                    o_ = ot[u % 2]
                    v0 = HALO + (n - 1) * span + r
                    for kc in range(2):
                        S.dma("sp", v_, lambda e: e.dma_start(out=v_[:, kc, :],
                                                             in_=k.vss[g][v0 + kc * span:v0 + kc * span + 127 * d + 1:d, :]), w=[v_])
                    core(lambda pair, ph: q_[ph, pair, r:span:d],
                         lambda kc, pair, ph: k_[ph, pair, kc * span + r:(kc + 1) * span:d],
                         lambda kc, head: v_[:, kc, head * 64:(head + 1) * 64],
                         mhalo if n == 0 else mprev, o_, (q_, k_, v_))
                    t0 = n * span + r
                    S.dma("sp", o_, lambda e: e.dma_start(out=k.swo[g][t0:t0 + 127 * d + 1:d, :], in_=o_[:, :]), r=[o_])
                    u += 1
        if not os.environ.get("P2BNOSMP"):
            sq = A("sq", [128, 2, 128], BF16); sk = A("sk", [128, 2, 256], BF16); sv = A("sv", [128, 2, 256], BF16)
            ck = A("ck", [128, 512], F32); ckb = A("ckb", [128, 512], BF16); vst = A("vst", [128, 256], F32)
            pt = pb.get("ptr", 128, F32, "ptr")
            for s_ in range(NS):
                for g in range(3):
                    d = DILS[g]
                    for r in range(1 if d == 1 else TS):
                        nq = TS if d == 1 else 1
                        tok0 = s_ * TS + (0 if d == 1 else r)
                        o_ = ot[u % 2]
                        for tt in (sq, sk, sv):
                            S.op("pool", lambda e: e.memset(tt[:, :, :], 0.0), w=[tt])
                        S.op("pool", lambda e: e.memset(vst[:, :], 0.0), w=[vst])
                        S.dma("sp", ck, lambda e: e.dma_start(out=ck[:, :], in_=k.cwin[g][s_, r:r + 127 * d + 1:d, :]), w=[ck])
                        S.op("dve", lambda e: e.tensor_copy(out=ckb[:, :], in_=ck[:, :]), r=[ck], w=[ckb])
                        for pair in range(2):
                            S.op("pool", lambda e: e.tensor_copy(out=sq[:, pair, 0:nq], in_=k.smp_qk[:, g, 0, pair, tok0:tok0 + nq]), r=[k.smp_qk], w=[sq])
                            S.op("pool", lambda e: e.tensor_copy(out=sk[:, pair, 128:128 + nq], in_=k.smp_qk[:, g, 1, pair, tok0:tok0 + nq]),
                                 r=[k.smp_qk], w=[sk])
                            S.op("pe", lambda e: e.matmul(pt[:, :], lhsT=ckb[:, pair * 128:(pair + 1) * 128], rhs=identb[:, :], start=True, stop=True),
                                 r=[ckb, identb], w=[pt])
                            S.op("dve", lambda e: e.tensor_copy(out=sk[:, pair, 0:128], in_=pt[:, :]), r=[pt], w=[sk])
                        S.op("pool", lambda e: e.tensor_copy(out=sv[:, 0, :], in_=ckb[:, 256:512]), r=[ckb], w=[sv])
                        S.dma("sp", vst, lambda e: e.dma_start(out=vst[0:nq, :], in_=k.smp_kv[tok0:tok0 + nq, g, 256:512]), r=[k.smp_kv], w=[vst])
                        S.op("dve", lambda e: e.tensor_copy(out=sv[:, 1, :], in_=vst[:, :]), r=[vst], w=[sv])
                        core(lambda pair, ph: sq[ph, pair, :], lambda kc, pair, ph: sk[ph, pair, kc * 128:(kc + 1) * 128],
                             lambda kc, head: sv[:, kc, head * 64:(head + 1) * 64], mprev, o_, (sq, sk, sv))
                        S.dma("sp", o_, lambda e: e.dma_start(out=k.smp_swo[tok0:tok0 + nq, g, :], in_=o_[0:nq, :]), r=[o_], w=[k.smp_swo])
                        u += 1
        k.end_phase()


def phase3(k):
    import os
    nc, S = k.nc, k.S
    NT = NS * TS
    with ExitStack() as es:
        A = lambda name, shape, dt: k.track(TL(es.enter_context(nc.sbuf_tensor(name, shape, dt)), name))
        pb = PsumBlocks(nc, es, "p3_")
        identb, onesb = k.ident_bf, k.ones_bf
        iota = k.cst["iota"]
        KmT = A("KmT", [128, 8, 256], BF16); Vm = A("Vm", [128, 2, 1024], BF16)
        KsT = A("KsT", [128, NS, 8, 256], BF16); Vs = A("Vs", [128, NS, 2, 1024], BF16)
        pbig = [pb.get(f"pbig{i}", 512, F32, f"big{i}") for i in range(4)]
        with ExitStack() as es2:
            A2 = lambda name, shape, dt: k.track(TL(es2.enter_context(nc.sbuf_tensor(name, shape, dt)), name))
            wkv = load_weight_bf16(k, es2, "w_mkv3", k.w_mkv, D, 2048, gdram=k.g_memkv, col_chunk=2048)
            mx = A2("m3x", [128, D], F32); msq = A2("m3sq", [128, D], BF16); mss = A2("m3ss", [128, 1], F32)
            mrs = A2("m3rs", [128, 1], F32); mab = A2("m3ab", [128, D], BF16); maT = A2("m3aT", [128, 8, 256], BF16)
            cs = A2("m3cs", [128, 2048], F32); csb = A2("m3csb", [128, 2048], BF16)
            for t in range(2):
                S.dma("sp", mx, lambda e: e.dma_start(out=mx[:, :], in_=k.mem[t * 128:(t + 1) * 128, :]), w=[mx])
                S.op("act", lambda e: e.activation(out=msq[:, :], in_=mx[:, :], func=AF.Square, accum_out=mss[:, 0:1]), r=[mx], w=[msq, mss])
                rsqrt(S, mrs, mrs[:, :], mss, mss[:, :], 1.0 / D, EPS)
                S.op("act", lambda e: e.activation(out=mab[:, :], in_=mx[:, :], func=AF.Copy, scale=mrs[:, 0:1]), r=[mx, mrs], w=[mab])
                for half in range(2):
                    for c in range(4):
                        cc = half * 4 + c
                        S.op("pe", lambda e: e.matmul(pbig[0][:, c * 128:(c + 1) * 128], lhsT=mab[:, cc * 128:(cc + 1) * 128], rhs=identb[:, :],
                                                      start=True, stop=True), r=[mab, identb], w=[pbig[0]])
                    for c in range(4):
                        cc = half * 4 + c
                        S.op("dve", lambda e: e.tensor_copy(out=maT[:, cc, t * 128:(t + 1) * 128], in_=pbig[0][:, c * 128:(c + 1) * 128]),
                             r=[pbig[0]], w=[maT])
            for hc in range(8):
                for c in range(8):
                    S.op("pe", lambda e: e.matmul(pbig[1][:, 0:256], lhsT=wkv[:, c, hc * 128:(hc + 1) * 128], rhs=maT[:, c, :],
                                                  start=(c == 0), stop=(c == 7)), r=[wkv, maT], w=[pbig[1]])
                S.op("dve", lambda e: e.tensor_copy(out=KmT[:, hc, :], in_=pbig[1][:, 0:256]), r=[pbig[1]], w=[KmT])
            for kc in range(2):
                for nb in range(2):
                    for c in range(8):
                        S.op("pe", lambda e: e.matmul(pbig[2][:, :], lhsT=maT[:, c, kc * 128:(kc + 1) * 128], rhs=wkv[:, c, 1024 + nb * 512:1024 + (nb + 1) * 512],
                                                      start=(c == 0), stop=(c == 7)), r=[wkv, maT], w=[pbig[2]])
                    S.op("dve", lambda e: e.tensor_copy(out=Vm[:, kc, nb * 512:(nb + 1) * 512], in_=pbig[2][:, :]), r=[pbig[2]], w=[Vm])
            for s_ in range(NS):
                for kc in range(2):
                    S.dma("sp", cs, lambda e: e.dma_start(out=cs[:, :], in_=k.cmem[s_, kc * 128:(kc + 1) * 128, :]), w=[cs])
                    S.op("dve", lambda e: e.tensor_copy(out=csb[:, :], in_=cs[:, :]), r=[cs], w=[csb])
                    S.op("pool", lambda e: e.tensor_copy(out=Vs[:, s_, kc, :], in_=csb[:, 1024:2048]), r=[csb], w=[Vs])
                    for half in range(2):
                        for c in range(4):
                            hc = half * 4 + c
                            S.op("pe", lambda e: e.matmul(pbig[3][:, c * 128:(c + 1) * 128], lhsT=csb[:, hc * 128:(hc + 1) * 128], rhs=identb[:, :],
                                                          start=True, stop=True), r=[csb, identb], w=[pbig[3]])
                        for c in range(4):
                            hc = half * 4 + c
                            S.op("dve", lambda e: e.tensor_copy(out=KsT[:, s_, hc, kc * 128:(kc + 1) * 128], in_=pbig[3][:, c * 128:(c + 1) * 128]),
                                 r=[pbig[3]], w=[KsT])
            k.S.barrier()
        wout = load_weight_bf16(k, es, "w_out3", k.w_out, 768, D, col_chunk=1024)
        wmq = load_weight_bf16(k, es, "w_mq3", k.w_mq, D, D, gdram=k.g_memq, col_chunk=1024)
        wmo = load_weight_bf16(k, es, "w_mo3", k.w_mo, D, D, col_chunk=1024)
        wpq = load_weight_bf16(k, es, "w_pq3", k.w_pq, D, 2048, col_chunk=2048)
        skT = A("skT", [128, 16, 128], BF16)
        with ExitStack() as es2:
            A2 = lambda name, shape, dt: k.track(TL(es2.enter_context(nc.sbuf_tensor(name, shape, dt)), name))
            skf = A2("skf", [128, 128], F32); skb = A2("skb", [128, 128], BF16)
            for hp in range(16):
                S.dma("sp", skf, lambda e: e.dma_start(out=skf[:, :], in_=k.sub_keys[hp, :, :]), w=[skf])
                S.op("dve", lambda e: e.tensor_copy(out=skb[:, :], in_=skf[:, :]), r=[skf], w=[skb])
                S.op("pe", lambda e: e.matmul(pbig[0][:, 0:128], lhsT=skb[:, :], rhs=identb[:, :], start=True, stop=True), r=[skb, identb], w=[pbig[0]])
                S.op("dve", lambda e: e.tensor_copy(out=skT[:, hp, :], in_=pbig[0][:, 0:128]), r=[pbig[0]], w=[skT])
            k.S.barrier()
        gffn = A("gffn", [128, D], F32); gfin = A("gfin", [128, D], F32)
        S.dma("sp", gffn, lambda e: e.dma_start(out=gffn[:, :], in_=k.g_ffn.ap().partition_broadcast(128)), w=[gffn])
        S.dma("sp", gfin, lambda e: e.dma_start(out=gfin[:, :], in_=k.g_final.ap().partition_broadcast(128)), w=[gfin])
        LOHI_INIT = True
        xt = A("x3", [128, D], F32); cat = A("cat3", [128, 768], BF16); sw = [A(f"sw3_{g}", [128, 260], F32) for g in range(3)]
        rden = A("rden3", [128, 4], F32); catT = A("catT3", [128, 6, 128], BF16)
        h = A("h3", [128, D], F32); sqj = A("sqj3", [128, D], BF16); ssq = A("ssq3", [128, 1], F32); rstd = A("rstd3", [128, 1], F32)
        cb = A("cb3", [128, D], BF16); cT = A("cT3", [128, 8, 128], BF16)
        qmT = A("qmT3", [128, 8, 128], BF16); PTm = A("PTm3", [128, 256], BF16); rdm = A("rdm3", [128, 128], F32)
        attT = A("attT3", [128, 8, 128], BF16)
        fb = A("fb3", [128, D], BF16); fT = A("fT3", [128, 8, 128], BF16)
        qpT = A("qpT3", [128, 16, 128], BF16); sc = A("sc3", [128, 16, 128], F32); sc2 = A("sc23", [128, 128], F32)
        mv = A("mv3", [128, 16, 16], F32); mi = A("mi3", [128, 16, 16], U32); mif = A("mif3", [128, 16, 16], F32)
        cand = A("cand3", [128, 256], F32); cand2 = A("cand23", [128, 256], F32)
        cv = A("cv3", [128, 8, 16], F32); ci = A("ci3", [128, 8, 16], U32); cif = A("cif3", [128, 8, 16], F32)
        ia = A("ia3", [128, 8, 16], F32); ib = A("ib3", [128, 8, 16], F32)
        oh = A("oh3", [128, 16, 16], F32); lo16 = A("lo163", [128, 16], F32); hi16 = A("hi163", [128, 16], F32); i1 = A("i13", [128, 8, 16], F32); i2 = A("i23", [128, 8, 16], F32)
        eidf = A("eidf3", [128, 128], F32); eid = A("eid3", [128, 128], I32)
        gate = A("gate3", [128, 8, 16], F32); gsum = A("gsum3", [128, 8], F32)
        hid = A("hid3", [128, 128], F32); hx = A("hx3", [128, 128], F32); wgt = A("wgt3", [128, 128], F32)
        NB = 2
        GBr = [A(f"GB{i}", [128, 2 * D], BF16) for i in range(2)]
        GB = list(GBr)
        if not os.environ.get("P3NOSMP"):
            for s_ in range(NS):
                GB.append(TLview(Vs, (lambda s_=s_: Vs[:, s_, :, :].rearrange("p a b -> p (a b)")), f"GBv{s_}"))
                GB.append(TLview(KsT, (lambda s_=s_: KsT[:, s_, :, :].rearrange("p a b -> p (a b)")), f"GBk{s_}"))
        NBG = len(GB)
        par = lambda t_: [t_.p] if isinstance(t_, TLview) else []
        prod = [sqj, cb]
        junk = sqj; dg = [A(f"dg{i}", [128, 128], BF16) for i in range(2)]
        yo = xt
        pout = [pb.get(f"pout{i}", 512, F32, f"out{i}") for i in range(2)]
        psm = pb.get("psm", 256, F32, "sm"); pden = pb.get("pden", 128, F32, "den")

        S.op("dve", lambda e: e.tensor_scalar(out=lo16[:, :], in0=iota[:, 0:16], scalar1=16.0, scalar2=None, op0=ALU.mult), r=[iota], w=[lo16])
        S.op("dve", lambda e: e.tensor_scalar(out=hi16[:, :], in0=iota[:, 0:16], scalar1=16.0, scalar2=16.0, op0=ALU.mult, op1=ALU.add), r=[iota], w=[hi16])

        def transposes(src_t, nchunk, dstT):
            for c0 in range(0, nchunk, 4):
                n_ = min(4, nchunk - c0)
                for c in range(n_):
                    S.op("pe", lambda e: e.matmul(pbig[0][:, c * 128:(c + 1) * 128], lhsT=src_t[:, (c0 + c) * 128:(c0 + c + 1) * 128], rhs=identb[:, :],
                                                  start=True, stop=True), r=[src_t, identb], w=[pbig[0]])
                for c in range(n_):
                    S.op("dve", lambda e: e.tensor_copy(out=dstT[:, c0 + c, :], in_=pbig[0][:, c * 128:(c + 1) * 128]), r=[pbig[0]], w=[dstT])

        def rmsn(src, out_bf, gvec=None):
            S.op("act", lambda e: e.activation(out=sqj[:, :], in_=src[:, :], func=AF.Square, accum_out=ssq[:, 0:1]), r=[src], w=[sqj, ssq])
            rsqrt(S, rstd, rstd[:, :], ssq, ssq[:, :], 1.0 / D, EPS)
            if gvec is None:
                S.op("act", lambda e: e.activation(out=out_bf[:, :], in_=src[:, :], func=AF.Copy, scale=rstd[:, 0:1]), r=[src, rstd], w=[out_bf])
            else:
                S.op("dve", lambda e: e.scalar_tensor_tensor(out=out_bf[:, :], in0=src[:, :], scalar=rstd[:, 0:1], in1=gvec[:, :],
                                                             op0=ALU.mult, op1=ALU.mult), r=[src, rstd, gvec], w=[out_bf])

        def top16(vals_t, vals_ap, scratch_t, scratch_ap, mv_ap, mi_ap, mv_t, mi_t):
            S.op("dve", lambda e: e.max(out=mv_ap[:, 0:8], in_=vals_ap), r=[vals_t], w=[mv_t])
            S.op("dve", lambda e: e.max_index(out=mi_ap[:, 0:8], in_max=mv_ap[:, 0:8], in_values=vals_ap), r=[vals_t, mv_t], w=[mi_t])
            S.op("dve", lambda e: e.match_replace(out=scratch_ap, in_to_replace=mv_ap[:, 0:8], in_values=vals_ap, imm_value=-1e30),
                 r=[vals_t, mv_t], w=[scratch_t])
            S.op("dve", lambda e: e.max(out=mv_ap[:, 8:16], in_=scratch_ap), r=[scratch_t], w=[mv_t])
            S.op("dve", lambda e: e.max_index(out=mi_ap[:, 8:16], in_max=mv_ap[:, 8:16], in_values=scratch_ap), r=[scratch_t, mv_t], w=[mi_t])

        tiles = ([] if os.environ.get("P3NOSMP") else ["smp"]) + list(range(int(os.environ.get("P3TILES", str(MAIN // 128)))))
        for tau in tiles:
            smp = tau == "smp"
            if smp:
                S.op("dve", lambda e: e.memset(xt[:, :], 0.0), w=[xt])
                S.op("pool", lambda e: e.memset(cat[:, :], 0.0), w=[cat])
                for g in range(3):
                    S.op("pool", lambda e: e.memset(sw[g][:, :], 1.0), w=[sw[g]])
                    S.dma("sp", sw[g], lambda e: e.dma_start(out=sw[g][0:NT, :], in_=k.smp_swo[:, g, :]), r=[k.smp_swo], w=[sw[g]])
                S.dma("sp", xt, lambda e: e.dma_start(out=xt[0:NT, :], in_=k.xs[:, :]), w=[xt])
                S.dma("sp", cat, lambda e: e.dma_start(out=cat[0:NT, 0:512], in_=k.smp_cat[:, :]), r=[k.smp_cat], w=[cat])
            else:
                t0 = tau * 128
                S.dma("sp", xt, lambda e: e.dma_start(out=xt[:, :], in_=k.xe[PRE + t0:PRE + t0 + 128, :]), w=[xt])
                S.dma("sp", cat, lambda e: e.dma_start(out=cat[:, 0:512], in_=k.cat[t0:t0 + 128, :]), w=[cat])
                for g in range(3):
                    S.dma("sp", sw[g], lambda e: e.dma_start(out=sw[g][:, :], in_=k.swo[g][t0:t0 + 128, :]), w=[sw[g]])
            S.op("dve", lambda e: e.tensor_tensor(out=sw[0][:, :], in0=sw[0][:, :], in1=sw[1][:, :], op=ALU.add), r=[sw[0], sw[1]], w=[sw[0]])
            S.op("dve", lambda e: e.tensor_tensor(out=sw[0][:, :], in0=sw[0][:, :], in1=sw[2][:, :], op=ALU.add), r=[sw[0], sw[2]], w=[sw[0]])
            for hd in range(4):
                S.op("dve", lambda e: e.reciprocal(out=rden[:, hd:hd + 1], in_=sw[0][:, hd * 65 + 64:hd * 65 + 65]), r=[sw[0]], w=[rden])
                S.op("dve", lambda e: e.tensor_scalar(out=cat[:, 512 + hd * 64:512 + (hd + 1) * 64], in0=sw[0][:, hd * 65:hd * 65 + 64],
                                                      scalar1=rden[:, hd:hd + 1], scalar2=None, op0=ALU.mult), r=[sw[0], rden], w=[cat])
            transposes(cat, 6, catT)
            for nb in range(2):
                for c in range(6):
                    S.op("pe", lambda e: e.matmul(pout[nb][:, :], lhsT=catT[:, c, :], rhs=wout[:, c, nb * 512:(nb + 1) * 512],
                                                  start=(c == 0), stop=(c == 5)), r=[catT, wout], w=[pout[nb]])
                S.op("dve", lambda e: e.tensor_tensor(out=h[:, nb * 512:(nb + 1) * 512], in0=pout[nb][:, :], in1=xt[:, nb * 512:(nb + 1) * 512], op=ALU.add),
                     r=[pout[nb], xt], w=[h])
            rmsn(h, cb)
            transposes(cb, 8, cT)
            for half in range(2):
                for c4 in range(4):
                    hc = half * 4 + c4
                    for c in range(8):
                        S.op("pe", lambda e: e.matmul(pbig[1][:, c4 * 128:(c4 + 1) * 128], lhsT=wmq[:, c, hc * 128:(hc + 1) * 128], rhs=cT[:, c, :],
                                                      start=(c == 0), stop=(c == 7)), r=[wmq, cT], w=[pbig[1]])
                for c4 in range(4):
                    hc = half * 4 + c4
                    S.op("dve", lambda e: e.tensor_copy(out=qmT[:, hc, :], in_=pbig[1][:, c4 * 128:(c4 + 1) * 128]), r=[pbig[1]], w=[qmT])
            segs = [(s_ * TS, TS, s_) for s_ in range(NS)] if smp else [(0, 128, None)]
            if smp:
                S.op("pool", lambda e: e.memset(attT[:, :, :], 0.0), w=[attT])
            for hd in range(4):
                for (q0, qn, s_) in segs:
                    kT_ap = (lambda hc, kc: KmT[:, hc, kc * 128:(kc + 1) * 128]) if s_ is None else (lambda hc, kc: KsT[:, s_, hc, kc * 128:(kc + 1) * 128])
                    v_ap = (lambda kc, col: Vm[:, kc, col:col + 128]) if s_ is None else (lambda kc, col: Vs[:, s_, kc, col:col + 128])
                    kt_t, v_t = (KmT, Vm) if s_ is None else (KsT, Vs)
                    for kc in range(2):
                        for cc in range(2):
                            S.op("pe", lambda e: e.matmul(psm[:, kc * 128:kc * 128 + qn], lhsT=kT_ap(hd * 2 + cc, kc), rhs=qmT[:, hd * 2 + cc, q0:q0 + qn],
                                                          start=(cc == 0), stop=(cc == 1)), r=[kt_t, qmT], w=[psm])
                    for kc in range(2):
                        S.op("act", lambda e: e.activation(out=PTm[:, kc * 128:kc * 128 + qn], in_=psm[:, kc * 128:kc * 128 + qn], func=AF.Exp, scale=1.0 / 16),
                             r=[psm], w=[PTm])
                    for kc in range(2):
                        S.op("pe", lambda e: e.matmul(pden[:, 0:qn], lhsT=onesb[:, :], rhs=PTm[:, kc * 128:kc * 128 + qn], start=(kc == 0), stop=(kc == 1)),
                             r=[onesb, PTm], w=[pden])
                    S.op("dve", lambda e: e.reciprocal(out=rdm[:, 0:qn], in_=pden[:, 0:qn]), r=[pden], w=[rdm])
                    for cc in range(2):
                        for kc in range(2):
                            S.op("pe", lambda e: e.matmul(pbig[2][:, cc * 128:cc * 128 + qn], lhsT=v_ap(kc, hd * 256 + cc * 128), rhs=PTm[:, kc * 128:kc * 128 + qn],
                                                          start=(kc == 0), stop=(kc == 1)), r=[v_t, PTm], w=[pbig[2]])
                    for cc in range(2):
                        S.op("dve", lambda e: e.tensor_tensor(out=attT[:, hd * 2 + cc, q0:q0 + qn], in0=pbig[2][:, cc * 128:cc * 128 + qn], in1=rdm[:, 0:qn], op=ALU.mult),
                             r=[pbig[2], rdm], w=[attT])
            for nb in range(2):
                for c in range(8):
                    S.op("pe", lambda e: e.matmul(pout[nb][:, :], lhsT=attT[:, c, :], rhs=wmo[:, c, nb * 512:(nb + 1) * 512],
                                                  start=(c == 0), stop=(c == 7)), r=[attT, wmo], w=[pout[nb]])
                S.op("dve", lambda e: e.tensor_tensor(out=h[:, nb * 512:(nb + 1) * 512], in0=pout[nb][:, :], in1=h[:, nb * 512:(nb + 1) * 512], op=ALU.add),
                     r=[pout[nb], h], w=[h])
            rmsn(h, fb, gffn)
            transposes(fb, 8, fT)
            for q4 in range(4):
                for c4 in range(4):
                    hp = q4 * 4 + c4
                    for c in range(8):
                        S.op("pe", lambda e: e.matmul(pbig[1][:, c4 * 128:(c4 + 1) * 128], lhsT=wpq[:, c, hp * 128:(hp + 1) * 128], rhs=fT[:, c, :],
                                                      start=(c == 0), stop=(c == 7)), r=[wpq, fT], w=[pbig[1]])
                for c4 in range(4):
                    hp = q4 * 4 + c4
                    S.op("dve", lambda e: e.tensor_copy(out=qpT[:, hp, :], in_=pbig[1][:, c4 * 128:(c4 + 1) * 128]), r=[pbig[1]], w=[qpT])
            for q4 in range(4):
                for c4 in range(4):
                    hp = q4 * 4 + c4
                    S.op("pe", lambda e: e.matmul(pbig[3][:, c4 * 128:(c4 + 1) * 128], lhsT=qpT[:, hp, :], rhs=skT[:, hp, :], start=True, stop=True),
                         r=[qpT, skT], w=[pbig[3]])
                for c4 in range(4):
                    hp = q4 * 4 + c4
                    S.op("dve", lambda e: e.tensor_copy(out=sc[:, hp, :], in_=pbig[3][:, c4 * 128:(c4 + 1) * 128]), r=[pbig[3]], w=[sc])
            for hp in range(16):
                top16(sc, sc[:, hp, :], sc2, sc2[:, :], mv[:, hp, :], mi[:, hp, :], mv, mi)
            S.op("dve", lambda e: e.tensor_copy(out=mif[:, :, :], in_=mi[:, :, :]), r=[mi], w=[mif])
            for hd in range(8):
                S.op("dve", lambda e: e.tensor_tensor(out=cand[:, :].rearrange("p (a b) -> p a b", a=16),
                                                      in0=mv[:, 2 * hd, :].unsqueeze(2).to_broadcast([128, 16, 16]),
                                                      in1=mv[:, 2 * hd + 1, :].unsqueeze(1).to_broadcast([128, 16, 16]), op=ALU.add), r=[mv], w=[cand])
                top16(cand, cand[:, :], cand2, cand2[:, :], cv[:, hd, :], ci[:, hd, :], cv, ci)
            S.op("dve", lambda e: e.tensor_copy(out=cif[:, :, :], in_=ci[:, :, :]), r=[ci], w=[cif])
            for hd in range(8):
                cb_ = cif[:, hd, :].unsqueeze(2).to_broadcast([128, 16, 16])
                S.op("dve", lambda e: e.tensor_tensor(out=oh[:, :, :], in0=cb_, in1=lo16[:, :].unsqueeze(1).to_broadcast([128, 16, 16]), op=ALU.is_ge),
                     r=[cif, lo16], w=[oh])
                S.op("dve", lambda e: e.tensor_tensor(out=cand2[:, :].rearrange("p (a b) -> p a b", a=16), in0=cb_, in1=hi16[:, :].unsqueeze(1).to_broadcast([128, 16, 16]), op=ALU.is_lt),
                     r=[cif, hi16], w=[cand2])
                S.op("dve", lambda e: e.tensor_tensor(out=oh[:, :, :], in0=oh[:, :, :], in1=cand2[:, :].rearrange("p (a b) -> p a b", a=16), op=ALU.mult), r=[oh, cand2], w=[oh])
                S.op("dve", lambda e: e.tensor_tensor(out=cand2[:, :].rearrange("p (a b) -> p a b", a=16), in0=oh[:, :, :], in1=mif[:, 2 * hd, :].unsqueeze(1).to_broadcast([128, 16, 16]), op=ALU.mult),
                     r=[oh, mif], w=[cand2])
                S.op("dve", lambda e: e.tensor_reduce(out=i1[:, hd, :], in_=cand2[:, :].rearrange("p (a b) -> p a b", a=16), axis=AX.X, op=ALU.add), r=[cand2], w=[i1])
                S.op("dve", lambda e: e.tensor_tensor(out=cand2[:, :].rearrange("p (a b) -> p a b", a=16), in0=oh[:, :, :], in1=lo16[:, :].unsqueeze(1).to_broadcast([128, 16, 16]), op=ALU.mult),
                     r=[oh, lo16], w=[cand2])
                S.op("dve", lambda e: e.tensor_reduce(out=ia[:, hd, :], in_=cand2[:, :].rearrange("p (a b) -> p a b", a=16), axis=AX.X, op=ALU.add), r=[cand2], w=[ia])
            S.op("dve", lambda e: e.tensor_tensor(out=ib[:, :, :], in0=cif[:, :, :], in1=ia[:, :, :], op=ALU.subtract), r=[cif, ia], w=[ib])
            for hd in range(8):
                S.op("dve", lambda e: e.tensor_tensor(out=oh[:, :, :], in0=ib[:, hd, :].unsqueeze(2).to_broadcast([128, 16, 16]),
                                                      in1=iota[:, 0:16].unsqueeze(1).to_broadcast([128, 16, 16]), op=ALU.is_equal), r=[ib, iota], w=[oh])
                S.op("dve", lambda e: e.tensor_tensor(out=oh[:, :, :], in0=oh[:, :, :], in1=mif[:, 2 * hd + 1, :].unsqueeze(1).to_broadcast([128, 16, 16]), op=ALU.mult),
                     r=[oh, mif], w=[oh])
                S.op("dve", lambda e: e.tensor_reduce(out=i2[:, hd, :], in_=oh[:, :, :], axis=AX.X, op=ALU.add), r=[oh], w=[i2])
            S.op("dve", lambda e: e.scalar_tensor_tensor(out=eidf[:, :], in0=i1[:, :, :].rearrange("p a b -> p (a b)"), scalar=128.0,
                                                         in1=i2[:, :, :].rearrange("p a b -> p (a b)"), op0=ALU.mult, op1=ALU.add), r=[i1, i2], w=[eidf])
            S.op("dve", lambda e: e.tensor_copy(out=eid[:, :], in_=eidf[:, :]), r=[eidf], w=[eid])
            for hd in range(8):
                S.op("dve", lambda e: e.tensor_scalar(out=gate[:, hd, :], in0=cv[:, hd, :], scalar1=cv[:, hd, 0:1], scalar2=None, op0=ALU.subtract),
                     r=[cv], w=[gate])
            S.op("act", lambda e: e.activation(out=gate[:, :, :].rearrange("p a b -> p (a b)"), in_=gate[:, :, :].rearrange("p a b -> p (a b)"), func=AF.Exp),
                 r=[gate], w=[gate])
            S.op("dve", lambda e: e.tensor_reduce(out=gsum[:, :], in_=gate[:, :, :], axis=AX.X, op=ALU.add), r=[gate], w=[gsum])
            S.op("dve", lambda e: e.reciprocal(out=gsum[:, :], in_=gsum[:, :]), r=[gsum], w=[gsum])
            for hd in range(8):
                S.op("dve", lambda e: e.tensor_scalar(out=gate[:, hd, :], in0=gate[:, hd, :], scalar1=gsum[:, hd:hd + 1], scalar2=None, op0=ALU.mult),
                     r=[gate, gsum], w=[gate])
            GS = 4
            gflat = gate[:, :, :].rearrange("p a b -> p (a b)")
            for grp in range(128 // GS):
                sls = range(grp * GS, (grp + 1) * GS)
                for sl in sls:
                    g_ = GB[sl % NBG]
                    pr_ = prod[sl % 2]
                    S.dma("pool", g_, lambda e: e.indirect_dma_start(out=g_[:, :], out_offset=None, in_=k.euv_bf.ap(),
                                                                     in_offset=bass.IndirectOffsetOnAxis(ap=eid[:, sl:sl + 1], axis=0)),
                          r=[eid] + par(g_), w=[g_])
                    S.op("dve", lambda e: e.tensor_tensor(out=pr_[:, :], in0=g_[:, 0:D], in1=fb[:, :], op=ALU.mult), r=[g_, fb] + par(g_), w=[pr_])
                    S.op("act", lambda e: e.activation(out=xt[:, :], in_=pr_[:, :], func=AF.Copy, accum_out=hid[:, sl:sl + 1]), r=[pr_], w=[xt, hid])
                c0, c1 = grp * GS, (grp + 1) * GS
                hg, xg, wg = hid[:, c0:c1], hx[:, c0:c1], wgt[:, c0:c1]
                S.op("dve", lambda e: e.tensor_tensor(out=xg, in0=hg, in1=hg, op=ALU.mult), r=[hid], w=[hx])
                S.op("dve", lambda e: e.tensor_scalar(out=xg, in0=xg, scalar1=0.044715, scalar2=1.0, op0=ALU.mult, op1=ALU.add), r=[hx], w=[hx])
                S.op("dve", lambda e: e.tensor_tensor(out=xg, in0=xg, in1=hg, op=ALU.mult), r=[hx, hid], w=[hx])
                S.op("act", lambda e: e.activation(out=xg, in_=xg, func=AF.Tanh, scale=0.7978845608028654), r=[hx], w=[hx])
                S.op("dve", lambda e: e.tensor_scalar(out=xg, in0=xg, scalar1=1.0, scalar2=0.5, op0=ALU.add, op1=ALU.mult), r=[hx], w=[hx])
                S.op("dve", lambda e: e.tensor_tensor(out=xg, in0=xg, in1=hg, op=ALU.mult), r=[hx, hid], w=[hx])
                S.op("dve", lambda e: e.tensor_tensor(out=wg, in0=xg, in1=gflat[:, c0:c1], op=ALU.mult), r=[hx, gate], w=[wgt])
                for sl in sls:
                    g_ = GB[sl % NBG]
                    d_ = dg[sl % 2]
                    S.op("dve", lambda e: e.tensor_scalar(out=d_[:, :], in0=identb[:, :], scalar1=wgt[:, sl:sl + 1], scalar2=None, op0=ALU.mult),
                         r=[identb, wgt], w=[d_])
                    for nb in range(2):
                        S.op("pe", lambda e: e.matmul(pout[nb][:, :], lhsT=d_[:, :], rhs=g_[:, D + nb * 512:D + (nb + 1) * 512],
                                                      start=(sl == 0), stop=(sl == 127)), r=[d_, g_] + par(g_), w=[pout[nb]])
            for nb in range(2):
                S.op("dve", lambda e: e.tensor_tensor(out=h[:, nb * 512:(nb + 1) * 512], in0=pout[nb][:, :], in1=h[:, nb * 512:(nb + 1) * 512], op=ALU.add),
                     r=[pout[nb], h], w=[h])
            S.op("act", lambda e: e.activation(out=sqj[:, :], in_=h[:, :], func=AF.Square, accum_out=ssq[:, 0:1]), r=[h], w=[sqj, ssq])
            rsqrt(S, rstd, rstd[:, :], ssq, ssq[:, :], 1.0 / D, EPS)
            S.op("dve", lambda e: e.scalar_tensor_tensor(out=yo[:, :], in0=h[:, :], scalar=rstd[:, 0:1], in1=gfin[:, :], op0=ALU.mult, op1=ALU.mult),
                 r=[h, rstd, gfin], w=[yo])
            if smp:
                S.dma("sp", yo, lambda e: e.dma_start(out=k.y_smp[:, :], in_=yo[0:NT, :]), r=[yo])
            else:
                S.dma("sp", yo, lambda e: e.dma_start(out=k.y_main[tau * 128:(tau + 1) * 128, :], in_=yo[:, :]), r=[yo])
        k.end_phase()
```
